# Optimizing a Trainium2 kernel written in Bass

```python
import math
import jax, jax.numpy as jnp
from jax import lax
import numpy as np

D_MODEL = 1024
BATCH = 4
SEQ = 4096
DEPTH = 1

HEAD_DIM = 64
DIFF_HEADS = 8
DIFF_VDIM = 2 * HEAD_DIM
NSA_HEADS = 16
NSA_GROUPS = 4
NSA_HPG = NSA_HEADS // NSA_GROUPS
CMP_BLOCK = 32
CMP_STRIDE = 16
CMP_HIDDEN = 256
SEL_BLOCK = 64
SEL_TOP = 16
WINDOW = 512
FORCED_SCORE = 1e4
D_FF = 4 * D_MODEL
ROPE_THETA = 10000.0
EPS = 1e-6
Q_BLOCK = 128

DIFF_QK = DIFF_HEADS * 2 * HEAD_DIM
DIFF_V = DIFF_HEADS * DIFF_VDIM
NSA_Q = NSA_HEADS * HEAD_DIM
NSA_KV = NSA_GROUPS * HEAD_DIM
NSA_GATE = NSA_HEADS * 3
MERGE_GATE = 2 * D_MODEL
COLUMN_SPLITS = (DIFF_QK, DIFF_QK, DIFF_V, NSA_Q, NSA_KV, NSA_KV, NSA_KV, NSA_KV, NSA_KV, NSA_KV, NSA_GATE, MERGE_GATE)
D_IN = DIFF_QK * 2 + DIFF_V + NSA_Q + 6 * NSA_KV + NSA_GATE + MERGE_GATE

kernel_name = "hybrid_diffattn_nsa_gated_block"


def rmsnorm(x, g):
    xf = x.astype(jnp.float32)
    y = xf * lax.rsqrt(jnp.mean(xf * xf, axis=-1, keepdims=True) + EPS)
    return (y * g.astype(jnp.float32)).astype(x.dtype)


def rope(x, pos):
    half = x.shape[-1] // 2
    inv_freq = ROPE_THETA ** (-jnp.arange(half, dtype=jnp.float32) / half)
    ang = pos.astype(jnp.float32)[:, None] * inv_freq[None, :]
    bshape = (1, ang.shape[0]) + (1,) * (x.ndim - 3) + (half,)
    cos = jnp.cos(ang).reshape(bshape)
    sin = jnp.sin(ang).reshape(bshape)
    xf = x.astype(jnp.float32)
    x1, x2 = xf[..., :half], xf[..., half:]
    return jnp.concatenate([x1 * cos - x2 * sin, x2 * cos + x1 * sin], axis=-1).astype(x.dtype)


def masked_softmax(s, mask):
    s = jnp.where(mask, s.astype(jnp.float32), -jnp.inf)
    m = jnp.max(s, axis=-1, keepdims=True)
    m = jnp.where(jnp.isfinite(m), m, 0.0)
    e = jnp.where(mask, jnp.exp(s - m), 0.0)
    return e / jnp.maximum(jnp.sum(e, axis=-1, keepdims=True), 1e-30)


def diff_attention(q, k, v, lam, lambda_init, subln_g):
    B, S, H, _, Dh = q.shape
    scale = Dh ** -0.5
    kpos = jnp.arange(S)

    def block(i):
        q0 = i * Q_BLOCK
        qb = lax.dynamic_slice_in_dim(q, q0, Q_BLOCK, axis=1)
        s = jnp.einsum('bqhcd,bkhcd->bhcqk', qb, k).astype(jnp.float32) * scale
        qpos = q0 + jnp.arange(Q_BLOCK)
        causal = kpos[None, :] <= qpos[:, None]
        p = jax.nn.softmax(jnp.where(causal, s, -jnp.inf), axis=-1)
        a = p[:, :, 0] - lam * p[:, :, 1]
        return jnp.einsum('bhqk,bkhe->bqhe', a.astype(v.dtype), v)

    o = lax.map(block, jnp.arange(S // Q_BLOCK))
    o = o.transpose(1, 0, 2, 3, 4).reshape(B, S, H, -1)
    o = rmsnorm(o, subln_g) * (1.0 - lambda_init)
    return o.reshape(B, S, H * o.shape[-1])


def compress(kv, pos_emb, w1, w2):
    B, S, G, Dh = kv.shape
    n_cmp = (S - CMP_BLOCK) // CMP_STRIDE + 1
    idx = jnp.arange(n_cmp)[:, None] * CMP_STRIDE + jnp.arange(CMP_BLOCK)[None, :]
    blocks = kv[:, idx] + pos_emb[None, None, :, None, :]
    blocks = blocks.transpose(0, 1, 3, 2, 4).reshape(B, n_cmp, G, CMP_BLOCK * Dh)
    return jax.nn.gelu(blocks @ w1) @ w2


def nsa_attention(q, k_cmp, v_cmp, k_slc, v_slc, k_win, v_win, gates):
    B, S, Hq, Dh = q.shape
    G, Hg = NSA_GROUPS, NSA_HPG
    scale = Dh ** -0.5
    dt = v_slc.dtype
    n_cmp = k_cmp.shape[1]
    n_sel = S // SEL_BLOCK
    n_top = min(SEL_TOP, n_sel)
    cmp_start = jnp.arange(n_cmp) * CMP_STRIDE
    cmp_end = cmp_start + CMP_BLOCK - 1
    sel_start = jnp.arange(n_sel) * SEL_BLOCK
    sel_ids = jnp.arange(n_sel)
    overlap = ((cmp_start[:, None] < sel_start[None, :] + SEL_BLOCK)
               & (cmp_end[:, None] >= sel_start[None, :])).astype(jnp.float32)
    k_blk = k_slc.reshape(B, n_sel, SEL_BLOCK, G, Dh).transpose(0, 3, 1, 2, 4)
    v_blk = v_slc.reshape(B, n_sel, SEL_BLOCK, G, Dh).transpose(0, 3, 1, 2, 4)
    kw_pad = jnp.pad(k_win, ((0, 0), (WINDOW, 0), (0, 0), (0, 0)))
    vw_pad = jnp.pad(v_win, ((0, 0), (WINDOW, 0), (0, 0), (0, 0)))
    b_idx = jnp.arange(B)[:, None, None, None]
    g_idx = jnp.arange(G)[None, :, None, None]

    def block(i):
        q0 = i * Q_BLOCK
        qpos = q0 + jnp.arange(Q_BLOCK)
        qb = lax.dynamic_slice_in_dim(q, q0, Q_BLOCK, axis=1).reshape(B, Q_BLOCK, G, Hg, Dh)
        gb = lax.dynamic_slice_in_dim(gates, q0, Q_BLOCK, axis=1)
        s = jnp.einsum('bqghd,bngd->bghqn', qb, k_cmp) * scale
        p_cmp = masked_softmax(s, cmp_end[None, :] <= qpos[:, None])
        o_cmp = jnp.einsum('bghqn,bngd->bqghd', p_cmp.astype(dt), v_cmp)
        imp = jnp.einsum('bghqn,nj->bgqj', p_cmp, overlap)
        cur = qpos // SEL_BLOCK
        forced = (sel_ids[None, :] == 0) | (sel_ids[None, :] == cur[:, None]) | (sel_ids[None, :] == cur[:, None] - 1)
        imp = jnp.where(forced, FORCED_SCORE, imp)
        imp = jnp.where(sel_start[None, :] <= qpos[:, None], imp, -jnp.inf)
        top_val, top_idx = lax.top_k(imp, n_top)
        sel_ok = jnp.isfinite(top_val)
        ks = k_blk[b_idx, g_idx, top_idx]
        vs = v_blk[b_idx, g_idx, top_idx]
        s = jnp.einsum('bqghd,bgqnld->bghqnl', qb, ks) * scale
        tok_pos = top_idx[..., None] * SEL_BLOCK + jnp.arange(SEL_BLOCK)
        m = sel_ok[..., None] & (tok_pos <= qpos[None, None, :, None, None])
        p = masked_softmax(s.reshape(B, G, Hg, Q_BLOCK, n_top * SEL_BLOCK),
                           m.reshape(B, G, 1, Q_BLOCK, n_top * SEL_BLOCK))
        p = p.reshape(B, G, Hg, Q_BLOCK, n_top, SEL_BLOCK)
        o_slc = jnp.einsum('bghqnl,bgqnld->bqghd', p.astype(dt), vs)
        kw = lax.dynamic_slice_in_dim(kw_pad, q0, WINDOW + Q_BLOCK, axis=1)
        vw = lax.dynamic_slice_in_dim(vw_pad, q0, WINDOW + Q_BLOCK, axis=1)
        kwpos = q0 - WINDOW + jnp.arange(WINDOW + Q_BLOCK)
        dist = qpos[:, None] - kwpos[None, :]
        s = jnp.einsum('bqghd,bkgd->bghqk', qb, kw) * scale
        p = masked_softmax(s, (dist >= 0) & (dist < WINDOW) & (kwpos[None, :] >= 0))
        o_win = jnp.einsum('bghqk,bkgd->bqghd', p.astype(dt), vw)
        shp = (B, Q_BLOCK, Hq, Dh)
        return (gb[..., 0:1] * o_cmp.reshape(shp) + gb[..., 1:2] * o_slc.reshape(shp)
                + gb[..., 2:3] * o_win.reshape(shp))

    o = lax.map(block, jnp.arange(S // Q_BLOCK))
    return o.transpose(1, 0, 2, 3, 4).reshape(B, S, Hq * Dh)


def setup_inputs(seed: int = 0) -> dict:
    key = jax.random.key(seed)
    k = jax.random.split(key, 26)
    f32 = jnp.float32
    nrm = lambda kk, shape, sc: jax.random.normal(kk, shape, f32) * sc
    gain = lambda kk, shape: 1.0 + 0.02 * jax.random.normal(kk, shape, f32)
    L = DEPTH
    return {
        "x": jax.random.normal(k[0], (BATCH, SEQ, D_MODEL), f32),
        "ln_mix_g": gain(k[1], (L, D_MODEL)),
        "w_in": nrm(k[2], (L, D_MODEL, D_IN), D_MODEL ** -0.5),
        "diff_q_norm_g": gain(k[3], (L, HEAD_DIM)),
        "diff_k_norm_g": gain(k[4], (L, HEAD_DIM)),
        "diff_lambda_q1": nrm(k[5], (L, HEAD_DIM), 0.1),
        "diff_lambda_k1": nrm(k[6], (L, HEAD_DIM), 0.1),
        "diff_lambda_q2": nrm(k[7], (L, HEAD_DIM), 0.1),
        "diff_lambda_k2": nrm(k[8], (L, HEAD_DIM), 0.1),
        "diff_subln_g": gain(k[9], (L, DIFF_VDIM)),
        "nsa_q_norm_g": gain(k[10], (L, HEAD_DIM)),
        "nsa_k_norm_g": gain(k[11], (L, 3, HEAD_DIM)),
        "cmp_pos_k": nrm(k[12], (L, CMP_BLOCK, HEAD_DIM), 0.1),
        "cmp_pos_v": nrm(k[13], (L, CMP_BLOCK, HEAD_DIM), 0.1),
        "cmp_k_w1": nrm(k[14], (L, CMP_BLOCK * HEAD_DIM, CMP_HIDDEN), (CMP_BLOCK * HEAD_DIM) ** -0.5),
        "cmp_k_w2": nrm(k[15], (L, CMP_HIDDEN, HEAD_DIM), CMP_HIDDEN ** -0.5),
        "cmp_v_w1": nrm(k[16], (L, CMP_BLOCK * HEAD_DIM, CMP_HIDDEN), (CMP_BLOCK * HEAD_DIM) ** -0.5),
        "cmp_v_w2": nrm(k[17], (L, CMP_HIDDEN, HEAD_DIM), CMP_HIDDEN ** -0.5),
        "w_proj_diff": nrm(k[18], (L, DIFF_V, D_MODEL), DIFF_V ** -0.5),
        "w_proj_nsa": nrm(k[19], (L, NSA_Q, D_MODEL), NSA_Q ** -0.5),
        "w_out": nrm(k[20], (L, D_MODEL, D_MODEL), D_MODEL ** -0.5),
        "ln_mlp_g": gain(k[21], (L, D_MODEL)),
        "w_mlp_up": nrm(k[22], (L, D_MODEL, D_FF), D_MODEL ** -0.5),
        "w_mlp_down": nrm(k[23], (L, D_FF, D_MODEL), D_FF ** -0.5),
    }


def reference(x, ln_mix_g, w_in, diff_q_norm_g, diff_k_norm_g, diff_lambda_q1, diff_lambda_k1,
              diff_lambda_q2, diff_lambda_k2, diff_subln_g, nsa_q_norm_g, nsa_k_norm_g,
              cmp_pos_k, cmp_pos_v, cmp_k_w1, cmp_k_w2, cmp_v_w1, cmp_v_w2,
              w_proj_diff, w_proj_nsa, w_out, ln_mlp_g, w_mlp_up, w_mlp_down):
    B, S, D = x.shape
    G = NSA_GROUPS
    pos = jnp.arange(S)
    n_cmp = (S - CMP_BLOCK) // CMP_STRIDE + 1
    cmp_center = jnp.arange(n_cmp) * CMP_STRIDE + (CMP_BLOCK - 1) / 2.0
    split_at = [int(c) for c in np.cumsum(COLUMN_SPLITS)[:-1]]
    for l in range(DEPTH):
        h = rmsnorm(x, ln_mix_g[l])
        proj = h @ w_in[l]
        (dq, dk, dv, nq, kc, vc, ksl, vsl, kwn, vwn, ng, mg) = jnp.split(proj, split_at, axis=-1)
        dq = rope(rmsnorm(dq.reshape(B, S, DIFF_HEADS, 2, HEAD_DIM), diff_q_norm_g[l]), pos)
        dk = rope(rmsnorm(dk.reshape(B, S, DIFF_HEADS, 2, HEAD_DIM), diff_k_norm_g[l]), pos)
        dv = dv.reshape(B, S, DIFF_HEADS, DIFF_VDIM)
        lambda_init = 0.8 - 0.6 * math.exp(-0.3 * l)
        lam = (jnp.exp(jnp.sum(diff_lambda_q1[l].astype(jnp.float32) * diff_lambda_k1[l].astype(jnp.float32)))
               - jnp.exp(jnp.sum(diff_lambda_q2[l].astype(jnp.float32) * diff_lambda_k2[l].astype(jnp.float32)))
               + lambda_init)
        y_diff = diff_attention(dq, dk, dv, lam, lambda_init, diff_subln_g[l])
        nq = rope(rmsnorm(nq.reshape(B, S, NSA_HEADS, HEAD_DIM), nsa_q_norm_g[l]), pos)
        k_cmp = compress(kc.reshape(B, S, G, HEAD_DIM), cmp_pos_k[l], cmp_k_w1[l], cmp_k_w2[l])
        k_cmp = rope(rmsnorm(k_cmp, nsa_k_norm_g[l, 0]), cmp_center)
        v_cmp = compress(vc.reshape(B, S, G, HEAD_DIM), cmp_pos_v[l], cmp_v_w1[l], cmp_v_w2[l])
        k_slc = rope(rmsnorm(ksl.reshape(B, S, G, HEAD_DIM), nsa_k_norm_g[l, 1]), pos)
        k_win = rope(rmsnorm(kwn.reshape(B, S, G, HEAD_DIM), nsa_k_norm_g[l, 2]), pos)
        nsa_gates = jax.nn.sigmoid(ng).reshape(B, S, NSA_HEADS, 3)
        y_nsa = nsa_attention(nq, k_cmp, v_cmp, k_slc, vsl.reshape(B, S, G, HEAD_DIM), k_win,
                              vwn.reshape(B, S, G, HEAD_DIM), nsa_gates)
        merge = jax.nn.sigmoid(mg).reshape(B, S, 2, D)
        mixed = merge[:, :, 0] * (y_diff @ w_proj_diff[l]) + merge[:, :, 1] * (y_nsa @ w_proj_nsa[l])
        x = x + mixed @ w_out[l]
        h = rmsnorm(x, ln_mlp_g[l])
        x = x + jnp.square(jax.nn.relu(h @ w_mlp_up[l])) @ w_mlp_down[l]
    return x
```

```python
import contextlib
import numpy as np
import ml_dtypes
import concourse.bass as bass
import concourse.mybir as mybir
from concourse.bass_utils import run_bass_kernel_spmd

F32 = mybir.dt.float32
BF16 = mybir.dt.bfloat16
ALU = mybir.AluOpType
AF = mybir.ActivationFunctionType
AX = mybir.AxisListType
NPBF = ml_dtypes.bfloat16

NDMASEM = 4
EPS = 1e-6
NEG = -30000.0


class Phase:
    ENGS = ("pe", "act", "dve", "pool", "sp")

    def __init__(self, nc, name):
        self.nc = nc
        self.name = name
        self.ops = []

    def op(self, eng, fn, reads=(), writes=(), dma=False):
        self.ops.append((eng, fn, tuple(reads), tuple(writes), dma))

    def dma(self, out, in_, reads=(), writes=(), eng="sp"):
        self.op(eng, lambda e: e.dma_start(out=out, in_=in_), reads, writes, dma=True)

    def emit(self):
        nc = self.nc
        ops = self.ops
        cnt = {e: 0 for e in self.ENGS}
        dcnt = {e: 0 for e in self.ENGS}
        info = []
        last_w = {}
        readers = {}
        deps = []
        for i, (eng, fn, rd, wr, dma) in enumerate(ops):
            d = set()
            for k in rd:
                if k in last_w:
                    d.add(last_w[k])
            for k in wr:
                if k in last_w:
                    d.add(last_w[k])
                for r in readers.get(k, ()):
                    d.add(r)
            d.discard(i)
            deps.append(d)
            for k in rd:
                readers.setdefault(k, []).append(i)
            for k in wr:
                last_w[k] = i
                readers[k] = []
            if dma:
                info.append((eng, "d", dcnt[eng]))
                dcnt[eng] += 1
            else:
                info.append((eng, "c", cnt[eng]))
                cnt[eng] += 1
        with contextlib.ExitStack() as st:
            csem = {e: st.enter_context(nc.semaphore(f"{self.name}_c_{e}")) for e in self.ENGS}
            dsem = {e: [st.enter_context(nc.semaphore(f"{self.name}_d_{e}{j}")) for j in range(NDMASEM)]
                    for e in self.ENGS if dcnt[e] > 0}
            block = st.enter_context(nc.Block())
            per_eng = {e: [i for i, o in enumerate(ops) if o[0] == e] for e in self.ENGS}

            def make(eng_name):
                def body(eng):
                    waited = {}

                    def wait(key, sem, val):
                        if waited.get(key, 0) >= val:
                            return
                        waited[key] = val
                        eng.wait_ge(sem, val)

                    for i in per_eng[eng_name]:
                        _, fn, rd, wr, dma = ops[i]
                        for p in sorted(deps[i]):
                            pe_, pk, pidx = info[p]
                            if pk == "c":
                                wait(("c", pe_), csem[pe_], pidx + 1)
                            else:
                                slot = pidx % NDMASEM
                                wait(("d", pe_, slot), dsem[pe_][slot], 16 * (pidx // NDMASEM + 1))
                        if dma:
                            didx = info[i][2]
                            slot = didx % NDMASEM
                            if didx >= NDMASEM:
                                wait(("d", eng_name, slot), dsem[eng_name][slot], 16 * (didx // NDMASEM))
                            fn(eng).then_inc(dsem[eng_name][slot], 16)
                        else:
                            fn(eng).then_inc(csem[eng_name], 1)
                    if eng_name in dsem:
                        n = dcnt[eng_name]
                        for slot in range(min(NDMASEM, n)):
                            total = (n - slot + NDMASEM - 1) // NDMASEM
                            wait(("d", eng_name, slot), dsem[eng_name][slot], 16 * total)
                return body

            if per_eng["pe"]:
                block.tensor(make("pe"))
            if per_eng["act"]:
                block.scalar(make("act"))
            if per_eng["dve"]:
                block.vector(make("dve"))
            if per_eng["pool"]:
                block.gpsimd(make("pool"))
            if per_eng["sp"]:
                block.sync(make("sp"))
        self.ops = []


KV_RANGES = [(1024, 2048), (4608, 4864), (5120, 5376), (2048, 3072), (4864, 5120), (5376, 5632),
             (4096, 4352), (4352, 4608)]
Q_RANGES = [(0, 1024), (3072, 4096), (5632, 5680), (5680, 7728)]
NKV = 3584
NQC = 4144


def build(S=4096, stage=99, dbg=False):
    NT = S // 128
    NCH = S // 512
    NSL = NCH // 2
    NOWN = NSL * 4
    SO = NSL * 512
    NCMP = (S - 32) // 16 + 1
    NCT = (NCMP + 127) // 128
    NCP = NCT * 128
    okind = "ExternalOutput" if dbg else "Internal"

    nc = bass.Bass("TRN2", target_bir_lowering=False)

    def din(name, shape, dt=F32):
        return nc.dram_tensor(name, list(shape), dt, kind="ExternalInput").ap()

    def dscr(name, shape, dt=BF16):
        return nc.dram_tensor(name, list(shape), dt, kind=okind).ap()

    xkv = din("xkv", [S, 1024])
    xq = din("xq", [SO, 1024])
    w_in = din("w_in", [1024, 7728])
    w_pd = din("w_pd", [1024, 1024])
    w_pn = din("w_pn", [1024, 1024])
    w_o = din("w_o", [1024, 1024])
    w_up = din("w_up", [1024, 4096])
    w_dn = din("w_dn", [4096, 1024])
    c_w1 = [din("c_w1k", [2048, 256]), din("c_w1v", [2048, 256])]
    c_w2 = [din("c_w2k", [256, 64]), din("c_w2v", [256, 64])]
    pos2 = [din("pos2k", [128, 16]), din("pos2v", [128, 16])]
    g_mix = din("g_mix", [1, 1024])
    g_mlp = din("g_mlp", [1, 1024])
    g_k24 = din("g_k24", [1, 1536])
    g_q32 = din("g_q32", [1, 2048])
    g_kc = din("g_kc", [1, 64])
    g_sub = din("g_sub", [1, 128])
    lam4 = din("lam4", [1, 256])
    rkv_c = din("rkv_c", [S, 32])
    rkv_s = din("rkv_s", [S, 32])
    rq_c = din("rq_c", [SO, 32])
    rq_s = din("rq_s", [SO, 32])
    rc_c = din("rc_c", [NCP, 32])
    rc_s = din("rc_s", [NCP, 32])
    identd = din("ident", [128, 128], BF16)
    eind = din("eind", [64, S], BF16)
    ovl = din("ovl", [NCP, 65], BF16)
    m_dense = din("m_dense", [128, 8, 512], BF16)
    m_win = din("m_win", [128, 12, 512], BF16)
    m_cmp = din("m_cmp", [128, NSL * NCT, 512], BF16)
    forced = din("forced", [SO, 64])
    out = nc.dram_tensor("out", [SO, 1024], F32, kind="ExternalOutput").ap()

    KdT = dscr("KdT", [8, 128, S])
    Vd = dscr("Vd", [S, 1024])
    KsT = dscr("KsT", [256, S])
    KwT = dscr("KwT", [256, S])
    kcT = dscr("kcT", [256, S])
    vcT = dscr("vcT", [256, S])
    Vs = dscr("Vs", [S, 256])
    Vw = dscr("Vw", [S, 256])
    QdT = dscr("QdT", [8, 128, SO])
    QnT = dscr("QnT", [1024, SO])
    BsT = dscr("BsT", [256, SO])
    mgS = dscr("mgS", [SO, 2048], F32)
    x1d = dscr("x1d", [SO, 1024], F32)

    with contextlib.ExitStack() as top:
        def sb(name, shape, dt, stack=top):
            return stack.enter_context(nc.sbuf_tensor(name, list(shape), dt))

        def ps(name, shape, dt, stack):
            return stack.enter_context(nc.psum_tensor(name, list(shape), dt))

        ident = sb("ident_sb", [128, 128], BF16)
        gates = sb("gates", [128, NOWN * 48], F32)
        KcT = sb("KcT", [128, 4 * NCP], BF16)
        Vca = sb("Vca", [128, NCT * 4 * 129], BF16)
        nlam = sb("nlam", [128, 1], F32)
        sgain = sb("sgain", [128, 128], F32)
        ss1 = sb("ss1", [128, 1], F32)
        rs1 = sb("rs1", [128, 1], F32)

        def bc(ap1, n):
            return ap1[0:1, :].broadcast_to([128, n])

        def rmsnorm_rows(P, xt, xkey, gt, hout, hkey, junk, sfx=""):
            P.op("pool", lambda e: e.memset(ss1[:], 0.0), writes=["ss1"])
            P.op("act", lambda e: e.activation(out=junk, in_=xt, func=AF.Square, accum_out=ss1[:]),
                 reads=[xkey, "ss1"], writes=["junk" + sfx, "ss1"])
            P.op("dve", lambda e: e.tensor_scalar(out=ss1[:], in0=ss1[:], scalar1=1.0 / 1024, scalar2=EPS,
                                                  op0=ALU.mult, op1=ALU.add), reads=["ss1"], writes=["ss1"])
            P.op("act", lambda e: e.activation(out=ss1[:], in_=ss1[:], func=AF.Ln), reads=["ss1"], writes=["ss1"])
            P.op("act", lambda e: e.activation(out=rs1[:], in_=ss1[:], func=AF.Exp, scale=-0.5),
                 reads=["ss1"], writes=["rs1"])
            P.op("dve", lambda e: e.scalar_tensor_tensor(out=hout, in0=xt, scalar=rs1[:], in1=gt,
                                                         op0=ALU.mult, op1=ALU.mult),
                 reads=[xkey, "rs1", "gt"], writes=[hkey])

        def normrope(P, src, srckey, U, gain, cos, sin, cskeys, outap, outkey, T, sfx):
            sq, ssu, rsu, xn, t1, t2, t3, t4 = T
            n = U * 64
            s3 = lambda ap: ap.rearrange("p (u d) -> p u d", d=64)
            P.op("act", lambda e: e.activation(out=sq[:, 0:n], in_=src, func=AF.Square),
                 reads=[srckey], writes=["sq" + sfx])
            P.op("dve", lambda e: e.tensor_reduce(out=ssu[:, 0:U], in_=s3(sq[:, 0:n]), axis=AX.X, op=ALU.add),
                 reads=["sq" + sfx], writes=["ssu" + sfx])
            P.op("dve", lambda e: e.tensor_scalar(out=ssu[:, 0:U], in0=ssu[:, 0:U], scalar1=1.0 / 64, scalar2=EPS,
                                                  op0=ALU.mult, op1=ALU.add), reads=["ssu" + sfx], writes=["ssu" + sfx])
            P.op("act", lambda e: e.activation(out=ssu[:, 0:U], in_=ssu[:, 0:U], func=AF.Ln),
                 reads=["ssu" + sfx], writes=["ssu" + sfx])
            P.op("act", lambda e: e.activation(out=rsu[:, 0:U], in_=ssu[:, 0:U], func=AF.Exp, scale=-0.5),
                 reads=["ssu" + sfx], writes=["rsu" + sfx])
            P.op("dve", lambda e: e.tensor_tensor(out=s3(xn[:, 0:n]), in0=s3(src),
                                                  in1=rsu[:, 0:U].unsqueeze(2).to_broadcast([128, U, 64]), op=ALU.mult),
                 reads=[srckey, "rsu" + sfx], writes=["xn" + sfx])
            P.op("pool", lambda e: e.tensor_tensor(out=xn[:, 0:n], in0=xn[:, 0:n], in1=gain, op=ALU.mult),
                 reads=["xn" + sfx, "gains"], writes=["xn" + sfx])
            x3 = s3(xn[:, 0:n])
            o3 = s3(outap)
            cb = cos.unsqueeze(1).to_broadcast([128, U, 32])
            sbb = sin.unsqueeze(1).to_broadcast([128, U, 32])
            h3 = lambda t: t[:, 0:U * 32].rearrange("p (u d) -> p u d", d=32)
            P.op("pool", lambda e: e.tensor_tensor(out=h3(t1), in0=x3[:, :, 0:32], in1=cb, op=ALU.mult),
                 reads=["xn" + sfx] + list(cskeys), writes=["t1" + sfx])
            P.op("pool", lambda e: e.tensor_tensor(out=h3(t2), in0=x3[:, :, 32:64], in1=sbb, op=ALU.mult),
                 reads=["xn" + sfx] + list(cskeys), writes=["t2" + sfx])
            P.op("pool", lambda e: e.tensor_tensor(out=o3[:, :, 0:32], in0=h3(t1), in1=h3(t2), op=ALU.subtract),
                 reads=["t1" + sfx, "t2" + sfx], writes=[outkey])
            P.op("dve", lambda e: e.tensor_tensor(out=h3(t3), in0=x3[:, :, 32:64], in1=cb, op=ALU.mult),
                 reads=["xn" + sfx] + list(cskeys), writes=["t3" + sfx])
            P.op("dve", lambda e: e.tensor_tensor(out=h3(t4), in0=x3[:, :, 0:32], in1=sbb, op=ALU.mult),
                 reads=["xn" + sfx] + list(cskeys), writes=["t4" + sfx])
            P.op("dve", lambda e: e.tensor_tensor(out=o3[:, :, 32:64], in0=h3(t3), in1=h3(t4), op=ALU.add),
                 reads=["t3" + sfx, "t4" + sfx], writes=[outkey])

        def load_w_cols(P, wdst, ranges, ncols, key, stg):
            for c in range(8):
                st_ = stg[c % 2]
                off = 0
                keys = []
                for ri, (a, b) in enumerate(ranges):
                    k = ("stg", c % 2, ri)
                    P.dma(st_[:, off:off + (b - a)], w_in[c * 128:(c + 1) * 128, a:b], writes=[k])
                    keys.append(k)
                    off += b - a
                P.op("pool", lambda e, c=c, st_=st_: e.tensor_copy(out=wdst[:, c * ncols:(c + 1) * ncols],
                                                                   in_=st_[:, 0:ncols]),
                     reads=keys, writes=[key])

        with contextlib.ExitStack() as sa:
            lt = sb("lt", [128, 256], F32, sa)
            pr = sb("pr", [128, 128], F32, sa)
            s2 = sb("s2", [128, 2], F32, sa)
            sgr = sb("sgr", [128, 128], F32, sa)
            P = Phase(nc, "pa")
            P.dma(ident[:], identd, writes=["ident"])
            P.dma(lt[:], bc(lam4, 256), writes=["lt"])
            P.dma(sgr[:], bc(g_sub, 128), writes=["sgr"])
            P.op("dve", lambda e: e.tensor_tensor(out=pr[:, 0:64], in0=lt[:, 0:64], in1=lt[:, 64:128], op=ALU.mult),
                 reads=["lt"], writes=["pr"])
            P.op("dve", lambda e: e.tensor_tensor(out=pr[:, 64:128], in0=lt[:, 128:192], in1=lt[:, 192:256], op=ALU.mult),
                 reads=["lt", "pr"], writes=["pr"])
            P.op("dve", lambda e: e.tensor_reduce(out=s2[:], in_=pr[:].rearrange("p (a d) -> p a d", d=64),
                                                  axis=AX.X, op=ALU.add), reads=["pr"], writes=["s2"])
            P.op("act", lambda e: e.activation(out=s2[:], in_=s2[:], func=AF.Exp), reads=["s2"], writes=["s2"])
            P.op("dve", lambda e: e.tensor_tensor(out=nlam[:], in0=s2[:, 1:2], in1=s2[:, 0:1], op=ALU.subtract),
                 reads=["s2"], writes=["nlam"])
            P.op("dve", lambda e: e.tensor_scalar(out=nlam[:], in0=nlam[:], scalar1=-0.2, scalar2=None, op0=ALU.add),
                 reads=["nlam"], writes=["nlam"])
            P.op("dve", lambda e: e.tensor_scalar(out=sgain[:], in0=sgr[:], scalar1=0.8, scalar2=None, op0=ALU.mult),
                 reads=["sgr"], writes=["sgain"])
            P.emit()

        with contextlib.ExitStack() as sbd:
            wbuf = sb("wbuf", [128, 8 * NQC], BF16, sbd)
            stg = [sb(f"stg{i}", [128, NQC], F32, sbd) for i in range(2)]
            gmix = sb("gmix", [128, 1024], F32, sbd)
            gains = sb("gains", [128, 2048], F32, sbd)
            xts = [sb(f"xt{i}", [128, 1024], F32, sbd) for i in range(2)]
            junk = sb("junk", [128, 1024], F32, sbd)
            hb = sb("hb", [128, 1024], BF16, sbd)
            hT = [sb(f"hT{i}", [128, 1024], BF16, sbd) for i in range(2)]
            cst = [sb(f"cs{i}", [128, 64], F32, sbd) for i in range(2)]
            TT = []
            for i in range(2):
                TT.append((sb(f"sq{i}", [128, 512], F32, sbd), sb(f"ssu{i}", [128, 8], F32, sbd),
                           sb(f"rsu{i}", [128, 8], F32, sbd), sb(f"xn{i}", [128, 512], F32, sbd),
                           sb(f"t1{i}", [128, 256], F32, sbd), sb(f"t2{i}", [128, 256], F32, sbd),
                           sb(f"t3{i}", [128, 256], F32, sbd), sb(f"t4{i}", [128, 256], F32, sbd)))
            kb = sb("kb", [128, 2048], BF16, sbd)
            vb = sb("vb", [128, 2048], BF16, sbd)
            ktb = [sb(f"ktb{i}", [128, 16 * 128], BF16, sbd) for i in range(2)]
            mgt = [sb(f"mgt{i}", [128, 512], F32, sbd) for i in range(2)]
            pT = ps("pT", [128, 1024], BF16, sbd)
            pO = [ps(f"pO{i}", [128, 512], F32, sbd) for i in range(3)]
            pK = [ps(f"pK{i}", [128, 1024], BF16, sbd) for i in range(2)]

            def tile_front(P, xsrc, t, gt):
                xt = xts[t % 2]
                rmsnorm_rows(P, xt[:], ("xt", t % 2), gt[:], hb[:], "hb", junk[:])
                for c in range(8):
                    P.op("pe", lambda e, c=c: e.transpose(out=pT[:, c * 128:(c + 1) * 128],
                                                          in_=hb[:, c * 128:(c + 1) * 128], identity=ident[:]),
                         reads=["hb", "ident"], writes=["pT"])
                P.op("dve", lambda e: e.tensor_copy(out=hT[t % 2][:], in_=pT[:]), reads=["pT"], writes=[("hT", t % 2)])

            def proj_group(P, t, j, ncols, col0, width):
                po = pO[j % 3]
                for c in range(8):
                    P.op("pe", lambda e, c=c: e.matmul(po[:, 0:width], lhsT=hT[t % 2][:, c * 128:(c + 1) * 128],
                                                       rhs=wbuf[:, c * ncols + col0:c * ncols + col0 + width],
                                                       start=(c == 0), stop=(c == 7)),
                         reads=[("hT", t % 2), "wbuf"], writes=[("pO", j % 3)])
                return po

            P = Phase(nc, "pb")
            load_w_cols(P, wbuf, KV_RANGES, NKV, "wbuf", stg)
            P.dma(gmix[:], bc(g_mix, 1024), writes=["gt"])
            P.dma(gains[:, 0:1536], bc(g_k24, 1536), writes=["gains"])
            P.dma(xts[0][:], xkv[0:128, :], writes=[("xt", 0)])
            for t in range(NT):
                if t + 1 < NT:
                    P.dma(xts[(t + 1) % 2][:], xkv[(t + 1) * 128:(t + 2) * 128, :], writes=[("xt", (t + 1) % 2)])
                cs = cst[t % 2]
                P.dma(cs[:, 0:32], rkv_c[t * 128:(t + 1) * 128, :], writes=[("cs", t % 2, 0)])
                P.dma(cs[:, 32:64], rkv_s[t * 128:(t + 1) * 128, :], writes=[("cs", t % 2, 1)])
                tile_front(P, xkv, t, gmix)
                for j in range(7):
                    po = proj_group(P, t, j, NKV, j * 512, 512)
                    if j < 3:
                        normrope(P, po[:], ("pO", j % 3), 8, gains[:, j * 512:(j + 1) * 512], cs[:, 0:32], cs[:, 32:64],
                                 [("cs", t % 2, 0), ("cs", t % 2, 1)], kb[:, j * 512:(j + 1) * 512], ("kb", j), TT[j % 2], str(j % 2))
                    else:
                        P.op("act", lambda e, po=po, j=j: e.activation(out=vb[:, (j - 3) * 512:(j - 2) * 512], in_=po[:],
                                                                       func=AF.Identity),
                             reads=[("pO", j % 3)], writes=[("vb", j)])
                kt = ktb[t % 2]
                for blk in range(16):
                    src = kb[:, blk * 128:(blk + 1) * 128] if blk < 12 else vb[:, 1536 + (blk - 12) * 128:1536 + (blk - 11) * 128]
                    rk = [("kb", blk // 4)] if blk < 12 else [("vb", 6)]
                    P.op("pe", lambda e, src=src, blk=blk: e.transpose(out=pK[blk // 8][:, (blk % 8) * 128:(blk % 8 + 1) * 128],
                                                                        in_=src, identity=ident[:]),
                         reads=rk + ["ident"], writes=[("pK", blk // 8)])
                for hf in range(2):
                    P.op("dve", lambda e, hf=hf, kt=kt: e.tensor_copy(out=kt[:, hf * 1024:(hf + 1) * 1024], in_=pK[hf][:]),
                         reads=[("pK", hf)], writes=[("ktb", t % 2, hf)])
                tok = slice(t * 128, (t + 1) * 128)
                P.dma(KdT[:, :, tok].rearrange("h p k -> p h k"), kt[:, 0:1024].rearrange("p (h k) -> p h k", k=128),
                      reads=[("ktb", t % 2, 0)])
                for i, dst in enumerate((KsT, KwT, kcT, vcT)):
                    P.dma(dst[:, tok].rearrange("(i p) k -> p i k", p=128),
                          kt[:, 1024 + i * 256:1024 + (i + 1) * 256].rearrange("p (i k) -> p i k", k=128),
                          reads=[("ktb", t % 2, 1)])
                P.dma(Vd[tok, :], vb[:, 0:1024], reads=[("vb", 3), ("vb", 4)])
                P.dma(Vs[tok, :], vb[:, 1024:1280], reads=[("vb", 5)])
                P.dma(Vw[tok, :], vb[:, 1280:1536], reads=[("vb", 5)])
            P.emit()
            if stage <= 1:
                return nc

            P = Phase(nc, "pd")
            load_w_cols(P, wbuf, Q_RANGES, NQC, "wbuf", stg)
            P.dma(gains[:, 0:2048], bc(g_q32, 2048), writes=["gains"])
            P.dma(xts[0][:], xq[0:128, :], writes=[("xt", 0)])
            for t in range(NOWN):
                if t + 1 < NOWN:
                    P.dma(xts[(t + 1) % 2][:], xq[(t + 1) * 128:(t + 2) * 128, :], writes=[("xt", (t + 1) % 2)])
                cs = cst[t % 2]
                P.dma(cs[:, 0:32], rq_c[t * 128:(t + 1) * 128, :], writes=[("cs", t % 2, 0)])
                P.dma(cs[:, 32:64], rq_s[t * 128:(t + 1) * 128, :], writes=[("cs", t % 2, 1)])
                tile_front(P, xq, t, gmix)
                tok = slice(t * 128, (t + 1) * 128)
                for j in range(4):
                    po = proj_group(P, t, j, NQC, j * 512, 512)
                    normrope(P, po[:], ("pO", j % 3), 8, gains[:, j * 512:(j + 1) * 512], cs[:, 0:32], cs[:, 32:64],
                             [("cs", t % 2, 0), ("cs", t % 2, 1)], kb[:, j * 512:(j + 1) * 512], ("kb", j), TT[j % 2], str(j % 2))
                po = proj_group(P, t, 4, NQC, 2048, 48)
                gsl = gates[:, t * 48:(t + 1) * 48]
                P.op("act", lambda e, po=po, gsl=gsl: e.activation(out=gsl, in_=po[:, 0:48], func=AF.Exp, scale=-1.0),
                     reads=[("pO", 1)], writes=["gates"])
                P.op("dve", lambda e, gsl=gsl: e.tensor_scalar(out=gsl, in0=gsl, scalar1=1.0, scalar2=None, op0=ALU.add),
                     reads=["gates"], writes=["gates"])
                P.op("dve", lambda e, gsl=gsl: e.reciprocal(out=gsl, in_=gsl), reads=["gates"], writes=["gates"])
                for j in range(5, 9):
                    po = proj_group(P, t, j, NQC, 2096 + (j - 5) * 512, 512)
                    mg_ = mgt[j % 2]
                    P.op("act", lambda e, po=po, mg_=mg_: e.activation(out=mg_[:], in_=po[:], func=AF.Exp, scale=-1.0),
                         reads=[("pO", j % 3)], writes=[("mgt", j % 2)])
                    P.op("pool", lambda e, mg_=mg_: e.tensor_scalar(out=mg_[:], in0=mg_[:], scalar1=1.0, scalar2=None, op0=ALU.add),
                         reads=[("mgt", j % 2)], writes=[("mgt", j % 2)])
                    P.op("dve", lambda e, mg_=mg_: e.reciprocal(out=mg_[:], in_=mg_[:]),
                         reads=[("mgt", j % 2)], writes=[("mgt", j % 2)])
                    P.dma(mgS[tok, (j - 5) * 512:(j - 4) * 512], mg_[:], reads=[("mgt", j % 2)])
                kt = ktb[t % 2]
                for blk in range(16):
                    P.op("pe", lambda e, blk=blk: e.transpose(out=pK[blk // 8][:, (blk % 8) * 128:(blk % 8 + 1) * 128],
                                                              in_=kb[:, blk * 128:(blk + 1) * 128], identity=ident[:]),
                         reads=[("kb", blk // 4), "ident"], writes=[("pK", blk // 8)])
                for hf in range(2):
                    P.op("dve", lambda e, hf=hf, kt=kt: e.tensor_copy(out=kt[:, hf * 1024:(hf + 1) * 1024], in_=pK[hf][:]),
                         reads=[("pK", hf)], writes=[("ktb", t % 2, hf)])
                P.dma(QdT[:, :, tok].rearrange("h p k -> p h k"), kt[:, 0:1024].rearrange("p (h k) -> p h k", k=128),
                      reads=[("ktb", t % 2, 0)])
                P.dma(QnT[:, tok].rearrange("(i p) k -> p i k", p=128), kt[:, 1024:2048].rearrange("p (i k) -> p i k", k=128),
                      reads=[("ktb", t % 2, 1)])
            P.emit()
        if stage <= 2:
            return nc

        ydiff = sb("ydiff", [128, NOWN * 1024], BF16)
        ynsa = sb("ynsa", [128, NOWN * 1024], BF16)
        with contextlib.ExitStack() as sc:
            W1s = [sb(f"W1s{i}", [128, 16 * 256], BF16, sc) for i in range(2)]
            W2s = [sb(f"W2s{i}", [128, 128], BF16, sc) for i in range(2)]
            w1stg = sb("w1stg", [128, 16 * 256], F32, sc)
            w2stg = sb("w2stg", [128, 128], F32, sc)
            p2f = sb("p2f", [128, 16], F32, sc)
            p2b = [sb(f"p2b{i}", [128, 16], BF16, sc) for i in range(2)]
            kc2 = [sb(f"kc2_{i}", [128, S], BF16, sc) for i in range(2)]
            biasv = [sb(f"biasv{i}", [128, 2], F32, sc) for i in range(2)]
            h1T = sb("h1T", [128, 2 * NCP], BF16, sc)
            xg = sb("xg", [128, NCP], F32, sc)
            x2 = sb("x2", [128, NCP], F32, sc)
            th = sb("th", [128, NCP], F32, sc)
            kcn = sb("kcn", [128, 128], BF16, sc)
            gkc = sb("gkc", [128, 64], F32, sc)
            csc = sb("csc", [128, NCT * 64], F32, sc)
            TC = (sb("c_sq", [128, 64], F32, sc), sb("c_ssu", [128, 8], F32, sc), sb("c_rsu", [128, 8], F32, sc),
                  sb("c_xn", [128, 64], F32, sc), sb("c_t1", [128, 32], F32, sc), sb("c_t2", [128, 32], F32, sc),
                  sb("c_t3", [128, 32], F32, sc), sb("c_t4", [128, 32], F32, sc))
            pH = [ps(f"pH{i}", [128, 512], F32, sc) for i in range(2)]
            pB = ps("pB", [128, 512], F32, sc)
            pC = ps("pC", [128, 512], F32, sc)
            pKc = ps("pKc", [128, 1024], BF16, sc)
            P = Phase(nc, "pc")
            P.op("pool", lambda e: e.memset(h1T[:], 0.0), writes=["h1T"])
            P.op("pool", lambda e: e.memset(kcn[:], 0.0), writes=["kcn"])
            P.dma(gkc[:], bc(g_kc, 64), writes=["gains"])
            for nt in range(NCT):
                P.dma(csc[:, nt * 64:nt * 64 + 32], rc_c[nt * 128:(nt + 1) * 128, :], writes=[("csc", nt, 0)])
                P.dma(csc[:, nt * 64 + 32:nt * 64 + 64], rc_s[nt * 128:(nt + 1) * 128, :], writes=[("csc", nt, 1)])
                for g in range(4):
                    o0 = (nt * 4 + g) * 129
                    P.dma(Vca[:, o0 + 64:o0 + 129], ovl[nt * 128:(nt + 1) * 128, :], writes=[("vca_c", nt, g)])
            for kv in range(2):
                P.dma(w1stg[:].rearrange("p (u h) -> p u h", h=256), c_w1[kv].rearrange("(u p) h -> p u h", p=128),
                      writes=["w1stg"])
                P.op("pool", lambda e, kv=kv: e.tensor_copy(out=W1s[kv][:], in_=w1stg[:]), reads=["w1stg"], writes=[("W1s", kv)])
                P.dma(w2stg[:].rearrange("p (a d) -> p a d", d=64), c_w2[kv].rearrange("(a p) d -> p a d", p=128),
                      writes=["w2stg"])
                P.op("pool", lambda e, kv=kv: e.tensor_copy(out=W2s[kv][:], in_=w2stg[:]), reads=["w2stg"], writes=[("W2s", kv)])
                P.dma(p2f[:], pos2[kv], writes=["p2f"])
                P.op("pool", lambda e, kv=kv: e.tensor_copy(out=p2b[kv][:], in_=p2f[:]), reads=["p2f"], writes=[("p2b", kv)])
                for half in range(2):
                    for u in range(16):
                        P.op("pe", lambda e, kv=kv, half=half, u=u: e.matmul(
                            pB[:, half:half + 1], lhsT=W1s[kv][:, u * 256 + half * 128:u * 256 + half * 128 + 128],
                            rhs=p2b[kv][:, u:u + 1], start=(u == 0), stop=(u == 15)),
                            reads=[("W1s", kv), ("p2b", kv)], writes=["pB"])
                P.op("dve", lambda e, kv=kv: e.tensor_copy(out=biasv[kv][:], in_=pB[:, 0:2]), reads=["pB"], writes=[("biasv", kv)])
                src = kcT if kv == 0 else vcT
                for g in range(4):
                    kb2 = kc2[g % 2]
                    kk = ("kc2", g % 2)
                    P.dma(kb2[0:64, :], src[g * 64:(g + 1) * 64, :], writes=[(kk, 0)])
                    P.dma(kb2[64:128, 0:S - 1], src[g * 64:(g + 1) * 64, 1:S], writes=[(kk, 1)])
                    for half in range(2):
                        for u in range(16):
                            rhs = bass.AP(kb2, 2 * u, [[S, 128], [16, NCMP]])
                            P.op("pe", lambda e, kv=kv, half=half, u=u, rhs=rhs: e.matmul(
                                pH[half][:, 0:NCMP], lhsT=W1s[kv][:, u * 256 + half * 128:u * 256 + half * 128 + 128],
                                rhs=rhs, start=(u == 0), stop=(u == 15)),
                                reads=[("W1s", kv), (kk, 0), (kk, 1)], writes=[("pH", half)])
                        hsl = h1T[:, half * NCP:half * NCP + NCMP]
                        P.op("dve", lambda e, kv=kv, half=half: e.tensor_scalar(
                            out=xg[:, 0:NCMP], in0=pH[half][:, 0:NCMP], scalar1=biasv[kv][:, half:half + 1], scalar2=None,
                            op0=ALU.add), reads=[("pH", half), ("biasv", kv)], writes=["xg"])
                        P.op("pool", lambda e: e.tensor_tensor(out=x2[:, 0:NCMP], in0=xg[:, 0:NCMP], in1=xg[:, 0:NCMP],
                                                               op=ALU.mult), reads=["xg"], writes=["x2"])
                        P.op("pool", lambda e: e.tensor_scalar(out=x2[:, 0:NCMP], in0=x2[:, 0:NCMP], scalar1=0.044715,
                                                               scalar2=1.0, op0=ALU.mult, op1=ALU.add),
                             reads=["x2"], writes=["x2"])
                        P.op("pool", lambda e: e.tensor_tensor(out=x2[:, 0:NCMP], in0=x2[:, 0:NCMP], in1=xg[:, 0:NCMP],
                                                               op=ALU.mult), reads=["x2", "xg"], writes=["x2"])
                        P.op("act", lambda e: e.activation(out=th[:, 0:NCMP], in_=x2[:, 0:NCMP], func=AF.Tanh,
                                                           scale=0.7978845608028654), reads=["x2"], writes=["th"])
                        P.op("dve", lambda e: e.tensor_scalar(out=th[:, 0:NCMP], in0=th[:, 0:NCMP], scalar1=0.5, scalar2=0.5,
                                                              op0=ALU.mult, op1=ALU.add), reads=["th"], writes=["th"])
                        P.op("dve", lambda e, hsl=hsl: e.tensor_tensor(out=hsl, in0=th[:, 0:NCMP], in1=xg[:, 0:NCMP],
                                                                        op=ALU.mult), reads=["th", "xg"], writes=["h1T"])
                    for nt in range(NCT):
                        for half in range(2):
                            P.op("pe", lambda e, kv=kv, nt=nt, half=half: e.matmul(
                                pC[:, 0:64], lhsT=h1T[:, half * NCP + nt * 128:half * NCP + nt * 128 + 128],
                                rhs=W2s[kv][:, half * 64:(half + 1) * 64], start=(half == 0), stop=(half == 1)),
                                reads=["h1T", ("W2s", kv)], writes=["pC"])
                        if kv == 0:
                            normrope(P, pC[:, 0:64], "pC", 1, gkc[:], csc[:, nt * 64:nt * 64 + 32],
                                     csc[:, nt * 64 + 32:nt * 64 + 64], [("csc", nt, 0), ("csc", nt, 1)],
                                     kcn[:, 0:64], "kcn", TC, "c")
                            P.op("pe", lambda e: e.transpose(out=pKc[:, 0:128], in_=kcn[:], identity=ident[:]),
                                 reads=["kcn", "ident"], writes=["pKc"])
                            P.op("dve", lambda e, g=g, nt=nt: e.tensor_copy(
                                out=KcT[0:64, g * NCP + nt * 128:g * NCP + nt * 128 + 128], in_=pKc[0:64, 0:128]),
                                reads=["pKc"], writes=["KcT"])
                        else:
                            o0 = (nt * 4 + g) * 129
                            P.op("act", lambda e, o0=o0: e.activation(out=Vca[:, o0:o0 + 64], in_=pC[:, 0:64], func=AF.Identity),
                                 reads=["pC"], writes=["Vca"])
            if dbg:
                dK = nc.dram_tensor("dbg_KcT", [64, 4 * NCP], BF16, kind="ExternalOutput").ap()
                dV = nc.dram_tensor("dbg_Vca", [128, NCT * 4 * 129], BF16, kind="ExternalOutput").ap()
                P.dma(dK, KcT[0:64, :], reads=["KcT"])
                P.dma(dV, Vca[:], reads=["Vca"] + [("vca_c", nt, g) for nt in range(NCT) for g in range(4)])
            P.emit()
        if stage <= 3:
            return nc

        with contextlib.ExitStack() as se:
            Qt = [sb(f"e_Qt{i}", [128, SO], BF16, se) for i in range(2)]
            mcmp = sb("e_mcmp", [128, NSL * NCT * 512], BF16, se)
            ET = [sb(f"e_ET{i}", [128, 512], BF16, se) for i in range(2)]
            imp = sb("e_imp", [128, NOWN * 64], F32, se)
            frc = sb("e_frc", [128, NOWN * 64], F32, se)
            rl = sb("e_rl", [128, 4], F32, se)
            impf = sb("e_impf", [128, 64], F32, se)
            tmpf = sb("e_tmpf", [128, 64], F32, se)
            m8 = sb("e_m8", [128, 8], F32, se)
            m8b = sb("e_m8b", [128, 8], F32, se)
            bt = sb("e_bt", [128, 128], BF16, se)
            bstage = sb("e_bstage", [128, SO], BF16, se)
            pS = [ps(f"e_pS{i}", [128, 512], F32, se) for i in range(2)]
            pU = [ps(f"e_pU{i}", [128, 512], F32, se) for i in range(2)]
            pBt = ps("e_pBt", [128, 1024], BF16, se)
            P = Phase(nc, "pe")
            P.dma(mcmp[:].rearrange("p (a q) -> p a q", q=512), m_cmp, writes=["mcmp"])
            P.dma(frc[:].rearrange("p (t j) -> p t j", j=64), forced.rearrange("(t p) j -> p t j", p=128), writes=["frc"])
            P.op("pool", lambda e: e.memset(bt[:], 0.0), writes=["bt"])
            it = 0
            for g in range(4):
                P.op("pool", lambda e: e.memset(imp[:], 0.0), writes=["imp"])
                for h in range(4):
                    hq = 4 * g + h
                    qt = Qt[hq % 2]
                    P.dma(qt[0:64, :], QnT[hq * 64:(hq + 1) * 64, :], writes=[("Qt", hq % 2)])
                    for s_ in range(NSL):
                        for nt in range(NCT):
                            i = it % 2
                            it += 1
                            P.op("pe", lambda e, i=i, g=g, nt=nt, s_=s_, qt=qt: e.matmul(
                                pS[i][:], lhsT=KcT[0:64, g * NCP + nt * 128:g * NCP + nt * 128 + 128],
                                rhs=qt[0:64, s_ * 512:(s_ + 1) * 512], start=True, stop=False),
                                reads=["KcT", ("Qt", hq % 2)], writes=[("pS", i)])
                            P.op("pe", lambda e, i=i, nt=nt, s_=s_: e.matmul(
                                pS[i][:], lhsT=ident[:], rhs=mcmp[:, (s_ * NCT + nt) * 512:(s_ * NCT + nt + 1) * 512],
                                start=False, stop=True), reads=["ident", "mcmp"], writes=[("pS", i)])
                            P.op("act", lambda e, i=i: e.activation(out=ET[i][:], in_=pS[i][:], func=AF.Exp, scale=0.125),
                                 reads=[("pS", i)], writes=[("ET", i)])
                            for qs in range(4):
                                o0 = (nt * 4 + g) * 129
                                P.op("pe", lambda e, i=i, qs=qs, o0=o0, nt=nt: e.matmul(
                                    pU[qs // 2][:, (qs % 2) * 129:(qs % 2) * 129 + 129], lhsT=ET[i][:, qs * 128:(qs + 1) * 128],
                                    rhs=Vca[:, o0:o0 + 129], start=(nt == 0 and qs % 2 == 0), stop=(nt == NCT - 1),
                                    skip_group_check=True),
                                    reads=[("ET", i), "Vca"], writes=[("pU", qs // 2)])
                        for qs in range(4):
                            tl = s_ * 4 + qs
                            u0 = (qs % 2) * 129
                            pu = pU[qs // 2]
                            pk = ("pU", qs // 2)
                            P.op("dve", lambda e, pu=pu, u0=u0, qs=qs: e.tensor_scalar(
                                out=rl[:, qs:qs + 1], in0=pu[:, u0 + 128:u0 + 129], scalar1=1e-30, scalar2=None, op0=ALU.max),
                                reads=[pk], writes=["rl"])
                            P.op("dve", lambda e, qs=qs: e.reciprocal(out=rl[:, qs:qs + 1], in_=rl[:, qs:qs + 1]),
                                 reads=["rl"], writes=["rl"])
                            ysl = ynsa[:, tl * 1024 + hq * 64:tl * 1024 + hq * 64 + 64]
                            gcol = gates[:, tl * 48 + hq * 3:tl * 48 + hq * 3 + 1]
                            P.op("dve", lambda e, pu=pu, u0=u0, qs=qs, ysl=ysl, gcol=gcol: e.tensor_scalar(
                                out=ysl, in0=pu[:, u0:u0 + 64], scalar1=rl[:, qs:qs + 1], scalar2=gcol,
                                op0=ALU.mult, op1=ALU.mult), reads=[pk, "rl", "gates"], writes=[("ynsa", tl)])
                            isl = imp[:, tl * 64:(tl + 1) * 64]
                            P.op("dve", lambda e, pu=pu, u0=u0, qs=qs, isl=isl: e.scalar_tensor_tensor(
                                out=isl, in0=pu[:, u0 + 64:u0 + 128], scalar=rl[:, qs:qs + 1], in1=isl,
                                op0=ALU.mult, op1=ALU.add), reads=[pk, "rl", "imp"], writes=["imp"])
                for tl in range(NOWN):
                    isl = imp[:, tl * 64:(tl + 1) * 64]
                    P.op("dve", lambda e, isl=isl, tl=tl: e.tensor_tensor(out=impf[:], in0=isl, in1=frc[:, tl * 64:(tl + 1) * 64],
                                                                          op=ALU.max), reads=["imp", "frc"], writes=["impf"])
                    P.op("dve", lambda e: e.max(out=m8[:], in_=impf[:]), reads=["impf"], writes=["m8"])
                    P.op("dve", lambda e: e.match_replace(out=tmpf[:], in_to_replace=m8[:], in_values=impf[:], imm_value=-1e9),
                         reads=["m8", "impf"], writes=["tmpf"])
                    P.op("dve", lambda e: e.max(out=m8b[:], in_=tmpf[:]), reads=["tmpf"], writes=["m8b"])
                    P.op("dve", lambda e: e.tensor_scalar(out=tmpf[:], in0=impf[:], scalar1=m8b[:, 7:8], scalar2=None,
                                                          op0=ALU.is_ge), reads=["impf", "m8b", "tmpf"], writes=["tmpf"])
                    P.op("dve", lambda e: e.tensor_scalar(out=bt[:, 64:128], in0=tmpf[:], scalar1=-1.0, scalar2=-NEG,
                                                          op0=ALU.add, op1=ALU.mult), reads=["tmpf"], writes=["bt"])
                    P.op("pe", lambda e: e.transpose(out=pBt[:, 0:128], in_=bt[:], identity=ident[:]),
                         reads=["bt", "ident"], writes=["pBt"])
                    P.op("dve", lambda e, tl=tl: e.tensor_copy(out=bstage[64:128, tl * 128:(tl + 1) * 128], in_=pBt[64:128, 0:128]),
                         reads=["pBt"], writes=["bstage"])
                P.dma(BsT[g * 64:(g + 1) * 64, :], bstage[64:128, :], reads=["bstage"])
            P.emit()
        if stage <= 4:
            return nc

        with contextlib.ExitStack() as sf:
            KD = [sb(f"f_KD{i}", [128, S], BF16, sf) for i in range(2)]
            VD = [sb(f"f_VD{i}", [128, NT * 129], BF16, sf) for i in range(2)]
            QD = [sb(f"f_QD{i}", [128, SO], BF16, sf) for i in range(2)]
            KS = sb("f_KS", [128, S], BF16, sf)
            KW = sb("f_KW", [128, S], BF16, sf)
            VS = sb("f_VS", [128, NT * 65], BF16, sf)
            VW = sb("f_VW", [128, NT * 65], BF16, sf)
            QS = [sb(f"f_QS{i}", [128, SO], BF16, sf) for i in range(2)]
            mden = sb("f_mden", [128, 8 * 512], BF16, sf)
            mwin = sb("f_mwin", [128, 12 * 512], BF16, sf)
            PT = [sb(f"f_PT{i}", [128, 512], BF16, sf) for i in range(4)]
            oa = sb("f_oa", [128, 128], F32, sf)
            ob = sb("f_ob", [128, 128], F32, sf)
            jk = sb("f_jk", [128, 128], F32, sf)
            r4 = sb("f_r4", [128, 4], F32, sf)
            ssd = sb("f_ssd", [128, 1], F32, sf)
            rsd = sb("f_rsd", [128, 1], F32, sf)
            tn = sb("f_tn", [128, 64], F32, sf)
            pS = [ps(f"f_pS{i}", [128, 512], F32, sf) for i in range(4)]
            pO = [ps(f"f_pO{i}", [128, 512], F32, sf) for i in range(3)]
            P = Phase(nc, "pf")
            P.dma(mden[:].rearrange("p (a q) -> p a q", q=512), m_dense, writes=["mden"])
            P.dma(mwin[:].rearrange("p (a q) -> p a q", q=512), m_win, writes=["mwin"])
            P.dma(KS[64:128, :], eind, writes=["KS_e"])
            for i in range(2):
                P.op("pool", lambda e, i=i: e.memset(VD[i][:], 1.0), writes=[("VD", i)])
            P.op("pool", lambda e: e.memset(VS[:], 1.0), writes=["VS"])
            P.op("pool", lambda e: e.memset(VW[:], 1.0), writes=["VW"])

            def load_v(dst, dkey, src2d, width, stride):
                d3 = dst[:].rearrange("p (t c) -> p t c", c=stride)
                s3 = src2d.rearrange("(t p) c -> p t c", p=128)
                step = 8
                for a in range(0, NT, step):
                    b_ = min(NT, a + step)
                    P.dma(d3[:, a:b_, 0:width], s3[:, a:b_, :], writes=[dkey])

            def load_diff(h):
                i = h % 2
                P.dma(KD[i][:], KdT[h], writes=[("KD", i)])
                load_v(VD[i], ("VD", i), Vd[:, h * 128:(h + 1) * 128], 128, 129)
                P.dma(QD[i][:], QdT[h], writes=[("QD", i)])

            load_diff(0)
            for h in range(8):
                if h + 1 < 8:
                    load_diff(h + 1)
                bi = h % 2
                for s_ in range(NSL):
                    nkt = 8 * (s_ + 1)

                    def qk(kt, s_=s_, nkt=nkt, bi=bi):
                        i2 = kt % 2
                        masked = kt >= nkt - 8
                        mi = kt - (nkt - 8)
                        for m in range(2):
                            bk = 2 * i2 + m
                            P.op("pe", lambda e, bk=bk, m=m, kt=kt: e.matmul(
                                pS[bk][:], lhsT=KD[bi][m * 64:(m + 1) * 64, kt * 128:(kt + 1) * 128],
                                rhs=QD[bi][m * 64:(m + 1) * 64, s_ * 512:(s_ + 1) * 512], start=True, stop=not masked),
                                reads=[("KD", bi), ("QD", bi)], writes=[("pS", bk)])
                            if masked:
                                P.op("pe", lambda e, bk=bk, mi=mi: e.matmul(
                                    pS[bk][:], lhsT=ident[:], rhs=mden[:, mi * 512:(mi + 1) * 512], start=False, stop=True),
                                    reads=["ident", "mden"], writes=[("pS", bk)])
                            P.op("act", lambda e, bk=bk: e.activation(out=PT[bk][:], in_=pS[bk][:], func=AF.Exp, scale=0.125),
                                 reads=[("pS", bk)], writes=[("PT", bk)])

                    def pv(kt, s_=s_, nkt=nkt, bi=bi):
                        i2 = kt % 2
                        for m in range(2):
                            bk = 2 * i2 + m
                            for qs in range(4):
                                a = m * 4 + qs
                                P.op("pe", lambda e, bk=bk, qs=qs, a=a, kt=kt: e.matmul(
                                    pO[a // 3][:, (a % 3) * 129:(a % 3) * 129 + 129], lhsT=PT[bk][:, qs * 128:(qs + 1) * 128],
                                    rhs=VD[bi][:, kt * 129:kt * 129 + 129], start=(kt == 0 and a % 3 == 0), stop=(kt == nkt - 1),
                                    skip_group_check=True),
                                    reads=[("PT", bk), ("VD", bi)], writes=[("pO", a // 3)])

                    qk(0)
                    for kt in range(nkt):
                        if kt + 1 < nkt:
                            qk(kt + 1)
                        pv(kt)
                    for qs in range(4):
                        tl = s_ * 4 + qs
                        a0, a1 = qs, 4 + qs
                        o0 = pO[a0 // 3][:, (a0 % 3) * 129:(a0 % 3) * 129 + 129]
                        o1 = pO[a1 // 3][:, (a1 % 3) * 129:(a1 % 3) * 129 + 129]
                        k0, k1 = ("pO", a0 // 3), ("pO", a1 // 3)
                        P.op("dve", lambda e, o0=o0: e.reciprocal(out=r4[:, 0:1], in_=o0[:, 128:129]), reads=[k0], writes=["r4"])
                        P.op("dve", lambda e, o1=o1: e.reciprocal(out=r4[:, 1:2], in_=o1[:, 128:129]), reads=[k1, "r4"], writes=["r4"])
                        P.op("dve", lambda e: e.tensor_tensor(out=r4[:, 2:3], in0=r4[:, 1:2], in1=nlam[:], op=ALU.mult),
                             reads=["r4", "nlam"], writes=["r4"])
                        P.op("dve", lambda e, o0=o0: e.tensor_scalar(out=oa[:], in0=o0[:, 0:128], scalar1=r4[:, 0:1], scalar2=None,
                                                                    op0=ALU.mult), reads=[k0, "r4"], writes=["oa"])
                        P.op("dve", lambda e, o1=o1: e.scalar_tensor_tensor(out=ob[:], in0=o1[:, 0:128], scalar=r4[:, 2:3], in1=oa[:],
                                                                           op0=ALU.mult, op1=ALU.add),
                             reads=[k1, "r4", "oa"], writes=["ob"])
                        P.op("pool", lambda e: e.memset(ssd[:], 0.0), writes=["ssd"])
                        P.op("act", lambda e: e.activation(out=jk[:], in_=ob[:], func=AF.Square, accum_out=ssd[:]),
                             reads=["ob", "ssd"], writes=["jk", "ssd"])
                        P.op("dve", lambda e: e.tensor_scalar(out=ssd[:], in0=ssd[:], scalar1=1.0 / 128, scalar2=EPS,
                                                              op0=ALU.mult, op1=ALU.add), reads=["ssd"], writes=["ssd"])
                        P.op("act", lambda e: e.activation(out=ssd[:], in_=ssd[:], func=AF.Ln), reads=["ssd"], writes=["ssd"])
                        P.op("act", lambda e: e.activation(out=rsd[:], in_=ssd[:], func=AF.Exp, scale=-0.5),
                             reads=["ssd"], writes=["rsd"])
                        ysl = ydiff[:, tl * 1024 + h * 128:tl * 1024 + (h + 1) * 128]
                        P.op("dve", lambda e, ysl=ysl: e.scalar_tensor_tensor(out=ysl, in0=ob[:], scalar=rsd[:], in1=sgain[:],
                                                                              op0=ALU.mult, op1=ALU.mult),
                             reads=["ob", "rsd", "sgain"], writes=[("ydiff", tl)])

            cnt = [0]
            for g in range(4):
                P.dma(KS[0:64, :], KsT[g * 64:(g + 1) * 64, :], writes=["KS"])
                P.dma(KW[0:64, :], KwT[g * 64:(g + 1) * 64, :], writes=["KW"])
                load_v(VS, "VS", Vs[:, g * 64:(g + 1) * 64], 64, 65)
                load_v(VW, "VW", Vw[:, g * 64:(g + 1) * 64], 64, 65)
                for h in range(4):
                    hq = 4 * g + h
                    qi = hq % 2
                    P.dma(QS[qi][0:64, :], QnT[hq * 64:(hq + 1) * 64, :], writes=[("QS", qi, 0)])
                    P.dma(QS[qi][64:128, :], BsT[g * 64:(g + 1) * 64, :], writes=[("QS", qi, 1)])
                    for kind in (2, 1):
                        for s_ in range(NSL):
                            if kind == 1:
                                kts = list(range(0, 8 * (s_ + 1)))
                                mis = [kt - 8 * s_ if kt >= 8 * s_ else None for kt in kts]
                            else:
                                kts = list(range(max(0, 8 * s_ - 4), 8 * s_ + 8))
                                mis = [kt - (8 * s_ - 4) for kt in kts]
                            banks = []
                            for _ in kts:
                                banks.append(cnt[0] % 4)
                                cnt[0] += 1

                            def qk(j, kind=kind, s_=s_, kts=kts, mis=mis, banks=banks, qi=qi):
                                kt, mi, bk = kts[j], mis[j], banks[j]
                                if kind == 1:
                                    lhsT = KS[:, kt * 128:(kt + 1) * 128]
                                    rhs = QS[qi][:, s_ * 512:(s_ + 1) * 512]
                                    rd = ["KS", "KS_e", ("QS", qi, 0), ("QS", qi, 1)]
                                    mt = mden
                                else:
                                    lhsT = KW[0:64, kt * 128:(kt + 1) * 128]
                                    rhs = QS[qi][0:64, s_ * 512:(s_ + 1) * 512]
                                    rd = ["KW", ("QS", qi, 0)]
                                    mt = mwin
                                P.op("pe", lambda e: e.matmul(pS[bk][:], lhsT=lhsT, rhs=rhs, start=True, stop=(mi is None)),
                                     reads=rd, writes=[("pS", bk)])
                                if mi is not None:
                                    P.op("pe", lambda e: e.matmul(pS[bk][:], lhsT=ident[:], rhs=mt[:, mi * 512:(mi + 1) * 512],
                                                                  start=False, stop=True),
                                         reads=["ident", "mden", "mwin"], writes=[("pS", bk)])
                                P.op("act", lambda e: e.activation(out=PT[bk][:], in_=pS[bk][:], func=AF.Exp, scale=0.125),
                                     reads=[("pS", bk)], writes=[("PT", bk)])

                            def pv(j, kind=kind, kts=kts, banks=banks):
                                kt, bk = kts[j], banks[j]
                                vt, vk = (VS, "VS") if kind == 1 else (VW, "VW")
                                for qs in range(4):
                                    P.op("pe", lambda e, qs=qs: e.matmul(
                                        pO[0][:, qs * 65:qs * 65 + 65], lhsT=PT[bk][:, qs * 128:(qs + 1) * 128],
                                        rhs=vt[:, kt * 65:kt * 65 + 65], start=(j == 0 and qs == 0), stop=(j == len(kts) - 1),
                                        skip_group_check=True),
                                        reads=[("PT", bk), vk], writes=[("pO", 0)])

                            qk(0)
                            for j in range(len(kts)):
                                if j + 1 < len(kts):
                                    qk(j + 1)
                                pv(j)
                            for qs in range(4):
                                tl = s_ * 4 + qs
                                oq = pO[0][:, qs * 65:qs * 65 + 65]
                                P.op("dve", lambda e, oq=oq, qs=qs: e.tensor_scalar(
                                    out=r4[:, qs:qs + 1], in0=oq[:, 64:65], scalar1=1e-30, scalar2=None, op0=ALU.max),
                                    reads=[("pO", 0)], writes=["r4"])
                                P.op("dve", lambda e, qs=qs: e.reciprocal(out=r4[:, qs:qs + 1], in_=r4[:, qs:qs + 1]),
                                     reads=["r4"], writes=["r4"])
                                gcol = gates[:, tl * 48 + hq * 3 + kind:tl * 48 + hq * 3 + kind + 1]
                                P.op("dve", lambda e, oq=oq, qs=qs, gcol=gcol: e.tensor_scalar(
                                    out=tn[:], in0=oq[:, 0:64], scalar1=r4[:, qs:qs + 1], scalar2=gcol,
                                    op0=ALU.mult, op1=ALU.mult), reads=[("pO", 0), "r4", "gates"], writes=["tn"])
                                ysl = ynsa[:, tl * 1024 + hq * 64:tl * 1024 + hq * 64 + 64]
                                P.op("pool", lambda e, ysl=ysl: e.tensor_tensor(out=ysl, in0=ysl, in1=tn[:], op=ALU.add),
                                     reads=["tn", ("ynsa", tl)], writes=[("ynsa", tl)])
            if dbg:
                dY = nc.dram_tensor("dbg_ydiff", [128, NOWN * 1024], BF16, kind="ExternalOutput").ap()
                dN = nc.dram_tensor("dbg_ynsa", [128, NOWN * 1024], BF16, kind="ExternalOutput").ap()
                P.dma(dY, ydiff[:], reads=[("ydiff", t) for t in range(NOWN)])
                P.dma(dN, ynsa[:], reads=[("ynsa", t) for t in range(NOWN)])
            P.emit()
        if stage <= 5:
            return nc

        with contextlib.ExitStack() as sg:
            Wpd = sb("g_Wpd", [128, 8 * 1024], BF16, sg)
            Wpn = sb("g_Wpn", [128, 8 * 1024], BF16, sg)
            Wo = sb("g_Wo", [128, 8 * 1024], BF16, sg)
            wst = [sb(f"g_wst{i}", [128, 4 * 1024], F32, sg) for i in range(2)]
            ydT = sb("g_ydT", [128, 1024], BF16, sg)
            ynT = sb("g_ynT", [128, 1024], BF16, sg)
            mgl = [sb(f"g_mgl{i}", [128, 2048], F32, sg) for i in range(2)]
            xql = [sb(f"g_xql{i}", [128, 1024], F32, sg) for i in range(2)]
            m1 = sb("g_m1", [128, 1024], F32, sg)
            m2 = sb("g_m2", [128, 1024], F32, sg)
            mixb = sb("g_mixb", [128, 1024], BF16, sg)
            mixT = sb("g_mixT", [128, 1024], BF16, sg)
            x1t = [sb(f"g_x1t{i}", [128, 1024], F32, sg) for i in range(2)]
            pTa = ps("g_pTa", [128, 1024], BF16, sg)
            pTb = ps("g_pTb", [128, 1024], BF16, sg)
            pD = [ps(f"g_pD{i}", [128, 512], F32, sg) for i in range(2)]
            pN = [ps(f"g_pN{i}", [128, 512], F32, sg) for i in range(2)]
            pW = [ps(f"g_pW{i}", [128, 512], F32, sg) for i in range(2)]
            P = Phase(nc, "pg")
            k = 0
            for (wd, wsrc, wk) in ((Wpd, w_pd, "Wpd"), (Wpn, w_pn, "Wpn"), (Wo, w_o, "Wo")):
                for hc in range(2):
                    st_ = wst[k % 2]
                    P.dma(st_[:].rearrange("p (c n) -> p c n", n=1024),
                          wsrc[hc * 512:(hc + 1) * 512, :].rearrange("(c p) n -> p c n", p=128), writes=[("wst", k % 2)])
                    P.op("pool", lambda e, wd=wd, hc=hc, st_=st_: e.tensor_copy(out=wd[:, hc * 4096:(hc + 1) * 4096], in_=st_[:]),
                         reads=[("wst", k % 2)], writes=[wk])
                    k += 1
            for t in range(NOWN):
                tok = slice(t * 128, (t + 1) * 128)
                mg_ = mgl[t % 2]
                xq_ = xql[t % 2]
                P.dma(mg_[:], mgS[tok, :], writes=[("mgl", t % 2)])
                P.dma(xq_[:], xq[tok, :], writes=[("xql", t % 2)])
                for c in range(8):
                    P.op("pe", lambda e, c=c, t=t: e.transpose(out=pTa[:, c * 128:(c + 1) * 128],
                                                              in_=ydiff[:, t * 1024 + c * 128:t * 1024 + (c + 1) * 128], identity=ident[:]),
                         reads=["ident"], writes=["pTa"])
                P.op("dve", lambda e: e.tensor_copy(out=ydT[:], in_=pTa[:]), reads=["pTa"], writes=["ydT"])
                for c in range(8):
                    P.op("pe", lambda e, c=c, t=t: e.transpose(out=pTb[:, c * 128:(c + 1) * 128],
                                                              in_=ynsa[:, t * 1024 + c * 128:t * 1024 + (c + 1) * 128], identity=ident[:]),
                         reads=["ident"], writes=["pTb"])
                P.op("dve", lambda e: e.tensor_copy(out=ynT[:], in_=pTb[:]), reads=["pTb"], writes=["ynT"])
                for hf in range(2):
                    for c in range(8):
                        P.op("pe", lambda e, c=c, hf=hf: e.matmul(pD[hf][:], lhsT=ydT[:, c * 128:(c + 1) * 128],
                                                                  rhs=Wpd[:, c * 1024 + hf * 512:c * 1024 + (hf + 1) * 512],
                                                                  start=(c == 0), stop=(c == 7)),
                             reads=["ydT", "Wpd"], writes=[("pD", hf)])
                    for c in range(8):
                        P.op("pe", lambda e, c=c, hf=hf: e.matmul(pN[hf][:], lhsT=ynT[:, c * 128:(c + 1) * 128],
                                                                  rhs=Wpn[:, c * 1024 + hf * 512:c * 1024 + (hf + 1) * 512],
                                                                  start=(c == 0), stop=(c == 7)),
                             reads=["ynT", "Wpn"], writes=[("pN", hf)])
                    P.op("dve", lambda e, hf=hf, mg_=mg_: e.tensor_tensor(out=m1[:, hf * 512:(hf + 1) * 512], in0=pD[hf][:],
                                                                          in1=mg_[:, hf * 512:(hf + 1) * 512], op=ALU.mult),
                         reads=[("pD", hf), ("mgl", t % 2)], writes=[("m1", hf)])
                    P.op("dve", lambda e, hf=hf, mg_=mg_: e.tensor_tensor(out=m2[:, hf * 512:(hf + 1) * 512], in0=pN[hf][:],
                                                                          in1=mg_[:, 1024 + hf * 512:1024 + (hf + 1) * 512], op=ALU.mult),
                         reads=[("pN", hf), ("mgl", t % 2)], writes=[("m2", hf)])
                    P.op("pool", lambda e, hf=hf: e.tensor_tensor(out=mixb[:, hf * 512:(hf + 1) * 512], in0=m1[:, hf * 512:(hf + 1) * 512],
                                                                  in1=m2[:, hf * 512:(hf + 1) * 512], op=ALU.add),
                         reads=[("m1", hf), ("m2", hf)], writes=[("mixb", hf)])
                for c in range(8):
                    P.op("pe", lambda e, c=c: e.transpose(out=pTa[:, c * 128:(c + 1) * 128], in_=mixb[:, c * 128:(c + 1) * 128],
                                                          identity=ident[:]),
                         reads=[("mixb", c // 4), "ident"], writes=["pTa"])
                P.op("dve", lambda e: e.tensor_copy(out=mixT[:], in_=pTa[:]), reads=["pTa"], writes=["mixT"])
                x1_ = x1t[t % 2]
                for hf in range(2):
                    for c in range(8):
                        P.op("pe", lambda e, c=c, hf=hf: e.matmul(pW[hf][:], lhsT=mixT[:, c * 128:(c + 1) * 128],
                                                                  rhs=Wo[:, c * 1024 + hf * 512:c * 1024 + (hf + 1) * 512],
                                                                  start=(c == 0), stop=(c == 7)),
                             reads=["mixT", "Wo"], writes=[("pW", hf)])
                    P.op("dve", lambda e, hf=hf, x1_=x1_, xq_=xq_: e.tensor_tensor(out=x1_[:, hf * 512:(hf + 1) * 512], in0=pW[hf][:],
                                                                                  in1=xq_[:, hf * 512:(hf + 1) * 512], op=ALU.add),
                         reads=[("pW", hf), ("xql", t % 2)], writes=[("x1t", t % 2, hf)])
                P.dma(x1d[tok, :], x1_[:], reads=[("x1t", t % 2, 0), ("x1t", t % 2, 1)])
            P.emit()
        if stage <= 6:
            return nc
    with contextlib.ExitStack() as top2:
        def sb2(name, shape, dt):
            return top2.enter_context(nc.sbuf_tensor(name, list(shape), dt))

        ident2 = sb2("ident2", [128, 128], BF16)
        Wup = sb2("h_Wup", [128, 8 * 2048], BF16)
        Wdn = sb2("h_Wdn", [128, 16 * 1024], BF16)
        hst = [sb2(f"h_st{i}", [128, 2048], F32) for i in range(2)]
        gml = sb2("h_gml", [128, 1024], F32)
        x1g = sb2("h_x1g", [128, 4 * 1024], F32)
        og = sb2("h_og", [128, 4 * 1024], F32)
        junk2 = sb2("h_junk", [128, 1024], F32)
        h2 = sb2("h_h2", [128, 1024], BF16)
        h2T = sb2("h_h2T", [128, 8 * 512], BF16)
        uT = sb2("h_uT", [128, 16 * 512], BF16)
        rr = [sb2(f"h_rr{i}", [128, 512], F32) for i in range(2)]
        ot = [sb2(f"h_ot{i}", [128, 512], F32) for i in range(2)]
        ss2 = sb2("h_ss", [128, 1], F32)
        rs2 = sb2("h_rs", [128, 1], F32)
        pT2 = top2.enter_context(nc.psum_tensor("h_pT", [128, 1024], BF16))
        pU2 = [top2.enter_context(nc.psum_tensor(f"h_pU{i}", [128, 512], F32)) for i in range(2)]
        pDn = [top2.enter_context(nc.psum_tensor(f"h_pDn{i}", [128, 512], F32)) for i in range(2)]
        for pss in range(2):
            P = Phase(nc, f"ph{pss}")
            P.dma(ident2[:], identd, writes=["ident"])
            P.dma(gml[:], g_mlp[0:1, :].broadcast_to([128, 1024]), writes=["gt"])
            k = 0
            for c in range(8):
                st_ = hst[k % 2]
                P.dma(st_[:], w_up[c * 128:(c + 1) * 128, pss * 2048:(pss + 1) * 2048], writes=[("hst", k % 2)])
                P.op("pool", lambda e, c=c, st_=st_: e.tensor_copy(out=Wup[:, c * 2048:(c + 1) * 2048], in_=st_[:]),
                     reads=[("hst", k % 2)], writes=["Wup"])
                k += 1
            for f2 in range(8):
                st_ = hst[k % 2]
                r0 = pss * 2048 + f2 * 256
                P.dma(st_[:].rearrange("p (a n) -> p a n", n=1024), w_dn[r0:r0 + 256, :].rearrange("(a p) n -> p a n", p=128),
                      writes=[("hst", k % 2)])
                P.op("pool", lambda e, f2=f2, st_=st_: e.tensor_copy(out=Wdn[:, f2 * 2048:(f2 + 1) * 2048], in_=st_[:]),
                     reads=[("hst", k % 2)], writes=["Wdn"])
                k += 1
            for gq in range(NSL):
                rows = slice(gq * 512, (gq + 1) * 512)
                P.dma(x1g[:].rearrange("p (t n) -> p t n", n=1024), x1d[rows, :].rearrange("(t p) n -> p t n", p=128), writes=["x1g"])
                if pss == 1:
                    P.dma(og[:].rearrange("p (t n) -> p t n", n=1024), out[rows, :].rearrange("(t p) n -> p t n", p=128), writes=["og"])
                base = x1g if pss == 0 else og
                bkey = "x1g" if pss == 0 else "og"
                for j in range(4):
                    xt_ = x1g[:, j * 1024:(j + 1) * 1024]
                    P.op("pool", lambda e: e.memset(ss2[:], 0.0), writes=["ss2"])
                    P.op("act", lambda e, xt_=xt_: e.activation(out=junk2[:], in_=xt_, func=AF.Square, accum_out=ss2[:]),
                         reads=["x1g", "ss2"], writes=["junk2", "ss2"])
                    P.op("dve", lambda e: e.tensor_scalar(out=ss2[:], in0=ss2[:], scalar1=1.0 / 1024, scalar2=EPS,
                                                          op0=ALU.mult, op1=ALU.add), reads=["ss2"], writes=["ss2"])
                    P.op("act", lambda e: e.activation(out=ss2[:], in_=ss2[:], func=AF.Ln), reads=["ss2"], writes=["ss2"])
                    P.op("act", lambda e: e.activation(out=rs2[:], in_=ss2[:], func=AF.Exp, scale=-0.5), reads=["ss2"], writes=["rs2"])
                    P.op("dve", lambda e, xt_=xt_: e.scalar_tensor_tensor(out=h2[:], in0=xt_, scalar=rs2[:], in1=gml[:],
                                                                         op0=ALU.mult, op1=ALU.mult),
                         reads=["x1g", "rs2", "gt"], writes=["h2"])
                    for c in range(8):
                        P.op("pe", lambda e, c=c: e.transpose(out=pT2[:, c * 128:(c + 1) * 128], in_=h2[:, c * 128:(c + 1) * 128],
                                                              identity=ident2[:]), reads=["h2", "ident"], writes=["pT2"])
                    P.op("dve", lambda e, j=j: e.tensor_copy(
                        out=h2T[:].rearrange("p (c q) -> p c q", q=512)[:, :, j * 128:(j + 1) * 128],
                        in_=pT2[:].rearrange("p (c q) -> p c q", q=128)), reads=["pT2"], writes=["h2T"])
                for f in range(16):
                    pu = pU2[f % 2]
                    for c in range(8):
                        P.op("pe", lambda e, c=c, f=f, pu=pu: e.matmul(pu[:], lhsT=Wup[:, c * 2048 + f * 128:c * 2048 + (f + 1) * 128],
                                                                      rhs=h2T[:, c * 512:(c + 1) * 512], start=(c == 0), stop=(c == 7)),
                             reads=["Wup", "h2T"], writes=[("pU2", f % 2)])
                    r_ = rr[f % 2]
                    P.op("act", lambda e, pu=pu, r_=r_: e.activation(out=r_[:], in_=pu[:], func=AF.Relu),
                         reads=[("pU2", f % 2)], writes=[("rr", f % 2)])
                    P.op("pool", lambda e, f=f, r_=r_: e.tensor_tensor(out=uT[:, f * 512:(f + 1) * 512], in0=r_[:], in1=r_[:], op=ALU.mult),
                         reads=[("rr", f % 2)], writes=["uT"])
                for j in range(4):
                    for hf in range(2):
                        i = (j * 2 + hf) % 2
                        for f in range(16):
                            P.op("pe", lambda e, f=f, j=j, hf=hf, i=i: e.matmul(
                                pDn[i][:], lhsT=uT[:, f * 512 + j * 128:f * 512 + (j + 1) * 128],
                                rhs=Wdn[:, f * 1024 + hf * 512:f * 1024 + (hf + 1) * 512], start=(f == 0), stop=(f == 15)),
                                reads=["uT", "Wdn"], writes=[("pDn", i)])
                        o_ = ot[i]
                        P.op("dve", lambda e, i=i, o_=o_, j=j, hf=hf, base=base: e.tensor_tensor(
                            out=o_[:], in0=pDn[i][:], in1=base[:, j * 1024 + hf * 512:j * 1024 + (hf + 1) * 512], op=ALU.add),
                            reads=[("pDn", i), bkey], writes=[("ot", i)])
                        P.dma(out[gq * 512 + j * 128:gq * 512 + (j + 1) * 128, hf * 512:(hf + 1) * 512], o_[:], reads=[("ot", i)],
                              writes=[("outrows", gq)])
            P.emit()
    return nc


def _rope_tab(pos):
    inv = (10000.0 ** (-(np.arange(32, dtype=np.float32)) / np.float32(32))).astype(np.float32)
    ang = pos.astype(np.float32)[:, None] * inv[None, :]
    return np.cos(ang).astype(np.float32), np.sin(ang).astype(np.float32)


def make_core_inputs(inp, core, S):
    b, par = core // 2, core % 2
    NSL = S // 1024
    SO = NSL * 512
    NCMP = (S - 32) // 16 + 1
    NCT = (NCMP + 127) // 128
    NCP = NCT * 128
    f = lambda a: np.ascontiguousarray(np.asarray(a, dtype=np.float32))
    x = np.asarray(inp["x"], dtype=np.float32)
    own_pos = np.concatenate([np.arange((2 * s + par) * 512, (2 * s + par) * 512 + 512) for s in range(NSL)])
    d = {}
    d["xkv"] = f(x[b])
    d["xq"] = f(x[b][own_pos])
    d["w_in"] = f(inp["w_in"][0])
    d["w_pd"] = f(inp["w_proj_diff"][0])
    d["w_pn"] = f(inp["w_proj_nsa"][0])
    d["w_o"] = f(inp["w_out"][0])
    d["w_up"] = f(inp["w_mlp_up"][0])
    d["w_dn"] = f(inp["w_mlp_down"][0])
    d["c_w1k"] = f(inp["cmp_k_w1"][0])
    d["c_w1v"] = f(inp["cmp_v_w1"][0])
    d["c_w2k"] = f(inp["cmp_k_w2"][0])
    d["c_w2v"] = f(inp["cmp_v_w2"][0])
    p2 = lambda p: f(np.asarray(p, np.float32).reshape(16, 2, 64).transpose(1, 2, 0).reshape(128, 16))
    d["pos2k"] = p2(inp["cmp_pos_k"][0])
    d["pos2v"] = p2(inp["cmp_pos_v"][0])
    d["g_mix"] = f(inp["ln_mix_g"][0][None, :])
    d["g_mlp"] = f(inp["ln_mlp_g"][0][None, :])
    gk = np.asarray(inp["nsa_k_norm_g"][0], np.float32)
    d["g_k24"] = f(np.concatenate([np.tile(np.asarray(inp["diff_k_norm_g"][0], np.float32), 16),
                                   np.tile(gk[1], 4), np.tile(gk[2], 4)])[None, :])
    d["g_q32"] = f(np.concatenate([np.tile(np.asarray(inp["diff_q_norm_g"][0], np.float32), 16),
                                   np.tile(np.asarray(inp["nsa_q_norm_g"][0], np.float32), 16)])[None, :])
    d["g_kc"] = f(gk[0][None, :])
    d["g_sub"] = f(inp["diff_subln_g"][0][None, :])
    d["lam4"] = f(np.concatenate([np.asarray(inp[k][0], np.float32) for k in
                                  ("diff_lambda_q1", "diff_lambda_k1", "diff_lambda_q2", "diff_lambda_k2")])[None, :])
    d["rkv_c"], d["rkv_s"] = _rope_tab(np.arange(S))
    d["rq_c"], d["rq_s"] = _rope_tab(own_pos)
    cc = np.zeros(NCP, np.float32)
    cc[:NCMP] = np.arange(NCMP) * 16 + 15.5
    d["rc_c"], d["rc_s"] = _rope_tab(cc)
    d["ident"] = np.eye(128, dtype=np.float32).astype(NPBF)
    kk = np.arange(S)
    d["eind"] = (kk[None, :] // 64 == np.arange(64)[:, None]).astype(np.float32).astype(NPBF)
    n = np.arange(NCP)
    cs, ce = n * 16, n * 16 + 31
    ss_ = np.arange(64) * 64
    ov = ((cs[:, None] < ss_[None, :] + 64) & (ce[:, None] >= ss_[None, :]) & (n[:, None] < NCMP)).astype(np.float32)
    d["ovl"] = np.concatenate([ov, np.ones((NCP, 1), np.float32)], axis=1).astype(NPBF)
    k128 = np.arange(128)[:, None, None]
    q512 = np.arange(512)[None, None, :]
    kw = np.arange(8)[None, :, None] * 128 + k128
    d["m_dense"] = np.where((kw - par * 512) <= q512, 0.0, NEG).astype(np.float32).astype(NPBF)
    kw = np.arange(12)[None, :, None] * 128 + k128 - 512
    dd = par * 512 + q512 - kw
    d["m_win"] = np.where((dd >= 0) & (dd < 512), 0.0, NEG).astype(np.float32).astype(NPBF)
    mc = np.zeros((128, NSL * NCT, 512), np.float32)
    for s in range(NSL):
        for nt in range(NCT):
            nn = nt * 128 + np.arange(128)[:, None]
            qpos = (2 * s + par) * 512 + np.arange(512)[None, :]
            mc[:, s * NCT + nt, :] = np.where((nn < NCMP) & (nn * 16 + 31 <= qpos), 0.0, NEG)
    d["m_cmp"] = mc.astype(NPBF)
    fo = np.zeros((SO, 64), np.float32)
    cur = own_pos // 64
    r = np.arange(SO)
    fo[r[cur >= 1], (cur - 1)[cur >= 1]] = 1e4
    fo[r, cur] = 2e4
    fo[:, 0] = 3e4
    d["forced"] = fo
    return d


_NC_CACHE = {}


def kernel(**inputs):
    S = int(np.asarray(inputs["x"]).shape[1])
    B = int(np.asarray(inputs["x"]).shape[0])
    ncores = 2 * B
    if S not in _NC_CACHE:
        _NC_CACHE[S] = build(S)
    nc = _NC_CACHE[S]
    in_maps = [make_core_inputs(inputs, c, S) for c in range(ncores)]
    res = run_bass_kernel_spmd(nc, in_maps, core_ids=list(range(ncores)))
    out = np.zeros((B, S, 1024), np.float32)
    NSL = S // 1024
    for c in range(ncores):
        b, par = c // 2, c % 2
        o = np.asarray(res.results[c]["out"], dtype=np.float32)
        for s in range(NSL):
            ch = 2 * s + par
            out[b, ch * 512:(ch + 1) * 512] = o[s * 512:(s + 1) * 512]
    return out
```

```python
import contextlib
import numpy as np
import ml_dtypes
import concourse.bass as bass
import concourse.mybir as mybir
from concourse.bass_utils import run_bass_kernel_spmd

F32 = mybir.dt.float32
BF16 = mybir.dt.bfloat16
ALU = mybir.AluOpType
AF = mybir.ActivationFunctionType
AX = mybir.AxisListType
NPBF = ml_dtypes.bfloat16

NDMASEM = 4
EPS = 1e-6
NEG = -30000.0


class Phase:
    ENGS = ("pe", "act", "dve", "pool", "sp")

    def __init__(self, nc, name):
        self.nc = nc
        self.name = name
        self.ops = []

    def op(self, eng, fn, reads=(), writes=(), dma=False):
        self.ops.append((eng, fn, tuple(reads), tuple(writes), dma))

    def dma(self, out, in_, reads=(), writes=(), eng="sp"):
        self.op(eng, lambda e: e.dma_start(out=out, in_=in_), reads, writes, dma=True)

    def emit(self):
        nc = self.nc
        ops = self.ops
        cnt = {e: 0 for e in self.ENGS}
        dcnt = {e: 0 for e in self.ENGS}
        info = []
        last_w = {}
        readers = {}
        deps = []
        for i, (eng, fn, rd, wr, dma) in enumerate(ops):
            d = set()
            for k in rd:
                if k in last_w:
                    d.add(last_w[k])
            for k in wr:
                if k in last_w:
                    d.add(last_w[k])
                for r in readers.get(k, ()):
                    d.add(r)
            d.discard(i)
            deps.append(d)
            for k in rd:
                readers.setdefault(k, []).append(i)
            for k in wr:
                last_w[k] = i
                readers[k] = []
            if dma:
                info.append((eng, "d", dcnt[eng]))
                dcnt[eng] += 1
            else:
                info.append((eng, "c", cnt[eng]))
                cnt[eng] += 1
        with contextlib.ExitStack() as st:
            csem = {e: st.enter_context(nc.semaphore(f"{self.name}_c_{e}")) for e in self.ENGS}
            dsem = {e: [st.enter_context(nc.semaphore(f"{self.name}_d_{e}{j}")) for j in range(NDMASEM)]
                    for e in self.ENGS if dcnt[e] > 0}
            block = st.enter_context(nc.Block())
            per_eng = {e: [i for i, o in enumerate(ops) if o[0] == e] for e in self.ENGS}

            def make(eng_name):
                def body(eng):
                    waited = {}

                    def wait(key, sem, val):
                        if waited.get(key, 0) >= val:
                            return
                        waited[key] = val
                        eng.wait_ge(sem, val)

                    for i in per_eng[eng_name]:
                        _, fn, rd, wr, dma = ops[i]
                        for p in sorted(deps[i]):
                            pe_, pk, pidx = info[p]
                            if pk == "c":
                                if pe_ == "pe" and eng_name == "pe":
                                    continue
                                wait(("c", pe_), csem[pe_], pidx + 1)
                            else:
                                slot = pidx % NDMASEM
                                wait(("d", pe_, slot), dsem[pe_][slot], 16 * (pidx // NDMASEM + 1))
                        if dma:
                            didx = info[i][2]
                            slot = didx % NDMASEM
                            if didx >= NDMASEM:
                                wait(("d", eng_name, slot), dsem[eng_name][slot], 16 * (didx // NDMASEM))
                            fn(eng).then_inc(dsem[eng_name][slot], 16)
                        else:
                            fn(eng).then_inc(csem[eng_name], 1)
                    if eng_name in dsem:
                        n = dcnt[eng_name]
                        for slot in range(min(NDMASEM, n)):
                            total = (n - slot + NDMASEM - 1) // NDMASEM
                            wait(("d", eng_name, slot), dsem[eng_name][slot], 16 * total)
                return body

            if per_eng["pe"]:
                block.tensor(make("pe"))
            if per_eng["act"]:
                block.scalar(make("act"))
            if per_eng["dve"]:
                block.vector(make("dve"))
            if per_eng["pool"]:
                block.gpsimd(make("pool"))
            if per_eng["sp"]:
                block.sync(make("sp"))
        self.ops = []


KV_RANGES = [(1024, 2048), (4608, 4864), (5120, 5376), (2048, 3072), (4864, 5120), (5376, 5632),
             (4096, 4352), (4352, 4608)]
Q_RANGES = [(0, 1024), (3072, 4096), (5632, 5680), (5680, 7728)]
NKV = 3584
NQC = 4144


def build(S=4096, stage=99, dbg=False):
    NT = S // 128
    NCH = S // 512
    NSL = NCH // 2
    NOWN = NSL * 4
    SO = NSL * 512
    NCMP = (S - 32) // 16 + 1
    NCT = (NCMP + 127) // 128
    NCP = NCT * 128
    okind = "ExternalOutput" if dbg else "Internal"

    nc = bass.Bass("TRN2", target_bir_lowering=False)

    def din(name, shape, dt=F32):
        return nc.dram_tensor(name, list(shape), dt, kind="ExternalInput").ap()

    def dscr(name, shape, dt=BF16):
        return nc.dram_tensor(name, list(shape), dt, kind=okind).ap()

    xkv = din("xkv", [S, 1024])
    xq = din("xq", [SO, 1024])
    w_in = din("w_in", [1024, 7728])
    w_pd = din("w_pd", [1024, 1024])
    w_pn = din("w_pn", [1024, 1024])
    w_o = din("w_o", [1024, 1024])
    w_up = din("w_up", [1024, 4096])
    w_dn = din("w_dn", [4096, 1024])
    c_w1 = [din("c_w1k", [2048, 256]), din("c_w1v", [2048, 256])]
    c_w2 = [din("c_w2k", [256, 64]), din("c_w2v", [256, 64])]
    pos2 = [din("pos2k", [128, 16]), din("pos2v", [128, 16])]
    g_mix = din("g_mix", [1, 1024])
    g_mlp = din("g_mlp", [1, 1024])
    g_k24 = din("g_k24", [1, 1536])
    g_q32 = din("g_q32", [1, 2048])
    g_kc = din("g_kc", [1, 64])
    g_sub = din("g_sub", [1, 128])
    lam4 = din("lam4", [1, 256])
    rkv_c = din("rkv_c", [S, 32])
    rkv_s = din("rkv_s", [S, 32])
    rq_c = din("rq_c", [SO, 32])
    rq_s = din("rq_s", [SO, 32])
    rc_c = din("rc_c", [NCP, 32])
    rc_s = din("rc_s", [NCP, 32])
    identd = din("ident", [128, 128], BF16)
    eind = din("eind", [64, S], BF16)
    ovl = din("ovl", [NCP, 65], BF16)
    m_dense = din("m_dense", [128, 8, 512], BF16)
    m_win = din("m_win", [128, 12, 512], BF16)
    m_cmp = din("m_cmp", [128, NSL * NCT, 512], BF16)
    forced = din("forced", [SO, 64])
    out = nc.dram_tensor("out", [SO, 1024], F32, kind="ExternalOutput").ap()

    KdT = dscr("KdT", [8, 128, S])
    Vd = dscr("Vd", [S, 1024])
    KsT = dscr("KsT", [256, S])
    KwT = dscr("KwT", [256, S])
    kcT = dscr("kcT", [256, S])
    vcT = dscr("vcT", [256, S])
    Vs = dscr("Vs", [S, 256])
    Vw = dscr("Vw", [S, 256])
    QdT = dscr("QdT", [8, 128, SO])
    QnT = dscr("QnT", [1024, SO])
    BsT = dscr("BsT", [256, SO])
    mgS = dscr("mgS", [SO, 2048], F32)
    x1d = dscr("x1d", [SO, 1024], F32)

    with contextlib.ExitStack() as top:
        def sb(name, shape, dt, stack=top):
            return stack.enter_context(nc.sbuf_tensor(name, list(shape), dt))

        def ps(name, shape, dt, stack):
            return stack.enter_context(nc.psum_tensor(name, list(shape), dt))

        ident = sb("ident_sb", [128, 128], BF16)
        gates = sb("gates", [128, NOWN * 48], F32)
        KcT = sb("KcT", [128, 4 * NCP], BF16)
        Vca = sb("Vca", [128, NCT * 4 * 129], BF16)
        nlam = sb("nlam", [128, 1], F32)
        sgain = sb("sgain", [128, 128], F32)
        ss1 = sb("ss1", [128, 1], F32)
        rs1 = sb("rs1", [128, 1], F32)

        def bc(ap1, n):
            return ap1[0:1, :].broadcast_to([128, n])

        def rmsnorm_rows(P, xt, xkey, gt, hout, hkey, junk, sfx=""):
            P.op("pool", lambda e: e.memset(ss1[:], 0.0), writes=["ss1"])
            P.op("act", lambda e: e.activation(out=junk, in_=xt, func=AF.Square, accum_out=ss1[:]),
                 reads=[xkey, "ss1"], writes=["junk" + sfx, "ss1"])
            P.op("dve", lambda e: e.tensor_scalar(out=ss1[:], in0=ss1[:], scalar1=1.0 / 1024, scalar2=EPS,
                                                  op0=ALU.mult, op1=ALU.add), reads=["ss1"], writes=["ss1"])
            P.op("act", lambda e: e.activation(out=ss1[:], in_=ss1[:], func=AF.Ln), reads=["ss1"], writes=["ss1"])
            P.op("act", lambda e: e.activation(out=rs1[:], in_=ss1[:], func=AF.Exp, scale=-0.5),
                 reads=["ss1"], writes=["rs1"])
            P.op("dve", lambda e: e.scalar_tensor_tensor(out=hout, in0=xt, scalar=rs1[:], in1=gt,
                                                         op0=ALU.mult, op1=ALU.mult),
                 reads=[xkey, "rs1", "gt"], writes=[hkey])

        def normrope(P, src, srckey, U, gain, cos, sin, cskeys, outap, outkey, T, sfx):
            sq, ssu, rsu, xn, t1, t2, t3, t4 = T
            n = U * 64
            s3 = lambda ap: ap.rearrange("p (u d) -> p u d", d=64)
            P.op("act", lambda e: e.activation(out=sq[:, 0:n], in_=src, func=AF.Square),
                 reads=[srckey], writes=["sq" + sfx])
            P.op("dve", lambda e: e.tensor_reduce(out=ssu[:, 0:U], in_=s3(sq[:, 0:n]), axis=AX.X, op=ALU.add),
                 reads=["sq" + sfx], writes=["ssu" + sfx])
            P.op("dve", lambda e: e.tensor_scalar(out=ssu[:, 0:U], in0=ssu[:, 0:U], scalar1=1.0 / 64, scalar2=EPS,
                                                  op0=ALU.mult, op1=ALU.add), reads=["ssu" + sfx], writes=["ssu" + sfx])
            P.op("act", lambda e: e.activation(out=ssu[:, 0:U], in_=ssu[:, 0:U], func=AF.Ln),
                 reads=["ssu" + sfx], writes=["ssu" + sfx])
            P.op("act", lambda e: e.activation(out=rsu[:, 0:U], in_=ssu[:, 0:U], func=AF.Exp, scale=-0.5),
                 reads=["ssu" + sfx], writes=["rsu" + sfx])
            P.op("dve", lambda e: e.tensor_tensor(out=s3(xn[:, 0:n]), in0=s3(src),
                                                  in1=rsu[:, 0:U].unsqueeze(2).to_broadcast([128, U, 64]), op=ALU.mult),
                 reads=[srckey, "rsu" + sfx], writes=["xn" + sfx])
            P.op("pool", lambda e: e.tensor_tensor(out=xn[:, 0:n], in0=xn[:, 0:n], in1=gain, op=ALU.mult),
                 reads=["xn" + sfx, "gains"], writes=["xn" + sfx])
            x3 = s3(xn[:, 0:n])
            o3 = s3(outap)
            cb = cos.unsqueeze(1).to_broadcast([128, U, 32])
            sbb = sin.unsqueeze(1).to_broadcast([128, U, 32])
            h3 = lambda t: t[:, 0:U * 32].rearrange("p (u d) -> p u d", d=32)
            P.op("pool", lambda e: e.tensor_tensor(out=h3(t1), in0=x3[:, :, 0:32], in1=cb, op=ALU.mult),
                 reads=["xn" + sfx] + list(cskeys), writes=["t1" + sfx])
            P.op("pool", lambda e: e.tensor_tensor(out=h3(t2), in0=x3[:, :, 32:64], in1=sbb, op=ALU.mult),
                 reads=["xn" + sfx] + list(cskeys), writes=["t2" + sfx])
            P.op("pool", lambda e: e.tensor_tensor(out=o3[:, :, 0:32], in0=h3(t1), in1=h3(t2), op=ALU.subtract),
                 reads=["t1" + sfx, "t2" + sfx], writes=[outkey])
            P.op("dve", lambda e: e.tensor_tensor(out=h3(t3), in0=x3[:, :, 32:64], in1=cb, op=ALU.mult),
                 reads=["xn" + sfx] + list(cskeys), writes=["t3" + sfx])
            P.op("dve", lambda e: e.tensor_tensor(out=h3(t4), in0=x3[:, :, 0:32], in1=sbb, op=ALU.mult),
                 reads=["xn" + sfx] + list(cskeys), writes=["t4" + sfx])
            P.op("dve", lambda e: e.tensor_tensor(out=o3[:, :, 32:64], in0=h3(t3), in1=h3(t4), op=ALU.add),
                 reads=["t3" + sfx, "t4" + sfx], writes=[outkey])

        def load_w_cols(P, wdst, ranges, ncols, key, stg):
            for c in range(8):
                st_ = stg[c % 2]
                off = 0
                keys = []
                for ri, (a, b) in enumerate(ranges):
                    k = ("stg", c % 2, ri)
                    P.dma(st_[:, off:off + (b - a)], w_in[c * 128:(c + 1) * 128, a:b], writes=[k])
                    keys.append(k)
                    off += b - a
                P.op("pool", lambda e, c=c, st_=st_: e.tensor_copy(out=wdst[:, c * ncols:(c + 1) * ncols],
                                                                   in_=st_[:, 0:ncols]),
                     reads=keys, writes=[key])

        with contextlib.ExitStack() as sa:
            lt = sb("lt", [128, 256], F32, sa)
            pr = sb("pr", [128, 128], F32, sa)
            s2 = sb("s2", [128, 2], F32, sa)
            sgr = sb("sgr", [128, 128], F32, sa)
            P = Phase(nc, "pa")
            P.dma(ident[:], identd, writes=["ident"])
            P.dma(lt[:], bc(lam4, 256), writes=["lt"])
            P.dma(sgr[:], bc(g_sub, 128), writes=["sgr"])
            P.op("dve", lambda e: e.tensor_tensor(out=pr[:, 0:64], in0=lt[:, 0:64], in1=lt[:, 64:128], op=ALU.mult),
                 reads=["lt"], writes=["pr"])
            P.op("dve", lambda e: e.tensor_tensor(out=pr[:, 64:128], in0=lt[:, 128:192], in1=lt[:, 192:256], op=ALU.mult),
                 reads=["lt", "pr"], writes=["pr"])
            P.op("dve", lambda e: e.tensor_reduce(out=s2[:], in_=pr[:].rearrange("p (a d) -> p a d", d=64),
                                                  axis=AX.X, op=ALU.add), reads=["pr"], writes=["s2"])
            P.op("act", lambda e: e.activation(out=s2[:], in_=s2[:], func=AF.Exp), reads=["s2"], writes=["s2"])
            P.op("dve", lambda e: e.tensor_tensor(out=nlam[:], in0=s2[:, 1:2], in1=s2[:, 0:1], op=ALU.subtract),
                 reads=["s2"], writes=["nlam"])
            P.op("dve", lambda e: e.tensor_scalar(out=nlam[:], in0=nlam[:], scalar1=-0.2, scalar2=None, op0=ALU.add),
                 reads=["nlam"], writes=["nlam"])
            P.op("dve", lambda e: e.tensor_scalar(out=sgain[:], in0=sgr[:], scalar1=0.8, scalar2=None, op0=ALU.mult),
                 reads=["sgr"], writes=["sgain"])
            P.emit()

        with contextlib.ExitStack() as sbd:
            wbuf = sb("wbuf", [128, 8 * NQC], BF16, sbd)
            stg = [sb(f"stg{i}", [128, NQC], F32, sbd) for i in range(2)]
            gmix = sb("gmix", [128, 1024], F32, sbd)
            gains = sb("gains", [128, 2048], F32, sbd)
            xts = [sb(f"xt{i}", [128, 1024], F32, sbd) for i in range(2)]
            junk = sb("junk", [128, 1024], F32, sbd)
            hb = sb("hb", [128, 1024], BF16, sbd)
            hT = [sb(f"hT{i}", [128, 1024], BF16, sbd) for i in range(2)]
            cst = [sb(f"cs{i}", [128, 64], F32, sbd) for i in range(2)]
            TT = []
            for i in range(2):
                TT.append((sb(f"sq{i}", [128, 512], F32, sbd), sb(f"ssu{i}", [128, 8], F32, sbd),
                           sb(f"rsu{i}", [128, 8], F32, sbd), sb(f"xn{i}", [128, 512], F32, sbd),
                           sb(f"t1{i}", [128, 256], F32, sbd), sb(f"t2{i}", [128, 256], F32, sbd),
                           sb(f"t3{i}", [128, 256], F32, sbd), sb(f"t4{i}", [128, 256], F32, sbd)))
            kb = sb("kb", [128, 2048], BF16, sbd)
            vb = sb("vb", [128, 2048], BF16, sbd)
            ktb = [sb(f"ktb{i}", [128, 16 * 128], BF16, sbd) for i in range(2)]
            mgt = [sb(f"mgt{i}", [128, 512], F32, sbd) for i in range(2)]
            pT = ps("pT", [128, 1024], BF16, sbd)
            pO = [ps(f"pO{i}", [128, 512], F32, sbd) for i in range(3)]
            pK = [ps(f"pK{i}", [128, 1024], BF16, sbd) for i in range(2)]

            def tile_front(P, xsrc, t, gt):
                xt = xts[t % 2]
                rmsnorm_rows(P, xt[:], ("xt", t % 2), gt[:], hb[:], "hb", junk[:])
                for c in range(8):
                    P.op("pe", lambda e, c=c: e.transpose(out=pT[:, c * 128:(c + 1) * 128],
                                                          in_=hb[:, c * 128:(c + 1) * 128], identity=ident[:]),
                         reads=["hb", "ident"], writes=["pT"])
                P.op("dve", lambda e: e.tensor_copy(out=hT[t % 2][:], in_=pT[:]), reads=["pT"], writes=[("hT", t % 2)])

            def proj_group(P, t, j, ncols, col0, width):
                po = pO[j % 3]
                for c in range(8):
                    P.op("pe", lambda e, c=c: e.matmul(po[:, 0:width], lhsT=hT[t % 2][:, c * 128:(c + 1) * 128],
                                                       rhs=wbuf[:, c * ncols + col0:c * ncols + col0 + width],
                                                       start=(c == 0), stop=(c == 7)),
                         reads=[("hT", t % 2), "wbuf"], writes=[("pO", j % 3)])
                return po

            P = Phase(nc, "pb")
            load_w_cols(P, wbuf, KV_RANGES, NKV, "wbuf", stg)
            P.dma(gmix[:], bc(g_mix, 1024), writes=["gt"])
            P.dma(gains[:, 0:1536], bc(g_k24, 1536), writes=["gains"])
            P.dma(xts[0][:], xkv[0:128, :], writes=[("xt", 0)])
            for t in range(NT):
                if t + 1 < NT:
                    P.dma(xts[(t + 1) % 2][:], xkv[(t + 1) * 128:(t + 2) * 128, :], writes=[("xt", (t + 1) % 2)])
                cs = cst[t % 2]
                P.dma(cs[:, 0:32], rkv_c[t * 128:(t + 1) * 128, :], writes=[("cs", t % 2, 0)])
                P.dma(cs[:, 32:64], rkv_s[t * 128:(t + 1) * 128, :], writes=[("cs", t % 2, 1)])
                tile_front(P, xkv, t, gmix)
                for j in range(7):
                    po = proj_group(P, t, j, NKV, j * 512, 512)
                    if j < 3:
                        normrope(P, po[:], ("pO", j % 3), 8, gains[:, j * 512:(j + 1) * 512], cs[:, 0:32], cs[:, 32:64],
                                 [("cs", t % 2, 0), ("cs", t % 2, 1)], kb[:, j * 512:(j + 1) * 512], ("kb", j), TT[j % 2], str(j % 2))
                    else:
                        P.op("act", lambda e, po=po, j=j: e.activation(out=vb[:, (j - 3) * 512:(j - 2) * 512], in_=po[:],
                                                                       func=AF.Identity),
                             reads=[("pO", j % 3)], writes=[("vb", j)])
                kt = ktb[t % 2]
                for blk in range(16):
                    src = kb[:, blk * 128:(blk + 1) * 128] if blk < 12 else vb[:, 1536 + (blk - 12) * 128:1536 + (blk - 11) * 128]
                    rk = [("kb", blk // 4)] if blk < 12 else [("vb", 6)]
                    P.op("pe", lambda e, src=src, blk=blk: e.transpose(out=pK[blk // 8][:, (blk % 8) * 128:(blk % 8 + 1) * 128],
                                                                        in_=src, identity=ident[:]),
                         reads=rk + ["ident"], writes=[("pK", blk // 8)])
                for hf in range(2):
                    P.op("dve", lambda e, hf=hf, kt=kt: e.tensor_copy(out=kt[:, hf * 1024:(hf + 1) * 1024], in_=pK[hf][:]),
                         reads=[("pK", hf)], writes=[("ktb", t % 2, hf)])
                tok = slice(t * 128, (t + 1) * 128)
                P.dma(KdT[:, :, tok].rearrange("h p k -> p h k"), kt[:, 0:1024].rearrange("p (h k) -> p h k", k=128),
                      reads=[("ktb", t % 2, 0)])
                for i, dst in enumerate((KsT, KwT, kcT, vcT)):
                    P.dma(dst[:, tok].rearrange("(i p) k -> p i k", p=128),
                          kt[:, 1024 + i * 256:1024 + (i + 1) * 256].rearrange("p (i k) -> p i k", k=128),
                          reads=[("ktb", t % 2, 1)])
                P.dma(Vd[tok, :], vb[:, 0:1024], reads=[("vb", 3), ("vb", 4)])
                P.dma(Vs[tok, :], vb[:, 1024:1280], reads=[("vb", 5)])
                P.dma(Vw[tok, :], vb[:, 1280:1536], reads=[("vb", 5)])
            P.emit()
            if stage <= 1:
                return nc

            P = Phase(nc, "pd")
            load_w_cols(P, wbuf, Q_RANGES, NQC, "wbuf", stg)
            P.dma(gains[:, 0:2048], bc(g_q32, 2048), writes=["gains"])
            P.dma(xts[0][:], xq[0:128, :], writes=[("xt", 0)])
            for t in range(NOWN):
                if t + 1 < NOWN:
                    P.dma(xts[(t + 1) % 2][:], xq[(t + 1) * 128:(t + 2) * 128, :], writes=[("xt", (t + 1) % 2)])
                cs = cst[t % 2]
                P.dma(cs[:, 0:32], rq_c[t * 128:(t + 1) * 128, :], writes=[("cs", t % 2, 0)])
                P.dma(cs[:, 32:64], rq_s[t * 128:(t + 1) * 128, :], writes=[("cs", t % 2, 1)])
                tile_front(P, xq, t, gmix)
                tok = slice(t * 128, (t + 1) * 128)
                for j in range(4):
                    po = proj_group(P, t, j, NQC, j * 512, 512)
                    normrope(P, po[:], ("pO", j % 3), 8, gains[:, j * 512:(j + 1) * 512], cs[:, 0:32], cs[:, 32:64],
                             [("cs", t % 2, 0), ("cs", t % 2, 1)], kb[:, j * 512:(j + 1) * 512], ("kb", j), TT[j % 2], str(j % 2))
                po = proj_group(P, t, 4, NQC, 2048, 48)
                gsl = gates[:, t * 48:(t + 1) * 48]
                P.op("act", lambda e, po=po, gsl=gsl: e.activation(out=gsl, in_=po[:, 0:48], func=AF.Exp, scale=-1.0),
                     reads=[("pO", 1)], writes=["gates"])
                P.op("dve", lambda e, gsl=gsl: e.tensor_scalar(out=gsl, in0=gsl, scalar1=1.0, scalar2=None, op0=ALU.add),
                     reads=["gates"], writes=["gates"])
                P.op("dve", lambda e, gsl=gsl: e.reciprocal(out=gsl, in_=gsl), reads=["gates"], writes=["gates"])
                for j in range(5, 9):
                    po = proj_group(P, t, j, NQC, 2096 + (j - 5) * 512, 512)
                    mg_ = mgt[j % 2]
                    P.op("act", lambda e, po=po, mg_=mg_: e.activation(out=mg_[:], in_=po[:], func=AF.Exp, scale=-1.0),
                         reads=[("pO", j % 3)], writes=[("mgt", j % 2)])
                    P.op("pool", lambda e, mg_=mg_: e.tensor_scalar(out=mg_[:], in0=mg_[:], scalar1=1.0, scalar2=None, op0=ALU.add),
                         reads=[("mgt", j % 2)], writes=[("mgt", j % 2)])
                    P.op("dve", lambda e, mg_=mg_: e.reciprocal(out=mg_[:], in_=mg_[:]),
                         reads=[("mgt", j % 2)], writes=[("mgt", j % 2)])
                    P.dma(mgS[tok, (j - 5) * 512:(j - 4) * 512], mg_[:], reads=[("mgt", j % 2)])
                kt = ktb[t % 2]
                for blk in range(16):
                    P.op("pe", lambda e, blk=blk: e.transpose(out=pK[blk // 8][:, (blk % 8) * 128:(blk % 8 + 1) * 128],
                                                              in_=kb[:, blk * 128:(blk + 1) * 128], identity=ident[:]),
                         reads=[("kb", blk // 4), "ident"], writes=[("pK", blk // 8)])
                for hf in range(2):
                    P.op("dve", lambda e, hf=hf, kt=kt: e.tensor_copy(out=kt[:, hf * 1024:(hf + 1) * 1024], in_=pK[hf][:]),
                         reads=[("pK", hf)], writes=[("ktb", t % 2, hf)])
                P.dma(QdT[:, :, tok].rearrange("h p k -> p h k"), kt[:, 0:1024].rearrange("p (h k) -> p h k", k=128),
                      reads=[("ktb", t % 2, 0)])
                P.dma(QnT[:, tok].rearrange("(i p) k -> p i k", p=128), kt[:, 1024:2048].rearrange("p (i k) -> p i k", k=128),
                      reads=[("ktb", t % 2, 1)])
            P.emit()
        if stage <= 2:
            return nc

        ydiff = sb("ydiff", [128, NOWN * 1024], BF16)
        ynsa = sb("ynsa", [128, NOWN * 1024], BF16)
        with contextlib.ExitStack() as sc:
            W1s = [sb(f"W1s{i}", [128, 16 * 256], BF16, sc) for i in range(2)]
            W2s = [sb(f"W2s{i}", [128, 128], BF16, sc) for i in range(2)]
            w1stg = sb("w1stg", [128, 16 * 256], F32, sc)
            w2stg = sb("w2stg", [128, 128], F32, sc)
            p2f = sb("p2f", [128, 16], F32, sc)
            p2b = [sb(f"p2b{i}", [128, 16], BF16, sc) for i in range(2)]
            kc2 = [sb(f"kc2_{i}", [128, S], BF16, sc) for i in range(2)]
            biasv = [sb(f"biasv{i}", [128, 2], F32, sc) for i in range(2)]
            h1T = sb("h1T", [128, 2 * NCP], BF16, sc)
            xg = sb("xg", [128, NCP], F32, sc)
            x2 = sb("x2", [128, NCP], F32, sc)
            th = sb("th", [128, NCP], F32, sc)
            kcn = sb("kcn", [128, 128], BF16, sc)
            gkc = sb("gkc", [128, 64], F32, sc)
            csc = sb("csc", [128, NCT * 64], F32, sc)
            TC = (sb("c_sq", [128, 64], F32, sc), sb("c_ssu", [128, 8], F32, sc), sb("c_rsu", [128, 8], F32, sc),
                  sb("c_xn", [128, 64], F32, sc), sb("c_t1", [128, 32], F32, sc), sb("c_t2", [128, 32], F32, sc),
                  sb("c_t3", [128, 32], F32, sc), sb("c_t4", [128, 32], F32, sc))
            pH = [ps(f"pH{i}", [128, 512], F32, sc) for i in range(2)]
            pB = ps("pB", [128, 512], F32, sc)
            pC = ps("pC", [128, 512], F32, sc)
            pKc = ps("pKc", [128, 1024], BF16, sc)
            P = Phase(nc, "pc")
            P.op("pool", lambda e: e.memset(h1T[:], 0.0), writes=["h1T"])
            P.op("pool", lambda e: e.memset(kcn[:], 0.0), writes=["kcn"])
            P.dma(gkc[:], bc(g_kc, 64), writes=["gains"])
            for nt in range(NCT):
                P.dma(csc[:, nt * 64:nt * 64 + 32], rc_c[nt * 128:(nt + 1) * 128, :], writes=[("csc", nt, 0)])
                P.dma(csc[:, nt * 64 + 32:nt * 64 + 64], rc_s[nt * 128:(nt + 1) * 128, :], writes=[("csc", nt, 1)])
                for g in range(4):
                    o0 = (nt * 4 + g) * 129
                    P.dma(Vca[:, o0 + 64:o0 + 129], ovl[nt * 128:(nt + 1) * 128, :], writes=[("vca_c", nt, g)])
            for kv in range(2):
                P.dma(w1stg[:].rearrange("p (u h) -> p u h", h=256), c_w1[kv].rearrange("(u p) h -> p u h", p=128),
                      writes=["w1stg"])
                P.op("pool", lambda e, kv=kv: e.tensor_copy(out=W1s[kv][:], in_=w1stg[:]), reads=["w1stg"], writes=[("W1s", kv)])
                P.dma(w2stg[:].rearrange("p (a d) -> p a d", d=64), c_w2[kv].rearrange("(a p) d -> p a d", p=128),
                      writes=["w2stg"])
                P.op("pool", lambda e, kv=kv: e.tensor_copy(out=W2s[kv][:], in_=w2stg[:]), reads=["w2stg"], writes=[("W2s", kv)])
                P.dma(p2f[:], pos2[kv], writes=["p2f"])
                P.op("pool", lambda e, kv=kv: e.tensor_copy(out=p2b[kv][:], in_=p2f[:]), reads=["p2f"], writes=[("p2b", kv)])
                for half in range(2):
                    for u in range(16):
                        P.op("pe", lambda e, kv=kv, half=half, u=u: e.matmul(
                            pB[:, half:half + 1], lhsT=W1s[kv][:, u * 256 + half * 128:u * 256 + half * 128 + 128],
                            rhs=p2b[kv][:, u:u + 1], start=(u == 0), stop=(u == 15)),
                            reads=[("W1s", kv), ("p2b", kv)], writes=["pB"])
                P.op("dve", lambda e, kv=kv: e.tensor_copy(out=biasv[kv][:], in_=pB[:, 0:2]), reads=["pB"], writes=[("biasv", kv)])
                src = kcT if kv == 0 else vcT
                for g in range(4):
                    kb2 = kc2[g % 2]
                    kk = ("kc2", g % 2)
                    P.dma(kb2[0:64, :], src[g * 64:(g + 1) * 64, :], writes=[(kk, 0)])
                    P.dma(kb2[64:128, 0:S - 1], src[g * 64:(g + 1) * 64, 1:S], writes=[(kk, 1)])
                    for half in range(2):
                        for u in range(16):
                            rhs = bass.AP(kb2, 2 * u, [[S, 128], [16, NCMP]])
                            P.op("pe", lambda e, kv=kv, half=half, u=u, rhs=rhs: e.matmul(
                                pH[half][:, 0:NCMP], lhsT=W1s[kv][:, u * 256 + half * 128:u * 256 + half * 128 + 128],
                                rhs=rhs, start=(u == 0), stop=(u == 15)),
                                reads=[("W1s", kv), (kk, 0), (kk, 1)], writes=[("pH", half)])
                        hsl = h1T[:, half * NCP:half * NCP + NCMP]
                        P.op("dve", lambda e, kv=kv, half=half: e.tensor_scalar(
                            out=xg[:, 0:NCMP], in0=pH[half][:, 0:NCMP], scalar1=biasv[kv][:, half:half + 1], scalar2=None,
                            op0=ALU.add), reads=[("pH", half), ("biasv", kv)], writes=["xg"])
                        P.op("pool", lambda e: e.tensor_tensor(out=x2[:, 0:NCMP], in0=xg[:, 0:NCMP], in1=xg[:, 0:NCMP],
                                                               op=ALU.mult), reads=["xg"], writes=["x2"])
                        P.op("pool", lambda e: e.tensor_scalar(out=x2[:, 0:NCMP], in0=x2[:, 0:NCMP], scalar1=0.044715,
                                                               scalar2=1.0, op0=ALU.mult, op1=ALU.add),
                             reads=["x2"], writes=["x2"])
                        P.op("pool", lambda e: e.tensor_tensor(out=x2[:, 0:NCMP], in0=x2[:, 0:NCMP], in1=xg[:, 0:NCMP],
                                                               op=ALU.mult), reads=["x2", "xg"], writes=["x2"])
                        P.op("act", lambda e: e.activation(out=th[:, 0:NCMP], in_=x2[:, 0:NCMP], func=AF.Tanh,
                                                           scale=0.7978845608028654), reads=["x2"], writes=["th"])
                        P.op("dve", lambda e: e.tensor_scalar(out=th[:, 0:NCMP], in0=th[:, 0:NCMP], scalar1=0.5, scalar2=0.5,
                                                              op0=ALU.mult, op1=ALU.add), reads=["th"], writes=["th"])
                        P.op("dve", lambda e, hsl=hsl: e.tensor_tensor(out=hsl, in0=th[:, 0:NCMP], in1=xg[:, 0:NCMP],
                                                                        op=ALU.mult), reads=["th", "xg"], writes=["h1T"])
                    for nt in range(NCT):
                        for half in range(2):
                            P.op("pe", lambda e, kv=kv, nt=nt, half=half: e.matmul(
                                pC[:, 0:64], lhsT=h1T[:, half * NCP + nt * 128:half * NCP + nt * 128 + 128],
                                rhs=W2s[kv][:, half * 64:(half + 1) * 64], start=(half == 0), stop=(half == 1)),
                                reads=["h1T", ("W2s", kv)], writes=["pC"])
                        if kv == 0:
                            normrope(P, pC[:, 0:64], "pC", 1, gkc[:], csc[:, nt * 64:nt * 64 + 32],
                                     csc[:, nt * 64 + 32:nt * 64 + 64], [("csc", nt, 0), ("csc", nt, 1)],
                                     kcn[:, 0:64], "kcn", TC, "c")
                            P.op("pe", lambda e: e.transpose(out=pKc[:, 0:128], in_=kcn[:], identity=ident[:]),
                                 reads=["kcn", "ident"], writes=["pKc"])
                            P.op("dve", lambda e, g=g, nt=nt: e.tensor_copy(
                                out=KcT[0:64, g * NCP + nt * 128:g * NCP + nt * 128 + 128], in_=pKc[0:64, 0:128]),
                                reads=["pKc"], writes=["KcT"])
                        else:
                            o0 = (nt * 4 + g) * 129
                            P.op("act", lambda e, o0=o0: e.activation(out=Vca[:, o0:o0 + 64], in_=pC[:, 0:64], func=AF.Identity),
                                 reads=["pC"], writes=["Vca"])
            if dbg:
                dK = nc.dram_tensor("dbg_KcT", [64, 4 * NCP], BF16, kind="ExternalOutput").ap()
                dV = nc.dram_tensor("dbg_Vca", [128, NCT * 4 * 129], BF16, kind="ExternalOutput").ap()
                P.dma(dK, KcT[0:64, :], reads=["KcT"])
                P.dma(dV, Vca[:], reads=["Vca"] + [("vca_c", nt, g) for nt in range(NCT) for g in range(4)])
            P.emit()
        if stage <= 3:
            return nc

        with contextlib.ExitStack() as se:
            Qt = [sb(f"e_Qt{i}", [128, SO], BF16, se) for i in range(2)]
            mcmp = sb("e_mcmp", [128, NSL * NCT * 512], BF16, se)
            ET = [sb(f"e_ET{i}", [128, 512], BF16, se) for i in range(2)]
            imp = sb("e_imp", [128, NOWN * 64], F32, se)
            frc = sb("e_frc", [128, NOWN * 64], F32, se)
            rl = sb("e_rl", [128, 4], F32, se)
            impf = sb("e_impf", [128, 64], F32, se)
            tmpf = sb("e_tmpf", [128, 64], F32, se)
            m8 = sb("e_m8", [128, 8], F32, se)
            m8b = sb("e_m8b", [128, 8], F32, se)
            bt = sb("e_bt", [128, 128], BF16, se)
            bstage = sb("e_bstage", [128, SO], BF16, se)
            pS = [ps(f"e_pS{i}", [128, 512], F32, se) for i in range(2)]
            pU = [ps(f"e_pU{i}", [128, 512], F32, se) for i in range(2)]
            pBt = ps("e_pBt", [128, 1024], BF16, se)
            P = Phase(nc, "pe")
            P.dma(mcmp[:].rearrange("p (a q) -> p a q", q=512), m_cmp, writes=["mcmp"])
            P.dma(frc[:].rearrange("p (t j) -> p t j", j=64), forced.rearrange("(t p) j -> p t j", p=128), writes=["frc"])
            P.op("pool", lambda e: e.memset(bt[:], 0.0), writes=["bt"])
            it = 0
            for g in range(4):
                P.op("pool", lambda e: e.memset(imp[:], 0.0), writes=["imp"])
                for h in range(4):
                    hq = 4 * g + h
                    qt = Qt[hq % 2]
                    P.dma(qt[0:64, :], QnT[hq * 64:(hq + 1) * 64, :], writes=[("Qt", hq % 2)])
                    for s_ in range(NSL):
                        for nt in range(NCT):
                            i = it % 2
                            it += 1
                            P.op("pe", lambda e, i=i, g=g, nt=nt, s_=s_, qt=qt: e.matmul(
                                pS[i][:], lhsT=KcT[0:64, g * NCP + nt * 128:g * NCP + nt * 128 + 128],
                                rhs=qt[0:64, s_ * 512:(s_ + 1) * 512], start=True, stop=False),
                                reads=["KcT", ("Qt", hq % 2)], writes=[("pS", i)])
                            P.op("pe", lambda e, i=i, nt=nt, s_=s_: e.matmul(
                                pS[i][:], lhsT=ident[:], rhs=mcmp[:, (s_ * NCT + nt) * 512:(s_ * NCT + nt + 1) * 512],
                                start=False, stop=True), reads=["ident", "mcmp"], writes=[("pS", i)])
                            P.op("act", lambda e, i=i: e.activation(out=ET[i][:], in_=pS[i][:], func=AF.Exp, scale=0.125),
                                 reads=[("pS", i)], writes=[("ET", i)])
                            for qs in range(4):
                                o0 = (nt * 4 + g) * 129
                                P.op("pe", lambda e, i=i, qs=qs, o0=o0, nt=nt: e.matmul(
                                    pU[qs // 2][:, (qs % 2) * 129:(qs % 2) * 129 + 129], lhsT=ET[i][:, qs * 128:(qs + 1) * 128],
                                    rhs=Vca[:, o0:o0 + 129], start=(nt == 0 and qs % 2 == 0), stop=(nt == NCT - 1),
                                    skip_group_check=True),
                                    reads=[("ET", i), "Vca"], writes=[("pU", qs // 2)])
                        for qs in range(4):
                            tl = s_ * 4 + qs
                            u0 = (qs % 2) * 129
                            pu = pU[qs // 2]
                            pk = ("pU", qs // 2)
                            P.op("dve", lambda e, pu=pu, u0=u0, qs=qs: e.tensor_scalar(
                                out=rl[:, qs:qs + 1], in0=pu[:, u0 + 128:u0 + 129], scalar1=1e-30, scalar2=None, op0=ALU.max),
                                reads=[pk], writes=["rl"])
                            P.op("dve", lambda e, qs=qs: e.reciprocal(out=rl[:, qs:qs + 1], in_=rl[:, qs:qs + 1]),
                                 reads=["rl"], writes=["rl"])
                            ysl = ynsa[:, tl * 1024 + hq * 64:tl * 1024 + hq * 64 + 64]
                            gcol = gates[:, tl * 48 + hq * 3:tl * 48 + hq * 3 + 1]
                            P.op("dve", lambda e, pu=pu, u0=u0, qs=qs, ysl=ysl, gcol=gcol: e.tensor_scalar(
                                out=ysl, in0=pu[:, u0:u0 + 64], scalar1=rl[:, qs:qs + 1], scalar2=gcol,
                                op0=ALU.mult, op1=ALU.mult), reads=[pk, "rl", "gates"], writes=[("ynsa", tl)])
                            isl = imp[:, tl * 64:(tl + 1) * 64]
                            P.op("dve", lambda e, pu=pu, u0=u0, qs=qs, isl=isl: e.scalar_tensor_tensor(
                                out=isl, in0=pu[:, u0 + 64:u0 + 128], scalar=rl[:, qs:qs + 1], in1=isl,
                                op0=ALU.mult, op1=ALU.add), reads=[pk, "rl", "imp"], writes=["imp"])
                for tl in range(NOWN):
                    isl = imp[:, tl * 64:(tl + 1) * 64]
                    P.op("dve", lambda e, isl=isl, tl=tl: e.tensor_tensor(out=impf[:], in0=isl, in1=frc[:, tl * 64:(tl + 1) * 64],
                                                                          op=ALU.max), reads=["imp", "frc"], writes=["impf"])
                    P.op("dve", lambda e: e.max(out=m8[:], in_=impf[:]), reads=["impf"], writes=["m8"])
                    P.op("dve", lambda e: e.match_replace(out=tmpf[:], in_to_replace=m8[:], in_values=impf[:], imm_value=-1e9),
                         reads=["m8", "impf"], writes=["tmpf"])
                    P.op("dve", lambda e: e.max(out=m8b[:], in_=tmpf[:]), reads=["tmpf"], writes=["m8b"])
                    P.op("dve", lambda e: e.tensor_scalar(out=tmpf[:], in0=impf[:], scalar1=m8b[:, 7:8], scalar2=None,
                                                          op0=ALU.is_ge), reads=["impf", "m8b", "tmpf"], writes=["tmpf"])
                    P.op("dve", lambda e: e.tensor_scalar(out=bt[:, 64:128], in0=tmpf[:], scalar1=-1.0, scalar2=-NEG,
                                                          op0=ALU.add, op1=ALU.mult), reads=["tmpf"], writes=["bt"])
                    P.op("pe", lambda e: e.transpose(out=pBt[:, 0:128], in_=bt[:], identity=ident[:]),
                         reads=["bt", "ident"], writes=["pBt"])
                    P.op("dve", lambda e, tl=tl: e.tensor_copy(out=bstage[64:128, tl * 128:(tl + 1) * 128], in_=pBt[64:128, 0:128]),
                         reads=["pBt"], writes=["bstage"])
                P.dma(BsT[g * 64:(g + 1) * 64, :], bstage[64:128, :], reads=["bstage"])
            P.emit()
        if stage <= 4:
            return nc

        with contextlib.ExitStack() as sf:
            KD = [sb(f"f_KD{i}", [128, S], BF16, sf) for i in range(2)]
            VD = [sb(f"f_VD{i}", [128, NT * 129], BF16, sf) for i in range(2)]
            QD = [sb(f"f_QD{i}", [128, SO], BF16, sf) for i in range(2)]
            KS = sb("f_KS", [128, S], BF16, sf)
            KW = sb("f_KW", [128, S], BF16, sf)
            VS = sb("f_VS", [128, NT * 65], BF16, sf)
            VW = sb("f_VW", [128, NT * 65], BF16, sf)
            QS = [sb(f"f_QS{i}", [128, SO], BF16, sf) for i in range(2)]
            mden = sb("f_mden", [128, 8 * 512], BF16, sf)
            mwin = sb("f_mwin", [128, 12 * 512], BF16, sf)
            PT = [sb(f"f_PT{i}", [128, 512], BF16, sf) for i in range(4)]
            oa = sb("f_oa", [128, 128], F32, sf)
            ob = sb("f_ob", [128, 128], F32, sf)
            jk = sb("f_jk", [128, 128], F32, sf)
            r4 = sb("f_r4", [128, 4], F32, sf)
            ssd = sb("f_ssd", [128, 1], F32, sf)
            rsd = sb("f_rsd", [128, 1], F32, sf)
            tn = sb("f_tn", [128, 64], F32, sf)
            pS = [ps(f"f_pS{i}", [128, 512], F32, sf) for i in range(4)]
            pO = [ps(f"f_pO{i}", [128, 512], F32, sf) for i in range(3)]
            P = Phase(nc, "pf")
            P.dma(mden[:].rearrange("p (a q) -> p a q", q=512), m_dense, writes=["mden"])
            P.dma(mwin[:].rearrange("p (a q) -> p a q", q=512), m_win, writes=["mwin"])
            P.dma(KS[64:128, :], eind, writes=["KS_e"])
            for i in range(2):
                P.op("pool", lambda e, i=i: e.memset(VD[i][:], 1.0), writes=[("VD", i)])
            P.op("pool", lambda e: e.memset(VS[:], 1.0), writes=["VS"])
            P.op("pool", lambda e: e.memset(VW[:], 1.0), writes=["VW"])

            def load_v(dst, dkey, src2d, width, stride):
                d3 = dst[:].rearrange("p (t c) -> p t c", c=stride)
                s3 = src2d.rearrange("(t p) c -> p t c", p=128)
                step = 8
                for a in range(0, NT, step):
                    b_ = min(NT, a + step)
                    P.dma(d3[:, a:b_, 0:width], s3[:, a:b_, :], writes=[dkey])

            def load_diff(h):
                i = h % 2
                P.dma(KD[i][:], KdT[h], writes=[("KD", i)])
                load_v(VD[i], ("VD", i), Vd[:, h * 128:(h + 1) * 128], 128, 129)
                P.dma(QD[i][:], QdT[h], writes=[("QD", i)])

            load_diff(0)
            for h in range(8):
                if h + 1 < 8:
                    load_diff(h + 1)
                bi = h % 2
                for s_ in range(NSL):
                    nkt = 8 * (s_ + 1)

                    def qk(kt, s_=s_, nkt=nkt, bi=bi):
                        i2 = kt % 2
                        masked = kt >= nkt - 8
                        mi = kt - (nkt - 8)
                        for m in range(2):
                            bk = 2 * i2 + m
                            P.op("pe", lambda e, bk=bk, m=m, kt=kt: e.matmul(
                                pS[bk][:], lhsT=KD[bi][m * 64:(m + 1) * 64, kt * 128:(kt + 1) * 128],
                                rhs=QD[bi][m * 64:(m + 1) * 64, s_ * 512:(s_ + 1) * 512], start=True, stop=not masked),
                                reads=[("KD", bi), ("QD", bi)], writes=[("pS", bk)])
                            if masked:
                                P.op("pe", lambda e, bk=bk, mi=mi: e.matmul(
                                    pS[bk][:], lhsT=ident[:], rhs=mden[:, mi * 512:(mi + 1) * 512], start=False, stop=True),
                                    reads=["ident", "mden"], writes=[("pS", bk)])
                            P.op("act", lambda e, bk=bk: e.activation(out=PT[bk][:], in_=pS[bk][:], func=AF.Exp, scale=0.125),
                                 reads=[("pS", bk)], writes=[("PT", bk)])

                    def pv(kt, s_=s_, nkt=nkt, bi=bi):
                        i2 = kt % 2
                        for m in range(2):
                            bk = 2 * i2 + m
                            for qs in range(4):
                                a = m * 4 + qs
                                P.op("pe", lambda e, bk=bk, qs=qs, a=a, kt=kt: e.matmul(
                                    pO[a // 3][:, (a % 3) * 129:(a % 3) * 129 + 129], lhsT=PT[bk][:, qs * 128:(qs + 1) * 128],
                                    rhs=VD[bi][:, kt * 129:kt * 129 + 129], start=(kt == 0 and a % 3 == 0), stop=(kt == nkt - 1),
                                    skip_group_check=True),
                                    reads=[("PT", bk), ("VD", bi)], writes=[("pO", a // 3)])

                    qk(0)
                    for kt in range(nkt):
                        if kt + 1 < nkt:
                            qk(kt + 1)
                        pv(kt)
                    for qs in range(4):
                        tl = s_ * 4 + qs
                        a0, a1 = qs, 4 + qs
                        o0 = pO[a0 // 3][:, (a0 % 3) * 129:(a0 % 3) * 129 + 129]
                        o1 = pO[a1 // 3][:, (a1 % 3) * 129:(a1 % 3) * 129 + 129]
                        k0, k1 = ("pO", a0 // 3), ("pO", a1 // 3)
                        P.op("dve", lambda e, o0=o0: e.reciprocal(out=r4[:, 0:1], in_=o0[:, 128:129]), reads=[k0], writes=["r4"])
                        P.op("dve", lambda e, o1=o1: e.reciprocal(out=r4[:, 1:2], in_=o1[:, 128:129]), reads=[k1, "r4"], writes=["r4"])
                        P.op("dve", lambda e: e.tensor_tensor(out=r4[:, 2:3], in0=r4[:, 1:2], in1=nlam[:], op=ALU.mult),
                             reads=["r4", "nlam"], writes=["r4"])
                        P.op("dve", lambda e, o0=o0: e.tensor_scalar(out=oa[:], in0=o0[:, 0:128], scalar1=r4[:, 0:1], scalar2=None,
                                                                    op0=ALU.mult), reads=[k0, "r4"], writes=["oa"])
                        P.op("dve", lambda e, o1=o1: e.scalar_tensor_tensor(out=ob[:], in0=o1[:, 0:128], scalar=r4[:, 2:3], in1=oa[:],
                                                                           op0=ALU.mult, op1=ALU.add),
                             reads=[k1, "r4", "oa"], writes=["ob"])
                        P.op("pool", lambda e: e.memset(ssd[:], 0.0), writes=["ssd"])
                        P.op("act", lambda e: e.activation(out=jk[:], in_=ob[:], func=AF.Square, accum_out=ssd[:]),
                             reads=["ob", "ssd"], writes=["jk", "ssd"])
                        P.op("dve", lambda e: e.tensor_scalar(out=ssd[:], in0=ssd[:], scalar1=1.0 / 128, scalar2=EPS,
                                                              op0=ALU.mult, op1=ALU.add), reads=["ssd"], writes=["ssd"])
                        P.op("act", lambda e: e.activation(out=ssd[:], in_=ssd[:], func=AF.Ln), reads=["ssd"], writes=["ssd"])
                        P.op("act", lambda e: e.activation(out=rsd[:], in_=ssd[:], func=AF.Exp, scale=-0.5),
                             reads=["ssd"], writes=["rsd"])
                        ysl = ydiff[:, tl * 1024 + h * 128:tl * 1024 + (h + 1) * 128]
                        P.op("dve", lambda e, ysl=ysl: e.scalar_tensor_tensor(out=ysl, in0=ob[:], scalar=rsd[:], in1=sgain[:],
                                                                              op0=ALU.mult, op1=ALU.mult),
                             reads=["ob", "rsd", "sgain"], writes=[("ydiff", tl)])

            cnt = [0]
            for g in range(4):
                P.dma(KS[0:64, :], KsT[g * 64:(g + 1) * 64, :], writes=["KS"])
                P.dma(KW[0:64, :], KwT[g * 64:(g + 1) * 64, :], writes=["KW"])
                load_v(VS, "VS", Vs[:, g * 64:(g + 1) * 64], 64, 65)
                load_v(VW, "VW", Vw[:, g * 64:(g + 1) * 64], 64, 65)
                for h in range(4):
                    hq = 4 * g + h
                    qi = hq % 2
                    P.dma(QS[qi][0:64, :], QnT[hq * 64:(hq + 1) * 64, :], writes=[("QS", qi, 0)])
                    P.dma(QS[qi][64:128, :], BsT[g * 64:(g + 1) * 64, :], writes=[("QS", qi, 1)])
                    for kind in (2, 1):
                        for s_ in range(NSL):
                            if kind == 1:
                                kts = list(range(0, 8 * (s_ + 1)))
                                mis = [kt - 8 * s_ if kt >= 8 * s_ else None for kt in kts]
                            else:
                                kts = list(range(max(0, 8 * s_ - 4), 8 * s_ + 8))
                                mis = [kt - (8 * s_ - 4) for kt in kts]
                            banks = []
                            for _ in kts:
                                banks.append(cnt[0] % 4)
                                cnt[0] += 1

                            def qk(j, kind=kind, s_=s_, kts=kts, mis=mis, banks=banks, qi=qi):
                                kt, mi, bk = kts[j], mis[j], banks[j]
                                if kind == 1:
                                    lhsT = KS[:, kt * 128:(kt + 1) * 128]
                                    rhs = QS[qi][:, s_ * 512:(s_ + 1) * 512]
                                    rd = ["KS", "KS_e", ("QS", qi, 0), ("QS", qi, 1)]
                                    mt = mden
                                else:
                                    lhsT = KW[0:64, kt * 128:(kt + 1) * 128]
                                    rhs = QS[qi][0:64, s_ * 512:(s_ + 1) * 512]
                                    rd = ["KW", ("QS", qi, 0)]
                                    mt = mwin
                                P.op("pe", lambda e: e.matmul(pS[bk][:], lhsT=lhsT, rhs=rhs, start=True, stop=(mi is None)),
                                     reads=rd, writes=[("pS", bk)])
                                if mi is not None:
                                    P.op("pe", lambda e: e.matmul(pS[bk][:], lhsT=ident[:], rhs=mt[:, mi * 512:(mi + 1) * 512],
                                                                  start=False, stop=True),
                                         reads=["ident", "mden", "mwin"], writes=[("pS", bk)])
                                P.op("act", lambda e: e.activation(out=PT[bk][:], in_=pS[bk][:], func=AF.Exp, scale=0.125),
                                     reads=[("pS", bk)], writes=[("PT", bk)])

                            def pv(j, kind=kind, kts=kts, banks=banks):
                                kt, bk = kts[j], banks[j]
                                vt, vk = (VS, "VS") if kind == 1 else (VW, "VW")
                                for qs in range(4):
                                    P.op("pe", lambda e, qs=qs: e.matmul(
                                        pO[0][:, qs * 65:qs * 65 + 65], lhsT=PT[bk][:, qs * 128:(qs + 1) * 128],
                                        rhs=vt[:, kt * 65:kt * 65 + 65], start=(j == 0 and qs == 0), stop=(j == len(kts) - 1),
                                        skip_group_check=True),
                                        reads=[("PT", bk), vk], writes=[("pO", 0)])

                            qk(0)
                            for j in range(len(kts)):
                                if j + 1 < len(kts):
                                    qk(j + 1)
                                pv(j)
                            for qs in range(4):
                                tl = s_ * 4 + qs
                                oq = pO[0][:, qs * 65:qs * 65 + 65]
                                P.op("dve", lambda e, oq=oq, qs=qs: e.tensor_scalar(
                                    out=r4[:, qs:qs + 1], in0=oq[:, 64:65], scalar1=1e-30, scalar2=None, op0=ALU.max),
                                    reads=[("pO", 0)], writes=["r4"])
                                P.op("dve", lambda e, qs=qs: e.reciprocal(out=r4[:, qs:qs + 1], in_=r4[:, qs:qs + 1]),
                                     reads=["r4"], writes=["r4"])
                                gcol = gates[:, tl * 48 + hq * 3 + kind:tl * 48 + hq * 3 + kind + 1]
                                P.op("dve", lambda e, oq=oq, qs=qs, gcol=gcol: e.tensor_scalar(
                                    out=tn[:], in0=oq[:, 0:64], scalar1=r4[:, qs:qs + 1], scalar2=gcol,
                                    op0=ALU.mult, op1=ALU.mult), reads=[("pO", 0), "r4", "gates"], writes=["tn"])
                                ysl = ynsa[:, tl * 1024 + hq * 64:tl * 1024 + hq * 64 + 64]
                                P.op("pool", lambda e, ysl=ysl: e.tensor_tensor(out=ysl, in0=ysl, in1=tn[:], op=ALU.add),
                                     reads=["tn", ("ynsa", tl)], writes=[("ynsa", tl)])
            if dbg:
                dY = nc.dram_tensor("dbg_ydiff", [128, NOWN * 1024], BF16, kind="ExternalOutput").ap()
                dN = nc.dram_tensor("dbg_ynsa", [128, NOWN * 1024], BF16, kind="ExternalOutput").ap()
                P.dma(dY, ydiff[:], reads=[("ydiff", t) for t in range(NOWN)])
                P.dma(dN, ynsa[:], reads=[("ynsa", t) for t in range(NOWN)])
            P.emit()
        if stage <= 5:
            return nc

        with contextlib.ExitStack() as sg:
            Wpd = sb("g_Wpd", [128, 8 * 1024], BF16, sg)
            Wpn = sb("g_Wpn", [128, 8 * 1024], BF16, sg)
            Wo = sb("g_Wo", [128, 8 * 1024], BF16, sg)
            wst = [sb(f"g_wst{i}", [128, 4 * 1024], F32, sg) for i in range(2)]
            ydT = sb("g_ydT", [128, 1024], BF16, sg)
            ynT = sb("g_ynT", [128, 1024], BF16, sg)
            mgl = [sb(f"g_mgl{i}", [128, 2048], F32, sg) for i in range(2)]
            xql = [sb(f"g_xql{i}", [128, 1024], F32, sg) for i in range(2)]
            m1 = sb("g_m1", [128, 1024], F32, sg)
            m2 = sb("g_m2", [128, 1024], F32, sg)
            mixb = sb("g_mixb", [128, 1024], BF16, sg)
            mixT = sb("g_mixT", [128, 1024], BF16, sg)
            x1t = [sb(f"g_x1t{i}", [128, 1024], F32, sg) for i in range(2)]
            pTa = ps("g_pTa", [128, 1024], BF16, sg)
            pTb = ps("g_pTb", [128, 1024], BF16, sg)
            pD = [ps(f"g_pD{i}", [128, 512], F32, sg) for i in range(2)]
            pN = [ps(f"g_pN{i}", [128, 512], F32, sg) for i in range(2)]
            pW = [ps(f"g_pW{i}", [128, 512], F32, sg) for i in range(2)]
            P = Phase(nc, "pg")
            k = 0
            for (wd, wsrc, wk) in ((Wpd, w_pd, "Wpd"), (Wpn, w_pn, "Wpn"), (Wo, w_o, "Wo")):
                for hc in range(2):
                    st_ = wst[k % 2]
                    P.dma(st_[:].rearrange("p (c n) -> p c n", n=1024),
                          wsrc[hc * 512:(hc + 1) * 512, :].rearrange("(c p) n -> p c n", p=128), writes=[("wst", k % 2)])
                    P.op("pool", lambda e, wd=wd, hc=hc, st_=st_: e.tensor_copy(out=wd[:, hc * 4096:(hc + 1) * 4096], in_=st_[:]),
                         reads=[("wst", k % 2)], writes=[wk])
                    k += 1
            for t in range(NOWN):
                tok = slice(t * 128, (t + 1) * 128)
                mg_ = mgl[t % 2]
                xq_ = xql[t % 2]
                P.dma(mg_[:], mgS[tok, :], writes=[("mgl", t % 2)])
                P.dma(xq_[:], xq[tok, :], writes=[("xql", t % 2)])
                for c in range(8):
                    P.op("pe", lambda e, c=c, t=t: e.transpose(out=pTa[:, c * 128:(c + 1) * 128],
                                                              in_=ydiff[:, t * 1024 + c * 128:t * 1024 + (c + 1) * 128], identity=ident[:]),
                         reads=["ident"], writes=["pTa"])
                P.op("dve", lambda e: e.tensor_copy(out=ydT[:], in_=pTa[:]), reads=["pTa"], writes=["ydT"])
                for c in range(8):
                    P.op("pe", lambda e, c=c, t=t: e.transpose(out=pTb[:, c * 128:(c + 1) * 128],
                                                              in_=ynsa[:, t * 1024 + c * 128:t * 1024 + (c + 1) * 128], identity=ident[:]),
                         reads=["ident"], writes=["pTb"])
                P.op("dve", lambda e: e.tensor_copy(out=ynT[:], in_=pTb[:]), reads=["pTb"], writes=["ynT"])
                for hf in range(2):
                    for c in range(8):
                        P.op("pe", lambda e, c=c, hf=hf: e.matmul(pD[hf][:], lhsT=ydT[:, c * 128:(c + 1) * 128],
                                                                  rhs=Wpd[:, c * 1024 + hf * 512:c * 1024 + (hf + 1) * 512],
                                                                  start=(c == 0), stop=(c == 7)),
                             reads=["ydT", "Wpd"], writes=[("pD", hf)])
                    for c in range(8):
                        P.op("pe", lambda e, c=c, hf=hf: e.matmul(pN[hf][:], lhsT=ynT[:, c * 128:(c + 1) * 128],
                                                                  rhs=Wpn[:, c * 1024 + hf * 512:c * 1024 + (hf + 1) * 512],
                                                                  start=(c == 0), stop=(c == 7)),
                             reads=["ynT", "Wpn"], writes=[("pN", hf)])
                    P.op("dve", lambda e, hf=hf, mg_=mg_: e.tensor_tensor(out=m1[:, hf * 512:(hf + 1) * 512], in0=pD[hf][:],
                                                                          in1=mg_[:, hf * 512:(hf + 1) * 512], op=ALU.mult),
                         reads=[("pD", hf), ("mgl", t % 2)], writes=[("m1", hf)])
                    P.op("dve", lambda e, hf=hf, mg_=mg_: e.tensor_tensor(out=m2[:, hf * 512:(hf + 1) * 512], in0=pN[hf][:],
                                                                          in1=mg_[:, 1024 + hf * 512:1024 + (hf + 1) * 512], op=ALU.mult),
                         reads=[("pN", hf), ("mgl", t % 2)], writes=[("m2", hf)])
                    P.op("pool", lambda e, hf=hf: e.tensor_tensor(out=mixb[:, hf * 512:(hf + 1) * 512], in0=m1[:, hf * 512:(hf + 1) * 512],
                                                                  in1=m2[:, hf * 512:(hf + 1) * 512], op=ALU.add),
                         reads=[("m1", hf), ("m2", hf)], writes=[("mixb", hf)])
                for c in range(8):
                    P.op("pe", lambda e, c=c: e.transpose(out=pTa[:, c * 128:(c + 1) * 128], in_=mixb[:, c * 128:(c + 1) * 128],
                                                          identity=ident[:]),
                         reads=[("mixb", c // 4), "ident"], writes=["pTa"])
                P.op("dve", lambda e: e.tensor_copy(out=mixT[:], in_=pTa[:]), reads=["pTa"], writes=["mixT"])
                x1_ = x1t[t % 2]
                for hf in range(2):
                    for c in range(8):
                        P.op("pe", lambda e, c=c, hf=hf: e.matmul(pW[hf][:], lhsT=mixT[:, c * 128:(c + 1) * 128],
                                                                  rhs=Wo[:, c * 1024 + hf * 512:c * 1024 + (hf + 1) * 512],
                                                                  start=(c == 0), stop=(c == 7)),
                             reads=["mixT", "Wo"], writes=[("pW", hf)])
                    P.op("dve", lambda e, hf=hf, x1_=x1_, xq_=xq_: e.tensor_tensor(out=x1_[:, hf * 512:(hf + 1) * 512], in0=pW[hf][:],
                                                                                  in1=xq_[:, hf * 512:(hf + 1) * 512], op=ALU.add),
                         reads=[("pW", hf), ("xql", t % 2)], writes=[("x1t", t % 2, hf)])
                P.dma(x1d[tok, :], x1_[:], reads=[("x1t", t % 2, 0), ("x1t", t % 2, 1)])
            P.emit()
        if stage <= 6:
            return nc
    with contextlib.ExitStack() as top2:
        def sb2(name, shape, dt):
            return top2.enter_context(nc.sbuf_tensor(name, list(shape), dt))

        ident2 = sb2("ident2", [128, 128], BF16)
        Wup = sb2("h_Wup", [128, 8 * 2048], BF16)
        Wdn = sb2("h_Wdn", [128, 16 * 1024], BF16)
        hst = [sb2(f"h_st{i}", [128, 2048], F32) for i in range(2)]
        gml = sb2("h_gml", [128, 1024], F32)
        x1g = sb2("h_x1g", [128, 4 * 1024], F32)
        og = sb2("h_og", [128, 4 * 1024], F32)
        junk2 = sb2("h_junk", [128, 1024], F32)
        h2 = sb2("h_h2", [128, 1024], BF16)
        h2T = sb2("h_h2T", [128, 8 * 512], BF16)
        uT = sb2("h_uT", [128, 16 * 512], BF16)
        rr = [sb2(f"h_rr{i}", [128, 512], F32) for i in range(2)]
        ot = [sb2(f"h_ot{i}", [128, 512], F32) for i in range(2)]
        ss2 = sb2("h_ss", [128, 1], F32)
        rs2 = sb2("h_rs", [128, 1], F32)
        pT2 = top2.enter_context(nc.psum_tensor("h_pT", [128, 1024], BF16))
        pU2 = [top2.enter_context(nc.psum_tensor(f"h_pU{i}", [128, 512], F32)) for i in range(2)]
        pDn = [top2.enter_context(nc.psum_tensor(f"h_pDn{i}", [128, 512], F32)) for i in range(2)]
        for pss in range(2):
            P = Phase(nc, f"ph{pss}")
            P.dma(ident2[:], identd, writes=["ident"])
            P.dma(gml[:], g_mlp[0:1, :].broadcast_to([128, 1024]), writes=["gt"])
            k = 0
            for c in range(8):
                st_ = hst[k % 2]
                P.dma(st_[:], w_up[c * 128:(c + 1) * 128, pss * 2048:(pss + 1) * 2048], writes=[("hst", k % 2)])
                P.op("pool", lambda e, c=c, st_=st_: e.tensor_copy(out=Wup[:, c * 2048:(c + 1) * 2048], in_=st_[:]),
                     reads=[("hst", k % 2)], writes=["Wup"])
                k += 1
            for f2 in range(8):
                st_ = hst[k % 2]
                r0 = pss * 2048 + f2 * 256
                P.dma(st_[:].rearrange("p (a n) -> p a n", n=1024), w_dn[r0:r0 + 256, :].rearrange("(a p) n -> p a n", p=128),
                      writes=[("hst", k % 2)])
                P.op("pool", lambda e, f2=f2, st_=st_: e.tensor_copy(out=Wdn[:, f2 * 2048:(f2 + 1) * 2048], in_=st_[:]),
                     reads=[("hst", k % 2)], writes=["Wdn"])
                k += 1
            for gq in range(NSL):
                rows = slice(gq * 512, (gq + 1) * 512)
                P.dma(x1g[:].rearrange("p (t n) -> p t n", n=1024), x1d[rows, :].rearrange("(t p) n -> p t n", p=128), writes=["x1g"])
                if pss == 1:
                    P.dma(og[:].rearrange("p (t n) -> p t n", n=1024), out[rows, :].rearrange("(t p) n -> p t n", p=128), writes=["og"])
                base = x1g if pss == 0 else og
                bkey = "x1g" if pss == 0 else "og"
                for j in range(4):
                    xt_ = x1g[:, j * 1024:(j + 1) * 1024]
                    P.op("pool", lambda e: e.memset(ss2[:], 0.0), writes=["ss2"])
                    P.op("act", lambda e, xt_=xt_: e.activation(out=junk2[:], in_=xt_, func=AF.Square, accum_out=ss2[:]),
                         reads=["x1g", "ss2"], writes=["junk2", "ss2"])
                    P.op("dve", lambda e: e.tensor_scalar(out=ss2[:], in0=ss2[:], scalar1=1.0 / 1024, scalar2=EPS,
                                                          op0=ALU.mult, op1=ALU.add), reads=["ss2"], writes=["ss2"])
                    P.op("act", lambda e: e.activation(out=ss2[:], in_=ss2[:], func=AF.Ln), reads=["ss2"], writes=["ss2"])
                    P.op("act", lambda e: e.activation(out=rs2[:], in_=ss2[:], func=AF.Exp, scale=-0.5), reads=["ss2"], writes=["rs2"])
                    P.op("dve", lambda e, xt_=xt_: e.scalar_tensor_tensor(out=h2[:], in0=xt_, scalar=rs2[:], in1=gml[:],
                                                                         op0=ALU.mult, op1=ALU.mult),
                         reads=["x1g", "rs2", "gt"], writes=["h2"])
                    for c in range(8):
                        P.op("pe", lambda e, c=c: e.transpose(out=pT2[:, c * 128:(c + 1) * 128], in_=h2[:, c * 128:(c + 1) * 128],
                                                              identity=ident2[:]), reads=["h2", "ident"], writes=["pT2"])
                    P.op("dve", lambda e, j=j: e.tensor_copy(
                        out=h2T[:].rearrange("p (c q) -> p c q", q=512)[:, :, j * 128:(j + 1) * 128],
                        in_=pT2[:].rearrange("p (c q) -> p c q", q=128)), reads=["pT2"], writes=["h2T"])
                for f in range(16):
                    pu = pU2[f % 2]
                    for c in range(8):
                        P.op("pe", lambda e, c=c, f=f, pu=pu: e.matmul(pu[:], lhsT=Wup[:, c * 2048 + f * 128:c * 2048 + (f + 1) * 128],
                                                                      rhs=h2T[:, c * 512:(c + 1) * 512], start=(c == 0), stop=(c == 7)),
                             reads=["Wup", "h2T"], writes=[("pU2", f % 2)])
                    r_ = rr[f % 2]
                    P.op("act", lambda e, pu=pu, r_=r_: e.activation(out=r_[:], in_=pu[:], func=AF.Relu),
                         reads=[("pU2", f % 2)], writes=[("rr", f % 2)])
                    P.op("pool", lambda e, f=f, r_=r_: e.tensor_tensor(out=uT[:, f * 512:(f + 1) * 512], in0=r_[:], in1=r_[:], op=ALU.mult),
                         reads=[("rr", f % 2)], writes=["uT"])
                for j in range(4):
                    for hf in range(2):
                        i = (j * 2 + hf) % 2
                        for f in range(16):
                            P.op("pe", lambda e, f=f, j=j, hf=hf, i=i: e.matmul(
                                pDn[i][:], lhsT=uT[:, f * 512 + j * 128:f * 512 + (j + 1) * 128],
                                rhs=Wdn[:, f * 1024 + hf * 512:f * 1024 + (hf + 1) * 512], start=(f == 0), stop=(f == 15)),
                                reads=["uT", "Wdn"], writes=[("pDn", i)])
                        o_ = ot[i]
                        P.op("dve", lambda e, i=i, o_=o_, j=j, hf=hf, base=base: e.tensor_tensor(
                            out=o_[:], in0=pDn[i][:], in1=base[:, j * 1024 + hf * 512:j * 1024 + (hf + 1) * 512], op=ALU.add),
                            reads=[("pDn", i), bkey], writes=[("ot", i)])
                        P.dma(out[gq * 512 + j * 128:gq * 512 + (j + 1) * 128, hf * 512:(hf + 1) * 512], o_[:], reads=[("ot", i)],
                              writes=[("outrows", gq)])
            P.emit()
    return nc


def _rope_tab(pos):
    inv = (10000.0 ** (-(np.arange(32, dtype=np.float32)) / np.float32(32))).astype(np.float32)
    ang = pos.astype(np.float32)[:, None] * inv[None, :]
    return np.cos(ang).astype(np.float32), np.sin(ang).astype(np.float32)


def make_core_inputs(inp, core, S):
    b, par = core // 2, core % 2
    NSL = S // 1024
    SO = NSL * 512
    NCMP = (S - 32) // 16 + 1
    NCT = (NCMP + 127) // 128
    NCP = NCT * 128
    f = lambda a: np.ascontiguousarray(np.asarray(a, dtype=np.float32))
    x = np.asarray(inp["x"], dtype=np.float32)
    own_pos = np.concatenate([np.arange((2 * s + par) * 512, (2 * s + par) * 512 + 512) for s in range(NSL)])
    d = {}
    d["xkv"] = f(x[b])
    d["xq"] = f(x[b][own_pos])
    d["w_in"] = f(inp["w_in"][0])
    d["w_pd"] = f(inp["w_proj_diff"][0])
    d["w_pn"] = f(inp["w_proj_nsa"][0])
    d["w_o"] = f(inp["w_out"][0])
    d["w_up"] = f(inp["w_mlp_up"][0])
    d["w_dn"] = f(inp["w_mlp_down"][0])
    d["c_w1k"] = f(inp["cmp_k_w1"][0])
    d["c_w1v"] = f(inp["cmp_v_w1"][0])
    d["c_w2k"] = f(inp["cmp_k_w2"][0])
    d["c_w2v"] = f(inp["cmp_v_w2"][0])
    p2 = lambda p: f(np.asarray(p, np.float32).reshape(16, 2, 64).transpose(1, 2, 0).reshape(128, 16))
    d["pos2k"] = p2(inp["cmp_pos_k"][0])
    d["pos2v"] = p2(inp["cmp_pos_v"][0])
    d["g_mix"] = f(inp["ln_mix_g"][0][None, :])
    d["g_mlp"] = f(inp["ln_mlp_g"][0][None, :])
    gk = np.asarray(inp["nsa_k_norm_g"][0], np.float32)
    d["g_k24"] = f(np.concatenate([np.tile(np.asarray(inp["diff_k_norm_g"][0], np.float32), 16),
                                   np.tile(gk[1], 4), np.tile(gk[2], 4)])[None, :])
    d["g_q32"] = f(np.concatenate([np.tile(np.asarray(inp["diff_q_norm_g"][0], np.float32), 16),
                                   np.tile(np.asarray(inp["nsa_q_norm_g"][0], np.float32), 16)])[None, :])
    d["g_kc"] = f(gk[0][None, :])
    d["g_sub"] = f(inp["diff_subln_g"][0][None, :])
    d["lam4"] = f(np.concatenate([np.asarray(inp[k][0], np.float32) for k in
                                  ("diff_lambda_q1", "diff_lambda_k1", "diff_lambda_q2", "diff_lambda_k2")])[None, :])
    d["rkv_c"], d["rkv_s"] = _rope_tab(np.arange(S))
    d["rq_c"], d["rq_s"] = _rope_tab(own_pos)
    cc = np.zeros(NCP, np.float32)
    cc[:NCMP] = np.arange(NCMP) * 16 + 15.5
    d["rc_c"], d["rc_s"] = _rope_tab(cc)
    d["ident"] = np.eye(128, dtype=np.float32).astype(NPBF)
    kk = np.arange(S)
    d["eind"] = (kk[None, :] // 64 == np.arange(64)[:, None]).astype(np.float32).astype(NPBF)
    n = np.arange(NCP)
    cs, ce = n * 16, n * 16 + 31
    ss_ = np.arange(64) * 64
    ov = ((cs[:, None] < ss_[None, :] + 64) & (ce[:, None] >= ss_[None, :]) & (n[:, None] < NCMP)).astype(np.float32)
    d["ovl"] = np.concatenate([ov, np.ones((NCP, 1), np.float32)], axis=1).astype(NPBF)
    k128 = np.arange(128)[:, None, None]
    q512 = np.arange(512)[None, None, :]
    kw = np.arange(8)[None, :, None] * 128 + k128
    d["m_dense"] = np.where((kw - par * 512) <= q512, 0.0, NEG).astype(np.float32).astype(NPBF)
    kw = np.arange(12)[None, :, None] * 128 + k128 - 512
    dd = par * 512 + q512 - kw
    d["m_win"] = np.where((dd >= 0) & (dd < 512), 0.0, NEG).astype(np.float32).astype(NPBF)
    mc = np.zeros((128, NSL * NCT, 512), np.float32)
    for s in range(NSL):
        for nt in range(NCT):
            nn = nt * 128 + np.arange(128)[:, None]
            qpos = (2 * s + par) * 512 + np.arange(512)[None, :]
            mc[:, s * NCT + nt, :] = np.where((nn < NCMP) & (nn * 16 + 31 <= qpos), 0.0, NEG)
    d["m_cmp"] = mc.astype(NPBF)
    fo = np.zeros((SO, 64), np.float32)
    cur = own_pos // 64
    r = np.arange(SO)
    fo[r[cur >= 1], (cur - 1)[cur >= 1]] = 1e4
    fo[r, cur] = 2e4
    fo[:, 0] = 3e4
    d["forced"] = fo
    return d


_NC_CACHE = {}


def kernel(**inputs):
    S = int(np.asarray(inputs["x"]).shape[1])
    B = int(np.asarray(inputs["x"]).shape[0])
    ncores = 2 * B
    if S not in _NC_CACHE:
        _NC_CACHE[S] = build(S)
    nc = _NC_CACHE[S]
    in_maps = [make_core_inputs(inputs, c, S) for c in range(ncores)]
    res = run_bass_kernel_spmd(nc, in_maps, core_ids=list(range(ncores)))
    out = np.zeros((B, S, 1024), np.float32)
    NSL = S // 1024
    for c in range(ncores):
        b, par = c // 2, c % 2
        o = np.asarray(res.results[c]["out"], dtype=np.float32)
        for s in range(NSL):
            ch = 2 * s + par
            out[b, ch * 512:(ch + 1) * 512] = o[s * 512:(s + 1) * 512]
    return out
```

```python
import contextlib
import numpy as np
import ml_dtypes
import concourse.bass as bass
import concourse.mybir as mybir
from concourse.bass_utils import run_bass_kernel_spmd

F32 = mybir.dt.float32
BF16 = mybir.dt.bfloat16
ALU = mybir.AluOpType
AF = mybir.ActivationFunctionType
AX = mybir.AxisListType
NPBF = ml_dtypes.bfloat16

NDMASEM = 4
EPS = 1e-6
NEG = -30000.0


class Phase:
    ENGS = ("pe", "act", "dve", "pool", "sp")

    def __init__(self, nc, name):
        self.nc = nc
        self.name = name
        self.ops = []

    def op(self, eng, fn, reads=(), writes=(), dma=False):
        self.ops.append((eng, fn, tuple(reads), tuple(writes), dma))

    def dma(self, out, in_, reads=(), writes=(), eng="sp"):
        self.op(eng, lambda e: e.dma_start(out=out, in_=in_), reads, writes, dma=True)

    def emit(self):
        nc = self.nc
        ops = self.ops
        cnt = {e: 0 for e in self.ENGS}
        dcnt = {e: 0 for e in self.ENGS}
        info = []
        last_w = {}
        readers = {}
        deps = []
        for i, (eng, fn, rd, wr, dma) in enumerate(ops):
            d = set()
            for k in rd:
                if k in last_w:
                    d.add(last_w[k])
            for k in wr:
                if k in last_w:
                    d.add(last_w[k])
                for r in readers.get(k, ()):
                    d.add(r)
            d.discard(i)
            deps.append(d)
            for k in rd:
                readers.setdefault(k, []).append(i)
            for k in wr:
                last_w[k] = i
                readers[k] = []
            if dma:
                info.append((eng, "d", dcnt[eng]))
                dcnt[eng] += 1
            else:
                info.append((eng, "c", cnt[eng]))
                cnt[eng] += 1
        with contextlib.ExitStack() as st:
            csem = {e: st.enter_context(nc.semaphore(f"{self.name}_c_{e}")) for e in self.ENGS}
            dsem = {e: [st.enter_context(nc.semaphore(f"{self.name}_d_{e}{j}")) for j in range(NDMASEM)]
                    for e in self.ENGS if dcnt[e] > 0}
            block = st.enter_context(nc.Block())
            per_eng = {e: [i for i, o in enumerate(ops) if o[0] == e] for e in self.ENGS}

            def make(eng_name):
                def body(eng):
                    waited = {}

                    def wait(key, sem, val):
                        if waited.get(key, 0) >= val:
                            return
                        waited[key] = val
                        eng.wait_ge(sem, val)

                    for i in per_eng[eng_name]:
                        _, fn, rd, wr, dma = ops[i]
                        for p in sorted(deps[i]):
                            pe_, pk, pidx = info[p]
                            if pk == "c":
                                if pe_ == "pe" and eng_name == "pe":
                                    continue
                                wait(("c", pe_), csem[pe_], pidx + 1)
                            else:
                                slot = pidx % NDMASEM
                                wait(("d", pe_, slot), dsem[pe_][slot], 16 * (pidx // NDMASEM + 1))
                        if dma:
                            didx = info[i][2]
                            slot = didx % NDMASEM
                            if didx >= NDMASEM:
                                wait(("d", eng_name, slot), dsem[eng_name][slot], 16 * (didx // NDMASEM))
                            fn(eng).then_inc(dsem[eng_name][slot], 16)
                        else:
                            fn(eng).then_inc(csem[eng_name], 1)
                    if eng_name in dsem:
                        n = dcnt[eng_name]
                        for slot in range(min(NDMASEM, n)):
                            total = (n - slot + NDMASEM - 1) // NDMASEM
                            wait(("d", eng_name, slot), dsem[eng_name][slot], 16 * total)
                return body

            if per_eng["pe"]:
                block.tensor(make("pe"))
            if per_eng["act"]:
                block.scalar(make("act"))
            if per_eng["dve"]:
                block.vector(make("dve"))
            if per_eng["pool"]:
                block.gpsimd(make("pool"))
            if per_eng["sp"]:
                block.sync(make("sp"))
        self.ops = []


KV_RANGES = [(1024, 2048), (4608, 4864), (5120, 5376), (2048, 3072), (4864, 5120), (5376, 5632),
             (4096, 4352), (4352, 4608)]
Q_RANGES = [(0, 1024), (3072, 4096), (5632, 5680), (5680, 7728)]
NKV = 3584
NQC = 4144


def build(S=4096, stage=99, dbg=False):
    NT = S // 128
    NCH = S // 512
    NSL = NCH // 2
    NOWN = NSL * 4
    SO = NSL * 512
    NCMP = (S - 32) // 16 + 1
    NCT = (NCMP + 127) // 128
    NCP = NCT * 128
    okind = "ExternalOutput" if dbg else "Internal"

    nc = bass.Bass("TRN2", target_bir_lowering=False)

    def din(name, shape, dt=F32):
        return nc.dram_tensor(name, list(shape), dt, kind="ExternalInput").ap()

    def dscr(name, shape, dt=BF16):
        return nc.dram_tensor(name, list(shape), dt, kind=okind).ap()

    xkv = din("xkv", [S, 1024])
    xq = din("xq", [SO, 1024])
    w_in = din("w_in", [1024, 7728])
    w_pd = din("w_pd", [1024, 1024])
    w_pn = din("w_pn", [1024, 1024])
    w_o = din("w_o", [1024, 1024])
    w_up = din("w_up", [1024, 4096])
    w_dn = din("w_dn", [4096, 1024])
    c_w1 = [din("c_w1k", [2048, 256]), din("c_w1v", [2048, 256])]
    c_w2 = [din("c_w2k", [256, 64]), din("c_w2v", [256, 64])]
    pos2 = [din("pos2k", [128, 16]), din("pos2v", [128, 16])]
    g_mix = din("g_mix", [1, 1024])
    g_mlp = din("g_mlp", [1, 1024])
    g_k24 = din("g_k24", [1, 1536])
    g_q32 = din("g_q32", [1, 2048])
    g_kc = din("g_kc", [1, 64])
    g_sub = din("g_sub", [1, 128])
    lam4 = din("lam4", [1, 256])
    rkv_c = din("rkv_c", [S, 32])
    rkv_s = din("rkv_s", [S, 32])
    rq_c = din("rq_c", [SO, 32])
    rq_s = din("rq_s", [SO, 32])
    rc_c = din("rc_c", [NCP, 32])
    rc_s = din("rc_s", [NCP, 32])
    identd = din("ident", [128, 128], BF16)
    eind = din("eind", [64, S], BF16)
    ovl = din("ovl", [NCP, 65], BF16)
    m_dense = din("m_dense", [128, 8, 512], BF16)
    m_win = din("m_win", [128, 12, 512], BF16)
    m_cmp = din("m_cmp", [128, NSL * NCT, 512], BF16)
    forced = din("forced", [SO, 64])
    out = nc.dram_tensor("out", [SO, 1024], F32, kind="ExternalOutput").ap()

    KdT = dscr("KdT", [8, 128, S])
    Vd = dscr("Vd", [S, 1024])
    KsT = dscr("KsT", [256, S])
    KwT = dscr("KwT", [256, S])
    kcT = dscr("kcT", [256, S])
    vcT = dscr("vcT", [256, S])
    Vs = dscr("Vs", [S, 256])
    Vw = dscr("Vw", [S, 256])
    QdT = dscr("QdT", [8, 128, SO])
    QnT = dscr("QnT", [1024, SO])
    BsT = dscr("BsT", [256, SO])
    mgS = dscr("mgS", [SO, 2048], F32)
    x1d = dscr("x1d", [SO, 1024], F32)

    with contextlib.ExitStack() as top:
        def sb(name, shape, dt, stack=top):
            return stack.enter_context(nc.sbuf_tensor(name, list(shape), dt))

        def ps(name, shape, dt, stack):
            return stack.enter_context(nc.psum_tensor(name, list(shape), dt))

        ident = sb("ident_sb", [128, 128], BF16)
        gates = sb("gates", [128, NOWN * 48], F32)
        KcT = sb("KcT", [128, 4 * NCP], BF16)
        Vca = sb("Vca", [128, NCT * 4 * 129], BF16)
        nlam = sb("nlam", [128, 1], F32)
        sgain = sb("sgain", [128, 128], F32)
        ss1 = sb("ss1", [128, 1], F32)
        rs1 = sb("rs1", [128, 1], F32)

        def bc(ap1, n):
            return ap1[0:1, :].broadcast_to([128, n])

        def rmsnorm_rows(P, xt, xkey, gt, hout, hkey, junk, sfx=""):
            P.op("pool", lambda e: e.memset(ss1[:], 0.0), writes=["ss1"])
            P.op("act", lambda e: e.activation(out=junk, in_=xt, func=AF.Square, accum_out=ss1[:]),
                 reads=[xkey, "ss1"], writes=["junk" + sfx, "ss1"])
            P.op("dve", lambda e: e.tensor_scalar(out=ss1[:], in0=ss1[:], scalar1=1.0 / 1024, scalar2=EPS,
                                                  op0=ALU.mult, op1=ALU.add), reads=["ss1"], writes=["ss1"])
            P.op("act", lambda e: e.activation(out=ss1[:], in_=ss1[:], func=AF.Ln), reads=["ss1"], writes=["ss1"])
            P.op("act", lambda e: e.activation(out=rs1[:], in_=ss1[:], func=AF.Exp, scale=-0.5),
                 reads=["ss1"], writes=["rs1"])
            P.op("dve", lambda e: e.scalar_tensor_tensor(out=hout, in0=xt, scalar=rs1[:], in1=gt,
                                                         op0=ALU.mult, op1=ALU.mult),
                 reads=[xkey, "rs1", "gt"], writes=[hkey])

        def normrope(P, src, srckey, U, gain, cos, sin, cskeys, outap, outkey, T, sfx):
            sq, ssu, rsu, xn, t1, t2, t3, t4 = T
            n = U * 64
            s3 = lambda ap: ap.rearrange("p (u d) -> p u d", d=64)
            P.op("act", lambda e: e.activation(out=sq[:, 0:n], in_=src, func=AF.Square),
                 reads=[srckey], writes=["sq" + sfx])
            P.op("dve", lambda e: e.tensor_reduce(out=ssu[:, 0:U], in_=s3(sq[:, 0:n]), axis=AX.X, op=ALU.add),
                 reads=["sq" + sfx], writes=["ssu" + sfx])
            P.op("dve", lambda e: e.tensor_scalar(out=ssu[:, 0:U], in0=ssu[:, 0:U], scalar1=1.0 / 64, scalar2=EPS,
                                                  op0=ALU.mult, op1=ALU.add), reads=["ssu" + sfx], writes=["ssu" + sfx])
            P.op("act", lambda e: e.activation(out=ssu[:, 0:U], in_=ssu[:, 0:U], func=AF.Ln),
                 reads=["ssu" + sfx], writes=["ssu" + sfx])
            P.op("act", lambda e: e.activation(out=rsu[:, 0:U], in_=ssu[:, 0:U], func=AF.Exp, scale=-0.5),
                 reads=["ssu" + sfx], writes=["rsu" + sfx])
            P.op("dve", lambda e: e.tensor_tensor(out=s3(xn[:, 0:n]), in0=s3(src),
                                                  in1=rsu[:, 0:U].unsqueeze(2).to_broadcast([128, U, 64]), op=ALU.mult),
                 reads=[srckey, "rsu" + sfx], writes=["xn" + sfx])
            P.op("pool", lambda e: e.tensor_tensor(out=xn[:, 0:n], in0=xn[:, 0:n], in1=gain, op=ALU.mult),
                 reads=["xn" + sfx, "gains"], writes=["xn" + sfx])
            x3 = s3(xn[:, 0:n])
            o3 = s3(outap)
            cb = cos.unsqueeze(1).to_broadcast([128, U, 32])
            sbb = sin.unsqueeze(1).to_broadcast([128, U, 32])
            h3 = lambda t: t[:, 0:U * 32].rearrange("p (u d) -> p u d", d=32)
            P.op("pool", lambda e: e.tensor_tensor(out=h3(t1), in0=x3[:, :, 0:32], in1=cb, op=ALU.mult),
                 reads=["xn" + sfx] + list(cskeys), writes=["t1" + sfx])
            P.op("pool", lambda e: e.tensor_tensor(out=h3(t2), in0=x3[:, :, 32:64], in1=sbb, op=ALU.mult),
                 reads=["xn" + sfx] + list(cskeys), writes=["t2" + sfx])
            P.op("pool", lambda e: e.tensor_tensor(out=o3[:, :, 0:32], in0=h3(t1), in1=h3(t2), op=ALU.subtract),
                 reads=["t1" + sfx, "t2" + sfx], writes=[outkey])
            P.op("dve", lambda e: e.tensor_tensor(out=h3(t3), in0=x3[:, :, 32:64], in1=cb, op=ALU.mult),
                 reads=["xn" + sfx] + list(cskeys), writes=["t3" + sfx])
            P.op("dve", lambda e: e.tensor_tensor(out=h3(t4), in0=x3[:, :, 0:32], in1=sbb, op=ALU.mult),
                 reads=["xn" + sfx] + list(cskeys), writes=["t4" + sfx])
            P.op("dve", lambda e: e.tensor_tensor(out=o3[:, :, 32:64], in0=h3(t3), in1=h3(t4), op=ALU.add),
                 reads=["t3" + sfx, "t4" + sfx], writes=[outkey])

        def load_w_cols(P, wdst, ranges, ncols):
            keys = []
            for c in range(8):
                off = 0
                kc_ = []
                for ri, (a, b) in enumerate(ranges):
                    k = ("wbuf", c, ri)
                    P.dma(wdst[:, c * ncols + off:c * ncols + off + (b - a)], w_in[c * 128:(c + 1) * 128, a:b],
                          writes=[k], eng="pool")
                    kc_.append(k)
                    off += b - a
                keys.append(kc_)
            return keys

        with contextlib.ExitStack() as sa:
            lt = sb("lt", [128, 256], F32, sa)
            pr = sb("pr", [128, 128], F32, sa)
            s2 = sb("s2", [128, 2], F32, sa)
            sgr = sb("sgr", [128, 128], F32, sa)
            P = Phase(nc, "pa")
            P.dma(ident[:], identd, writes=["ident"])
            P.dma(lt[:], bc(lam4, 256), writes=["lt"])
            P.dma(sgr[:], bc(g_sub, 128), writes=["sgr"])
            P.op("dve", lambda e: e.tensor_tensor(out=pr[:, 0:64], in0=lt[:, 0:64], in1=lt[:, 64:128], op=ALU.mult),
                 reads=["lt"], writes=["pr"])
            P.op("dve", lambda e: e.tensor_tensor(out=pr[:, 64:128], in0=lt[:, 128:192], in1=lt[:, 192:256], op=ALU.mult),
                 reads=["lt", "pr"], writes=["pr"])
            P.op("dve", lambda e: e.tensor_reduce(out=s2[:], in_=pr[:].rearrange("p (a d) -> p a d", d=64),
                                                  axis=AX.X, op=ALU.add), reads=["pr"], writes=["s2"])
            P.op("act", lambda e: e.activation(out=s2[:], in_=s2[:], func=AF.Exp), reads=["s2"], writes=["s2"])
            P.op("dve", lambda e: e.tensor_tensor(out=nlam[:], in0=s2[:, 1:2], in1=s2[:, 0:1], op=ALU.subtract),
                 reads=["s2"], writes=["nlam"])
            P.op("dve", lambda e: e.tensor_scalar(out=nlam[:], in0=nlam[:], scalar1=-0.2, scalar2=None, op0=ALU.add),
                 reads=["nlam"], writes=["nlam"])
            P.op("dve", lambda e: e.tensor_scalar(out=sgain[:], in0=sgr[:], scalar1=0.8, scalar2=None, op0=ALU.mult),
                 reads=["sgr"], writes=["sgain"])
            P.emit()

        with contextlib.ExitStack() as sbd:
            wbuf = sb("wbuf", [128, 8 * NQC], BF16, sbd)
            gmix = sb("gmix", [128, 1024], F32, sbd)
            gains = sb("gains", [128, 2048], F32, sbd)
            xts = [sb(f"xt{i}", [128, 1024], F32, sbd) for i in range(2)]
            junk = sb("junk", [128, 1024], F32, sbd)
            hb = sb("hb", [128, 1024], BF16, sbd)
            hT = [sb(f"hT{i}", [128, 1024], BF16, sbd) for i in range(2)]
            cst = [sb(f"cs{i}", [128, 64], F32, sbd) for i in range(2)]
            TT = []
            for i in range(2):
                TT.append((sb(f"sq{i}", [128, 512], F32, sbd), sb(f"ssu{i}", [128, 8], F32, sbd),
                           sb(f"rsu{i}", [128, 8], F32, sbd), sb(f"xn{i}", [128, 512], F32, sbd),
                           sb(f"t1{i}", [128, 256], F32, sbd), sb(f"t2{i}", [128, 256], F32, sbd),
                           sb(f"t3{i}", [128, 256], F32, sbd), sb(f"t4{i}", [128, 256], F32, sbd)))
            kb = sb("kb", [128, 2048], BF16, sbd)
            vb = sb("vb", [128, 2048], BF16, sbd)
            ktb = [sb(f"ktb{i}", [128, 16 * 128], BF16, sbd) for i in range(2)]
            mgt = [sb(f"mgt{i}", [128, 512], F32, sbd) for i in range(2)]
            pT = ps("pT", [128, 1024], BF16, sbd)
            pO = [ps(f"pO{i}", [128, 512], F32, sbd) for i in range(3)]
            pK = [ps(f"pK{i}", [128, 1024], BF16, sbd) for i in range(2)]

            def front_a(P, t, gt):
                xt = xts[t % 2]
                rmsnorm_rows(P, xt[:], ("xt", t % 2), gt[:], hb[:], "hb", junk[:])

            def front_b(P, t):
                for c in range(8):
                    P.op("pe", lambda e, c=c: e.transpose(out=pT[:, c * 128:(c + 1) * 128],
                                                          in_=hb[:, c * 128:(c + 1) * 128], identity=ident[:]),
                         reads=["hb", "ident"], writes=["pT"])
                P.op("dve", lambda e: e.tensor_copy(out=hT[t % 2][:], in_=pT[:]), reads=["pT"], writes=[("hT", t % 2)])

            def proj_group(P, t, j, ncols, col0, width, wkeys):
                po = pO[j % 3]
                for c in range(8):
                    P.op("pe", lambda e, c=c: e.matmul(po[:, 0:width], lhsT=hT[t % 2][:, c * 128:(c + 1) * 128],
                                                       rhs=wbuf[:, c * ncols + col0:c * ncols + col0 + width],
                                                       start=(c == 0), stop=(c == 7)),
                         reads=[("hT", t % 2)] + wkeys[c], writes=[("pO", j % 3)])
                return po

            P = Phase(nc, "pb")
            wk = load_w_cols(P, wbuf, KV_RANGES, NKV)
            P.dma(gmix[:], bc(g_mix, 1024), writes=["gt"])
            P.dma(gains[:, 0:1536], bc(g_k24, 1536), writes=["gains"])
            P.dma(xts[0][:], xkv[0:128, :], writes=[("xt", 0)])
            P.dma(xts[1][:], xkv[128:256, :], writes=[("xt", 1)])
            front_a(P, 0, gmix)
            front_b(P, 0)
            for t in range(NT):
                cs = cst[t % 2]
                P.dma(cs[:, 0:32], rkv_c[t * 128:(t + 1) * 128, :], writes=[("cs", t % 2, 0)])
                P.dma(cs[:, 32:64], rkv_s[t * 128:(t + 1) * 128, :], writes=[("cs", t % 2, 1)])
                if t + 1 < NT:
                    front_a(P, t + 1, gmix)
                for j in range(7):
                    po = proj_group(P, t, j, NKV, j * 512, 512, wk)
                    if j < 3:
                        normrope(P, po[:], ("pO", j % 3), 8, gains[:, j * 512:(j + 1) * 512], cs[:, 0:32], cs[:, 32:64],
                                 [("cs", t % 2, 0), ("cs", t % 2, 1)], kb[:, j * 512:(j + 1) * 512], ("kb", j), TT[j % 2], str(j % 2))
                    else:
                        P.op("act", lambda e, po=po, j=j: e.activation(out=vb[:, (j - 3) * 512:(j - 2) * 512], in_=po[:],
                                                                       func=AF.Identity),
                             reads=[("pO", j % 3)], writes=[("vb", j)])
                    if j == 2 and t + 1 < NT:
                        front_b(P, t + 1)
                        if t + 2 < NT:
                            P.dma(xts[t % 2][:], xkv[(t + 2) * 128:(t + 3) * 128, :], writes=[("xt", t % 2)])
                kt = ktb[t % 2]
                for blk in range(16):
                    src = kb[:, blk * 128:(blk + 1) * 128] if blk < 12 else vb[:, 1536 + (blk - 12) * 128:1536 + (blk - 11) * 128]
                    rk = [("kb", blk // 4)] if blk < 12 else [("vb", 6)]
                    P.op("pe", lambda e, src=src, blk=blk: e.transpose(out=pK[blk // 8][:, (blk % 8) * 128:(blk % 8 + 1) * 128],
                                                                        in_=src, identity=ident[:]),
                         reads=rk + ["ident"], writes=[("pK", blk // 8)])
                for hf in range(2):
                    P.op("dve", lambda e, hf=hf, kt=kt: e.tensor_copy(out=kt[:, hf * 1024:(hf + 1) * 1024], in_=pK[hf][:]),
                         reads=[("pK", hf)], writes=[("ktb", t % 2, hf)])
                tok = slice(t * 128, (t + 1) * 128)
                P.dma(KdT[:, :, tok].rearrange("h p k -> p h k"), kt[:, 0:1024].rearrange("p (h k) -> p h k", k=128),
                      reads=[("ktb", t % 2, 0)])
                for i, dst in enumerate((KsT, KwT, kcT, vcT)):
                    P.dma(dst[:, tok].rearrange("(i p) k -> p i k", p=128),
                          kt[:, 1024 + i * 256:1024 + (i + 1) * 256].rearrange("p (i k) -> p i k", k=128),
                          reads=[("ktb", t % 2, 1)])
                P.dma(Vd[tok, :], vb[:, 0:1024], reads=[("vb", 3), ("vb", 4)])
                P.dma(Vs[tok, :], vb[:, 1024:1280], reads=[("vb", 5)])
                P.dma(Vw[tok, :], vb[:, 1280:1536], reads=[("vb", 5)])
            P.emit()
            if stage <= 1:
                return nc

            P = Phase(nc, "pd")
            wk = load_w_cols(P, wbuf, Q_RANGES, NQC)
            P.dma(gains[:, 0:2048], bc(g_q32, 2048), writes=["gains"])
            P.dma(xts[0][:], xq[0:128, :], writes=[("xt", 0)])
            P.dma(xts[1][:], xq[128:256, :], writes=[("xt", 1)])
            front_a(P, 0, gmix)
            front_b(P, 0)
            for t in range(NOWN):
                cs = cst[t % 2]
                P.dma(cs[:, 0:32], rq_c[t * 128:(t + 1) * 128, :], writes=[("cs", t % 2, 0)])
                P.dma(cs[:, 32:64], rq_s[t * 128:(t + 1) * 128, :], writes=[("cs", t % 2, 1)])
                if t + 1 < NOWN:
                    front_a(P, t + 1, gmix)
                tok = slice(t * 128, (t + 1) * 128)
                for j in range(4):
                    po = proj_group(P, t, j, NQC, j * 512, 512, wk)
                    normrope(P, po[:], ("pO", j % 3), 8, gains[:, j * 512:(j + 1) * 512], cs[:, 0:32], cs[:, 32:64],
                             [("cs", t % 2, 0), ("cs", t % 2, 1)], kb[:, j * 512:(j + 1) * 512], ("kb", j), TT[j % 2], str(j % 2))
                    if j == 2 and t + 1 < NOWN:
                        front_b(P, t + 1)
                        if t + 2 < NOWN:
                            P.dma(xts[t % 2][:], xq[(t + 2) * 128:(t + 3) * 128, :], writes=[("xt", t % 2)])
                po = proj_group(P, t, 4, NQC, 2048, 48, wk)
                gsl = gates[:, t * 48:(t + 1) * 48]
                P.op("act", lambda e, po=po, gsl=gsl: e.activation(out=gsl, in_=po[:, 0:48], func=AF.Tanh, scale=0.5),
                     reads=[("pO", 1)], writes=["gates"])
                P.op("dve", lambda e, gsl=gsl: e.tensor_scalar(out=gsl, in0=gsl, scalar1=0.5, scalar2=0.5, op0=ALU.mult, op1=ALU.add),
                     reads=["gates"], writes=["gates"])
                for j in range(5, 9):
                    po = proj_group(P, t, j, NQC, 2096 + (j - 5) * 512, 512, wk)
                    mg_ = mgt[j % 2]
                    P.op("act", lambda e, po=po, mg_=mg_: e.activation(out=mg_[:], in_=po[:], func=AF.Tanh, scale=0.5),
                         reads=[("pO", j % 3)], writes=[("mgt", j % 2)])
                    P.op("pool" if j % 2 else "dve", lambda e, mg_=mg_: e.tensor_scalar(out=mg_[:], in0=mg_[:], scalar1=0.5, scalar2=0.5,
                                                                                       op0=ALU.mult, op1=ALU.add),
                         reads=[("mgt", j % 2)], writes=[("mgt", j % 2)])
                    P.dma(mgS[tok, (j - 5) * 512:(j - 4) * 512], mg_[:], reads=[("mgt", j % 2)])
                kt = ktb[t % 2]
                for blk in range(16):
                    P.op("pe", lambda e, blk=blk: e.transpose(out=pK[blk // 8][:, (blk % 8) * 128:(blk % 8 + 1) * 128],
                                                              in_=kb[:, blk * 128:(blk + 1) * 128], identity=ident[:]),
                         reads=[("kb", blk // 4), "ident"], writes=[("pK", blk // 8)])
                for hf in range(2):
                    P.op("dve", lambda e, hf=hf, kt=kt: e.tensor_copy(out=kt[:, hf * 1024:(hf + 1) * 1024], in_=pK[hf][:]),
                         reads=[("pK", hf)], writes=[("ktb", t % 2, hf)])
                P.dma(QdT[:, :, tok].rearrange("h p k -> p h k"), kt[:, 0:1024].rearrange("p (h k) -> p h k", k=128),
                      reads=[("ktb", t % 2, 0)])
                P.dma(QnT[:, tok].rearrange("(i p) k -> p i k", p=128), kt[:, 1024:2048].rearrange("p (i k) -> p i k", k=128),
                      reads=[("ktb", t % 2, 1)])
            P.emit()
        if stage <= 2:
            return nc

        ydiff = sb("ydiff", [128, NOWN * 1024], BF16)
        ynsa = sb("ynsa", [128, NOWN * 1024], BF16)
        with contextlib.ExitStack() as sc:
            W1s = [sb(f"W1s{i}", [128, 16 * 256], BF16, sc) for i in range(2)]
            W2s = [sb(f"W2s{i}", [128, 128], BF16, sc) for i in range(2)]
            p2b = [sb(f"p2b{i}", [128, 16], BF16, sc) for i in range(2)]
            kc2 = [sb(f"kc2_{i}", [128, S], BF16, sc) for i in range(2)]
            biasv = [sb(f"biasv{i}", [128, 2], F32, sc) for i in range(2)]
            h1T = sb("h1T", [128, 2 * NCP], BF16, sc)
            xg = sb("xg", [128, NCP], F32, sc)
            x2 = sb("x2", [128, NCP], F32, sc)
            th = sb("th", [128, NCP], F32, sc)
            kcn = sb("kcn", [128, 128], BF16, sc)
            gkc = sb("gkc", [128, 64], F32, sc)
            csc = sb("csc", [128, NCT * 64], F32, sc)
            TC = (sb("c_sq", [128, 64], F32, sc), sb("c_ssu", [128, 8], F32, sc), sb("c_rsu", [128, 8], F32, sc),
                  sb("c_xn", [128, 64], F32, sc), sb("c_t1", [128, 32], F32, sc), sb("c_t2", [128, 32], F32, sc),
                  sb("c_t3", [128, 32], F32, sc), sb("c_t4", [128, 32], F32, sc))
            pH = [ps(f"pH{i}", [128, 512], F32, sc) for i in range(2)]
            pB = ps("pB", [128, 512], F32, sc)
            pC = ps("pC", [128, 512], F32, sc)
            pKc = ps("pKc", [128, 1024], BF16, sc)
            P = Phase(nc, "pc")
            P.op("pool", lambda e: e.memset(h1T[:], 0.0), writes=["h1T"])
            P.op("pool", lambda e: e.memset(kcn[:], 0.0), writes=["kcn"])
            P.dma(gkc[:], bc(g_kc, 64), writes=["gains"])
            for nt in range(NCT):
                P.dma(csc[:, nt * 64:nt * 64 + 32], rc_c[nt * 128:(nt + 1) * 128, :], writes=[("csc", nt, 0)])
                P.dma(csc[:, nt * 64 + 32:nt * 64 + 64], rc_s[nt * 128:(nt + 1) * 128, :], writes=[("csc", nt, 1)])
                for g in range(4):
                    o0 = (nt * 4 + g) * 129
                    P.dma(Vca[:, o0 + 64:o0 + 129], ovl[nt * 128:(nt + 1) * 128, :], writes=[("vca_c", nt, g)])
            for kv in range(2):
                P.dma(W1s[kv][:].rearrange("p (u h) -> p u h", h=256), c_w1[kv].rearrange("(u p) h -> p u h", p=128),
                      writes=[("W1s", kv)], eng="pool")
                P.dma(W2s[kv][:].rearrange("p (a d) -> p a d", d=64), c_w2[kv].rearrange("(a p) d -> p a d", p=128),
                      writes=[("W2s", kv)], eng="pool")
                P.dma(p2b[kv][:], pos2[kv], writes=[("p2b", kv)], eng="pool")
                for half in range(2):
                    for u in range(16):
                        P.op("pe", lambda e, kv=kv, half=half, u=u: e.matmul(
                            pB[:, half:half + 1], lhsT=W1s[kv][:, u * 256 + half * 128:u * 256 + half * 128 + 128],
                            rhs=p2b[kv][:, u:u + 1], start=(u == 0), stop=(u == 15)),
                            reads=[("W1s", kv), ("p2b", kv)], writes=["pB"])
                P.op("dve", lambda e, kv=kv: e.tensor_copy(out=biasv[kv][:], in_=pB[:, 0:2]), reads=["pB"], writes=[("biasv", kv)])
                src = kcT if kv == 0 else vcT
                for g in range(4):
                    kb2 = kc2[g % 2]
                    kk = ("kc2", g % 2)
                    P.dma(kb2[0:64, :], src[g * 64:(g + 1) * 64, :], writes=[(kk, 0)])
                    P.dma(kb2[64:128, 0:S - 1], src[g * 64:(g + 1) * 64, 1:S], writes=[(kk, 1)])
                    for half in range(2):
                        for u in range(16):
                            rhs = bass.AP(kb2, 2 * u, [[S, 128], [16, NCMP]])
                            P.op("pe", lambda e, kv=kv, half=half, u=u, rhs=rhs: e.matmul(
                                pH[half][:, 0:NCMP], lhsT=W1s[kv][:, u * 256 + half * 128:u * 256 + half * 128 + 128],
                                rhs=rhs, start=(u == 0), stop=(u == 15)),
                                reads=[("W1s", kv), (kk, 0), (kk, 1)], writes=[("pH", half)])
                        hsl = h1T[:, half * NCP:half * NCP + NCMP]
                        P.op("dve", lambda e, kv=kv, half=half: e.tensor_scalar(
                            out=xg[:, 0:NCMP], in0=pH[half][:, 0:NCMP], scalar1=biasv[kv][:, half:half + 1], scalar2=None,
                            op0=ALU.add), reads=[("pH", half), ("biasv", kv)], writes=["xg"])
                        P.op("pool", lambda e: e.tensor_tensor(out=x2[:, 0:NCMP], in0=xg[:, 0:NCMP], in1=xg[:, 0:NCMP],
                                                               op=ALU.mult), reads=["xg"], writes=["x2"])
                        P.op("pool", lambda e: e.tensor_scalar(out=x2[:, 0:NCMP], in0=x2[:, 0:NCMP], scalar1=0.044715,
                                                               scalar2=1.0, op0=ALU.mult, op1=ALU.add),
                             reads=["x2"], writes=["x2"])
                        P.op("pool", lambda e: e.tensor_tensor(out=x2[:, 0:NCMP], in0=x2[:, 0:NCMP], in1=xg[:, 0:NCMP],
                                                               op=ALU.mult), reads=["x2", "xg"], writes=["x2"])
                        P.op("act", lambda e: e.activation(out=th[:, 0:NCMP], in_=x2[:, 0:NCMP], func=AF.Tanh,
                                                           scale=0.7978845608028654), reads=["x2"], writes=["th"])
                        P.op("dve", lambda e: e.tensor_scalar(out=th[:, 0:NCMP], in0=th[:, 0:NCMP], scalar1=0.5, scalar2=0.5,
                                                              op0=ALU.mult, op1=ALU.add), reads=["th"], writes=["th"])
                        P.op("dve", lambda e, hsl=hsl: e.tensor_tensor(out=hsl, in0=th[:, 0:NCMP], in1=xg[:, 0:NCMP],
                                                                        op=ALU.mult), reads=["th", "xg"], writes=["h1T"])
                    for nt in range(NCT):
                        for half in range(2):
                            P.op("pe", lambda e, kv=kv, nt=nt, half=half: e.matmul(
                                pC[:, 0:64], lhsT=h1T[:, half * NCP + nt * 128:half * NCP + nt * 128 + 128],
                                rhs=W2s[kv][:, half * 64:(half + 1) * 64], start=(half == 0), stop=(half == 1)),
                                reads=["h1T", ("W2s", kv)], writes=["pC"])
                        if kv == 0:
                            normrope(P, pC[:, 0:64], "pC", 1, gkc[:], csc[:, nt * 64:nt * 64 + 32],
                                     csc[:, nt * 64 + 32:nt * 64 + 64], [("csc", nt, 0), ("csc", nt, 1)],
                                     kcn[:, 0:64], "kcn", TC, "c")
                            P.op("pe", lambda e: e.transpose(out=pKc[:, 0:128], in_=kcn[:], identity=ident[:]),
                                 reads=["kcn", "ident"], writes=["pKc"])
                            P.op("dve", lambda e, g=g, nt=nt: e.tensor_copy(
                                out=KcT[0:64, g * NCP + nt * 128:g * NCP + nt * 128 + 128], in_=pKc[0:64, 0:128]),
                                reads=["pKc"], writes=["KcT"])
                        else:
                            o0 = (nt * 4 + g) * 129
                            P.op("act", lambda e, o0=o0: e.activation(out=Vca[:, o0:o0 + 64], in_=pC[:, 0:64], func=AF.Identity),
                                 reads=["pC"], writes=["Vca"])
            if dbg:
                dK = nc.dram_tensor("dbg_KcT", [64, 4 * NCP], BF16, kind="ExternalOutput").ap()
                dV = nc.dram_tensor("dbg_Vca", [128, NCT * 4 * 129], BF16, kind="ExternalOutput").ap()
                P.dma(dK, KcT[0:64, :], reads=["KcT"])
                P.dma(dV, Vca[:], reads=["Vca"] + [("vca_c", nt, g) for nt in range(NCT) for g in range(4)])
            P.emit()
        if stage <= 3:
            return nc

        with contextlib.ExitStack() as se:
            Qt = [sb(f"e_Qt{i}", [128, SO], BF16, se) for i in range(2)]
            mcmp = sb("e_mcmp", [128, NSL * NCT * 512], BF16, se)
            ET = [sb(f"e_ET{i}", [128, 512], BF16, se) for i in range(2)]
            imp = sb("e_imp", [128, NOWN * 64], F32, se)
            frc = sb("e_frc", [128, NOWN * 64], F32, se)
            rl = sb("e_rl", [128, 4], F32, se)
            impf = sb("e_impf", [128, 64], F32, se)
            tmpf = sb("e_tmpf", [128, 64], F32, se)
            m8 = sb("e_m8", [128, 8], F32, se)
            m8b = sb("e_m8b", [128, 8], F32, se)
            bt = sb("e_bt", [128, 128], BF16, se)
            bstage = sb("e_bstage", [128, SO], BF16, se)
            pS = [ps(f"e_pS{i}", [128, 512], F32, se) for i in range(2)]
            pU = [ps(f"e_pU{i}", [128, 512], F32, se) for i in range(2)]
            pBt = ps("e_pBt", [128, 1024], BF16, se)
            P = Phase(nc, "pe")
            P.dma(mcmp[:].rearrange("p (a q) -> p a q", q=512), m_cmp, writes=["mcmp"])
            P.dma(frc[:].rearrange("p (t j) -> p t j", j=64), forced.rearrange("(t p) j -> p t j", p=128), writes=["frc"])
            P.op("pool", lambda e: e.memset(bt[:], 0.0), writes=["bt"])
            it = 0
            for g in range(4):
                P.op("pool", lambda e: e.memset(imp[:], 0.0), writes=["imp"])
                for h in range(4):
                    hq = 4 * g + h
                    qt = Qt[hq % 2]
                    P.dma(qt[0:64, :], QnT[hq * 64:(hq + 1) * 64, :], writes=[("Qt", hq % 2)])
                    for s_ in range(NSL):
                        for nt in range(NCT):
                            i = it % 2
                            it += 1
                            P.op("pe", lambda e, i=i, g=g, nt=nt, s_=s_, qt=qt: e.matmul(
                                pS[i][:], lhsT=KcT[0:64, g * NCP + nt * 128:g * NCP + nt * 128 + 128],
                                rhs=qt[0:64, s_ * 512:(s_ + 1) * 512], start=True, stop=False),
                                reads=["KcT", ("Qt", hq % 2)], writes=[("pS", i)])
                            P.op("pe", lambda e, i=i, nt=nt, s_=s_: e.matmul(
                                pS[i][:], lhsT=ident[:], rhs=mcmp[:, (s_ * NCT + nt) * 512:(s_ * NCT + nt + 1) * 512],
                                start=False, stop=True), reads=["ident", "mcmp"], writes=[("pS", i)])
                            P.op("act", lambda e, i=i: e.activation(out=ET[i][:], in_=pS[i][:], func=AF.Exp, scale=0.125),
                                 reads=[("pS", i)], writes=[("ET", i)])
                            for qs in range(4):
                                o0 = (nt * 4 + g) * 129
                                P.op("pe", lambda e, i=i, qs=qs, o0=o0, nt=nt: e.matmul(
                                    pU[qs // 2][:, (qs % 2) * 129:(qs % 2) * 129 + 129], lhsT=ET[i][:, qs * 128:(qs + 1) * 128],
                                    rhs=Vca[:, o0:o0 + 129], start=(nt == 0 and qs % 2 == 0), stop=(nt == NCT - 1),
                                    skip_group_check=True),
                                    reads=[("ET", i), "Vca"], writes=[("pU", qs // 2)])
                        for qs in range(4):
                            tl = s_ * 4 + qs
                            u0 = (qs % 2) * 129
                            pu = pU[qs // 2]
                            pk = ("pU", qs // 2)
                            P.op("dve", lambda e, pu=pu, u0=u0, qs=qs: e.tensor_scalar(
                                out=rl[:, qs:qs + 1], in0=pu[:, u0 + 128:u0 + 129], scalar1=1e-30, scalar2=None, op0=ALU.max),
                                reads=[pk], writes=["rl"])
                            P.op("dve", lambda e, qs=qs: e.reciprocal(out=rl[:, qs:qs + 1], in_=rl[:, qs:qs + 1]),
                                 reads=["rl"], writes=["rl"])
                            ysl = ynsa[:, tl * 1024 + hq * 64:tl * 1024 + hq * 64 + 64]
                            gcol = gates[:, tl * 48 + hq * 3:tl * 48 + hq * 3 + 1]
                            P.op("dve", lambda e, pu=pu, u0=u0, qs=qs, ysl=ysl, gcol=gcol: e.tensor_scalar(
                                out=ysl, in0=pu[:, u0:u0 + 64], scalar1=rl[:, qs:qs + 1], scalar2=gcol,
                                op0=ALU.mult, op1=ALU.mult), reads=[pk, "rl", "gates"], writes=[("ynsa", tl)])
                            isl = imp[:, tl * 64:(tl + 1) * 64]
                            P.op("dve", lambda e, pu=pu, u0=u0, qs=qs, isl=isl: e.scalar_tensor_tensor(
                                out=isl, in0=pu[:, u0 + 64:u0 + 128], scalar=rl[:, qs:qs + 1], in1=isl,
                                op0=ALU.mult, op1=ALU.add), reads=[pk, "rl", "imp"], writes=["imp"])
                for tl in range(NOWN):
                    isl = imp[:, tl * 64:(tl + 1) * 64]
                    P.op("dve", lambda e, isl=isl, tl=tl: e.tensor_tensor(out=impf[:], in0=isl, in1=frc[:, tl * 64:(tl + 1) * 64],
                                                                          op=ALU.max), reads=["imp", "frc"], writes=["impf"])
                    P.op("dve", lambda e: e.max(out=m8[:], in_=impf[:]), reads=["impf"], writes=["m8"])
                    P.op("dve", lambda e: e.match_replace(out=tmpf[:], in_to_replace=m8[:], in_values=impf[:], imm_value=-1e9),
                         reads=["m8", "impf"], writes=["tmpf"])
                    P.op("dve", lambda e: e.max(out=m8b[:], in_=tmpf[:]), reads=["tmpf"], writes=["m8b"])
                    P.op("dve", lambda e: e.tensor_scalar(out=tmpf[:], in0=impf[:], scalar1=m8b[:, 7:8], scalar2=None,
                                                          op0=ALU.is_ge), reads=["impf", "m8b", "tmpf"], writes=["tmpf"])
                    P.op("dve", lambda e: e.tensor_scalar(out=bt[:, 64:128], in0=tmpf[:], scalar1=-1.0, scalar2=-NEG,
                                                          op0=ALU.add, op1=ALU.mult), reads=["tmpf"], writes=["bt"])
                    P.op("pe", lambda e: e.transpose(out=pBt[:, 0:128], in_=bt[:], identity=ident[:]),
                         reads=["bt", "ident"], writes=["pBt"])
                    P.op("dve", lambda e, tl=tl: e.tensor_copy(out=bstage[64:128, tl * 128:(tl + 1) * 128], in_=pBt[64:128, 0:128]),
                         reads=["pBt"], writes=["bstage"])
                P.dma(BsT[g * 64:(g + 1) * 64, :], bstage[64:128, :], reads=["bstage"])
            P.emit()
        if stage <= 4:
            return nc

        with contextlib.ExitStack() as sf:
            KD = [sb(f"f_KD{i}", [128, S], BF16, sf) for i in range(2)]
            VD = [sb(f"f_VD{i}", [128, NT * 129], BF16, sf) for i in range(2)]
            QD = [sb(f"f_QD{i}", [128, SO], BF16, sf) for i in range(2)]
            KS = sb("f_KS", [128, S], BF16, sf)
            KW = sb("f_KW", [128, S], BF16, sf)
            VS = sb("f_VS", [128, NT * 65], BF16, sf)
            VW = sb("f_VW", [128, NT * 65], BF16, sf)
            QS = [sb(f"f_QS{i}", [128, SO], BF16, sf) for i in range(2)]
            mden = sb("f_mden", [128, 8 * 512], BF16, sf)
            mwin = sb("f_mwin", [128, 12 * 512], BF16, sf)
            PT = [sb(f"f_PT{i}", [128, 512], BF16, sf) for i in range(4)]
            oa = sb("f_oa", [128, 128], F32, sf)
            ob = sb("f_ob", [128, 128], F32, sf)
            jk = sb("f_jk", [128, 128], F32, sf)
            r4 = sb("f_r4", [128, 4], F32, sf)
            ssd = sb("f_ssd", [128, 1], F32, sf)
            rsd = sb("f_rsd", [128, 1], F32, sf)
            tn = sb("f_tn", [128, 64], F32, sf)
            pS = [ps(f"f_pS{i}", [128, 512], F32, sf) for i in range(4)]
            pO = [ps(f"f_pO{i}", [128, 512], F32, sf) for i in range(3)]
            P = Phase(nc, "pf")
            P.dma(mden[:].rearrange("p (a q) -> p a q", q=512), m_dense, writes=["mden"])
            P.dma(mwin[:].rearrange("p (a q) -> p a q", q=512), m_win, writes=["mwin"])
            P.dma(KS[64:128, :], eind, writes=["KS_e"])
            for i in range(2):
                P.op("pool", lambda e, i=i: e.memset(VD[i][:], 1.0), writes=[("VD", i)])
            P.op("pool", lambda e: e.memset(VS[:], 1.0), writes=["VS"])
            P.op("pool", lambda e: e.memset(VW[:], 1.0), writes=["VW"])

            def load_v(dst, dkey, src2d, width, stride):
                d3 = dst[:].rearrange("p (t c) -> p t c", c=stride)
                s3 = src2d.rearrange("(t p) c -> p t c", p=128)
                step = 8
                for a in range(0, NT, step):
                    b_ = min(NT, a + step)
                    P.dma(d3[:, a:b_, 0:width], s3[:, a:b_, :], writes=[dkey])

            def load_diff(h):
                i = h % 2
                P.dma(KD[i][:], KdT[h], writes=[("KD", i)])
                load_v(VD[i], ("VD", i), Vd[:, h * 128:(h + 1) * 128], 128, 129)
                P.dma(QD[i][:], QdT[h], writes=[("QD", i)])

            load_diff(0)
            for h in range(8):
                if h + 1 < 8:
                    load_diff(h + 1)
                bi = h % 2
                for s_ in range(NSL):
                    nkt = 8 * (s_ + 1)

                    def qk(kt, s_=s_, nkt=nkt, bi=bi):
                        i2 = kt % 2
                        masked = kt >= nkt - 8
                        mi = kt - (nkt - 8)
                        for m in range(2):
                            bk = 2 * i2 + m
                            P.op("pe", lambda e, bk=bk, m=m, kt=kt: e.matmul(
                                pS[bk][:], lhsT=KD[bi][m * 64:(m + 1) * 64, kt * 128:(kt + 1) * 128],
                                rhs=QD[bi][m * 64:(m + 1) * 64, s_ * 512:(s_ + 1) * 512], start=True, stop=not masked),
                                reads=[("KD", bi), ("QD", bi)], writes=[("pS", bk)])
                            if masked:
                                P.op("pe", lambda e, bk=bk, mi=mi: e.matmul(
                                    pS[bk][:], lhsT=ident[:], rhs=mden[:, mi * 512:(mi + 1) * 512], start=False, stop=True),
                                    reads=["ident", "mden"], writes=[("pS", bk)])
                            P.op("act", lambda e, bk=bk: e.activation(out=PT[bk][:], in_=pS[bk][:], func=AF.Exp, scale=0.125),
                                 reads=[("pS", bk)], writes=[("PT", bk)])

                    def pv(kt, s_=s_, nkt=nkt, bi=bi):
                        i2 = kt % 2
                        for m in range(2):
                            bk = 2 * i2 + m
                            for qs in range(4):
                                a = m * 4 + qs
                                P.op("pe", lambda e, bk=bk, qs=qs, a=a, kt=kt: e.matmul(
                                    pO[a // 3][:, (a % 3) * 129:(a % 3) * 129 + 129], lhsT=PT[bk][:, qs * 128:(qs + 1) * 128],
                                    rhs=VD[bi][:, kt * 129:kt * 129 + 129], start=(kt == 0 and a % 3 == 0), stop=(kt == nkt - 1),
                                    skip_group_check=True),
                                    reads=[("PT", bk), ("VD", bi)], writes=[("pO", a // 3)])

                    qk(0)
                    for kt in range(nkt):
                        if kt + 1 < nkt:
                            qk(kt + 1)
                        pv(kt)
                    for qs in range(4):
                        tl = s_ * 4 + qs
                        a0, a1 = qs, 4 + qs
                        o0 = pO[a0 // 3][:, (a0 % 3) * 129:(a0 % 3) * 129 + 129]
                        o1 = pO[a1 // 3][:, (a1 % 3) * 129:(a1 % 3) * 129 + 129]
                        k0, k1 = ("pO", a0 // 3), ("pO", a1 // 3)
                        P.op("dve", lambda e, o0=o0: e.reciprocal(out=r4[:, 0:1], in_=o0[:, 128:129]), reads=[k0], writes=["r4"])
                        P.op("dve", lambda e, o1=o1: e.reciprocal(out=r4[:, 1:2], in_=o1[:, 128:129]), reads=[k1, "r4"], writes=["r4"])
                        P.op("dve", lambda e: e.tensor_tensor(out=r4[:, 2:3], in0=r4[:, 1:2], in1=nlam[:], op=ALU.mult),
                             reads=["r4", "nlam"], writes=["r4"])
                        P.op("dve", lambda e, o0=o0: e.tensor_scalar(out=oa[:], in0=o0[:, 0:128], scalar1=r4[:, 0:1], scalar2=None,
                                                                    op0=ALU.mult), reads=[k0, "r4"], writes=["oa"])
                        P.op("dve", lambda e, o1=o1: e.scalar_tensor_tensor(out=ob[:], in0=o1[:, 0:128], scalar=r4[:, 2:3], in1=oa[:],
                                                                           op0=ALU.mult, op1=ALU.add),
                             reads=[k1, "r4", "oa"], writes=["ob"])
                        P.op("pool", lambda e: e.memset(ssd[:], 0.0), writes=["ssd"])
                        P.op("act", lambda e: e.activation(out=jk[:], in_=ob[:], func=AF.Square, accum_out=ssd[:]),
                             reads=["ob", "ssd"], writes=["jk", "ssd"])
                        P.op("dve", lambda e: e.tensor_scalar(out=ssd[:], in0=ssd[:], scalar1=1.0 / 128, scalar2=EPS,
                                                              op0=ALU.mult, op1=ALU.add), reads=["ssd"], writes=["ssd"])
                        P.op("act", lambda e: e.activation(out=ssd[:], in_=ssd[:], func=AF.Ln), reads=["ssd"], writes=["ssd"])
                        P.op("act", lambda e: e.activation(out=rsd[:], in_=ssd[:], func=AF.Exp, scale=-0.5),
                             reads=["ssd"], writes=["rsd"])
                        ysl = ydiff[:, tl * 1024 + h * 128:tl * 1024 + (h + 1) * 128]
                        P.op("dve", lambda e, ysl=ysl: e.scalar_tensor_tensor(out=ysl, in0=ob[:], scalar=rsd[:], in1=sgain[:],
                                                                              op0=ALU.mult, op1=ALU.mult),
                             reads=["ob", "rsd", "sgain"], writes=[("ydiff", tl)])

            cnt = [0]
            for g in range(4):
                P.dma(KS[0:64, :], KsT[g * 64:(g + 1) * 64, :], writes=["KS"])
                P.dma(KW[0:64, :], KwT[g * 64:(g + 1) * 64, :], writes=["KW"])
                load_v(VS, "VS", Vs[:, g * 64:(g + 1) * 64], 64, 65)
                load_v(VW, "VW", Vw[:, g * 64:(g + 1) * 64], 64, 65)
                for h in range(4):
                    hq = 4 * g + h
                    qi = hq % 2
                    P.dma(QS[qi][0:64, :], QnT[hq * 64:(hq + 1) * 64, :], writes=[("QS", qi, 0)])
                    P.dma(QS[qi][64:128, :], BsT[g * 64:(g + 1) * 64, :], writes=[("QS", qi, 1)])
                    for kind in (2, 1):
                        for s_ in range(NSL):
                            if kind == 1:
                                kts = list(range(0, 8 * (s_ + 1)))
                                mis = [kt - 8 * s_ if kt >= 8 * s_ else None for kt in kts]
                            else:
                                kts = list(range(max(0, 8 * s_ - 4), 8 * s_ + 8))
                                mis = [kt - (8 * s_ - 4) for kt in kts]
                            banks = []
                            for _ in kts:
                                banks.append(cnt[0] % 4)
                                cnt[0] += 1

                            def qk(j, kind=kind, s_=s_, kts=kts, mis=mis, banks=banks, qi=qi):
                                kt, mi, bk = kts[j], mis[j], banks[j]
                                if kind == 1:
                                    lhsT = KS[:, kt * 128:(kt + 1) * 128]
                                    rhs = QS[qi][:, s_ * 512:(s_ + 1) * 512]
                                    rd = ["KS", "KS_e", ("QS", qi, 0), ("QS", qi, 1)]
                                    mt = mden
                                else:
                                    lhsT = KW[0:64, kt * 128:(kt + 1) * 128]
                                    rhs = QS[qi][0:64, s_ * 512:(s_ + 1) * 512]
                                    rd = ["KW", ("QS", qi, 0)]
                                    mt = mwin
                                P.op("pe", lambda e: e.matmul(pS[bk][:], lhsT=lhsT, rhs=rhs, start=True, stop=(mi is None)),
                                     reads=rd, writes=[("pS", bk)])
                                if mi is not None:
                                    P.op("pe", lambda e: e.matmul(pS[bk][:], lhsT=ident[:], rhs=mt[:, mi * 512:(mi + 1) * 512],
                                                                  start=False, stop=True),
                                         reads=["ident", "mden", "mwin"], writes=[("pS", bk)])
                                P.op("act", lambda e: e.activation(out=PT[bk][:], in_=pS[bk][:], func=AF.Exp, scale=0.125),
                                     reads=[("pS", bk)], writes=[("PT", bk)])

                            def pv(j, kind=kind, kts=kts, banks=banks):
                                kt, bk = kts[j], banks[j]
                                vt, vk = (VS, "VS") if kind == 1 else (VW, "VW")
                                for qs in range(4):
                                    P.op("pe", lambda e, qs=qs: e.matmul(
                                        pO[0][:, qs * 65:qs * 65 + 65], lhsT=PT[bk][:, qs * 128:(qs + 1) * 128],
                                        rhs=vt[:, kt * 65:kt * 65 + 65], start=(j == 0 and qs == 0), stop=(j == len(kts) - 1),
                                        skip_group_check=True),
                                        reads=[("PT", bk), vk], writes=[("pO", 0)])

                            qk(0)
                            for j in range(len(kts)):
                                if j + 1 < len(kts):
                                    qk(j + 1)
                                pv(j)
                            for qs in range(4):
                                tl = s_ * 4 + qs
                                oq = pO[0][:, qs * 65:qs * 65 + 65]
                                P.op("dve", lambda e, oq=oq, qs=qs: e.tensor_scalar(
                                    out=r4[:, qs:qs + 1], in0=oq[:, 64:65], scalar1=1e-30, scalar2=None, op0=ALU.max),
                                    reads=[("pO", 0)], writes=["r4"])
                                P.op("dve", lambda e, qs=qs: e.reciprocal(out=r4[:, qs:qs + 1], in_=r4[:, qs:qs + 1]),
                                     reads=["r4"], writes=["r4"])
                                gcol = gates[:, tl * 48 + hq * 3 + kind:tl * 48 + hq * 3 + kind + 1]
                                P.op("dve", lambda e, oq=oq, qs=qs, gcol=gcol: e.tensor_scalar(
                                    out=tn[:], in0=oq[:, 0:64], scalar1=r4[:, qs:qs + 1], scalar2=gcol,
                                    op0=ALU.mult, op1=ALU.mult), reads=[("pO", 0), "r4", "gates"], writes=["tn"])
                                ysl = ynsa[:, tl * 1024 + hq * 64:tl * 1024 + hq * 64 + 64]
                                P.op("pool", lambda e, ysl=ysl: e.tensor_tensor(out=ysl, in0=ysl, in1=tn[:], op=ALU.add),
                                     reads=["tn", ("ynsa", tl)], writes=[("ynsa", tl)])
            if dbg:
                dY = nc.dram_tensor("dbg_ydiff", [128, NOWN * 1024], BF16, kind="ExternalOutput").ap()
                dN = nc.dram_tensor("dbg_ynsa", [128, NOWN * 1024], BF16, kind="ExternalOutput").ap()
                P.dma(dY, ydiff[:], reads=[("ydiff", t) for t in range(NOWN)])
                P.dma(dN, ynsa[:], reads=[("ynsa", t) for t in range(NOWN)])
            P.emit()
        if stage <= 5:
            return nc

        with contextlib.ExitStack() as sg:
            Wpd = sb("g_Wpd", [128, 8 * 1024], BF16, sg)
            Wpn = sb("g_Wpn", [128, 8 * 1024], BF16, sg)
            Wo = sb("g_Wo", [128, 8 * 1024], BF16, sg)
            ydT = sb("g_ydT", [128, 1024], BF16, sg)
            ynT = sb("g_ynT", [128, 1024], BF16, sg)
            mgl = [sb(f"g_mgl{i}", [128, 2048], F32, sg) for i in range(2)]
            xql = [sb(f"g_xql{i}", [128, 1024], F32, sg) for i in range(2)]
            m1 = sb("g_m1", [128, 1024], F32, sg)
            m2 = sb("g_m2", [128, 1024], F32, sg)
            mixb = sb("g_mixb", [128, 1024], BF16, sg)
            mixT = sb("g_mixT", [128, 1024], BF16, sg)
            x1t = [sb(f"g_x1t{i}", [128, 1024], F32, sg) for i in range(2)]
            pTa = ps("g_pTa", [128, 1024], BF16, sg)
            pTb = ps("g_pTb", [128, 1024], BF16, sg)
            pD = [ps(f"g_pD{i}", [128, 512], F32, sg) for i in range(2)]
            pN = [ps(f"g_pN{i}", [128, 512], F32, sg) for i in range(2)]
            pW = [ps(f"g_pW{i}", [128, 512], F32, sg) for i in range(2)]
            P = Phase(nc, "pg")
            for (wd, wsrc, wk_) in ((Wpd, w_pd, "Wpd"), (Wpn, w_pn, "Wpn"), (Wo, w_o, "Wo")):
                for hc in range(2):
                    P.dma(wd[:, hc * 4096:(hc + 1) * 4096].rearrange("p (c n) -> p c n", n=1024),
                          wsrc[hc * 512:(hc + 1) * 512, :].rearrange("(c p) n -> p c n", p=128), writes=[(wk_, hc)], eng="pool")
            for t in range(NOWN):
                tok = slice(t * 128, (t + 1) * 128)
                mg_ = mgl[t % 2]
                xq_ = xql[t % 2]
                P.dma(mg_[:], mgS[tok, :], writes=[("mgl", t % 2)])
                P.dma(xq_[:], xq[tok, :], writes=[("xql", t % 2)])
                for c in range(8):
                    P.op("pe", lambda e, c=c, t=t: e.transpose(out=pTa[:, c * 128:(c + 1) * 128],
                                                              in_=ydiff[:, t * 1024 + c * 128:t * 1024 + (c + 1) * 128], identity=ident[:]),
                         reads=["ident"], writes=["pTa"])
                P.op("dve", lambda e: e.tensor_copy(out=ydT[:], in_=pTa[:]), reads=["pTa"], writes=["ydT"])
                for c in range(8):
                    P.op("pe", lambda e, c=c, t=t: e.transpose(out=pTb[:, c * 128:(c + 1) * 128],
                                                              in_=ynsa[:, t * 1024 + c * 128:t * 1024 + (c + 1) * 128], identity=ident[:]),
                         reads=["ident"], writes=["pTb"])
                P.op("dve", lambda e: e.tensor_copy(out=ynT[:], in_=pTb[:]), reads=["pTb"], writes=["ynT"])
                for hf in range(2):
                    for c in range(8):
                        P.op("pe", lambda e, c=c, hf=hf: e.matmul(pD[hf][:], lhsT=ydT[:, c * 128:(c + 1) * 128],
                                                                  rhs=Wpd[:, c * 1024 + hf * 512:c * 1024 + (hf + 1) * 512],
                                                                  start=(c == 0), stop=(c == 7)),
                             reads=["ydT", ("Wpd", c // 4)], writes=[("pD", hf)])
                    for c in range(8):
                        P.op("pe", lambda e, c=c, hf=hf: e.matmul(pN[hf][:], lhsT=ynT[:, c * 128:(c + 1) * 128],
                                                                  rhs=Wpn[:, c * 1024 + hf * 512:c * 1024 + (hf + 1) * 512],
                                                                  start=(c == 0), stop=(c == 7)),
                             reads=["ynT", ("Wpn", c // 4)], writes=[("pN", hf)])
                    P.op("dve", lambda e, hf=hf, mg_=mg_: e.tensor_tensor(out=m1[:, hf * 512:(hf + 1) * 512], in0=pD[hf][:],
                                                                          in1=mg_[:, hf * 512:(hf + 1) * 512], op=ALU.mult),
                         reads=[("pD", hf), ("mgl", t % 2)], writes=[("m1", hf)])
                    P.op("dve", lambda e, hf=hf, mg_=mg_: e.tensor_tensor(out=m2[:, hf * 512:(hf + 1) * 512], in0=pN[hf][:],
                                                                          in1=mg_[:, 1024 + hf * 512:1024 + (hf + 1) * 512], op=ALU.mult),
                         reads=[("pN", hf), ("mgl", t % 2)], writes=[("m2", hf)])
                    P.op("pool", lambda e, hf=hf: e.tensor_tensor(out=mixb[:, hf * 512:(hf + 1) * 512], in0=m1[:, hf * 512:(hf + 1) * 512],
                                                                  in1=m2[:, hf * 512:(hf + 1) * 512], op=ALU.add),
                         reads=[("m1", hf), ("m2", hf)], writes=[("mixb", hf)])
                for c in range(8):
                    P.op("pe", lambda e, c=c: e.transpose(out=pTa[:, c * 128:(c + 1) * 128], in_=mixb[:, c * 128:(c + 1) * 128],
                                                          identity=ident[:]),
                         reads=[("mixb", c // 4), "ident"], writes=["pTa"])
                P.op("dve", lambda e: e.tensor_copy(out=mixT[:], in_=pTa[:]), reads=["pTa"], writes=["mixT"])
                x1_ = x1t[t % 2]
                for hf in range(2):
                    for c in range(8):
                        P.op("pe", lambda e, c=c, hf=hf: e.matmul(pW[hf][:], lhsT=mixT[:, c * 128:(c + 1) * 128],
                                                                  rhs=Wo[:, c * 1024 + hf * 512:c * 1024 + (hf + 1) * 512],
                                                                  start=(c == 0), stop=(c == 7)),
                             reads=["mixT", ("Wo", c // 4)], writes=[("pW", hf)])
                    P.op("dve", lambda e, hf=hf, x1_=x1_, xq_=xq_: e.tensor_tensor(out=x1_[:, hf * 512:(hf + 1) * 512], in0=pW[hf][:],
                                                                                  in1=xq_[:, hf * 512:(hf + 1) * 512], op=ALU.add),
                         reads=[("pW", hf), ("xql", t % 2)], writes=[("x1t", t % 2, hf)])
                P.dma(x1d[tok, :], x1_[:], reads=[("x1t", t % 2, 0), ("x1t", t % 2, 1)])
            P.emit()
        if stage <= 6:
            return nc
    with contextlib.ExitStack() as top2:
        def sb2(name, shape, dt):
            return top2.enter_context(nc.sbuf_tensor(name, list(shape), dt))

        ident2 = sb2("ident2", [128, 128], BF16)
        Wup = sb2("h_Wup", [128, 8 * 4096], BF16)
        Wdn = sb2("h_Wdn", [128, 32 * 1024], BF16)
        gml = sb2("h_gml", [128, 1024], F32)
        x1g = sb2("h_x1g", [128, 4 * 1024], F32)
        junk2 = sb2("h_junk", [128, 1024], F32)
        h2 = sb2("h_h2", [128, 1024], BF16)
        h2T = sb2("h_h2T", [128, 8 * 512], BF16)
        uT = sb2("h_uT", [128, 32 * 512], BF16)
        rr = [sb2(f"h_rr{i}", [128, 512], F32) for i in range(2)]
        ot = [sb2(f"h_ot{i}", [128, 512], F32) for i in range(2)]
        ss2 = sb2("h_ss", [128, 1], F32)
        rs2 = sb2("h_rs", [128, 1], F32)
        pT2 = top2.enter_context(nc.psum_tensor("h_pT", [128, 1024], BF16))
        pU2 = [top2.enter_context(nc.psum_tensor(f"h_pU{i}", [128, 512], F32)) for i in range(2)]
        pDn = [top2.enter_context(nc.psum_tensor(f"h_pDn{i}", [128, 512], F32)) for i in range(2)]
        P = Phase(nc, "ph")
        P.dma(ident2[:], identd, writes=["ident"])
        P.dma(gml[:], g_mlp[0:1, :].broadcast_to([128, 1024]), writes=["gt"])
        for c in range(8):
            P.dma(Wup[:, c * 4096:(c + 1) * 4096], w_up[c * 128:(c + 1) * 128, :], writes=[("Wup", c)], eng="pool")
        for f4 in range(8):
            P.dma(Wdn[:, f4 * 4096:(f4 + 1) * 4096].rearrange("p (a n) -> p a n", n=1024),
                  w_dn[f4 * 512:(f4 + 1) * 512, :].rearrange("(a p) n -> p a n", p=128), writes=[("Wdn", f4)], eng="pool")
        for gq in range(NSL):
            rows = slice(gq * 512, (gq + 1) * 512)
            P.dma(x1g[:].rearrange("p (t n) -> p t n", n=1024), x1d[rows, :].rearrange("(t p) n -> p t n", p=128), writes=["x1g"])
            for j in range(4):
                xt_ = x1g[:, j * 1024:(j + 1) * 1024]
                P.op("pool", lambda e: e.memset(ss2[:], 0.0), writes=["ss2"])
                P.op("act", lambda e, xt_=xt_: e.activation(out=junk2[:], in_=xt_, func=AF.Square, accum_out=ss2[:]),
                     reads=["x1g", "ss2"], writes=["junk2", "ss2"])
                P.op("dve", lambda e: e.tensor_scalar(out=ss2[:], in0=ss2[:], scalar1=1.0 / 1024, scalar2=EPS,
                                                      op0=ALU.mult, op1=ALU.add), reads=["ss2"], writes=["ss2"])
                P.op("act", lambda e: e.activation(out=ss2[:], in_=ss2[:], func=AF.Ln), reads=["ss2"], writes=["ss2"])
                P.op("act", lambda e: e.activation(out=rs2[:], in_=ss2[:], func=AF.Exp, scale=-0.5), reads=["ss2"], writes=["rs2"])
                P.op("dve", lambda e, xt_=xt_: e.scalar_tensor_tensor(out=h2[:], in0=xt_, scalar=rs2[:], in1=gml[:],
                                                                     op0=ALU.mult, op1=ALU.mult),
                     reads=["x1g", "rs2", "gt"], writes=["h2"])
                for c in range(8):
                    P.op("pe", lambda e, c=c: e.transpose(out=pT2[:, c * 128:(c + 1) * 128], in_=h2[:, c * 128:(c + 1) * 128],
                                                          identity=ident2[:]), reads=["h2", "ident"], writes=["pT2"])
                P.op("dve", lambda e, j=j: e.tensor_copy(
                    out=h2T[:].rearrange("p (c q) -> p c q", q=512)[:, :, j * 128:(j + 1) * 128],
                    in_=pT2[:].rearrange("p (c q) -> p c q", q=128)), reads=["pT2"], writes=["h2T"])
            for f in range(32):
                pu = pU2[f % 2]
                for c in range(8):
                    P.op("pe", lambda e, c=c, f=f, pu=pu: e.matmul(pu[:], lhsT=Wup[:, c * 4096 + f * 128:c * 4096 + (f + 1) * 128],
                                                                  rhs=h2T[:, c * 512:(c + 1) * 512], start=(c == 0), stop=(c == 7)),
                         reads=[("Wup", c), "h2T"], writes=[("pU2", f % 2)])
                r_ = rr[f % 2]
                P.op("act", lambda e, pu=pu, r_=r_: e.activation(out=r_[:], in_=pu[:], func=AF.Relu),
                     reads=[("pU2", f % 2)], writes=[("rr", f % 2)])
                P.op("pool" if f % 2 else "dve", lambda e, f=f, r_=r_: e.tensor_tensor(out=uT[:, f * 512:(f + 1) * 512], in0=r_[:], in1=r_[:],
                                                                                     op=ALU.mult),
                     reads=[("rr", f % 2)], writes=[("uT", f)])
            for j in range(4):
                for hf in range(2):
                    i = (j * 2 + hf) % 2
                    for f in range(32):
                        P.op("pe", lambda e, f=f, j=j, hf=hf, i=i: e.matmul(
                            pDn[i][:], lhsT=uT[:, f * 512 + j * 128:f * 512 + (j + 1) * 128],
                            rhs=Wdn[:, f * 1024 + hf * 512:f * 1024 + (hf + 1) * 512], start=(f == 0), stop=(f == 31)),
                            reads=[("uT", f), ("Wdn", f // 4)], writes=[("pDn", i)])
                    o_ = ot[i]
                    P.op("dve", lambda e, i=i, o_=o_, j=j, hf=hf: e.tensor_tensor(
                        out=o_[:], in0=pDn[i][:], in1=x1g[:, j * 1024 + hf * 512:j * 1024 + (hf + 1) * 512], op=ALU.add),
                        reads=[("pDn", i), "x1g"], writes=[("ot", i)])
                    P.dma(out[gq * 512 + j * 128:gq * 512 + (j + 1) * 128, hf * 512:(hf + 1) * 512], o_[:], reads=[("ot", i)])
        P.emit()
    return nc


def _rope_tab(pos):
    inv = (10000.0 ** (-(np.arange(32, dtype=np.float32)) / np.float32(32))).astype(np.float32)
    ang = pos.astype(np.float32)[:, None] * inv[None, :]
    return np.cos(ang).astype(np.float32), np.sin(ang).astype(np.float32)


def make_core_inputs(inp, core, S):
    b, par = core // 2, core % 2
    NSL = S // 1024
    SO = NSL * 512
    NCMP = (S - 32) // 16 + 1
    NCT = (NCMP + 127) // 128
    NCP = NCT * 128
    f = lambda a: np.ascontiguousarray(np.asarray(a, dtype=np.float32))
    x = np.asarray(inp["x"], dtype=np.float32)
    own_pos = np.concatenate([np.arange((2 * s + par) * 512, (2 * s + par) * 512 + 512) for s in range(NSL)])
    d = {}
    d["xkv"] = f(x[b])
    d["xq"] = f(x[b][own_pos])
    d["w_in"] = f(inp["w_in"][0])
    d["w_pd"] = f(inp["w_proj_diff"][0])
    d["w_pn"] = f(inp["w_proj_nsa"][0])
    d["w_o"] = f(inp["w_out"][0])
    d["w_up"] = f(inp["w_mlp_up"][0])
    d["w_dn"] = f(inp["w_mlp_down"][0])
    d["c_w1k"] = f(inp["cmp_k_w1"][0])
    d["c_w1v"] = f(inp["cmp_v_w1"][0])
    d["c_w2k"] = f(inp["cmp_k_w2"][0])
    d["c_w2v"] = f(inp["cmp_v_w2"][0])
    p2 = lambda p: f(np.asarray(p, np.float32).reshape(16, 2, 64).transpose(1, 2, 0).reshape(128, 16))
    d["pos2k"] = p2(inp["cmp_pos_k"][0])
    d["pos2v"] = p2(inp["cmp_pos_v"][0])
    d["g_mix"] = f(inp["ln_mix_g"][0][None, :])
    d["g_mlp"] = f(inp["ln_mlp_g"][0][None, :])
    gk = np.asarray(inp["nsa_k_norm_g"][0], np.float32)
    d["g_k24"] = f(np.concatenate([np.tile(np.asarray(inp["diff_k_norm_g"][0], np.float32), 16),
                                   np.tile(gk[1], 4), np.tile(gk[2], 4)])[None, :])
    d["g_q32"] = f(np.concatenate([np.tile(np.asarray(inp["diff_q_norm_g"][0], np.float32), 16),
                                   np.tile(np.asarray(inp["nsa_q_norm_g"][0], np.float32), 16)])[None, :])
    d["g_kc"] = f(gk[0][None, :])
    d["g_sub"] = f(inp["diff_subln_g"][0][None, :])
    d["lam4"] = f(np.concatenate([np.asarray(inp[k][0], np.float32) for k in
                                  ("diff_lambda_q1", "diff_lambda_k1", "diff_lambda_q2", "diff_lambda_k2")])[None, :])
    d["rkv_c"], d["rkv_s"] = _rope_tab(np.arange(S))
    d["rq_c"], d["rq_s"] = _rope_tab(own_pos)
    cc = np.zeros(NCP, np.float32)
    cc[:NCMP] = np.arange(NCMP) * 16 + 15.5
    d["rc_c"], d["rc_s"] = _rope_tab(cc)
    d["ident"] = np.eye(128, dtype=np.float32).astype(NPBF)
    kk = np.arange(S)
    d["eind"] = (kk[None, :] // 64 == np.arange(64)[:, None]).astype(np.float32).astype(NPBF)
    n = np.arange(NCP)
    cs, ce = n * 16, n * 16 + 31
    ss_ = np.arange(64) * 64
    ov = ((cs[:, None] < ss_[None, :] + 64) & (ce[:, None] >= ss_[None, :]) & (n[:, None] < NCMP)).astype(np.float32)
    d["ovl"] = np.concatenate([ov, np.ones((NCP, 1), np.float32)], axis=1).astype(NPBF)
    k128 = np.arange(128)[:, None, None]
    q512 = np.arange(512)[None, None, :]
    kw = np.arange(8)[None, :, None] * 128 + k128
    d["m_dense"] = np.where((kw - par * 512) <= q512, 0.0, NEG).astype(np.float32).astype(NPBF)
    kw = np.arange(12)[None, :, None] * 128 + k128 - 512
    dd = par * 512 + q512 - kw
    d["m_win"] = np.where((dd >= 0) & (dd < 512), 0.0, NEG).astype(np.float32).astype(NPBF)
    mc = np.zeros((128, NSL * NCT, 512), np.float32)
    for s in range(NSL):
        for nt in range(NCT):
            nn = nt * 128 + np.arange(128)[:, None]
            qpos = (2 * s + par) * 512 + np.arange(512)[None, :]
            mc[:, s * NCT + nt, :] = np.where((nn < NCMP) & (nn * 16 + 31 <= qpos), 0.0, NEG)
    d["m_cmp"] = mc.astype(NPBF)
    fo = np.zeros((SO, 64), np.float32)
    cur = own_pos // 64
    r = np.arange(SO)
    fo[r[cur >= 1], (cur - 1)[cur >= 1]] = 1e4
    fo[r, cur] = 2e4
    fo[:, 0] = 3e4
    d["forced"] = fo
    return d


_NC_CACHE = {}


def kernel(**inputs):
    S = int(np.asarray(inputs["x"]).shape[1])
    B = int(np.asarray(inputs["x"]).shape[0])
    ncores = 2 * B
    if S not in _NC_CACHE:
        _NC_CACHE[S] = build(S)
    nc = _NC_CACHE[S]
    in_maps = [make_core_inputs(inputs, c, S) for c in range(ncores)]
    res = run_bass_kernel_spmd(nc, in_maps, core_ids=list(range(ncores)))
    out = np.zeros((B, S, 1024), np.float32)
    NSL = S // 1024
    for c in range(ncores):
        b, par = c // 2, c % 2
        o = np.asarray(res.results[c]["out"], dtype=np.float32)
        for s in range(NSL):
            ch = 2 * s + par
            out[b, ch * 512:(ch + 1) * 512] = o[s * 512:(s + 1) * 512]
    return out
```

```python
import contextlib
import numpy as np
import ml_dtypes
import concourse.bass as bass
import concourse.mybir as mybir
from concourse.bass_utils import run_bass_kernel_spmd

F32 = mybir.dt.float32
BF16 = mybir.dt.bfloat16
ALU = mybir.AluOpType
AF = mybir.ActivationFunctionType
AX = mybir.AxisListType
NPBF = ml_dtypes.bfloat16

NDMASEM = 4
EPS = 1e-6
NEG = -30000.0


class Phase:
    ENGS = ("pe", "act", "dve", "pool", "sp")

    def __init__(self, nc, name):
        self.nc = nc
        self.name = name
        self.ops = []

    def op(self, eng, fn, reads=(), writes=(), dma=False):
        self.ops.append((eng, fn, tuple(reads), tuple(writes), dma))

    def dma(self, out, in_, reads=(), writes=(), eng="sp"):
        self.op(eng, lambda e: e.dma_start(out=out, in_=in_), reads, writes, dma=True)

    def emit(self):
        nc = self.nc
        ops = self.ops
        cnt = {e: 0 for e in self.ENGS}
        dcnt = {e: 0 for e in self.ENGS}
        info = []
        last_w = {}
        readers = {}
        deps = []
        for i, (eng, fn, rd, wr, dma) in enumerate(ops):
            d = set()
            for k in rd:
                if k in last_w:
                    d.add(last_w[k])
            for k in wr:
                if k in last_w:
                    d.add(last_w[k])
                for r in readers.get(k, ()):
                    d.add(r)
            d.discard(i)
            deps.append(d)
            for k in rd:
                readers.setdefault(k, []).append(i)
            for k in wr:
                last_w[k] = i
                readers[k] = []
            if dma:
                info.append((eng, "d", dcnt[eng]))
                dcnt[eng] += 1
            else:
                info.append((eng, "c", cnt[eng]))
                cnt[eng] += 1
        with contextlib.ExitStack() as st:
            csem = {e: st.enter_context(nc.semaphore(f"{self.name}_c_{e}")) for e in self.ENGS}
            dsem = {e: [st.enter_context(nc.semaphore(f"{self.name}_d_{e}{j}")) for j in range(NDMASEM)]
                    for e in self.ENGS if dcnt[e] > 0}
            block = st.enter_context(nc.Block())
            per_eng = {e: [i for i, o in enumerate(ops) if o[0] == e] for e in self.ENGS}

            def make(eng_name):
                def body(eng):
                    waited = {}

                    def wait(key, sem, val):
                        if waited.get(key, 0) >= val:
                            return
                        waited[key] = val
                        eng.wait_ge(sem, val)

                    for i in per_eng[eng_name]:
                        _, fn, rd, wr, dma = ops[i]
                        for p in sorted(deps[i]):
                            pe_, pk, pidx = info[p]
                            if pk == "c":
                                if pe_ == "pe" and eng_name == "pe":
                                    continue
                                wait(("c", pe_), csem[pe_], pidx + 1)
                            else:
                                slot = pidx % NDMASEM
                                wait(("d", pe_, slot), dsem[pe_][slot], 16 * (pidx // NDMASEM + 1))
                        if dma:
                            didx = info[i][2]
                            slot = didx % NDMASEM
                            if didx >= NDMASEM:
                                wait(("d", eng_name, slot), dsem[eng_name][slot], 16 * (didx // NDMASEM))
                            fn(eng).then_inc(dsem[eng_name][slot], 16)
                        else:
                            fn(eng).then_inc(csem[eng_name], 1)
                    if eng_name in dsem:
                        n = dcnt[eng_name]
                        for slot in range(min(NDMASEM, n)):
                            total = (n - slot + NDMASEM - 1) // NDMASEM
                            wait(("d", eng_name, slot), dsem[eng_name][slot], 16 * total)
                return body

            if per_eng["pe"]:
                block.tensor(make("pe"))
            if per_eng["act"]:
                block.scalar(make("act"))
            if per_eng["dve"]:
                block.vector(make("dve"))
            if per_eng["pool"]:
                block.gpsimd(make("pool"))
            if per_eng["sp"]:
                block.sync(make("sp"))
        self.ops = []


KV_RANGES = [(1024, 2048), (4608, 4864), (5120, 5376), (2048, 3072), (4864, 5120), (5376, 5632),
             (4096, 4352), (4352, 4608)]
Q_RANGES = [(0, 1024), (3072, 4096), (5632, 5680), (5680, 7728)]
NKV = 3584
NQC = 4144


def build(S=4096, stage=99, dbg=False):
    NT = S // 128
    NCH = S // 512
    NSL = NCH // 2
    NOWN = NSL * 4
    SO = NSL * 512
    NCMP = (S - 32) // 16 + 1
    NCT = (NCMP + 127) // 128
    NCP = NCT * 128
    okind = "ExternalOutput" if dbg else "Internal"

    nc = bass.Bass("TRN2", target_bir_lowering=False)

    def din(name, shape, dt=F32):
        return nc.dram_tensor(name, list(shape), dt, kind="ExternalInput").ap()

    def dscr(name, shape, dt=BF16):
        return nc.dram_tensor(name, list(shape), dt, kind=okind).ap()

    xkv = din("xkv", [S, 1024])
    xq = din("xq", [SO, 1024])
    w_in = din("w_in", [1024, 7728])
    w_pd = din("w_pd", [1024, 1024])
    w_pn = din("w_pn", [1024, 1024])
    w_o = din("w_o", [1024, 1024])
    w_up = din("w_up", [1024, 4096])
    w_dn = din("w_dn", [4096, 1024])
    c_w1 = [din("c_w1k", [2048, 256]), din("c_w1v", [2048, 256])]
    c_w2 = [din("c_w2k", [256, 64]), din("c_w2v", [256, 64])]
    pos2 = [din("pos2k", [128, 16]), din("pos2v", [128, 16])]
    g_mix = din("g_mix", [1, 1024])
    g_mlp = din("g_mlp", [1, 1024])
    g_k24 = din("g_k24", [1, 1536])
    g_q32 = din("g_q32", [1, 2048])
    g_kc = din("g_kc", [1, 64])
    g_sub = din("g_sub", [1, 128])
    lam4 = din("lam4", [1, 256])
    rkv_c = din("rkv_c", [S, 32])
    rkv_s = din("rkv_s", [S, 32])
    rq_c = din("rq_c", [SO, 32])
    rq_s = din("rq_s", [SO, 32])
    rc_c = din("rc_c", [NCP, 32])
    rc_s = din("rc_s", [NCP, 32])
    identd = din("ident", [128, 128], BF16)
    eind = din("eind", [64, S], BF16)
    ovl = din("ovl", [NCP, 65], BF16)
    m_dense = din("m_dense", [128, 8, 512], BF16)
    m_win = din("m_win", [128, 12, 512], BF16)
    m_cmp = din("m_cmp", [128, NSL * NCT, 512], BF16)
    forced = din("forced", [SO, 64])
    out = nc.dram_tensor("out", [SO, 1024], F32, kind="ExternalOutput").ap()

    KdT = dscr("KdT", [8, 128, S])
    Vd = dscr("Vd", [S, 1024])
    KsT = dscr("KsT", [256, S])
    KwT = dscr("KwT", [256, S])
    kcT = dscr("kcT", [256, S])
    vcT = dscr("vcT", [256, S])
    Vs = dscr("Vs", [S, 256])
    Vw = dscr("Vw", [S, 256])
    QdT = dscr("QdT", [8, 128, SO])
    QnT = dscr("QnT", [1024, SO])
    BsT = dscr("BsT", [256, SO])
    mgS = dscr("mgS", [SO, 2048], F32)
    x1d = dscr("x1d", [SO, 1024], F32)

    with contextlib.ExitStack() as top:
        def sb(name, shape, dt, stack=top):
            return stack.enter_context(nc.sbuf_tensor(name, list(shape), dt))

        def ps(name, shape, dt, stack):
            return stack.enter_context(nc.psum_tensor(name, list(shape), dt))

        ident = sb("ident_sb", [128, 128], BF16)
        gates = sb("gates", [128, NOWN * 48], F32)
        KcT = sb("KcT", [128, 4 * NCP], BF16)
        Vca = sb("Vca", [128, NCT * 4 * 129], BF16)
        nlam = sb("nlam", [128, 1], F32)
        sgain = sb("sgain", [128, 128], F32)
        ss1 = sb("ss1", [128, 1], F32)
        rs1 = sb("rs1", [128, 1], F32)

        def bc(ap1, n):
            return ap1[0:1, :].broadcast_to([128, n])

        def rmsnorm_rows(P, xt, xkey, gt, hout, hkey, junk, sfx=""):
            P.op("pool", lambda e: e.memset(ss1[:], 0.0), writes=["ss1"])
            P.op("act", lambda e: e.activation(out=junk, in_=xt, func=AF.Square, accum_out=ss1[:]),
                 reads=[xkey, "ss1"], writes=["junk" + sfx, "ss1"])
            P.op("dve", lambda e: e.tensor_scalar(out=ss1[:], in0=ss1[:], scalar1=1.0 / 1024, scalar2=EPS,
                                                  op0=ALU.mult, op1=ALU.add), reads=["ss1"], writes=["ss1"])
            P.op("act", lambda e: e.activation(out=ss1[:], in_=ss1[:], func=AF.Ln), reads=["ss1"], writes=["ss1"])
            P.op("act", lambda e: e.activation(out=rs1[:], in_=ss1[:], func=AF.Exp, scale=-0.5),
                 reads=["ss1"], writes=["rs1"])
            P.op("dve", lambda e: e.scalar_tensor_tensor(out=hout, in0=xt, scalar=rs1[:], in1=gt,
                                                         op0=ALU.mult, op1=ALU.mult),
                 reads=[xkey, "rs1", "gt"], writes=[hkey])

        def normrope(P, src, srckey, U, gain, cos, sin, cskeys, outap, outkey, T, sfx):
            sq, ssu, rsu, xn, t1, t2, t3, t4 = T
            n = U * 64
            s3 = lambda ap: ap.rearrange("p (u d) -> p u d", d=64)
            P.op("act", lambda e: e.activation(out=sq[:, 0:n], in_=src, func=AF.Square),
                 reads=[srckey], writes=["sq" + sfx])
            P.op("dve", lambda e: e.tensor_reduce(out=ssu[:, 0:U], in_=s3(sq[:, 0:n]), axis=AX.X, op=ALU.add),
                 reads=["sq" + sfx], writes=["ssu" + sfx])
            P.op("dve", lambda e: e.tensor_scalar(out=ssu[:, 0:U], in0=ssu[:, 0:U], scalar1=1.0 / 64, scalar2=EPS,
                                                  op0=ALU.mult, op1=ALU.add), reads=["ssu" + sfx], writes=["ssu" + sfx])
            P.op("act", lambda e: e.activation(out=ssu[:, 0:U], in_=ssu[:, 0:U], func=AF.Ln),
                 reads=["ssu" + sfx], writes=["ssu" + sfx])
            P.op("act", lambda e: e.activation(out=rsu[:, 0:U], in_=ssu[:, 0:U], func=AF.Exp, scale=-0.5),
                 reads=["ssu" + sfx], writes=["rsu" + sfx])
            P.op("dve", lambda e: e.tensor_tensor(out=s3(xn[:, 0:n]), in0=s3(src),
                                                  in1=rsu[:, 0:U].unsqueeze(2).to_broadcast([128, U, 64]), op=ALU.mult),
                 reads=[srckey, "rsu" + sfx], writes=["xn" + sfx])
            P.op("pool", lambda e: e.tensor_tensor(out=xn[:, 0:n], in0=xn[:, 0:n], in1=gain, op=ALU.mult),
                 reads=["xn" + sfx, "gains"], writes=["xn" + sfx])
            x3 = s3(xn[:, 0:n])
            o3 = s3(outap)
            cb = cos.unsqueeze(1).to_broadcast([128, U, 32])
            sbb = sin.unsqueeze(1).to_broadcast([128, U, 32])
            h3 = lambda t: t[:, 0:U * 32].rearrange("p (u d) -> p u d", d=32)
            P.op("pool", lambda e: e.tensor_tensor(out=h3(t1), in0=x3[:, :, 0:32], in1=cb, op=ALU.mult),
                 reads=["xn" + sfx] + list(cskeys), writes=["t1" + sfx])
            P.op("pool", lambda e: e.tensor_tensor(out=h3(t2), in0=x3[:, :, 32:64], in1=sbb, op=ALU.mult),
                 reads=["xn" + sfx] + list(cskeys), writes=["t2" + sfx])
            P.op("pool", lambda e: e.tensor_tensor(out=o3[:, :, 0:32], in0=h3(t1), in1=h3(t2), op=ALU.subtract),
                 reads=["t1" + sfx, "t2" + sfx], writes=[outkey])
            P.op("dve", lambda e: e.tensor_tensor(out=h3(t3), in0=x3[:, :, 32:64], in1=cb, op=ALU.mult),
                 reads=["xn" + sfx] + list(cskeys), writes=["t3" + sfx])
            P.op("dve", lambda e: e.tensor_tensor(out=h3(t4), in0=x3[:, :, 0:32], in1=sbb, op=ALU.mult),
                 reads=["xn" + sfx] + list(cskeys), writes=["t4" + sfx])
            P.op("dve", lambda e: e.tensor_tensor(out=o3[:, :, 32:64], in0=h3(t3), in1=h3(t4), op=ALU.add),
                 reads=["t3" + sfx, "t4" + sfx], writes=[outkey])

        def load_w_cols(P, wdst, ranges, ncols):
            keys = []
            for c in range(8):
                off = 0
                kc_ = []
                for ri, (a, b) in enumerate(ranges):
                    k = ("wbuf", c, ri)
                    P.dma(wdst[:, c * ncols + off:c * ncols + off + (b - a)], w_in[c * 128:(c + 1) * 128, a:b],
                          writes=[k], eng="pool")
                    kc_.append(k)
                    off += b - a
                keys.append(kc_)
            return keys

        with contextlib.ExitStack() as sa:
            lt = sb("lt", [128, 256], F32, sa)
            pr = sb("pr", [128, 128], F32, sa)
            s2 = sb("s2", [128, 2], F32, sa)
            sgr = sb("sgr", [128, 128], F32, sa)
            P = Phase(nc, "pa")
            P.dma(ident[:], identd, writes=["ident"])
            P.dma(lt[:], bc(lam4, 256), writes=["lt"])
            P.dma(sgr[:], bc(g_sub, 128), writes=["sgr"])
            P.op("dve", lambda e: e.tensor_tensor(out=pr[:, 0:64], in0=lt[:, 0:64], in1=lt[:, 64:128], op=ALU.mult),
                 reads=["lt"], writes=["pr"])
            P.op("dve", lambda e: e.tensor_tensor(out=pr[:, 64:128], in0=lt[:, 128:192], in1=lt[:, 192:256], op=ALU.mult),
                 reads=["lt", "pr"], writes=["pr"])
            P.op("dve", lambda e: e.tensor_reduce(out=s2[:], in_=pr[:].rearrange("p (a d) -> p a d", d=64),
                                                  axis=AX.X, op=ALU.add), reads=["pr"], writes=["s2"])
            P.op("act", lambda e: e.activation(out=s2[:], in_=s2[:], func=AF.Exp), reads=["s2"], writes=["s2"])
            P.op("dve", lambda e: e.tensor_tensor(out=nlam[:], in0=s2[:, 1:2], in1=s2[:, 0:1], op=ALU.subtract),
                 reads=["s2"], writes=["nlam"])
            P.op("dve", lambda e: e.tensor_scalar(out=nlam[:], in0=nlam[:], scalar1=-0.2, scalar2=None, op0=ALU.add),
                 reads=["nlam"], writes=["nlam"])
            P.op("dve", lambda e: e.tensor_scalar(out=sgain[:], in0=sgr[:], scalar1=0.8, scalar2=None, op0=ALU.mult),
                 reads=["sgr"], writes=["sgain"])
            P.emit()

        with contextlib.ExitStack() as sbd:
            wbuf = sb("wbuf", [128, 8 * NQC], BF16, sbd)
            gmix = sb("gmix", [128, 1024], F32, sbd)
            gains = sb("gains", [128, 2048], F32, sbd)
            xts = [sb(f"xt{i}", [128, 1024], F32, sbd) for i in range(2)]
            junk = sb("junk", [128, 1024], F32, sbd)
            hb = sb("hb", [128, 1024], BF16, sbd)
            hT = [sb(f"hT{i}", [128, 1024], BF16, sbd) for i in range(2)]
            cst = [sb(f"cs{i}", [128, 64], F32, sbd) for i in range(2)]
            TT = []
            for i in range(2):
                TT.append((sb(f"sq{i}", [128, 512], F32, sbd), sb(f"ssu{i}", [128, 8], F32, sbd),
                           sb(f"rsu{i}", [128, 8], F32, sbd), sb(f"xn{i}", [128, 512], F32, sbd),
                           sb(f"t1{i}", [128, 256], F32, sbd), sb(f"t2{i}", [128, 256], F32, sbd),
                           sb(f"t3{i}", [128, 256], F32, sbd), sb(f"t4{i}", [128, 256], F32, sbd)))
            kb = sb("kb", [128, 2048], BF16, sbd)
            vb = sb("vb", [128, 2048], BF16, sbd)
            ktb = [sb(f"ktb{i}", [128, 16 * 128], BF16, sbd) for i in range(2)]
            mgt = [sb(f"mgt{i}", [128, 512], F32, sbd) for i in range(2)]
            pT = ps("pT", [128, 1024], BF16, sbd)
            pO = [ps(f"pO{i}", [128, 512], F32, sbd) for i in range(3)]
            pK = [ps(f"pK{i}", [128, 1024], BF16, sbd) for i in range(2)]

            def front_a(P, t, gt):
                xt = xts[t % 2]
                rmsnorm_rows(P, xt[:], ("xt", t % 2), gt[:], hb[:], "hb", junk[:])

            def front_b(P, t):
                for c in range(8):
                    P.op("pe", lambda e, c=c: e.transpose(out=pT[:, c * 128:(c + 1) * 128],
                                                          in_=hb[:, c * 128:(c + 1) * 128], identity=ident[:]),
                         reads=["hb", "ident"], writes=["pT"])
                P.op("dve", lambda e: e.tensor_copy(out=hT[t % 2][:], in_=pT[:]), reads=["pT"], writes=[("hT", t % 2)])

            def proj_group(P, t, j, ncols, col0, width, wkeys):
                po = pO[j % 3]
                for c in range(8):
                    P.op("pe", lambda e, c=c: e.matmul(po[:, 0:width], lhsT=hT[t % 2][:, c * 128:(c + 1) * 128],
                                                       rhs=wbuf[:, c * ncols + col0:c * ncols + col0 + width],
                                                       start=(c == 0), stop=(c == 7)),
                         reads=[("hT", t % 2)] + wkeys[c], writes=[("pO", j % 3)])
                return po

            P = Phase(nc, "pb")
            wk = load_w_cols(P, wbuf, KV_RANGES, NKV)
            P.dma(gmix[:], bc(g_mix, 1024), writes=["gt"])
            P.dma(gains[:, 0:1536], bc(g_k24, 1536), writes=["gains"])
            P.dma(xts[0][:], xkv[0:128, :], writes=[("xt", 0)])
            P.dma(xts[1][:], xkv[128:256, :], writes=[("xt", 1)])
            front_a(P, 0, gmix)
            front_b(P, 0)
            for t in range(NT):
                cs = cst[t % 2]
                P.dma(cs[:, 0:32], rkv_c[t * 128:(t + 1) * 128, :], writes=[("cs", t % 2, 0)])
                P.dma(cs[:, 32:64], rkv_s[t * 128:(t + 1) * 128, :], writes=[("cs", t % 2, 1)])
                if t + 1 < NT:
                    front_a(P, t + 1, gmix)
                for j in range(7):
                    po = proj_group(P, t, j, NKV, j * 512, 512, wk)
                    if j < 3:
                        normrope(P, po[:], ("pO", j % 3), 8, gains[:, j * 512:(j + 1) * 512], cs[:, 0:32], cs[:, 32:64],
                                 [("cs", t % 2, 0), ("cs", t % 2, 1)], kb[:, j * 512:(j + 1) * 512], ("kb", j), TT[j % 2], str(j % 2))
                    else:
                        P.op("act", lambda e, po=po, j=j: e.activation(out=vb[:, (j - 3) * 512:(j - 2) * 512], in_=po[:],
                                                                       func=AF.Identity),
                             reads=[("pO", j % 3)], writes=[("vb", j)])
                    if j == 2 and t + 1 < NT:
                        front_b(P, t + 1)
                        if t + 2 < NT:
                            P.dma(xts[t % 2][:], xkv[(t + 2) * 128:(t + 3) * 128, :], writes=[("xt", t % 2)])
                kt = ktb[t % 2]
                for blk in range(16):
                    src = kb[:, blk * 128:(blk + 1) * 128] if blk < 12 else vb[:, 1536 + (blk - 12) * 128:1536 + (blk - 11) * 128]
                    rk = [("kb", blk // 4)] if blk < 12 else [("vb", 6)]
                    P.op("pe", lambda e, src=src, blk=blk: e.transpose(out=pK[blk // 8][:, (blk % 8) * 128:(blk % 8 + 1) * 128],
                                                                        in_=src, identity=ident[:]),
                         reads=rk + ["ident"], writes=[("pK", blk // 8)])
                for hf in range(2):
                    P.op("dve", lambda e, hf=hf, kt=kt: e.tensor_copy(out=kt[:, hf * 1024:(hf + 1) * 1024], in_=pK[hf][:]),
                         reads=[("pK", hf)], writes=[("ktb", t % 2, hf)])
                tok = slice(t * 128, (t + 1) * 128)
                P.dma(KdT[:, :, tok].rearrange("h p k -> p h k"), kt[:, 0:1024].rearrange("p (h k) -> p h k", k=128),
                      reads=[("ktb", t % 2, 0)])
                for i, dst in enumerate((KsT, KwT, kcT, vcT)):
                    P.dma(dst[:, tok].rearrange("(i p) k -> p i k", p=128),
                          kt[:, 1024 + i * 256:1024 + (i + 1) * 256].rearrange("p (i k) -> p i k", k=128),
                          reads=[("ktb", t % 2, 1)])
                P.dma(Vd[tok, :], vb[:, 0:1024], reads=[("vb", 3), ("vb", 4)])
                P.dma(Vs[tok, :], vb[:, 1024:1280], reads=[("vb", 5)])
                P.dma(Vw[tok, :], vb[:, 1280:1536], reads=[("vb", 5)])
            P.emit()
            if stage <= 1:
                return nc

            P = Phase(nc, "pd")
            wk = load_w_cols(P, wbuf, Q_RANGES, NQC)
            P.dma(gains[:, 0:2048], bc(g_q32, 2048), writes=["gains"])
            P.dma(xts[0][:], xq[0:128, :], writes=[("xt", 0)])
            P.dma(xts[1][:], xq[128:256, :], writes=[("xt", 1)])
            front_a(P, 0, gmix)
            front_b(P, 0)
            for t in range(NOWN):
                cs = cst[t % 2]
                P.dma(cs[:, 0:32], rq_c[t * 128:(t + 1) * 128, :], writes=[("cs", t % 2, 0)])
                P.dma(cs[:, 32:64], rq_s[t * 128:(t + 1) * 128, :], writes=[("cs", t % 2, 1)])
                if t + 1 < NOWN:
                    front_a(P, t + 1, gmix)
                tok = slice(t * 128, (t + 1) * 128)
                for j in range(4):
                    po = proj_group(P, t, j, NQC, j * 512, 512, wk)
                    normrope(P, po[:], ("pO", j % 3), 8, gains[:, j * 512:(j + 1) * 512], cs[:, 0:32], cs[:, 32:64],
                             [("cs", t % 2, 0), ("cs", t % 2, 1)], kb[:, j * 512:(j + 1) * 512], ("kb", j), TT[j % 2], str(j % 2))
                    if j == 2 and t + 1 < NOWN:
                        front_b(P, t + 1)
                        if t + 2 < NOWN:
                            P.dma(xts[t % 2][:], xq[(t + 2) * 128:(t + 3) * 128, :], writes=[("xt", t % 2)])
                po = proj_group(P, t, 4, NQC, 2048, 48, wk)
                gsl = gates[:, t * 48:(t + 1) * 48]
                P.op("act", lambda e, po=po, gsl=gsl: e.activation(out=gsl, in_=po[:, 0:48], func=AF.Tanh, scale=0.5),
                     reads=[("pO", 1)], writes=["gates"])
                P.op("dve", lambda e, gsl=gsl: e.tensor_scalar(out=gsl, in0=gsl, scalar1=0.5, scalar2=0.5, op0=ALU.mult, op1=ALU.add),
                     reads=["gates"], writes=["gates"])
                for j in range(5, 9):
                    po = proj_group(P, t, j, NQC, 2096 + (j - 5) * 512, 512, wk)
                    mg_ = mgt[j % 2]
                    P.op("act", lambda e, po=po, mg_=mg_: e.activation(out=mg_[:], in_=po[:], func=AF.Tanh, scale=0.5),
                         reads=[("pO", j % 3)], writes=[("mgt", j % 2)])
                    P.op("pool" if j % 2 else "dve", lambda e, mg_=mg_: e.tensor_scalar(out=mg_[:], in0=mg_[:], scalar1=0.5, scalar2=0.5,
                                                                                       op0=ALU.mult, op1=ALU.add),
                         reads=[("mgt", j % 2)], writes=[("mgt", j % 2)])
                    P.dma(mgS[tok, (j - 5) * 512:(j - 4) * 512], mg_[:], reads=[("mgt", j % 2)])
                kt = ktb[t % 2]
                for blk in range(16):
                    P.op("pe", lambda e, blk=blk: e.transpose(out=pK[blk // 8][:, (blk % 8) * 128:(blk % 8 + 1) * 128],
                                                              in_=kb[:, blk * 128:(blk + 1) * 128], identity=ident[:]),
                         reads=[("kb", blk // 4), "ident"], writes=[("pK", blk // 8)])
                for hf in range(2):
                    P.op("dve", lambda e, hf=hf, kt=kt: e.tensor_copy(out=kt[:, hf * 1024:(hf + 1) * 1024], in_=pK[hf][:]),
                         reads=[("pK", hf)], writes=[("ktb", t % 2, hf)])
                P.dma(QdT[:, :, tok].rearrange("h p k -> p h k"), kt[:, 0:1024].rearrange("p (h k) -> p h k", k=128),
                      reads=[("ktb", t % 2, 0)])
                P.dma(QnT[:, tok].rearrange("(i p) k -> p i k", p=128), kt[:, 1024:2048].rearrange("p (i k) -> p i k", k=128),
                      reads=[("ktb", t % 2, 1)])
            P.emit()
        if stage <= 2:
            return nc

        ydiff = sb("ydiff", [128, NOWN * 1024], BF16)
        ynsa = sb("ynsa", [128, NOWN * 1024], BF16)
        with contextlib.ExitStack() as sc:
            W1s = [sb(f"W1s{i}", [128, 16 * 256], BF16, sc) for i in range(2)]
            W2s = [sb(f"W2s{i}", [128, 128], BF16, sc) for i in range(2)]
            p2b = [sb(f"p2b{i}", [128, 16], BF16, sc) for i in range(2)]
            kc2 = [sb(f"kc2_{i}", [128, S], BF16, sc) for i in range(2)]
            biasv = [sb(f"biasv{i}", [128, 2], F32, sc) for i in range(2)]
            h1T = sb("h1T", [128, 2 * NCP], BF16, sc)
            xg = sb("xg", [128, NCP], F32, sc)
            x2 = sb("x2", [128, NCP], F32, sc)
            th = sb("th", [128, NCP], F32, sc)
            kcn = sb("kcn", [128, 128], BF16, sc)
            gkc = sb("gkc", [128, 64], F32, sc)
            csc = sb("csc", [128, NCT * 64], F32, sc)
            TC = (sb("c_sq", [128, 64], F32, sc), sb("c_ssu", [128, 8], F32, sc), sb("c_rsu", [128, 8], F32, sc),
                  sb("c_xn", [128, 64], F32, sc), sb("c_t1", [128, 32], F32, sc), sb("c_t2", [128, 32], F32, sc),
                  sb("c_t3", [128, 32], F32, sc), sb("c_t4", [128, 32], F32, sc))
            pH = [ps(f"pH{i}", [128, 512], F32, sc) for i in range(2)]
            pB = ps("pB", [128, 512], F32, sc)
            pC = ps("pC", [128, 512], F32, sc)
            pKc = ps("pKc", [128, 1024], BF16, sc)
            P = Phase(nc, "pc")
            P.op("pool", lambda e: e.memset(h1T[:], 0.0), writes=["h1T"])
            P.op("pool", lambda e: e.memset(kcn[:], 0.0), writes=["kcn"])
            P.dma(gkc[:], bc(g_kc, 64), writes=["gains"])
            for nt in range(NCT):
                P.dma(csc[:, nt * 64:nt * 64 + 32], rc_c[nt * 128:(nt + 1) * 128, :], writes=[("csc", nt, 0)])
                P.dma(csc[:, nt * 64 + 32:nt * 64 + 64], rc_s[nt * 128:(nt + 1) * 128, :], writes=[("csc", nt, 1)])
                for g in range(4):
                    o0 = (nt * 4 + g) * 129
                    P.dma(Vca[:, o0 + 64:o0 + 129], ovl[nt * 128:(nt + 1) * 128, :], writes=[("vca_c", nt, g)])
            for kv in range(2):
                P.dma(W1s[kv][:].rearrange("p (u h) -> p u h", h=256), c_w1[kv].rearrange("(u p) h -> p u h", p=128),
                      writes=[("W1s", kv)], eng="pool")
                P.dma(W2s[kv][:].rearrange("p (a d) -> p a d", d=64), c_w2[kv].rearrange("(a p) d -> p a d", p=128),
                      writes=[("W2s", kv)], eng="pool")
                P.dma(p2b[kv][:], pos2[kv], writes=[("p2b", kv)], eng="pool")
                for half in range(2):
                    for u in range(16):
                        P.op("pe", lambda e, kv=kv, half=half, u=u: e.matmul(
                            pB[:, half:half + 1], lhsT=W1s[kv][:, u * 256 + half * 128:u * 256 + half * 128 + 128],
                            rhs=p2b[kv][:, u:u + 1], start=(u == 0), stop=(u == 15)),
                            reads=[("W1s", kv), ("p2b", kv)], writes=["pB"])
                P.op("dve", lambda e, kv=kv: e.tensor_copy(out=biasv[kv][:], in_=pB[:, 0:2]), reads=["pB"], writes=[("biasv", kv)])
                src = kcT if kv == 0 else vcT
                for g in range(4):
                    kb2 = kc2[g % 2]
                    kk = ("kc2", g % 2)
                    P.dma(kb2[0:64, :], src[g * 64:(g + 1) * 64, :], writes=[(kk, 0)])
                    P.dma(kb2[64:128, 0:S - 1], src[g * 64:(g + 1) * 64, 1:S], writes=[(kk, 1)])
                    for half in range(2):
                        for u in range(16):
                            rhs = bass.AP(kb2, 2 * u, [[S, 128], [16, NCMP]])
                            P.op("pe", lambda e, kv=kv, half=half, u=u, rhs=rhs: e.matmul(
                                pH[half][:, 0:NCMP], lhsT=W1s[kv][:, u * 256 + half * 128:u * 256 + half * 128 + 128],
                                rhs=rhs, start=(u == 0), stop=(u == 15)),
                                reads=[("W1s", kv), (kk, 0), (kk, 1)], writes=[("pH", half)])
                        hsl = h1T[:, half * NCP:half * NCP + NCMP]
                        P.op("dve", lambda e, kv=kv, half=half: e.tensor_scalar(
                            out=xg[:, 0:NCMP], in0=pH[half][:, 0:NCMP], scalar1=biasv[kv][:, half:half + 1], scalar2=None,
                            op0=ALU.add), reads=[("pH", half), ("biasv", kv)], writes=["xg"])
                        P.op("pool", lambda e: e.tensor_tensor(out=x2[:, 0:NCMP], in0=xg[:, 0:NCMP], in1=xg[:, 0:NCMP],
                                                               op=ALU.mult), reads=["xg"], writes=["x2"])
                        P.op("pool", lambda e: e.tensor_scalar(out=x2[:, 0:NCMP], in0=x2[:, 0:NCMP], scalar1=0.044715,
                                                               scalar2=1.0, op0=ALU.mult, op1=ALU.add),
                             reads=["x2"], writes=["x2"])
                        P.op("pool", lambda e: e.tensor_tensor(out=x2[:, 0:NCMP], in0=x2[:, 0:NCMP], in1=xg[:, 0:NCMP],
                                                               op=ALU.mult), reads=["x2", "xg"], writes=["x2"])
                        P.op("act", lambda e: e.activation(out=th[:, 0:NCMP], in_=x2[:, 0:NCMP], func=AF.Tanh,
                                                           scale=0.7978845608028654), reads=["x2"], writes=["th"])
                        P.op("dve", lambda e: e.tensor_scalar(out=th[:, 0:NCMP], in0=th[:, 0:NCMP], scalar1=0.5, scalar2=0.5,
                                                              op0=ALU.mult, op1=ALU.add), reads=["th"], writes=["th"])
                        P.op("dve", lambda e, hsl=hsl: e.tensor_tensor(out=hsl, in0=th[:, 0:NCMP], in1=xg[:, 0:NCMP],
                                                                        op=ALU.mult), reads=["th", "xg"], writes=["h1T"])
                    for nt in range(NCT):
                        for half in range(2):
                            P.op("pe", lambda e, kv=kv, nt=nt, half=half: e.matmul(
                                pC[:, 0:64], lhsT=h1T[:, half * NCP + nt * 128:half * NCP + nt * 128 + 128],
                                rhs=W2s[kv][:, half * 64:(half + 1) * 64], start=(half == 0), stop=(half == 1)),
                                reads=["h1T", ("W2s", kv)], writes=["pC"])
                        if kv == 0:
                            normrope(P, pC[:, 0:64], "pC", 1, gkc[:], csc[:, nt * 64:nt * 64 + 32],
                                     csc[:, nt * 64 + 32:nt * 64 + 64], [("csc", nt, 0), ("csc", nt, 1)],
                                     kcn[:, 0:64], "kcn", TC, "c")
                            P.op("pe", lambda e: e.transpose(out=pKc[:, 0:128], in_=kcn[:], identity=ident[:]),
                                 reads=["kcn", "ident"], writes=["pKc"])
                            P.op("dve", lambda e, g=g, nt=nt: e.tensor_copy(
                                out=KcT[0:64, g * NCP + nt * 128:g * NCP + nt * 128 + 128], in_=pKc[0:64, 0:128]),
                                reads=["pKc"], writes=["KcT"])
                        else:
                            o0 = (nt * 4 + g) * 129
                            P.op("act", lambda e, o0=o0: e.activation(out=Vca[:, o0:o0 + 64], in_=pC[:, 0:64], func=AF.Identity),
                                 reads=["pC"], writes=["Vca"])
            if dbg:
                dK = nc.dram_tensor("dbg_KcT", [64, 4 * NCP], BF16, kind="ExternalOutput").ap()
                dV = nc.dram_tensor("dbg_Vca", [128, NCT * 4 * 129], BF16, kind="ExternalOutput").ap()
                P.dma(dK, KcT[0:64, :], reads=["KcT"])
                P.dma(dV, Vca[:], reads=["Vca"] + [("vca_c", nt, g) for nt in range(NCT) for g in range(4)])
            P.emit()
        if stage <= 3:
            return nc

        with contextlib.ExitStack() as se:
            Qt = [sb(f"e_Qt{i}", [128, SO], BF16, se) for i in range(2)]
            mcmp = sb("e_mcmp", [128, NSL * NCT * 512], BF16, se)
            ET = [sb(f"e_ET{i}", [128, 512], BF16, se) for i in range(2)]
            imp = sb("e_imp", [128, NOWN * 64], F32, se)
            frc = sb("e_frc", [128, NOWN * 64], F32, se)
            rl = sb("e_rl", [128, 4], F32, se)
            impf = sb("e_impf", [128, 64], F32, se)
            tmpf = sb("e_tmpf", [128, 64], F32, se)
            m8 = sb("e_m8", [128, 8], F32, se)
            m8b = sb("e_m8b", [128, 8], F32, se)
            bt = sb("e_bt", [128, 128], BF16, se)
            bstage = sb("e_bstage", [128, SO], BF16, se)
            pS = [ps(f"e_pS{i}", [128, 512], F32, se) for i in range(2)]
            pU = [ps(f"e_pU{i}", [128, 512], F32, se) for i in range(2)]
            pBt = ps("e_pBt", [128, 1024], BF16, se)
            P = Phase(nc, "pe")
            P.dma(mcmp[:].rearrange("p (a q) -> p a q", q=512), m_cmp, writes=["mcmp"])
            P.dma(frc[:].rearrange("p (t j) -> p t j", j=64), forced.rearrange("(t p) j -> p t j", p=128), writes=["frc"])
            P.op("pool", lambda e: e.memset(bt[:], 0.0), writes=["bt"])
            it = 0
            for g in range(4):
                P.op("pool", lambda e: e.memset(imp[:], 0.0), writes=["imp"])
                for h in range(4):
                    hq = 4 * g + h
                    qt = Qt[hq % 2]
                    P.dma(qt[0:64, :], QnT[hq * 64:(hq + 1) * 64, :], writes=[("Qt", hq % 2)])
                    for s_ in range(NSL):
                        for nt in range(NCT):
                            i = it % 2
                            it += 1
                            P.op("pe", lambda e, i=i, g=g, nt=nt, s_=s_, qt=qt: e.matmul(
                                pS[i][:], lhsT=KcT[0:64, g * NCP + nt * 128:g * NCP + nt * 128 + 128],
                                rhs=qt[0:64, s_ * 512:(s_ + 1) * 512], start=True, stop=False),
                                reads=["KcT", ("Qt", hq % 2)], writes=[("pS", i)])
                            P.op("pe", lambda e, i=i, nt=nt, s_=s_: e.matmul(
                                pS[i][:], lhsT=ident[:], rhs=mcmp[:, (s_ * NCT + nt) * 512:(s_ * NCT + nt + 1) * 512],
                                start=False, stop=True), reads=["ident", "mcmp"], writes=[("pS", i)])
                            P.op("act", lambda e, i=i: e.activation(out=ET[i][:], in_=pS[i][:], func=AF.Exp, scale=0.125),
                                 reads=[("pS", i)], writes=[("ET", i)])
                            for qs in range(4):
                                o0 = (nt * 4 + g) * 129
                                P.op("pe", lambda e, i=i, qs=qs, o0=o0, nt=nt: e.matmul(
                                    pU[qs // 2][:, (qs % 2) * 129:(qs % 2) * 129 + 129], lhsT=ET[i][:, qs * 128:(qs + 1) * 128],
                                    rhs=Vca[:, o0:o0 + 129], start=(nt == 0 and qs % 2 == 0), stop=(nt == NCT - 1),
                                    skip_group_check=True),
                                    reads=[("ET", i), "Vca"], writes=[("pU", qs // 2)])
                        for qs in range(4):
                            tl = s_ * 4 + qs
                            u0 = (qs % 2) * 129
                            pu = pU[qs // 2]
                            pk = ("pU", qs // 2)
                            P.op("dve", lambda e, pu=pu, u0=u0, qs=qs: e.tensor_scalar(
                                out=rl[:, qs:qs + 1], in0=pu[:, u0 + 128:u0 + 129], scalar1=1e-30, scalar2=None, op0=ALU.max),
                                reads=[pk], writes=["rl"])
                            P.op("dve", lambda e, qs=qs: e.reciprocal(out=rl[:, qs:qs + 1], in_=rl[:, qs:qs + 1]),
                                 reads=["rl"], writes=["rl"])
                            ysl = ynsa[:, tl * 1024 + hq * 64:tl * 1024 + hq * 64 + 64]
                            gcol = gates[:, tl * 48 + hq * 3:tl * 48 + hq * 3 + 1]
                            P.op("dve", lambda e, pu=pu, u0=u0, qs=qs, ysl=ysl, gcol=gcol: e.tensor_scalar(
                                out=ysl, in0=pu[:, u0:u0 + 64], scalar1=rl[:, qs:qs + 1], scalar2=gcol,
                                op0=ALU.mult, op1=ALU.mult), reads=[pk, "rl", "gates"], writes=[("ynsa", tl)])
                            isl = imp[:, tl * 64:(tl + 1) * 64]
                            P.op("dve", lambda e, pu=pu, u0=u0, qs=qs, isl=isl: e.scalar_tensor_tensor(
                                out=isl, in0=pu[:, u0 + 64:u0 + 128], scalar=rl[:, qs:qs + 1], in1=isl,
                                op0=ALU.mult, op1=ALU.add), reads=[pk, "rl", "imp"], writes=["imp"])
                for tl in range(NOWN):
                    isl = imp[:, tl * 64:(tl + 1) * 64]
                    P.op("dve", lambda e, isl=isl, tl=tl: e.tensor_tensor(out=impf[:], in0=isl, in1=frc[:, tl * 64:(tl + 1) * 64],
                                                                          op=ALU.max), reads=["imp", "frc"], writes=["impf"])
                    P.op("dve", lambda e: e.max(out=m8[:], in_=impf[:]), reads=["impf"], writes=["m8"])
                    P.op("dve", lambda e: e.match_replace(out=tmpf[:], in_to_replace=m8[:], in_values=impf[:], imm_value=-1e9),
                         reads=["m8", "impf"], writes=["tmpf"])
                    P.op("dve", lambda e: e.max(out=m8b[:], in_=tmpf[:]), reads=["tmpf"], writes=["m8b"])
                    P.op("dve", lambda e: e.tensor_scalar(out=tmpf[:], in0=impf[:], scalar1=m8b[:, 7:8], scalar2=None,
                                                          op0=ALU.is_ge), reads=["impf", "m8b", "tmpf"], writes=["tmpf"])
                    P.op("dve", lambda e: e.tensor_scalar(out=bt[:, 64:128], in0=tmpf[:], scalar1=-1.0, scalar2=-NEG,
                                                          op0=ALU.add, op1=ALU.mult), reads=["tmpf"], writes=["bt"])
                    P.op("pe", lambda e: e.transpose(out=pBt[:, 0:128], in_=bt[:], identity=ident[:]),
                         reads=["bt", "ident"], writes=["pBt"])
                    P.op("dve", lambda e, tl=tl: e.tensor_copy(out=bstage[64:128, tl * 128:(tl + 1) * 128], in_=pBt[64:128, 0:128]),
                         reads=["pBt"], writes=["bstage"])
                P.dma(BsT[g * 64:(g + 1) * 64, :], bstage[64:128, :], reads=["bstage"])
            P.emit()
        if stage <= 4:
            return nc

        with contextlib.ExitStack() as sf:
            KD = [sb(f"f_KD{i}", [128, S], BF16, sf) for i in range(2)]
            VD = [sb(f"f_VD{i}", [128, NT * 129], BF16, sf) for i in range(2)]
            QD = [sb(f"f_QD{i}", [128, SO], BF16, sf) for i in range(2)]
            KS = sb("f_KS", [128, S], BF16, sf)
            KW = sb("f_KW", [128, S], BF16, sf)
            VS = sb("f_VS", [128, NT * 65], BF16, sf)
            VW = sb("f_VW", [128, NT * 65], BF16, sf)
            QS = [sb(f"f_QS{i}", [128, SO], BF16, sf) for i in range(2)]
            mden = sb("f_mden", [128, 8 * 512], BF16, sf)
            mwin = sb("f_mwin", [128, 12 * 512], BF16, sf)
            PT = [sb(f"f_PT{i}", [128, 512], BF16, sf) for i in range(4)]
            oa = sb("f_oa", [128, 128], F32, sf)
            ob = sb("f_ob", [128, 128], F32, sf)
            jk = sb("f_jk", [128, 128], F32, sf)
            r4 = sb("f_r4", [128, 4], F32, sf)
            ssd = sb("f_ssd", [128, 1], F32, sf)
            rsd = sb("f_rsd", [128, 1], F32, sf)
            tn = sb("f_tn", [128, 64], F32, sf)
            pS = [ps(f"f_pS{i}", [128, 512], F32, sf) for i in range(4)]
            pO = [ps(f"f_pO{i}", [128, 512], F32, sf) for i in range(3)]
            P = Phase(nc, "pf")
            P.dma(mden[:].rearrange("p (a q) -> p a q", q=512), m_dense, writes=["mden"])
            P.dma(mwin[:].rearrange("p (a q) -> p a q", q=512), m_win, writes=["mwin"])
            P.dma(KS[64:128, :], eind, writes=["KS_e"])
            for i in range(2):
                P.op("pool", lambda e, i=i: e.memset(VD[i][:], 1.0), writes=[("VD", i)])
            P.op("pool", lambda e: e.memset(VS[:], 1.0), writes=["VS"])
            P.op("pool", lambda e: e.memset(VW[:], 1.0), writes=["VW"])

            def load_v(dst, dkey, src2d, width, stride):
                d3 = dst[:].rearrange("p (t c) -> p t c", c=stride)
                s3 = src2d.rearrange("(t p) c -> p t c", p=128)
                step = 8
                for a in range(0, NT, step):
                    b_ = min(NT, a + step)
                    P.dma(d3[:, a:b_, 0:width], s3[:, a:b_, :], writes=[dkey])

            def load_diff(h):
                i = h % 2
                P.dma(KD[i][:], KdT[h], writes=[("KD", i)])
                load_v(VD[i], ("VD", i), Vd[:, h * 128:(h + 1) * 128], 128, 129)
                P.dma(QD[i][:], QdT[h], writes=[("QD", i)])

            load_diff(0)
            for h in range(8):
                if h + 1 < 8:
                    load_diff(h + 1)
                bi = h % 2
                for s_ in range(NSL):
                    nkt = 8 * (s_ + 1)

                    def qk(kt, s_=s_, nkt=nkt, bi=bi):
                        i2 = kt % 2
                        masked = kt >= nkt - 8
                        mi = kt - (nkt - 8)
                        for m in range(2):
                            bk = 2 * i2 + m
                            P.op("pe", lambda e, bk=bk, m=m, kt=kt: e.matmul(
                                pS[bk][:], lhsT=KD[bi][m * 64:(m + 1) * 64, kt * 128:(kt + 1) * 128],
                                rhs=QD[bi][m * 64:(m + 1) * 64, s_ * 512:(s_ + 1) * 512], start=True, stop=True),
                                reads=[("KD", bi), ("QD", bi)], writes=[("pS", bk)])
                            P.op("act", lambda e, bk=bk: e.activation(out=PT[bk][:], in_=pS[bk][:], func=AF.Exp, scale=0.125),
                                 reads=[("pS", bk)], writes=[("PT", bk)])
                            if masked:
                                P.op("dve", lambda e, bk=bk, mi=mi: e.tensor_tensor(
                                    out=PT[bk][:], in0=PT[bk][:], in1=mden[:, mi * 512:(mi + 1) * 512], op=ALU.mult),
                                    reads=[("PT", bk), "mden"], writes=[("PT", bk)])

                    def pv(kt, s_=s_, nkt=nkt, bi=bi):
                        i2 = kt % 2
                        for m in range(2):
                            bk = 2 * i2 + m
                            for qs in range(4):
                                a = m * 4 + qs
                                P.op("pe", lambda e, bk=bk, qs=qs, a=a, kt=kt: e.matmul(
                                    pO[a // 3][:, (a % 3) * 129:(a % 3) * 129 + 129], lhsT=PT[bk][:, qs * 128:(qs + 1) * 128],
                                    rhs=VD[bi][:, kt * 129:kt * 129 + 129], start=(kt == 0 and a % 3 == 0), stop=(kt == nkt - 1),
                                    skip_group_check=True),
                                    reads=[("PT", bk), ("VD", bi)], writes=[("pO", a // 3)])

                    qk(0)
                    for kt in range(nkt):
                        if kt + 1 < nkt:
                            qk(kt + 1)
                        pv(kt)
                    for qs in range(4):
                        tl = s_ * 4 + qs
                        a0, a1 = qs, 4 + qs
                        o0 = pO[a0 // 3][:, (a0 % 3) * 129:(a0 % 3) * 129 + 129]
                        o1 = pO[a1 // 3][:, (a1 % 3) * 129:(a1 % 3) * 129 + 129]
                        k0, k1 = ("pO", a0 // 3), ("pO", a1 // 3)
                        P.op("dve", lambda e, o0=o0: e.reciprocal(out=r4[:, 0:1], in_=o0[:, 128:129]), reads=[k0], writes=["r4"])
                        P.op("dve", lambda e, o1=o1: e.reciprocal(out=r4[:, 1:2], in_=o1[:, 128:129]), reads=[k1, "r4"], writes=["r4"])
                        P.op("dve", lambda e: e.tensor_tensor(out=r4[:, 2:3], in0=r4[:, 1:2], in1=nlam[:], op=ALU.mult),
                             reads=["r4", "nlam"], writes=["r4"])
                        P.op("dve", lambda e, o0=o0: e.tensor_scalar(out=oa[:], in0=o0[:, 0:128], scalar1=r4[:, 0:1], scalar2=None,
                                                                    op0=ALU.mult), reads=[k0, "r4"], writes=["oa"])
                        P.op("dve", lambda e, o1=o1: e.scalar_tensor_tensor(out=ob[:], in0=o1[:, 0:128], scalar=r4[:, 2:3], in1=oa[:],
                                                                           op0=ALU.mult, op1=ALU.add),
                             reads=[k1, "r4", "oa"], writes=["ob"])
                        P.op("pool", lambda e: e.memset(ssd[:], 0.0), writes=["ssd"])
                        P.op("act", lambda e: e.activation(out=jk[:], in_=ob[:], func=AF.Square, accum_out=ssd[:]),
                             reads=["ob", "ssd"], writes=["jk", "ssd"])
                        P.op("dve", lambda e: e.tensor_scalar(out=ssd[:], in0=ssd[:], scalar1=1.0 / 128, scalar2=EPS,
                                                              op0=ALU.mult, op1=ALU.add), reads=["ssd"], writes=["ssd"])
                        P.op("act", lambda e: e.activation(out=ssd[:], in_=ssd[:], func=AF.Ln), reads=["ssd"], writes=["ssd"])
                        P.op("act", lambda e: e.activation(out=rsd[:], in_=ssd[:], func=AF.Exp, scale=-0.5),
                             reads=["ssd"], writes=["rsd"])
                        ysl = ydiff[:, tl * 1024 + h * 128:tl * 1024 + (h + 1) * 128]
                        P.op("dve", lambda e, ysl=ysl: e.scalar_tensor_tensor(out=ysl, in0=ob[:], scalar=rsd[:], in1=sgain[:],
                                                                              op0=ALU.mult, op1=ALU.mult),
                             reads=["ob", "rsd", "sgain"], writes=[("ydiff", tl)])

            cnt = [0]
            for g in range(4):
                P.dma(KS[0:64, :], KsT[g * 64:(g + 1) * 64, :], writes=["KS"])
                P.dma(KW[0:64, :], KwT[g * 64:(g + 1) * 64, :], writes=["KW"])
                load_v(VS, "VS", Vs[:, g * 64:(g + 1) * 64], 64, 65)
                load_v(VW, "VW", Vw[:, g * 64:(g + 1) * 64], 64, 65)
                for h in range(4):
                    hq = 4 * g + h
                    qi = hq % 2
                    P.dma(QS[qi][0:64, :], QnT[hq * 64:(hq + 1) * 64, :], writes=[("QS", qi, 0)])
                    P.dma(QS[qi][64:128, :], BsT[g * 64:(g + 1) * 64, :], writes=[("QS", qi, 1)])
                    for kind in (2, 1):
                        for s_ in range(NSL):
                            if kind == 1:
                                kts = list(range(0, 8 * (s_ + 1)))
                                mis = [kt - 8 * s_ if kt >= 8 * s_ else None for kt in kts]
                            else:
                                kts = list(range(max(0, 8 * s_ - 4), 8 * s_ + 8))
                                mis = [kt - (8 * s_ - 4) for kt in kts]
                            banks = []
                            for _ in kts:
                                banks.append(cnt[0] % 4)
                                cnt[0] += 1

                            def qk(j, kind=kind, s_=s_, kts=kts, mis=mis, banks=banks, qi=qi):
                                kt, mi, bk = kts[j], mis[j], banks[j]
                                if kind == 1:
                                    lhsT = KS[:, kt * 128:(kt + 1) * 128]
                                    rhs = QS[qi][:, s_ * 512:(s_ + 1) * 512]
                                    rd = ["KS", "KS_e", ("QS", qi, 0), ("QS", qi, 1)]
                                    mt = mden
                                else:
                                    lhsT = KW[0:64, kt * 128:(kt + 1) * 128]
                                    rhs = QS[qi][0:64, s_ * 512:(s_ + 1) * 512]
                                    rd = ["KW", ("QS", qi, 0)]
                                    mt = mwin
                                P.op("pe", lambda e: e.matmul(pS[bk][:], lhsT=lhsT, rhs=rhs, start=True, stop=True),
                                     reads=rd, writes=[("pS", bk)])
                                P.op("act", lambda e: e.activation(out=PT[bk][:], in_=pS[bk][:], func=AF.Exp, scale=0.125),
                                     reads=[("pS", bk)], writes=[("PT", bk)])
                                if mi is not None:
                                    P.op("dve", lambda e: e.tensor_tensor(out=PT[bk][:], in0=PT[bk][:], in1=mt[:, mi * 512:(mi + 1) * 512],
                                                                          op=ALU.mult),
                                         reads=[("PT", bk), "mden", "mwin"], writes=[("PT", bk)])

                            def pv(j, kind=kind, kts=kts, banks=banks):
                                kt, bk = kts[j], banks[j]
                                vt, vk = (VS, "VS") if kind == 1 else (VW, "VW")
                                for qs in range(4):
                                    P.op("pe", lambda e, qs=qs: e.matmul(
                                        pO[0][:, qs * 65:qs * 65 + 65], lhsT=PT[bk][:, qs * 128:(qs + 1) * 128],
                                        rhs=vt[:, kt * 65:kt * 65 + 65], start=(j == 0 and qs == 0), stop=(j == len(kts) - 1),
                                        skip_group_check=True),
                                        reads=[("PT", bk), vk], writes=[("pO", 0)])

                            qk(0)
                            for j in range(len(kts)):
                                if j + 1 < len(kts):
                                    qk(j + 1)
                                pv(j)
                            for qs in range(4):
                                tl = s_ * 4 + qs
                                oq = pO[0][:, qs * 65:qs * 65 + 65]
                                P.op("dve", lambda e, oq=oq, qs=qs: e.tensor_scalar(
                                    out=r4[:, qs:qs + 1], in0=oq[:, 64:65], scalar1=1e-30, scalar2=None, op0=ALU.max),
                                    reads=[("pO", 0)], writes=["r4"])
                                P.op("dve", lambda e, qs=qs: e.reciprocal(out=r4[:, qs:qs + 1], in_=r4[:, qs:qs + 1]),
                                     reads=["r4"], writes=["r4"])
                                gcol = gates[:, tl * 48 + hq * 3 + kind:tl * 48 + hq * 3 + kind + 1]
                                P.op("dve", lambda e, oq=oq, qs=qs, gcol=gcol: e.tensor_scalar(
                                    out=tn[:], in0=oq[:, 0:64], scalar1=r4[:, qs:qs + 1], scalar2=gcol,
                                    op0=ALU.mult, op1=ALU.mult), reads=[("pO", 0), "r4", "gates"], writes=["tn"])
                                ysl = ynsa[:, tl * 1024 + hq * 64:tl * 1024 + hq * 64 + 64]
                                P.op("pool", lambda e, ysl=ysl: e.tensor_tensor(out=ysl, in0=ysl, in1=tn[:], op=ALU.add),
                                     reads=["tn", ("ynsa", tl)], writes=[("ynsa", tl)])
            if dbg:
                dY = nc.dram_tensor("dbg_ydiff", [128, NOWN * 1024], BF16, kind="ExternalOutput").ap()
                dN = nc.dram_tensor("dbg_ynsa", [128, NOWN * 1024], BF16, kind="ExternalOutput").ap()
                P.dma(dY, ydiff[:], reads=[("ydiff", t) for t in range(NOWN)])
                P.dma(dN, ynsa[:], reads=[("ynsa", t) for t in range(NOWN)])
            P.emit()
        if stage <= 5:
            return nc

        with contextlib.ExitStack() as sg:
            Wpd = sb("g_Wpd", [128, 8 * 1024], BF16, sg)
            Wpn = sb("g_Wpn", [128, 8 * 1024], BF16, sg)
            Wo = sb("g_Wo", [128, 8 * 1024], BF16, sg)
            ydT = sb("g_ydT", [128, 1024], BF16, sg)
            ynT = sb("g_ynT", [128, 1024], BF16, sg)
            mgl = [sb(f"g_mgl{i}", [128, 2048], F32, sg) for i in range(2)]
            xql = [sb(f"g_xql{i}", [128, 1024], F32, sg) for i in range(2)]
            m1 = sb("g_m1", [128, 1024], F32, sg)
            m2 = sb("g_m2", [128, 1024], F32, sg)
            mixb = sb("g_mixb", [128, 1024], BF16, sg)
            mixT = sb("g_mixT", [128, 1024], BF16, sg)
            x1t = [sb(f"g_x1t{i}", [128, 1024], F32, sg) for i in range(2)]
            pTa = ps("g_pTa", [128, 1024], BF16, sg)
            pTb = ps("g_pTb", [128, 1024], BF16, sg)
            pD = [ps(f"g_pD{i}", [128, 512], F32, sg) for i in range(2)]
            pN = [ps(f"g_pN{i}", [128, 512], F32, sg) for i in range(2)]
            pW = [ps(f"g_pW{i}", [128, 512], F32, sg) for i in range(2)]
            P = Phase(nc, "pg")
            for (wd, wsrc, wk_) in ((Wpd, w_pd, "Wpd"), (Wpn, w_pn, "Wpn"), (Wo, w_o, "Wo")):
                for hc in range(2):
                    P.dma(wd[:, hc * 4096:(hc + 1) * 4096].rearrange("p (c n) -> p c n", n=1024),
                          wsrc[hc * 512:(hc + 1) * 512, :].rearrange("(c p) n -> p c n", p=128), writes=[(wk_, hc)], eng="pool")
            for t in range(NOWN):
                tok = slice(t * 128, (t + 1) * 128)
                mg_ = mgl[t % 2]
                xq_ = xql[t % 2]
                P.dma(mg_[:], mgS[tok, :], writes=[("mgl", t % 2)])
                P.dma(xq_[:], xq[tok, :], writes=[("xql", t % 2)])
                for c in range(8):
                    P.op("pe", lambda e, c=c, t=t: e.transpose(out=pTa[:, c * 128:(c + 1) * 128],
                                                              in_=ydiff[:, t * 1024 + c * 128:t * 1024 + (c + 1) * 128], identity=ident[:]),
                         reads=["ident"], writes=["pTa"])
                P.op("dve", lambda e: e.tensor_copy(out=ydT[:], in_=pTa[:]), reads=["pTa"], writes=["ydT"])
                for c in range(8):
                    P.op("pe", lambda e, c=c, t=t: e.transpose(out=pTb[:, c * 128:(c + 1) * 128],
                                                              in_=ynsa[:, t * 1024 + c * 128:t * 1024 + (c + 1) * 128], identity=ident[:]),
                         reads=["ident"], writes=["pTb"])
                P.op("dve", lambda e: e.tensor_copy(out=ynT[:], in_=pTb[:]), reads=["pTb"], writes=["ynT"])
                for hf in range(2):
                    for c in range(8):
                        P.op("pe", lambda e, c=c, hf=hf: e.matmul(pD[hf][:], lhsT=ydT[:, c * 128:(c + 1) * 128],
                                                                  rhs=Wpd[:, c * 1024 + hf * 512:c * 1024 + (hf + 1) * 512],
                                                                  start=(c == 0), stop=(c == 7)),
                             reads=["ydT", ("Wpd", c // 4)], writes=[("pD", hf)])
                    for c in range(8):
                        P.op("pe", lambda e, c=c, hf=hf: e.matmul(pN[hf][:], lhsT=ynT[:, c * 128:(c + 1) * 128],
                                                                  rhs=Wpn[:, c * 1024 + hf * 512:c * 1024 + (hf + 1) * 512],
                                                                  start=(c == 0), stop=(c == 7)),
                             reads=["ynT", ("Wpn", c // 4)], writes=[("pN", hf)])
                    P.op("dve", lambda e, hf=hf, mg_=mg_: e.tensor_tensor(out=m1[:, hf * 512:(hf + 1) * 512], in0=pD[hf][:],
                                                                          in1=mg_[:, hf * 512:(hf + 1) * 512], op=ALU.mult),
                         reads=[("pD", hf), ("mgl", t % 2)], writes=[("m1", hf)])
                    P.op("dve", lambda e, hf=hf, mg_=mg_: e.tensor_tensor(out=m2[:, hf * 512:(hf + 1) * 512], in0=pN[hf][:],
                                                                          in1=mg_[:, 1024 + hf * 512:1024 + (hf + 1) * 512], op=ALU.mult),
                         reads=[("pN", hf), ("mgl", t % 2)], writes=[("m2", hf)])
                    P.op("pool", lambda e, hf=hf: e.tensor_tensor(out=mixb[:, hf * 512:(hf + 1) * 512], in0=m1[:, hf * 512:(hf + 1) * 512],
                                                                  in1=m2[:, hf * 512:(hf + 1) * 512], op=ALU.add),
                         reads=[("m1", hf), ("m2", hf)], writes=[("mixb", hf)])
                for c in range(8):
                    P.op("pe", lambda e, c=c: e.transpose(out=pTa[:, c * 128:(c + 1) * 128], in_=mixb[:, c * 128:(c + 1) * 128],
                                                          identity=ident[:]),
                         reads=[("mixb", c // 4), "ident"], writes=["pTa"])
                P.op("dve", lambda e: e.tensor_copy(out=mixT[:], in_=pTa[:]), reads=["pTa"], writes=["mixT"])
                x1_ = x1t[t % 2]
                for hf in range(2):
                    for c in range(8):
                        P.op("pe", lambda e, c=c, hf=hf: e.matmul(pW[hf][:], lhsT=mixT[:, c * 128:(c + 1) * 128],
                                                                  rhs=Wo[:, c * 1024 + hf * 512:c * 1024 + (hf + 1) * 512],
                                                                  start=(c == 0), stop=(c == 7)),
                             reads=["mixT", ("Wo", c // 4)], writes=[("pW", hf)])
                    P.op("dve", lambda e, hf=hf, x1_=x1_, xq_=xq_: e.tensor_tensor(out=x1_[:, hf * 512:(hf + 1) * 512], in0=pW[hf][:],
                                                                                  in1=xq_[:, hf * 512:(hf + 1) * 512], op=ALU.add),
                         reads=[("pW", hf), ("xql", t % 2)], writes=[("x1t", t % 2, hf)])
                P.dma(x1d[tok, :], x1_[:], reads=[("x1t", t % 2, 0), ("x1t", t % 2, 1)])
            P.emit()
        if stage <= 6:
            return nc
    with contextlib.ExitStack() as top2:
        def sb2(name, shape, dt):
            return top2.enter_context(nc.sbuf_tensor(name, list(shape), dt))

        ident2 = sb2("ident2", [128, 128], BF16)
        Wup = sb2("h_Wup", [128, 8 * 4096], BF16)
        Wdn = sb2("h_Wdn", [128, 32 * 1024], BF16)
        gml = sb2("h_gml", [128, 1024], F32)
        x1g = sb2("h_x1g", [128, 4 * 1024], F32)
        junk2 = sb2("h_junk", [128, 1024], F32)
        h2 = sb2("h_h2", [128, 1024], BF16)
        h2T = sb2("h_h2T", [128, 8 * 512], BF16)
        uT = sb2("h_uT", [128, 32 * 512], BF16)
        rr = [sb2(f"h_rr{i}", [128, 512], F32) for i in range(2)]
        ot = [sb2(f"h_ot{i}", [128, 512], F32) for i in range(2)]
        ss2 = sb2("h_ss", [128, 1], F32)
        rs2 = sb2("h_rs", [128, 1], F32)
        pT2 = top2.enter_context(nc.psum_tensor("h_pT", [128, 1024], BF16))
        pU2 = [top2.enter_context(nc.psum_tensor(f"h_pU{i}", [128, 512], F32)) for i in range(2)]
        pDn = [top2.enter_context(nc.psum_tensor(f"h_pDn{i}", [128, 512], F32)) for i in range(2)]
        P = Phase(nc, "ph")
        P.dma(ident2[:], identd, writes=["ident"])
        P.dma(gml[:], g_mlp[0:1, :].broadcast_to([128, 1024]), writes=["gt"])
        for c in range(8):
            P.dma(Wup[:, c * 4096:(c + 1) * 4096], w_up[c * 128:(c + 1) * 128, :], writes=[("Wup", c)], eng="pool")
        for f4 in range(8):
            P.dma(Wdn[:, f4 * 4096:(f4 + 1) * 4096].rearrange("p (a n) -> p a n", n=1024),
                  w_dn[f4 * 512:(f4 + 1) * 512, :].rearrange("(a p) n -> p a n", p=128), writes=[("Wdn", f4)], eng="pool")
        for gq in range(NSL):
            rows = slice(gq * 512, (gq + 1) * 512)
            P.dma(x1g[:].rearrange("p (t n) -> p t n", n=1024), x1d[rows, :].rearrange("(t p) n -> p t n", p=128), writes=["x1g"])
            for j in range(4):
                xt_ = x1g[:, j * 1024:(j + 1) * 1024]
                P.op("pool", lambda e: e.memset(ss2[:], 0.0), writes=["ss2"])
                P.op("act", lambda e, xt_=xt_: e.activation(out=junk2[:], in_=xt_, func=AF.Square, accum_out=ss2[:]),
                     reads=["x1g", "ss2"], writes=["junk2", "ss2"])
                P.op("dve", lambda e: e.tensor_scalar(out=ss2[:], in0=ss2[:], scalar1=1.0 / 1024, scalar2=EPS,
                                                      op0=ALU.mult, op1=ALU.add), reads=["ss2"], writes=["ss2"])
                P.op("act", lambda e: e.activation(out=ss2[:], in_=ss2[:], func=AF.Ln), reads=["ss2"], writes=["ss2"])
                P.op("act", lambda e: e.activation(out=rs2[:], in_=ss2[:], func=AF.Exp, scale=-0.5), reads=["ss2"], writes=["rs2"])
                P.op("dve", lambda e, xt_=xt_: e.scalar_tensor_tensor(out=h2[:], in0=xt_, scalar=rs2[:], in1=gml[:],
                                                                     op0=ALU.mult, op1=ALU.mult),
                     reads=["x1g", "rs2", "gt"], writes=["h2"])
                for c in range(8):
                    P.op("pe", lambda e, c=c: e.transpose(out=pT2[:, c * 128:(c + 1) * 128], in_=h2[:, c * 128:(c + 1) * 128],
                                                          identity=ident2[:]), reads=["h2", "ident"], writes=["pT2"])
                P.op("dve", lambda e, j=j: e.tensor_copy(
                    out=h2T[:].rearrange("p (c q) -> p c q", q=512)[:, :, j * 128:(j + 1) * 128],
                    in_=pT2[:].rearrange("p (c q) -> p c q", q=128)), reads=["pT2"], writes=["h2T"])
            for f in range(32):
                pu = pU2[f % 2]
                for c in range(8):
                    P.op("pe", lambda e, c=c, f=f, pu=pu: e.matmul(pu[:], lhsT=Wup[:, c * 4096 + f * 128:c * 4096 + (f + 1) * 128],
                                                                  rhs=h2T[:, c * 512:(c + 1) * 512], start=(c == 0), stop=(c == 7)),
                         reads=[("Wup", c), "h2T"], writes=[("pU2", f % 2)])
                r_ = rr[f % 2]
                P.op("act", lambda e, pu=pu, r_=r_: e.activation(out=r_[:], in_=pu[:], func=AF.Relu),
                     reads=[("pU2", f % 2)], writes=[("rr", f % 2)])
                P.op("pool" if f % 2 else "dve", lambda e, f=f, r_=r_: e.tensor_tensor(out=uT[:, f * 512:(f + 1) * 512], in0=r_[:], in1=r_[:],
                                                                                     op=ALU.mult),
                     reads=[("rr", f % 2)], writes=[("uT", f)])
            for j in range(4):
                for hf in range(2):
                    i = (j * 2 + hf) % 2
                    for f in range(32):
                        P.op("pe", lambda e, f=f, j=j, hf=hf, i=i: e.matmul(
                            pDn[i][:], lhsT=uT[:, f * 512 + j * 128:f * 512 + (j + 1) * 128],
                            rhs=Wdn[:, f * 1024 + hf * 512:f * 1024 + (hf + 1) * 512], start=(f == 0), stop=(f == 31)),
                            reads=[("uT", f), ("Wdn", f // 4)], writes=[("pDn", i)])
                    o_ = ot[i]
                    P.op("dve", lambda e, i=i, o_=o_, j=j, hf=hf: e.tensor_tensor(
                        out=o_[:], in0=pDn[i][:], in1=x1g[:, j * 1024 + hf * 512:j * 1024 + (hf + 1) * 512], op=ALU.add),
                        reads=[("pDn", i), "x1g"], writes=[("ot", i)])
                    P.dma(out[gq * 512 + j * 128:gq * 512 + (j + 1) * 128, hf * 512:(hf + 1) * 512], o_[:], reads=[("ot", i)])
        P.emit()
    return nc


def _rope_tab(pos):
    inv = (10000.0 ** (-(np.arange(32, dtype=np.float32)) / np.float32(32))).astype(np.float32)
    ang = pos.astype(np.float32)[:, None] * inv[None, :]
    return np.cos(ang).astype(np.float32), np.sin(ang).astype(np.float32)


def make_core_inputs(inp, core, S):
    b, par = core // 2, core % 2
    NSL = S // 1024
    SO = NSL * 512
    NCMP = (S - 32) // 16 + 1
    NCT = (NCMP + 127) // 128
    NCP = NCT * 128
    f = lambda a: np.ascontiguousarray(np.asarray(a, dtype=np.float32))
    x = np.asarray(inp["x"], dtype=np.float32)
    own_pos = np.concatenate([np.arange((2 * s + par) * 512, (2 * s + par) * 512 + 512) for s in range(NSL)])
    d = {}
    d["xkv"] = f(x[b])
    d["xq"] = f(x[b][own_pos])
    d["w_in"] = f(inp["w_in"][0])
    d["w_pd"] = f(inp["w_proj_diff"][0])
    d["w_pn"] = f(inp["w_proj_nsa"][0])
    d["w_o"] = f(inp["w_out"][0])
    d["w_up"] = f(inp["w_mlp_up"][0])
    d["w_dn"] = f(inp["w_mlp_down"][0])
    d["c_w1k"] = f(inp["cmp_k_w1"][0])
    d["c_w1v"] = f(inp["cmp_v_w1"][0])
    d["c_w2k"] = f(inp["cmp_k_w2"][0])
    d["c_w2v"] = f(inp["cmp_v_w2"][0])
    p2 = lambda p: f(np.asarray(p, np.float32).reshape(16, 2, 64).transpose(1, 2, 0).reshape(128, 16))
    d["pos2k"] = p2(inp["cmp_pos_k"][0])
    d["pos2v"] = p2(inp["cmp_pos_v"][0])
    d["g_mix"] = f(inp["ln_mix_g"][0][None, :])
    d["g_mlp"] = f(inp["ln_mlp_g"][0][None, :])
    gk = np.asarray(inp["nsa_k_norm_g"][0], np.float32)
    d["g_k24"] = f(np.concatenate([np.tile(np.asarray(inp["diff_k_norm_g"][0], np.float32), 16),
                                   np.tile(gk[1], 4), np.tile(gk[2], 4)])[None, :])
    d["g_q32"] = f(np.concatenate([np.tile(np.asarray(inp["diff_q_norm_g"][0], np.float32), 16),
                                   np.tile(np.asarray(inp["nsa_q_norm_g"][0], np.float32), 16)])[None, :])
    d["g_kc"] = f(gk[0][None, :])
    d["g_sub"] = f(inp["diff_subln_g"][0][None, :])
    d["lam4"] = f(np.concatenate([np.asarray(inp[k][0], np.float32) for k in
                                  ("diff_lambda_q1", "diff_lambda_k1", "diff_lambda_q2", "diff_lambda_k2")])[None, :])
    d["rkv_c"], d["rkv_s"] = _rope_tab(np.arange(S))
    d["rq_c"], d["rq_s"] = _rope_tab(own_pos)
    cc = np.zeros(NCP, np.float32)
    cc[:NCMP] = np.arange(NCMP) * 16 + 15.5
    d["rc_c"], d["rc_s"] = _rope_tab(cc)
    d["ident"] = np.eye(128, dtype=np.float32).astype(NPBF)
    kk = np.arange(S)
    d["eind"] = (kk[None, :] // 64 == np.arange(64)[:, None]).astype(np.float32).astype(NPBF)
    n = np.arange(NCP)
    cs, ce = n * 16, n * 16 + 31
    ss_ = np.arange(64) * 64
    ov = ((cs[:, None] < ss_[None, :] + 64) & (ce[:, None] >= ss_[None, :]) & (n[:, None] < NCMP)).astype(np.float32)
    d["ovl"] = np.concatenate([ov, np.ones((NCP, 1), np.float32)], axis=1).astype(NPBF)
    k128 = np.arange(128)[:, None, None]
    q512 = np.arange(512)[None, None, :]
    kw = np.arange(8)[None, :, None] * 128 + k128
    d["m_dense"] = np.where((kw - par * 512) <= q512, 1.0, 0.0).astype(np.float32).astype(NPBF)
    kw = np.arange(12)[None, :, None] * 128 + k128 - 512
    dd = par * 512 + q512 - kw
    d["m_win"] = np.where((dd >= 0) & (dd < 512), 1.0, 0.0).astype(np.float32).astype(NPBF)
    mc = np.zeros((128, NSL * NCT, 512), np.float32)
    for s in range(NSL):
        for nt in range(NCT):
            nn = nt * 128 + np.arange(128)[:, None]
            qpos = (2 * s + par) * 512 + np.arange(512)[None, :]
            mc[:, s * NCT + nt, :] = np.where((nn < NCMP) & (nn * 16 + 31 <= qpos), 0.0, NEG)
    d["m_cmp"] = mc.astype(NPBF)
    fo = np.zeros((SO, 64), np.float32)
    cur = own_pos // 64
    r = np.arange(SO)
    fo[r[cur >= 1], (cur - 1)[cur >= 1]] = 1e4
    fo[r, cur] = 2e4
    fo[:, 0] = 3e4
    d["forced"] = fo
    return d


_NC_CACHE = {}


def kernel(**inputs):
    S = int(np.asarray(inputs["x"]).shape[1])
    B = int(np.asarray(inputs["x"]).shape[0])
    ncores = 2 * B
    if S not in _NC_CACHE:
        _NC_CACHE[S] = build(S)
    nc = _NC_CACHE[S]
    in_maps = [make_core_inputs(inputs, c, S) for c in range(ncores)]
    res = run_bass_kernel_spmd(nc, in_maps, core_ids=list(range(ncores)))
    out = np.zeros((B, S, 1024), np.float32)
    NSL = S // 1024
    for c in range(ncores):
        b, par = c // 2, c % 2
        o = np.asarray(res.results[c]["out"], dtype=np.float32)
        for s in range(NSL):
            ch = 2 * s + par
            out[b, ch * 512:(ch + 1) * 512] = o[s * 512:(s + 1) * 512]
    return out
```

```python
import contextlib
import numpy as np
import ml_dtypes
import concourse.bass as bass
import concourse.mybir as mybir
from concourse.bass_utils import run_bass_kernel_spmd

F32 = mybir.dt.float32
BF16 = mybir.dt.bfloat16
ALU = mybir.AluOpType
AF = mybir.ActivationFunctionType
AX = mybir.AxisListType
NPBF = ml_dtypes.bfloat16

NDMASEM = 4
EPS = 1e-6
NEG = -30000.0


class Phase:
    ENGS = ("pe", "act", "dve", "pool", "sp")

    def __init__(self, nc, name):
        self.nc = nc
        self.name = name
        self.ops = []

    def op(self, eng, fn, reads=(), writes=(), dma=False):
        self.ops.append((eng, fn, tuple(reads), tuple(writes), dma))

    def dma(self, out, in_, reads=(), writes=(), eng="sp"):
        self.op(eng, lambda e: e.dma_start(out=out, in_=in_), reads, writes, dma=True)

    def emit(self):
        nc = self.nc
        ops = self.ops
        cnt = {e: 0 for e in self.ENGS}
        dcnt = {e: 0 for e in self.ENGS}
        info = []
        last_w = {}
        readers = {}
        deps = []
        for i, (eng, fn, rd, wr, dma) in enumerate(ops):
            d = set()
            for k in rd:
                if k in last_w:
                    d.add(last_w[k])
            for k in wr:
                if k in last_w:
                    d.add(last_w[k])
                for r in readers.get(k, ()):
                    d.add(r)
            d.discard(i)
            deps.append(d)
            for k in rd:
                readers.setdefault(k, []).append(i)
            for k in wr:
                last_w[k] = i
                readers[k] = []
            if dma:
                info.append((eng, "d", dcnt[eng]))
                dcnt[eng] += 1
            else:
                info.append((eng, "c", cnt[eng]))
                cnt[eng] += 1
        with contextlib.ExitStack() as st:
            csem = {e: st.enter_context(nc.semaphore(f"{self.name}_c_{e}")) for e in self.ENGS}
            dsem = {e: [st.enter_context(nc.semaphore(f"{self.name}_d_{e}{j}")) for j in range(NDMASEM)]
                    for e in self.ENGS if dcnt[e] > 0}
            block = st.enter_context(nc.Block())
            per_eng = {e: [i for i, o in enumerate(ops) if o[0] == e] for e in self.ENGS}

            def make(eng_name):
                def body(eng):
                    waited = {}

                    def wait(key, sem, val):
                        if waited.get(key, 0) >= val:
                            return
                        waited[key] = val
                        eng.wait_ge(sem, val)

                    for i in per_eng[eng_name]:
                        _, fn, rd, wr, dma = ops[i]
                        for p in sorted(deps[i]):
                            pe_, pk, pidx = info[p]
                            if pk == "c":
                                if pe_ == "pe" and eng_name == "pe":
                                    continue
                                wait(("c", pe_), csem[pe_], pidx + 1)
                            else:
                                slot = pidx % NDMASEM
                                wait(("d", pe_, slot), dsem[pe_][slot], 16 * (pidx // NDMASEM + 1))
                        if dma:
                            didx = info[i][2]
                            slot = didx % NDMASEM
                            if didx >= NDMASEM:
                                wait(("d", eng_name, slot), dsem[eng_name][slot], 16 * (didx // NDMASEM))
                            fn(eng).then_inc(dsem[eng_name][slot], 16)
                        else:
                            fn(eng).then_inc(csem[eng_name], 1)
                    if eng_name in dsem:
                        n = dcnt[eng_name]
                        for slot in range(min(NDMASEM, n)):
                            total = (n - slot + NDMASEM - 1) // NDMASEM
                            wait(("d", eng_name, slot), dsem[eng_name][slot], 16 * total)
                return body

            if per_eng["pe"]:
                block.tensor(make("pe"))
            if per_eng["act"]:
                block.scalar(make("act"))
            if per_eng["dve"]:
                block.vector(make("dve"))
            if per_eng["pool"]:
                block.gpsimd(make("pool"))
            if per_eng["sp"]:
                block.sync(make("sp"))
        self.ops = []


KV_RANGES = [(1024, 2048), (4608, 4864), (5120, 5376), (2048, 3072), (4864, 5120), (5376, 5632),
             (4096, 4352), (4352, 4608)]
Q_RANGES = [(0, 1024), (3072, 4096), (5632, 5680), (5680, 7728)]
NKV = 3584
NQC = 4144


def build(S=4096, stage=99, dbg=False):
    NT = S // 128
    NCH = S // 512
    NSL = NCH // 2
    NOWN = NSL * 4
    SO = NSL * 512
    NCMP = (S - 32) // 16 + 1
    NCT = (NCMP + 127) // 128
    NCP = NCT * 128
    okind = "ExternalOutput" if dbg else "Internal"

    nc = bass.Bass("TRN2", target_bir_lowering=False)

    def din(name, shape, dt=F32):
        return nc.dram_tensor(name, list(shape), dt, kind="ExternalInput").ap()

    def dscr(name, shape, dt=BF16):
        return nc.dram_tensor(name, list(shape), dt, kind=okind).ap()

    xkv = din("xkv", [S, 1024])
    xq = din("xq", [SO, 1024])
    w_in = din("w_in", [1024, 7728])
    w_pd = din("w_pd", [1024, 1024])
    w_pn = din("w_pn", [1024, 1024])
    w_o = din("w_o", [1024, 1024])
    w_up = din("w_up", [1024, 4096])
    w_dn = din("w_dn", [4096, 1024])
    c_w1 = [din("c_w1k", [2048, 256]), din("c_w1v", [2048, 256])]
    c_w2 = [din("c_w2k", [256, 64]), din("c_w2v", [256, 64])]
    pos2 = [din("pos2k", [128, 16]), din("pos2v", [128, 16])]
    g_mix = din("g_mix", [1, 1024])
    g_mlp = din("g_mlp", [1, 1024])
    g_k24 = din("g_k24", [1, 1536])
    g_q32 = din("g_q32", [1, 2048])
    g_kc = din("g_kc", [1, 64])
    g_sub = din("g_sub", [1, 128])
    lam4 = din("lam4", [1, 256])
    rkv_c = din("rkv_c", [S, 32])
    rkv_s = din("rkv_s", [S, 32])
    rq_c = din("rq_c", [SO, 32])
    rq_s = din("rq_s", [SO, 32])
    rc_c = din("rc_c", [NCP, 32])
    rc_s = din("rc_s", [NCP, 32])
    identd = din("ident", [128, 128], BF16)
    eind = din("eind", [64, S], BF16)
    ovl = din("ovl", [NCP, 65], BF16)
    m_dense = din("m_dense", [128, 8, 512], BF16)
    m_win = din("m_win", [128, 12, 512], BF16)
    m_cmp = din("m_cmp", [128, NSL * NCT, 512], BF16)
    forced = din("forced", [SO, 64])
    out = nc.dram_tensor("out", [SO, 1024], F32, kind="ExternalOutput").ap()

    KdT = dscr("KdT", [8, 128, S])
    Vd = dscr("Vd", [S, 1024])
    KsT = dscr("KsT", [256, S])
    KwT = dscr("KwT", [256, S])
    kcT = dscr("kcT", [256, S])
    vcT = dscr("vcT", [256, S])
    Vs = dscr("Vs", [S, 256])
    Vw = dscr("Vw", [S, 256])
    QdT = dscr("QdT", [8, 128, SO])
    QnT = dscr("QnT", [1024, SO])
    BsT = dscr("BsT", [256, SO])
    mgS = dscr("mgS", [SO, 2048], F32)
    x1d = dscr("x1d", [SO, 1024], F32)

    with contextlib.ExitStack() as top:
        def sb(name, shape, dt, stack=top):
            return stack.enter_context(nc.sbuf_tensor(name, list(shape), dt))

        def ps(name, shape, dt, stack):
            return stack.enter_context(nc.psum_tensor(name, list(shape), dt))

        ident = sb("ident_sb", [128, 128], BF16)
        gates = sb("gates", [128, NOWN * 48], F32)
        KcT = sb("KcT", [128, 4 * NCP], BF16)
        Vca = sb("Vca", [128, NCT * 4 * 129], BF16)
        nlam = sb("nlam", [128, 1], F32)
        sgain = sb("sgain", [128, 128], F32)
        ss1 = sb("ss1", [128, 1], F32)
        rs1 = sb("rs1", [128, 1], F32)

        def bc(ap1, n):
            return ap1[0:1, :].broadcast_to([128, n])

        def rmsnorm_rows(P, xt, xkey, gt, hout, hkey, junk, sfx=""):
            P.op("pool", lambda e: e.memset(ss1[:], 0.0), writes=["ss1"])
            P.op("act", lambda e: e.activation(out=junk, in_=xt, func=AF.Square, accum_out=ss1[:]),
                 reads=[xkey, "ss1"], writes=["junk" + sfx, "ss1"])
            P.op("dve", lambda e: e.tensor_scalar(out=ss1[:], in0=ss1[:], scalar1=1.0 / 1024, scalar2=EPS,
                                                  op0=ALU.mult, op1=ALU.add), reads=["ss1"], writes=["ss1"])
            P.op("act", lambda e: e.activation(out=ss1[:], in_=ss1[:], func=AF.Ln), reads=["ss1"], writes=["ss1"])
            P.op("act", lambda e: e.activation(out=rs1[:], in_=ss1[:], func=AF.Exp, scale=-0.5),
                 reads=["ss1"], writes=["rs1"])
            P.op("dve", lambda e: e.scalar_tensor_tensor(out=hout, in0=xt, scalar=rs1[:], in1=gt,
                                                         op0=ALU.mult, op1=ALU.mult),
                 reads=[xkey, "rs1", "gt"], writes=[hkey])

        def normrope(P, src, srckey, U, gain, cos, sin, cskeys, outap, outkey, T, sfx):
            sq, ssu, rsu, xn, t1, t2, t3, t4 = T
            n = U * 64
            s3 = lambda ap: ap.rearrange("p (u d) -> p u d", d=64)
            P.op("act", lambda e: e.activation(out=sq[:, 0:n], in_=src, func=AF.Square),
                 reads=[srckey], writes=["sq" + sfx])
            P.op("dve", lambda e: e.tensor_reduce(out=ssu[:, 0:U], in_=s3(sq[:, 0:n]), axis=AX.X, op=ALU.add),
                 reads=["sq" + sfx], writes=["ssu" + sfx])
            P.op("dve", lambda e: e.tensor_scalar(out=ssu[:, 0:U], in0=ssu[:, 0:U], scalar1=1.0 / 64, scalar2=EPS,
                                                  op0=ALU.mult, op1=ALU.add), reads=["ssu" + sfx], writes=["ssu" + sfx])
            P.op("act", lambda e: e.activation(out=ssu[:, 0:U], in_=ssu[:, 0:U], func=AF.Ln),
                 reads=["ssu" + sfx], writes=["ssu" + sfx])
            P.op("act", lambda e: e.activation(out=rsu[:, 0:U], in_=ssu[:, 0:U], func=AF.Exp, scale=-0.5),
                 reads=["ssu" + sfx], writes=["rsu" + sfx])
            P.op("dve", lambda e: e.tensor_tensor(out=s3(xn[:, 0:n]), in0=s3(src),
                                                  in1=rsu[:, 0:U].unsqueeze(2).to_broadcast([128, U, 64]), op=ALU.mult),
                 reads=[srckey, "rsu" + sfx], writes=["xn" + sfx])
            P.op("pool", lambda e: e.tensor_tensor(out=xn[:, 0:n], in0=xn[:, 0:n], in1=gain, op=ALU.mult),
                 reads=["xn" + sfx, "gains"], writes=["xn" + sfx])
            x3 = s3(xn[:, 0:n])
            o3 = s3(outap)
            cb = cos.unsqueeze(1).to_broadcast([128, U, 32])
            sbb = sin.unsqueeze(1).to_broadcast([128, U, 32])
            h3 = lambda t: t[:, 0:U * 32].rearrange("p (u d) -> p u d", d=32)
            P.op("pool", lambda e: e.tensor_tensor(out=h3(t1), in0=x3[:, :, 0:32], in1=cb, op=ALU.mult),
                 reads=["xn" + sfx] + list(cskeys), writes=["t1" + sfx])
            P.op("pool", lambda e: e.tensor_tensor(out=h3(t2), in0=x3[:, :, 32:64], in1=sbb, op=ALU.mult),
                 reads=["xn" + sfx] + list(cskeys), writes=["t2" + sfx])
            P.op("pool", lambda e: e.tensor_tensor(out=o3[:, :, 0:32], in0=h3(t1), in1=h3(t2), op=ALU.subtract),
                 reads=["t1" + sfx, "t2" + sfx], writes=[outkey])
            P.op("dve", lambda e: e.tensor_tensor(out=h3(t3), in0=x3[:, :, 32:64], in1=cb, op=ALU.mult),
                 reads=["xn" + sfx] + list(cskeys), writes=["t3" + sfx])
            P.op("dve", lambda e: e.tensor_tensor(out=h3(t4), in0=x3[:, :, 0:32], in1=sbb, op=ALU.mult),
                 reads=["xn" + sfx] + list(cskeys), writes=["t4" + sfx])
            P.op("dve", lambda e: e.tensor_tensor(out=o3[:, :, 32:64], in0=h3(t3), in1=h3(t4), op=ALU.add),
                 reads=["t3" + sfx, "t4" + sfx], writes=[outkey])

        def load_w_cols(P, wdst, ranges, ncols):
            keys = []
            for c in range(8):
                off = 0
                kc_ = []
                for ri, (a, b) in enumerate(ranges):
                    k = ("wbuf", c, ri)
                    P.dma(wdst[:, c * ncols + off:c * ncols + off + (b - a)], w_in[c * 128:(c + 1) * 128, a:b],
                          writes=[k], eng="pool")
                    kc_.append(k)
                    off += b - a
                keys.append(kc_)
            return keys

        with contextlib.ExitStack() as sa:
            lt = sb("lt", [128, 256], F32, sa)
            pr = sb("pr", [128, 128], F32, sa)
            s2 = sb("s2", [128, 2], F32, sa)
            sgr = sb("sgr", [128, 128], F32, sa)
            P = Phase(nc, "pa")
            P.dma(ident[:], identd, writes=["ident"])
            P.dma(lt[:], bc(lam4, 256), writes=["lt"])
            P.dma(sgr[:], bc(g_sub, 128), writes=["sgr"])
            P.op("dve", lambda e: e.tensor_tensor(out=pr[:, 0:64], in0=lt[:, 0:64], in1=lt[:, 64:128], op=ALU.mult),
                 reads=["lt"], writes=["pr"])
            P.op("dve", lambda e: e.tensor_tensor(out=pr[:, 64:128], in0=lt[:, 128:192], in1=lt[:, 192:256], op=ALU.mult),
                 reads=["lt", "pr"], writes=["pr"])
            P.op("dve", lambda e: e.tensor_reduce(out=s2[:], in_=pr[:].rearrange("p (a d) -> p a d", d=64),
                                                  axis=AX.X, op=ALU.add), reads=["pr"], writes=["s2"])
            P.op("act", lambda e: e.activation(out=s2[:], in_=s2[:], func=AF.Exp), reads=["s2"], writes=["s2"])
            P.op("dve", lambda e: e.tensor_tensor(out=nlam[:], in0=s2[:, 1:2], in1=s2[:, 0:1], op=ALU.subtract),
                 reads=["s2"], writes=["nlam"])
            P.op("dve", lambda e: e.tensor_scalar(out=nlam[:], in0=nlam[:], scalar1=-0.2, scalar2=None, op0=ALU.add),
                 reads=["nlam"], writes=["nlam"])
            P.op("dve", lambda e: e.tensor_scalar(out=sgain[:], in0=sgr[:], scalar1=0.8, scalar2=None, op0=ALU.mult),
                 reads=["sgr"], writes=["sgain"])
            P.emit()

        with contextlib.ExitStack() as sbd:
            wbuf = sb("wbuf", [128, 8 * NQC], BF16, sbd)
            gmix = sb("gmix", [128, 1024], F32, sbd)
            gains = sb("gains", [128, 2048], F32, sbd)
            xts = [sb(f"xt{i}", [128, 1024], F32, sbd) for i in range(2)]
            junk = sb("junk", [128, 1024], F32, sbd)
            hb = sb("hb", [128, 1024], BF16, sbd)
            hT = [sb(f"hT{i}", [128, 1024], BF16, sbd) for i in range(2)]
            cst = [sb(f"cs{i}", [128, 64], F32, sbd) for i in range(2)]
            TT = []
            for i in range(2):
                TT.append((sb(f"sq{i}", [128, 512], F32, sbd), sb(f"ssu{i}", [128, 8], F32, sbd),
                           sb(f"rsu{i}", [128, 8], F32, sbd), sb(f"xn{i}", [128, 512], F32, sbd),
                           sb(f"t1{i}", [128, 256], F32, sbd), sb(f"t2{i}", [128, 256], F32, sbd),
                           sb(f"t3{i}", [128, 256], F32, sbd), sb(f"t4{i}", [128, 256], F32, sbd)))
            kb = sb("kb", [128, 2048], BF16, sbd)
            vb = sb("vb", [128, 2048], BF16, sbd)
            ktb = [sb(f"ktb{i}", [128, 16 * 128], BF16, sbd) for i in range(2)]
            mgt = [sb(f"mgt{i}", [128, 512], F32, sbd) for i in range(2)]
            pT = ps("pT", [128, 1024], BF16, sbd)
            pO = [ps(f"pO{i}", [128, 512], F32, sbd) for i in range(3)]
            pK = [ps(f"pK{i}", [128, 1024], BF16, sbd) for i in range(2)]

            def front_a(P, t, gt):
                xt = xts[t % 2]
                rmsnorm_rows(P, xt[:], ("xt", t % 2), gt[:], hb[:], "hb", junk[:])

            def front_b(P, t):
                for c in range(8):
                    P.op("pe", lambda e, c=c: e.transpose(out=pT[:, c * 128:(c + 1) * 128],
                                                          in_=hb[:, c * 128:(c + 1) * 128], identity=ident[:]),
                         reads=["hb", "ident"], writes=["pT"])
                P.op("dve", lambda e: e.tensor_copy(out=hT[t % 2][:], in_=pT[:]), reads=["pT"], writes=[("hT", t % 2)])

            def proj_group(P, t, j, ncols, col0, width, wkeys):
                po = pO[j % 3]
                for c in range(8):
                    P.op("pe", lambda e, c=c: e.matmul(po[:, 0:width], lhsT=hT[t % 2][:, c * 128:(c + 1) * 128],
                                                       rhs=wbuf[:, c * ncols + col0:c * ncols + col0 + width],
                                                       start=(c == 0), stop=(c == 7)),
                         reads=[("hT", t % 2)] + wkeys[c], writes=[("pO", j % 3)])
                return po

            P = Phase(nc, "pb")
            wk = load_w_cols(P, wbuf, KV_RANGES, NKV)
            P.dma(gmix[:], bc(g_mix, 1024), writes=["gt"])
            P.dma(gains[:, 0:1536], bc(g_k24, 1536), writes=["gains"])
            P.dma(xts[0][:], xkv[0:128, :], writes=[("xt", 0)])
            P.dma(xts[1][:], xkv[128:256, :], writes=[("xt", 1)])
            front_a(P, 0, gmix)
            front_b(P, 0)
            for t in range(NT):
                cs = cst[t % 2]
                P.dma(cs[:, 0:32], rkv_c[t * 128:(t + 1) * 128, :], writes=[("cs", t % 2, 0)])
                P.dma(cs[:, 32:64], rkv_s[t * 128:(t + 1) * 128, :], writes=[("cs", t % 2, 1)])
                if t + 1 < NT:
                    front_a(P, t + 1, gmix)
                for j in range(7):
                    po = proj_group(P, t, j, NKV, j * 512, 512, wk)
                    if j < 3:
                        normrope(P, po[:], ("pO", j % 3), 8, gains[:, j * 512:(j + 1) * 512], cs[:, 0:32], cs[:, 32:64],
                                 [("cs", t % 2, 0), ("cs", t % 2, 1)], kb[:, j * 512:(j + 1) * 512], ("kb", j), TT[j % 2], str(j % 2))
                    else:
                        P.op("act", lambda e, po=po, j=j: e.activation(out=vb[:, (j - 3) * 512:(j - 2) * 512], in_=po[:],
                                                                       func=AF.Identity),
                             reads=[("pO", j % 3)], writes=[("vb", j)])
                    if j == 2 and t + 1 < NT:
                        front_b(P, t + 1)
                        if t + 2 < NT:
                            P.dma(xts[t % 2][:], xkv[(t + 2) * 128:(t + 3) * 128, :], writes=[("xt", t % 2)])
                kt = ktb[t % 2]
                for blk in range(16):
                    src = kb[:, blk * 128:(blk + 1) * 128] if blk < 12 else vb[:, 1536 + (blk - 12) * 128:1536 + (blk - 11) * 128]
                    rk = [("kb", blk // 4)] if blk < 12 else [("vb", 6)]
                    P.op("pe", lambda e, src=src, blk=blk: e.transpose(out=pK[blk // 8][:, (blk % 8) * 128:(blk % 8 + 1) * 128],
                                                                        in_=src, identity=ident[:]),
                         reads=rk + ["ident"], writes=[("pK", blk // 8)])
                for hf in range(2):
                    P.op("dve", lambda e, hf=hf, kt=kt: e.tensor_copy(out=kt[:, hf * 1024:(hf + 1) * 1024], in_=pK[hf][:]),
                         reads=[("pK", hf)], writes=[("ktb", t % 2, hf)])
                tok = slice(t * 128, (t + 1) * 128)
                P.dma(KdT[:, :, tok].rearrange("h p k -> p h k"), kt[:, 0:1024].rearrange("p (h k) -> p h k", k=128),
                      reads=[("ktb", t % 2, 0)])
                for i, dst in enumerate((KsT, KwT, kcT, vcT)):
                    P.dma(dst[:, tok].rearrange("(i p) k -> p i k", p=128),
                          kt[:, 1024 + i * 256:1024 + (i + 1) * 256].rearrange("p (i k) -> p i k", k=128),
                          reads=[("ktb", t % 2, 1)])
                P.dma(Vd[tok, :], vb[:, 0:1024], reads=[("vb", 3), ("vb", 4)])
                P.dma(Vs[tok, :], vb[:, 1024:1280], reads=[("vb", 5)])
                P.dma(Vw[tok, :], vb[:, 1280:1536], reads=[("vb", 5)])
            P.emit()
            if stage <= 1:
                return nc

            P = Phase(nc, "pd")
            wk = load_w_cols(P, wbuf, Q_RANGES, NQC)
            P.dma(gains[:, 0:2048], bc(g_q32, 2048), writes=["gains"])
            P.dma(xts[0][:], xq[0:128, :], writes=[("xt", 0)])
            P.dma(xts[1][:], xq[128:256, :], writes=[("xt", 1)])
            front_a(P, 0, gmix)
            front_b(P, 0)
            for t in range(NOWN):
                cs = cst[t % 2]
                P.dma(cs[:, 0:32], rq_c[t * 128:(t + 1) * 128, :], writes=[("cs", t % 2, 0)])
                P.dma(cs[:, 32:64], rq_s[t * 128:(t + 1) * 128, :], writes=[("cs", t % 2, 1)])
                if t + 1 < NOWN:
                    front_a(P, t + 1, gmix)
                tok = slice(t * 128, (t + 1) * 128)
                for j in range(4):
                    po = proj_group(P, t, j, NQC, j * 512, 512, wk)
                    normrope(P, po[:], ("pO", j % 3), 8, gains[:, j * 512:(j + 1) * 512], cs[:, 0:32], cs[:, 32:64],
                             [("cs", t % 2, 0), ("cs", t % 2, 1)], kb[:, j * 512:(j + 1) * 512], ("kb", j), TT[j % 2], str(j % 2))
                    if j == 2 and t + 1 < NOWN:
                        front_b(P, t + 1)
                        if t + 2 < NOWN:
                            P.dma(xts[t % 2][:], xq[(t + 2) * 128:(t + 3) * 128, :], writes=[("xt", t % 2)])
                po = proj_group(P, t, 4, NQC, 2048, 48, wk)
                gsl = gates[:, t * 48:(t + 1) * 48]
                P.op("act", lambda e, po=po, gsl=gsl: e.activation(out=gsl, in_=po[:, 0:48], func=AF.Tanh, scale=0.5),
                     reads=[("pO", 1)], writes=["gates"])
                P.op("dve", lambda e, gsl=gsl: e.tensor_scalar(out=gsl, in0=gsl, scalar1=0.5, scalar2=0.5, op0=ALU.mult, op1=ALU.add),
                     reads=["gates"], writes=["gates"])
                for j in range(5, 9):
                    po = proj_group(P, t, j, NQC, 2096 + (j - 5) * 512, 512, wk)
                    mg_ = mgt[j % 2]
                    P.op("act", lambda e, po=po, mg_=mg_: e.activation(out=mg_[:], in_=po[:], func=AF.Tanh, scale=0.5),
                         reads=[("pO", j % 3)], writes=[("mgt", j % 2)])
                    P.op("pool" if j % 2 else "dve", lambda e, mg_=mg_: e.tensor_scalar(out=mg_[:], in0=mg_[:], scalar1=0.5, scalar2=0.5,
                                                                                       op0=ALU.mult, op1=ALU.add),
                         reads=[("mgt", j % 2)], writes=[("mgt", j % 2)])
                    P.dma(mgS[tok, (j - 5) * 512:(j - 4) * 512], mg_[:], reads=[("mgt", j % 2)])
                kt = ktb[t % 2]
                for blk in range(16):
                    P.op("pe", lambda e, blk=blk: e.transpose(out=pK[blk // 8][:, (blk % 8) * 128:(blk % 8 + 1) * 128],
                                                              in_=kb[:, blk * 128:(blk + 1) * 128], identity=ident[:]),
                         reads=[("kb", blk // 4), "ident"], writes=[("pK", blk // 8)])
                for hf in range(2):
                    P.op("dve", lambda e, hf=hf, kt=kt: e.tensor_copy(out=kt[:, hf * 1024:(hf + 1) * 1024], in_=pK[hf][:]),
                         reads=[("pK", hf)], writes=[("ktb", t % 2, hf)])
                P.dma(QdT[:, :, tok].rearrange("h p k -> p h k"), kt[:, 0:1024].rearrange("p (h k) -> p h k", k=128),
                      reads=[("ktb", t % 2, 0)])
                P.dma(QnT[:, tok].rearrange("(i p) k -> p i k", p=128), kt[:, 1024:2048].rearrange("p (i k) -> p i k", k=128),
                      reads=[("ktb", t % 2, 1)])
            P.emit()
        if stage <= 2:
            return nc

        ydiff = sb("ydiff", [128, NOWN * 1024], BF16)
        ynsa = sb("ynsa", [128, NOWN * 1024], BF16)
        with contextlib.ExitStack() as sc:
            W1s = [sb(f"W1s{i}", [128, 16 * 256], BF16, sc) for i in range(2)]
            W2s = [sb(f"W2s{i}", [128, 128], BF16, sc) for i in range(2)]
            p2b = [sb(f"p2b{i}", [128, 16], BF16, sc) for i in range(2)]
            kc2 = [sb(f"kc2_{i}", [128, S], BF16, sc) for i in range(2)]
            biasv = [sb(f"biasv{i}", [128, 2], F32, sc) for i in range(2)]
            h1T = sb("h1T", [128, 2 * NCP], BF16, sc)
            xg = sb("xg", [128, NCP], F32, sc)
            x2 = sb("x2", [128, NCP], F32, sc)
            th = sb("th", [128, NCP], F32, sc)
            kcn = sb("kcn", [128, 128], BF16, sc)
            gkc = sb("gkc", [128, 64], F32, sc)
            csc = sb("csc", [128, NCT * 64], F32, sc)
            TC = (sb("c_sq", [128, 64], F32, sc), sb("c_ssu", [128, 8], F32, sc), sb("c_rsu", [128, 8], F32, sc),
                  sb("c_xn", [128, 64], F32, sc), sb("c_t1", [128, 32], F32, sc), sb("c_t2", [128, 32], F32, sc),
                  sb("c_t3", [128, 32], F32, sc), sb("c_t4", [128, 32], F32, sc))
            pH = [ps(f"pH{i}", [128, 512], F32, sc) for i in range(2)]
            pB = ps("pB", [128, 512], F32, sc)
            pC = ps("pC", [128, 512], F32, sc)
            pKc = ps("pKc", [128, 1024], BF16, sc)
            P = Phase(nc, "pc")
            P.op("pool", lambda e: e.memset(h1T[:], 0.0), writes=["h1T"])
            P.op("pool", lambda e: e.memset(kcn[:], 0.0), writes=["kcn"])
            P.dma(gkc[:], bc(g_kc, 64), writes=["gains"])
            for nt in range(NCT):
                P.dma(csc[:, nt * 64:nt * 64 + 32], rc_c[nt * 128:(nt + 1) * 128, :], writes=[("csc", nt, 0)])
                P.dma(csc[:, nt * 64 + 32:nt * 64 + 64], rc_s[nt * 128:(nt + 1) * 128, :], writes=[("csc", nt, 1)])
                for g in range(4):
                    o0 = (nt * 4 + g) * 129
                    P.dma(Vca[:, o0 + 64:o0 + 129], ovl[nt * 128:(nt + 1) * 128, :], writes=[("vca_c", nt, g)])
            for kv in range(2):
                P.dma(W1s[kv][:].rearrange("p (u h) -> p u h", h=256), c_w1[kv].rearrange("(u p) h -> p u h", p=128),
                      writes=[("W1s", kv)], eng="pool")
                P.dma(W2s[kv][:].rearrange("p (a d) -> p a d", d=64), c_w2[kv].rearrange("(a p) d -> p a d", p=128),
                      writes=[("W2s", kv)], eng="pool")
                P.dma(p2b[kv][:], pos2[kv], writes=[("p2b", kv)], eng="pool")
                for half in range(2):
                    for u in range(16):
                        P.op("pe", lambda e, kv=kv, half=half, u=u: e.matmul(
                            pB[:, half:half + 1], lhsT=W1s[kv][:, u * 256 + half * 128:u * 256 + half * 128 + 128],
                            rhs=p2b[kv][:, u:u + 1], start=(u == 0), stop=(u == 15)),
                            reads=[("W1s", kv), ("p2b", kv)], writes=["pB"])
                P.op("dve", lambda e, kv=kv: e.tensor_copy(out=biasv[kv][:], in_=pB[:, 0:2]), reads=["pB"], writes=[("biasv", kv)])
                src = kcT if kv == 0 else vcT
                for g in range(4):
                    kb2 = kc2[g % 2]
                    kk = ("kc2", g % 2)
                    P.dma(kb2[0:64, :], src[g * 64:(g + 1) * 64, :], writes=[(kk, 0)])
                    P.dma(kb2[64:128, 0:S - 1], src[g * 64:(g + 1) * 64, 1:S], writes=[(kk, 1)])
                    for half in range(2):
                        for u in range(16):
                            rhs = bass.AP(kb2, 2 * u, [[S, 128], [16, NCMP]])
                            P.op("pe", lambda e, kv=kv, half=half, u=u, rhs=rhs: e.matmul(
                                pH[half][:, 0:NCMP], lhsT=W1s[kv][:, u * 256 + half * 128:u * 256 + half * 128 + 128],
                                rhs=rhs, start=(u == 0), stop=(u == 15)),
                                reads=[("W1s", kv), (kk, 0), (kk, 1)], writes=[("pH", half)])
                        hsl = h1T[:, half * NCP:half * NCP + NCMP]
                        P.op("dve", lambda e, kv=kv, half=half: e.tensor_scalar(
                            out=xg[:, 0:NCMP], in0=pH[half][:, 0:NCMP], scalar1=biasv[kv][:, half:half + 1], scalar2=None,
                            op0=ALU.add), reads=[("pH", half), ("biasv", kv)], writes=["xg"])
                        P.op("pool", lambda e: e.tensor_tensor(out=x2[:, 0:NCMP], in0=xg[:, 0:NCMP], in1=xg[:, 0:NCMP],
                                                               op=ALU.mult), reads=["xg"], writes=["x2"])
                        P.op("pool", lambda e: e.tensor_scalar(out=x2[:, 0:NCMP], in0=x2[:, 0:NCMP], scalar1=0.044715,
                                                               scalar2=1.0, op0=ALU.mult, op1=ALU.add),
                             reads=["x2"], writes=["x2"])
                        P.op("pool", lambda e: e.tensor_tensor(out=x2[:, 0:NCMP], in0=x2[:, 0:NCMP], in1=xg[:, 0:NCMP],
                                                               op=ALU.mult), reads=["x2", "xg"], writes=["x2"])
                        P.op("act", lambda e: e.activation(out=th[:, 0:NCMP], in_=x2[:, 0:NCMP], func=AF.Tanh,
                                                           scale=0.7978845608028654), reads=["x2"], writes=["th"])
                        P.op("dve", lambda e: e.tensor_scalar(out=th[:, 0:NCMP], in0=th[:, 0:NCMP], scalar1=0.5, scalar2=0.5,
                                                              op0=ALU.mult, op1=ALU.add), reads=["th"], writes=["th"])
                        P.op("dve", lambda e, hsl=hsl: e.tensor_tensor(out=hsl, in0=th[:, 0:NCMP], in1=xg[:, 0:NCMP],
                                                                        op=ALU.mult), reads=["th", "xg"], writes=["h1T"])
                    for nt in range(NCT):
                        for half in range(2):
                            P.op("pe", lambda e, kv=kv, nt=nt, half=half: e.matmul(
                                pC[:, 0:64], lhsT=h1T[:, half * NCP + nt * 128:half * NCP + nt * 128 + 128],
                                rhs=W2s[kv][:, half * 64:(half + 1) * 64], start=(half == 0), stop=(half == 1)),
                                reads=["h1T", ("W2s", kv)], writes=["pC"])
                        if kv == 0:
                            normrope(P, pC[:, 0:64], "pC", 1, gkc[:], csc[:, nt * 64:nt * 64 + 32],
                                     csc[:, nt * 64 + 32:nt * 64 + 64], [("csc", nt, 0), ("csc", nt, 1)],
                                     kcn[:, 0:64], "kcn", TC, "c")
                            P.op("pe", lambda e: e.transpose(out=pKc[:, 0:128], in_=kcn[:], identity=ident[:]),
                                 reads=["kcn", "ident"], writes=["pKc"])
                            P.op("dve", lambda e, g=g, nt=nt: e.tensor_copy(
                                out=KcT[0:64, g * NCP + nt * 128:g * NCP + nt * 128 + 128], in_=pKc[0:64, 0:128]),
                                reads=["pKc"], writes=["KcT"])
                        else:
                            o0 = (nt * 4 + g) * 129
                            P.op("act", lambda e, o0=o0: e.activation(out=Vca[:, o0:o0 + 64], in_=pC[:, 0:64], func=AF.Identity),
                                 reads=["pC"], writes=["Vca"])
            if dbg:
                dK = nc.dram_tensor("dbg_KcT", [64, 4 * NCP], BF16, kind="ExternalOutput").ap()
                dV = nc.dram_tensor("dbg_Vca", [128, NCT * 4 * 129], BF16, kind="ExternalOutput").ap()
                P.dma(dK, KcT[0:64, :], reads=["KcT"])
                P.dma(dV, Vca[:], reads=["Vca"] + [("vca_c", nt, g) for nt in range(NCT) for g in range(4)])
            P.emit()
        if stage <= 3:
            return nc

        with contextlib.ExitStack() as se:
            Qt = [sb(f"e_Qt{i}", [128, SO], BF16, se) for i in range(2)]
            mcmp = sb("e_mcmp", [128, NSL * NCT * 512], BF16, se)
            ET = [sb(f"e_ET{i}", [128, 512], BF16, se) for i in range(2)]
            imp = sb("e_imp", [128, NOWN * 64], F32, se)
            frc = sb("e_frc", [128, NOWN * 64], F32, se)
            rl = sb("e_rl", [128, 4], F32, se)
            impf = sb("e_impf", [128, 64], F32, se)
            tmpf = sb("e_tmpf", [128, 64], F32, se)
            m8 = sb("e_m8", [128, 8], F32, se)
            m8b = sb("e_m8b", [128, 8], F32, se)
            bt = sb("e_bt", [128, 128], BF16, se)
            bstage = sb("e_bstage", [128, SO], BF16, se)
            pS = [ps(f"e_pS{i}", [128, 512], F32, se) for i in range(2)]
            pU = [ps(f"e_pU{i}", [128, 512], F32, se) for i in range(2)]
            pBt = ps("e_pBt", [128, 1024], BF16, se)
            P = Phase(nc, "pe")
            P.dma(mcmp[:].rearrange("p (a q) -> p a q", q=512), m_cmp, writes=["mcmp"])
            P.dma(frc[:].rearrange("p (t j) -> p t j", j=64), forced.rearrange("(t p) j -> p t j", p=128), writes=["frc"])
            P.op("pool", lambda e: e.memset(bt[:], 0.0), writes=["bt"])
            it = 0
            for g in range(4):
                P.op("pool", lambda e: e.memset(imp[:], 0.0), writes=["imp"])
                for h in range(4):
                    hq = 4 * g + h
                    qt = Qt[hq % 2]
                    P.dma(qt[0:64, :], QnT[hq * 64:(hq + 1) * 64, :], writes=[("Qt", hq % 2)])
                    for s_ in range(NSL):
                        for nt in range(NCT):
                            i = it % 2
                            it += 1
                            P.op("pe", lambda e, i=i, g=g, nt=nt, s_=s_, qt=qt: e.matmul(
                                pS[i][:], lhsT=KcT[0:64, g * NCP + nt * 128:g * NCP + nt * 128 + 128],
                                rhs=qt[0:64, s_ * 512:(s_ + 1) * 512], start=True, stop=False),
                                reads=["KcT", ("Qt", hq % 2)], writes=[("pS", i)])
                            P.op("pe", lambda e, i=i, nt=nt, s_=s_: e.matmul(
                                pS[i][:], lhsT=ident[:], rhs=mcmp[:, (s_ * NCT + nt) * 512:(s_ * NCT + nt + 1) * 512],
                                start=False, stop=True), reads=["ident", "mcmp"], writes=[("pS", i)])
                            P.op("act", lambda e, i=i: e.activation(out=ET[i][:], in_=pS[i][:], func=AF.Exp, scale=0.125),
                                 reads=[("pS", i)], writes=[("ET", i)])
                            for qs in range(4):
                                o0 = (nt * 4 + g) * 129
                                P.op("pe", lambda e, i=i, qs=qs, o0=o0, nt=nt: e.matmul(
                                    pU[qs // 2][:, (qs % 2) * 129:(qs % 2) * 129 + 129], lhsT=ET[i][:, qs * 128:(qs + 1) * 128],
                                    rhs=Vca[:, o0:o0 + 129], start=(nt == 0 and qs % 2 == 0), stop=(nt == NCT - 1),
                                    skip_group_check=True),
                                    reads=[("ET", i), "Vca"], writes=[("pU", qs // 2)])
                        for qs in range(4):
                            tl = s_ * 4 + qs
                            u0 = (qs % 2) * 129
                            pu = pU[qs // 2]
                            pk = ("pU", qs // 2)
                            P.op("dve", lambda e, pu=pu, u0=u0, qs=qs: e.tensor_scalar(
                                out=rl[:, qs:qs + 1], in0=pu[:, u0 + 128:u0 + 129], scalar1=1e-30, scalar2=None, op0=ALU.max),
                                reads=[pk], writes=["rl"])
                            P.op("dve", lambda e, qs=qs: e.reciprocal(out=rl[:, qs:qs + 1], in_=rl[:, qs:qs + 1]),
                                 reads=["rl"], writes=["rl"])
                            ysl = ynsa[:, tl * 1024 + hq * 64:tl * 1024 + hq * 64 + 64]
                            gcol = gates[:, tl * 48 + hq * 3:tl * 48 + hq * 3 + 1]
                            P.op("dve", lambda e, pu=pu, u0=u0, qs=qs, ysl=ysl, gcol=gcol: e.tensor_scalar(
                                out=ysl, in0=pu[:, u0:u0 + 64], scalar1=rl[:, qs:qs + 1], scalar2=gcol,
                                op0=ALU.mult, op1=ALU.mult), reads=[pk, "rl", "gates"], writes=[("ynsa", tl)])
                            isl = imp[:, tl * 64:(tl + 1) * 64]
                            P.op("dve", lambda e, pu=pu, u0=u0, qs=qs, isl=isl: e.scalar_tensor_tensor(
                                out=isl, in0=pu[:, u0 + 64:u0 + 128], scalar=rl[:, qs:qs + 1], in1=isl,
                                op0=ALU.mult, op1=ALU.add), reads=[pk, "rl", "imp"], writes=["imp"])
                for tl in range(NOWN):
                    isl = imp[:, tl * 64:(tl + 1) * 64]
                    P.op("dve", lambda e, isl=isl, tl=tl: e.tensor_tensor(out=impf[:], in0=isl, in1=frc[:, tl * 64:(tl + 1) * 64],
                                                                          op=ALU.max), reads=["imp", "frc"], writes=["impf"])
                    P.op("dve", lambda e: e.max(out=m8[:], in_=impf[:]), reads=["impf"], writes=["m8"])
                    P.op("dve", lambda e: e.match_replace(out=tmpf[:], in_to_replace=m8[:], in_values=impf[:], imm_value=-1e9),
                         reads=["m8", "impf"], writes=["tmpf"])
                    P.op("dve", lambda e: e.max(out=m8b[:], in_=tmpf[:]), reads=["tmpf"], writes=["m8b"])
                    P.op("dve", lambda e: e.tensor_scalar(out=tmpf[:], in0=impf[:], scalar1=m8b[:, 7:8], scalar2=None,
                                                          op0=ALU.is_ge), reads=["impf", "m8b", "tmpf"], writes=["tmpf"])
                    P.op("dve", lambda e: e.tensor_scalar(out=bt[:, 64:128], in0=tmpf[:], scalar1=-1.0, scalar2=-NEG,
                                                          op0=ALU.add, op1=ALU.mult), reads=["tmpf"], writes=["bt"])
                    P.op("pe", lambda e: e.transpose(out=pBt[:, 0:128], in_=bt[:], identity=ident[:]),
                         reads=["bt", "ident"], writes=["pBt"])
                    P.op("dve", lambda e, tl=tl: e.tensor_copy(out=bstage[64:128, tl * 128:(tl + 1) * 128], in_=pBt[64:128, 0:128]),
                         reads=["pBt"], writes=["bstage"])
                P.dma(BsT[g * 64:(g + 1) * 64, :], bstage[64:128, :], reads=["bstage"])
            P.emit()
        if stage <= 4:
            return nc

        with contextlib.ExitStack() as sf:
            KD = [sb(f"f_KD{i}", [128, S], BF16, sf) for i in range(2)]
            VD = [sb(f"f_VD{i}", [128, NT * 129], BF16, sf) for i in range(2)]
            QD = [sb(f"f_QD{i}", [128, SO], BF16, sf) for i in range(2)]
            KS = sb("f_KS", [128, S], BF16, sf)
            KW = sb("f_KW", [128, S], BF16, sf)
            VS = sb("f_VS", [128, NT * 65], BF16, sf)
            VW = sb("f_VW", [128, NT * 65], BF16, sf)
            QS = [sb(f"f_QS{i}", [128, SO], BF16, sf) for i in range(2)]
            mden = sb("f_mden", [128, 8 * 512], BF16, sf)
            mwin = sb("f_mwin", [128, 12 * 512], BF16, sf)
            PT = [sb(f"f_PT{i}", [128, 512], BF16, sf) for i in range(4)]
            oa4 = sb("f_oa4", [128, 512], F32, sf)
            ob4 = sb("f_ob4", [128, 512], F32, sf)
            r8 = sb("f_r8", [128, 12], F32, sf)
            ssd4 = sb("f_ssd4", [128, 4], F32, sf)
            rsd4 = sb("f_rsd4", [128, 4], F32, sf)
            jk = sb("f_jk", [128, 128], F32, sf)
            r4 = sb("f_r4", [128, 4], F32, sf)
            ssd = sb("f_ssd", [128, 1], F32, sf)
            rsd = sb("f_rsd", [128, 1], F32, sf)
            tn = sb("f_tn", [128, 64], F32, sf)
            pS = [ps(f"f_pS{i}", [128, 512], F32, sf) for i in range(4)]
            pO = [ps(f"f_pO{i}", [128, 512], F32, sf) for i in range(3)]
            P = Phase(nc, "pf")
            P.dma(mden[:].rearrange("p (a q) -> p a q", q=512), m_dense, writes=["mden"])
            P.dma(mwin[:].rearrange("p (a q) -> p a q", q=512), m_win, writes=["mwin"])
            P.dma(KS[64:128, :], eind, writes=["KS_e"])
            for i in range(2):
                P.op("pool", lambda e, i=i: e.memset(VD[i][:], 1.0), writes=[("VD", i)])
            P.op("pool", lambda e: e.memset(VS[:], 1.0), writes=["VS"])
            P.op("pool", lambda e: e.memset(VW[:], 1.0), writes=["VW"])

            def load_v(dst, dkey, src2d, width, stride):
                d3 = dst[:].rearrange("p (t c) -> p t c", c=stride)
                s3 = src2d.rearrange("(t p) c -> p t c", p=128)
                step = 8
                for a in range(0, NT, step):
                    b_ = min(NT, a + step)
                    P.dma(d3[:, a:b_, 0:width], s3[:, a:b_, :], writes=[dkey])

            def load_diff(h):
                i = h % 2
                P.dma(KD[i][:], KdT[h], writes=[("KD", i)])
                load_v(VD[i], ("VD", i), Vd[:, h * 128:(h + 1) * 128], 128, 129)
                P.dma(QD[i][:], QdT[h], writes=[("QD", i)])

            load_diff(0)
            for h in range(8):
                if h + 1 < 8:
                    load_diff(h + 1)
                bi = h % 2
                for s_ in range(NSL):
                    nkt = 8 * (s_ + 1)

                    def qk(kt, s_=s_, nkt=nkt, bi=bi):
                        i2 = kt % 2
                        masked = kt >= nkt - 8
                        mi = kt - (nkt - 8)
                        for m in range(2):
                            bk = 2 * i2 + m
                            P.op("pe", lambda e, bk=bk, m=m, kt=kt: e.matmul(
                                pS[bk][:], lhsT=KD[bi][m * 64:(m + 1) * 64, kt * 128:(kt + 1) * 128],
                                rhs=QD[bi][m * 64:(m + 1) * 64, s_ * 512:(s_ + 1) * 512], start=True, stop=True),
                                reads=[("KD", bi), ("QD", bi)], writes=[("pS", bk)])
                            P.op("act", lambda e, bk=bk: e.activation(out=PT[bk][:], in_=pS[bk][:], func=AF.Exp, scale=0.125),
                                 reads=[("pS", bk)], writes=[("PT", bk)])
                            if masked:
                                P.op("dve", lambda e, bk=bk, mi=mi: e.tensor_tensor(
                                    out=PT[bk][:], in0=PT[bk][:], in1=mden[:, mi * 512:(mi + 1) * 512], op=ALU.mult),
                                    reads=[("PT", bk), "mden"], writes=[("PT", bk)])

                    def pv(kt, s_=s_, nkt=nkt, bi=bi):
                        i2 = kt % 2
                        for m in range(2):
                            bk = 2 * i2 + m
                            for qs in range(4):
                                a = m * 4 + qs
                                P.op("pe", lambda e, bk=bk, qs=qs, a=a, kt=kt: e.matmul(
                                    pO[a // 3][:, (a % 3) * 129:(a % 3) * 129 + 129], lhsT=PT[bk][:, qs * 128:(qs + 1) * 128],
                                    rhs=VD[bi][:, kt * 129:kt * 129 + 129], start=(kt == 0 and a % 3 == 0), stop=(kt == nkt - 1),
                                    skip_group_check=True),
                                    reads=[("PT", bk), ("VD", bi)], writes=[("pO", a // 3)])

                    qk(0)
                    for kt in range(nkt):
                        if kt + 1 < nkt:
                            qk(kt + 1)
                        pv(kt)
                    okeys = [("pO", 0), ("pO", 1), ("pO", 2)]
                    ov = lambda a: pO[a // 3][:, (a % 3) * 129:(a % 3) * 129 + 129]
                    for qs in range(4):
                        P.op("dve", lambda e, qs=qs: e.reciprocal(out=r8[:, qs:qs + 1], in_=ov(qs)[:, 128:129]),
                             reads=okeys, writes=["r8"])
                        P.op("dve", lambda e, qs=qs: e.reciprocal(out=r8[:, 4 + qs:5 + qs], in_=ov(4 + qs)[:, 128:129]),
                             reads=okeys + ["r8"], writes=["r8"])
                    P.op("dve", lambda e: e.tensor_scalar(out=r8[:, 8:12], in0=r8[:, 4:8], scalar1=nlam[:, 0:1], scalar2=None,
                                                          op0=ALU.mult), reads=["r8", "nlam"], writes=["r8"])
                    for qs in range(4):
                        P.op("dve", lambda e, qs=qs: e.tensor_scalar(out=oa4[:, qs * 128:(qs + 1) * 128], in0=ov(qs)[:, 0:128],
                                                                     scalar1=r8[:, qs:qs + 1], scalar2=None, op0=ALU.mult),
                             reads=okeys + ["r8"], writes=["oa4"])
                        P.op("dve", lambda e, qs=qs: e.scalar_tensor_tensor(
                            out=ob4[:, qs * 128:(qs + 1) * 128], in0=ov(4 + qs)[:, 0:128], scalar=r8[:, 8 + qs:9 + qs],
                            in1=oa4[:, qs * 128:(qs + 1) * 128], op0=ALU.mult, op1=ALU.add),
                            reads=okeys + ["r8", "oa4"], writes=["ob4"])
                    P.op("pool", lambda e: e.memset(ssd4[:], 0.0), writes=["ssd4"])
                    for qs in range(4):
                        P.op("act", lambda e, qs=qs: e.activation(out=jk[:], in_=ob4[:, qs * 128:(qs + 1) * 128], func=AF.Square,
                                                                  accum_out=ssd4[:, qs:qs + 1]),
                             reads=["ob4", "ssd4"], writes=["jk", "ssd4"])
                    P.op("dve", lambda e: e.tensor_scalar(out=ssd4[:], in0=ssd4[:], scalar1=1.0 / 128, scalar2=EPS,
                                                          op0=ALU.mult, op1=ALU.add), reads=["ssd4"], writes=["ssd4"])
                    P.op("act", lambda e: e.activation(out=ssd4[:], in_=ssd4[:], func=AF.Ln), reads=["ssd4"], writes=["ssd4"])
                    P.op("act", lambda e: e.activation(out=rsd4[:], in_=ssd4[:], func=AF.Exp, scale=-0.5),
                         reads=["ssd4"], writes=["rsd4"])
                    for qs in range(4):
                        tl = s_ * 4 + qs
                        ysl = ydiff[:, tl * 1024 + h * 128:tl * 1024 + (h + 1) * 128]
                        P.op("dve", lambda e, ysl=ysl, qs=qs: e.scalar_tensor_tensor(
                            out=ysl, in0=ob4[:, qs * 128:(qs + 1) * 128], scalar=rsd4[:, qs:qs + 1], in1=sgain[:],
                            op0=ALU.mult, op1=ALU.mult), reads=["ob4", "rsd4", "sgain"], writes=[("ydiff", tl)])

            cnt = [0]
            ucnt = [0]
            for g in range(4):
                P.dma(KS[0:64, :], KsT[g * 64:(g + 1) * 64, :], writes=["KS"])
                P.dma(KW[0:64, :], KwT[g * 64:(g + 1) * 64, :], writes=["KW"])
                load_v(VS, "VS", Vs[:, g * 64:(g + 1) * 64], 64, 65)
                load_v(VW, "VW", Vw[:, g * 64:(g + 1) * 64], 64, 65)
                for h in range(4):
                    hq = 4 * g + h
                    qi = hq % 2
                    P.dma(QS[qi][0:64, :], QnT[hq * 64:(hq + 1) * 64, :], writes=[("QS", qi, 0)])
                    P.dma(QS[qi][64:128, :], BsT[g * 64:(g + 1) * 64, :], writes=[("QS", qi, 1)])
                    for kind in (2, 1):
                        for s_ in range(NSL):
                            if kind == 1:
                                kts = list(range(0, 8 * (s_ + 1)))
                                mis = [kt - 8 * s_ if kt >= 8 * s_ else None for kt in kts]
                            else:
                                kts = list(range(max(0, 8 * s_ - 4), 8 * s_ + 8))
                                mis = [kt - (8 * s_ - 4) for kt in kts]
                            banks = []
                            for _ in kts:
                                banks.append(cnt[0] % 4)
                                cnt[0] += 1
                            ob_i = ucnt[0] % 3
                            ucnt[0] += 1

                            def qk(j, kind=kind, s_=s_, kts=kts, mis=mis, banks=banks, qi=qi):
                                kt, mi, bk = kts[j], mis[j], banks[j]
                                if kind == 1:
                                    lhsT = KS[:, kt * 128:(kt + 1) * 128]
                                    rhs = QS[qi][:, s_ * 512:(s_ + 1) * 512]
                                    rd = ["KS", "KS_e", ("QS", qi, 0), ("QS", qi, 1)]
                                    mt = mden
                                else:
                                    lhsT = KW[0:64, kt * 128:(kt + 1) * 128]
                                    rhs = QS[qi][0:64, s_ * 512:(s_ + 1) * 512]
                                    rd = ["KW", ("QS", qi, 0)]
                                    mt = mwin
                                P.op("pe", lambda e: e.matmul(pS[bk][:], lhsT=lhsT, rhs=rhs, start=True, stop=True),
                                     reads=rd, writes=[("pS", bk)])
                                P.op("act", lambda e: e.activation(out=PT[bk][:], in_=pS[bk][:], func=AF.Exp, scale=0.125),
                                     reads=[("pS", bk)], writes=[("PT", bk)])
                                if mi is not None:
                                    P.op("dve", lambda e: e.tensor_tensor(out=PT[bk][:], in0=PT[bk][:], in1=mt[:, mi * 512:(mi + 1) * 512],
                                                                          op=ALU.mult),
                                         reads=[("PT", bk), "mden", "mwin"], writes=[("PT", bk)])

                            def pv(j, kind=kind, kts=kts, banks=banks, ob_i=ob_i):
                                kt, bk = kts[j], banks[j]
                                vt, vk = (VS, "VS") if kind == 1 else (VW, "VW")
                                for qs in range(4):
                                    P.op("pe", lambda e, qs=qs: e.matmul(
                                        pO[ob_i][:, qs * 65:qs * 65 + 65], lhsT=PT[bk][:, qs * 128:(qs + 1) * 128],
                                        rhs=vt[:, kt * 65:kt * 65 + 65], start=(j == 0 and qs == 0), stop=(j == len(kts) - 1),
                                        skip_group_check=True),
                                        reads=[("PT", bk), vk], writes=[("pO", ob_i)])

                            qk(0)
                            for j in range(len(kts)):
                                if j + 1 < len(kts):
                                    qk(j + 1)
                                pv(j)
                            for qs in range(4):
                                tl = s_ * 4 + qs
                                oq = pO[ob_i][:, qs * 65:qs * 65 + 65]
                                P.op("dve", lambda e, oq=oq, qs=qs: e.tensor_scalar(
                                    out=r4[:, qs:qs + 1], in0=oq[:, 64:65], scalar1=1e-30, scalar2=None, op0=ALU.max),
                                    reads=[("pO", ob_i)], writes=["r4"])
                                P.op("dve", lambda e, qs=qs: e.reciprocal(out=r4[:, qs:qs + 1], in_=r4[:, qs:qs + 1]),
                                     reads=["r4"], writes=["r4"])
                                gcol = gates[:, tl * 48 + hq * 3 + kind:tl * 48 + hq * 3 + kind + 1]
                                P.op("dve", lambda e, oq=oq, qs=qs, gcol=gcol: e.tensor_scalar(
                                    out=tn[:], in0=oq[:, 0:64], scalar1=r4[:, qs:qs + 1], scalar2=gcol,
                                    op0=ALU.mult, op1=ALU.mult), reads=[("pO", ob_i), "r4", "gates"], writes=["tn"])
                                ysl = ynsa[:, tl * 1024 + hq * 64:tl * 1024 + hq * 64 + 64]
                                P.op("pool", lambda e, ysl=ysl: e.tensor_tensor(out=ysl, in0=ysl, in1=tn[:], op=ALU.add),
                                     reads=["tn", ("ynsa", tl)], writes=[("ynsa", tl)])
            if dbg:
                dY = nc.dram_tensor("dbg_ydiff", [128, NOWN * 1024], BF16, kind="ExternalOutput").ap()
                dN = nc.dram_tensor("dbg_ynsa", [128, NOWN * 1024], BF16, kind="ExternalOutput").ap()
                P.dma(dY, ydiff[:], reads=[("ydiff", t) for t in range(NOWN)])
                P.dma(dN, ynsa[:], reads=[("ynsa", t) for t in range(NOWN)])
            P.emit()
        if stage <= 5:
            return nc

        with contextlib.ExitStack() as sg:
            Wpd = sb("g_Wpd", [128, 8 * 1024], BF16, sg)
            Wpn = sb("g_Wpn", [128, 8 * 1024], BF16, sg)
            Wo = sb("g_Wo", [128, 8 * 1024], BF16, sg)
            ydT = sb("g_ydT", [128, 1024], BF16, sg)
            ynT = sb("g_ynT", [128, 1024], BF16, sg)
            mgl = [sb(f"g_mgl{i}", [128, 2048], F32, sg) for i in range(2)]
            xql = [sb(f"g_xql{i}", [128, 1024], F32, sg) for i in range(2)]
            m1 = sb("g_m1", [128, 1024], F32, sg)
            m2 = sb("g_m2", [128, 1024], F32, sg)
            mixb = sb("g_mixb", [128, 1024], BF16, sg)
            mixT = sb("g_mixT", [128, 1024], BF16, sg)
            x1t = [sb(f"g_x1t{i}", [128, 1024], F32, sg) for i in range(2)]
            pTa = ps("g_pTa", [128, 1024], BF16, sg)
            pTb = ps("g_pTb", [128, 1024], BF16, sg)
            pD = [ps(f"g_pD{i}", [128, 512], F32, sg) for i in range(2)]
            pN = [ps(f"g_pN{i}", [128, 512], F32, sg) for i in range(2)]
            pW = [ps(f"g_pW{i}", [128, 512], F32, sg) for i in range(2)]
            P = Phase(nc, "pg")
            for (wd, wsrc, wk_) in ((Wpd, w_pd, "Wpd"), (Wpn, w_pn, "Wpn"), (Wo, w_o, "Wo")):
                for hc in range(2):
                    P.dma(wd[:, hc * 4096:(hc + 1) * 4096].rearrange("p (c n) -> p c n", n=1024),
                          wsrc[hc * 512:(hc + 1) * 512, :].rearrange("(c p) n -> p c n", p=128), writes=[(wk_, hc)], eng="pool")
            for t in range(NOWN):
                tok = slice(t * 128, (t + 1) * 128)
                mg_ = mgl[t % 2]
                xq_ = xql[t % 2]
                P.dma(mg_[:], mgS[tok, :], writes=[("mgl", t % 2)])
                P.dma(xq_[:], xq[tok, :], writes=[("xql", t % 2)])
                for c in range(8):
                    P.op("pe", lambda e, c=c, t=t: e.transpose(out=pTa[:, c * 128:(c + 1) * 128],
                                                              in_=ydiff[:, t * 1024 + c * 128:t * 1024 + (c + 1) * 128], identity=ident[:]),
                         reads=["ident"], writes=["pTa"])
                P.op("dve", lambda e: e.tensor_copy(out=ydT[:], in_=pTa[:]), reads=["pTa"], writes=["ydT"])
                for c in range(8):
                    P.op("pe", lambda e, c=c, t=t: e.transpose(out=pTb[:, c * 128:(c + 1) * 128],
                                                              in_=ynsa[:, t * 1024 + c * 128:t * 1024 + (c + 1) * 128], identity=ident[:]),
                         reads=["ident"], writes=["pTb"])
                P.op("dve", lambda e: e.tensor_copy(out=ynT[:], in_=pTb[:]), reads=["pTb"], writes=["ynT"])
                for hf in range(2):
                    for c in range(8):
                        P.op("pe", lambda e, c=c, hf=hf: e.matmul(pD[hf][:], lhsT=ydT[:, c * 128:(c + 1) * 128],
                                                                  rhs=Wpd[:, c * 1024 + hf * 512:c * 1024 + (hf + 1) * 512],
                                                                  start=(c == 0), stop=(c == 7)),
                             reads=["ydT", ("Wpd", c // 4)], writes=[("pD", hf)])
                    for c in range(8):
                        P.op("pe", lambda e, c=c, hf=hf: e.matmul(pN[hf][:], lhsT=ynT[:, c * 128:(c + 1) * 128],
                                                                  rhs=Wpn[:, c * 1024 + hf * 512:c * 1024 + (hf + 1) * 512],
                                                                  start=(c == 0), stop=(c == 7)),
                             reads=["ynT", ("Wpn", c // 4)], writes=[("pN", hf)])
                    P.op("dve", lambda e, hf=hf, mg_=mg_: e.tensor_tensor(out=m1[:, hf * 512:(hf + 1) * 512], in0=pD[hf][:],
                                                                          in1=mg_[:, hf * 512:(hf + 1) * 512], op=ALU.mult),
                         reads=[("pD", hf), ("mgl", t % 2)], writes=[("m1", hf)])
                    P.op("dve", lambda e, hf=hf, mg_=mg_: e.tensor_tensor(out=m2[:, hf * 512:(hf + 1) * 512], in0=pN[hf][:],
                                                                          in1=mg_[:, 1024 + hf * 512:1024 + (hf + 1) * 512], op=ALU.mult),
                         reads=[("pN", hf), ("mgl", t % 2)], writes=[("m2", hf)])
                    P.op("pool", lambda e, hf=hf: e.tensor_tensor(out=mixb[:, hf * 512:(hf + 1) * 512], in0=m1[:, hf * 512:(hf + 1) * 512],
                                                                  in1=m2[:, hf * 512:(hf + 1) * 512], op=ALU.add),
                         reads=[("m1", hf), ("m2", hf)], writes=[("mixb", hf)])
                for c in range(8):
                    P.op("pe", lambda e, c=c: e.transpose(out=pTa[:, c * 128:(c + 1) * 128], in_=mixb[:, c * 128:(c + 1) * 128],
                                                          identity=ident[:]),
                         reads=[("mixb", c // 4), "ident"], writes=["pTa"])
                P.op("dve", lambda e: e.tensor_copy(out=mixT[:], in_=pTa[:]), reads=["pTa"], writes=["mixT"])
                x1_ = x1t[t % 2]
                for hf in range(2):
                    for c in range(8):
                        P.op("pe", lambda e, c=c, hf=hf: e.matmul(pW[hf][:], lhsT=mixT[:, c * 128:(c + 1) * 128],
                                                                  rhs=Wo[:, c * 1024 + hf * 512:c * 1024 + (hf + 1) * 512],
                                                                  start=(c == 0), stop=(c == 7)),
                             reads=["mixT", ("Wo", c // 4)], writes=[("pW", hf)])
                    P.op("dve", lambda e, hf=hf, x1_=x1_, xq_=xq_: e.tensor_tensor(out=x1_[:, hf * 512:(hf + 1) * 512], in0=pW[hf][:],
                                                                                  in1=xq_[:, hf * 512:(hf + 1) * 512], op=ALU.add),
                         reads=[("pW", hf), ("xql", t % 2)], writes=[("x1t", t % 2, hf)])
                P.dma(x1d[tok, :], x1_[:], reads=[("x1t", t % 2, 0), ("x1t", t % 2, 1)])
            P.emit()
        if stage <= 6:
            return nc
    with contextlib.ExitStack() as top2:
        def sb2(name, shape, dt):
            return top2.enter_context(nc.sbuf_tensor(name, list(shape), dt))

        ident2 = sb2("ident2", [128, 128], BF16)
        Wup = sb2("h_Wup", [128, 8 * 4096], BF16)
        Wdn = sb2("h_Wdn", [128, 32 * 1024], BF16)
        gml = sb2("h_gml", [128, 1024], F32)
        x1g = sb2("h_x1g", [128, 4 * 1024], F32)
        junk2 = sb2("h_junk", [128, 1024], F32)
        h2 = sb2("h_h2", [128, 1024], BF16)
        h2T = sb2("h_h2T", [128, 8 * 512], BF16)
        uT = sb2("h_uT", [128, 32 * 512], BF16)
        rr = [sb2(f"h_rr{i}", [128, 512], F32) for i in range(2)]
        ot = [sb2(f"h_ot{i}", [128, 512], F32) for i in range(2)]
        ss2 = sb2("h_ss", [128, 1], F32)
        rs2 = sb2("h_rs", [128, 1], F32)
        pT2 = top2.enter_context(nc.psum_tensor("h_pT", [128, 1024], BF16))
        pU2 = [top2.enter_context(nc.psum_tensor(f"h_pU{i}", [128, 512], F32)) for i in range(2)]
        pDn = [top2.enter_context(nc.psum_tensor(f"h_pDn{i}", [128, 512], F32)) for i in range(2)]
        P = Phase(nc, "ph")
        P.dma(ident2[:], identd, writes=["ident"])
        P.dma(gml[:], g_mlp[0:1, :].broadcast_to([128, 1024]), writes=["gt"])
        for c in range(8):
            P.dma(Wup[:, c * 4096:(c + 1) * 4096], w_up[c * 128:(c + 1) * 128, :], writes=[("Wup", c)], eng="pool")
        for f4 in range(8):
            P.dma(Wdn[:, f4 * 4096:(f4 + 1) * 4096].rearrange("p (a n) -> p a n", n=1024),
                  w_dn[f4 * 512:(f4 + 1) * 512, :].rearrange("(a p) n -> p a n", p=128), writes=[("Wdn", f4)], eng="pool")
        for gq in range(NSL):
            rows = slice(gq * 512, (gq + 1) * 512)
            P.dma(x1g[:].rearrange("p (t n) -> p t n", n=1024), x1d[rows, :].rearrange("(t p) n -> p t n", p=128), writes=["x1g"])
            for j in range(4):
                xt_ = x1g[:, j * 1024:(j + 1) * 1024]
                P.op("pool", lambda e: e.memset(ss2[:], 0.0), writes=["ss2"])
                P.op("act", lambda e, xt_=xt_: e.activation(out=junk2[:], in_=xt_, func=AF.Square, accum_out=ss2[:]),
                     reads=["x1g", "ss2"], writes=["junk2", "ss2"])
                P.op("dve", lambda e: e.tensor_scalar(out=ss2[:], in0=ss2[:], scalar1=1.0 / 1024, scalar2=EPS,
                                                      op0=ALU.mult, op1=ALU.add), reads=["ss2"], writes=["ss2"])
                P.op("act", lambda e: e.activation(out=ss2[:], in_=ss2[:], func=AF.Ln), reads=["ss2"], writes=["ss2"])
                P.op("act", lambda e: e.activation(out=rs2[:], in_=ss2[:], func=AF.Exp, scale=-0.5), reads=["ss2"], writes=["rs2"])
                P.op("dve", lambda e, xt_=xt_: e.scalar_tensor_tensor(out=h2[:], in0=xt_, scalar=rs2[:], in1=gml[:],
                                                                     op0=ALU.mult, op1=ALU.mult),
                     reads=["x1g", "rs2", "gt"], writes=["h2"])
                for c in range(8):
                    P.op("pe", lambda e, c=c: e.transpose(out=pT2[:, c * 128:(c + 1) * 128], in_=h2[:, c * 128:(c + 1) * 128],
                                                          identity=ident2[:]), reads=["h2", "ident"], writes=["pT2"])
                P.op("dve", lambda e, j=j: e.tensor_copy(
                    out=h2T[:].rearrange("p (c q) -> p c q", q=512)[:, :, j * 128:(j + 1) * 128],
                    in_=pT2[:].rearrange("p (c q) -> p c q", q=128)), reads=["pT2"], writes=["h2T"])
            for f in range(32):
                pu = pU2[f % 2]
                for c in range(8):
                    P.op("pe", lambda e, c=c, f=f, pu=pu: e.matmul(pu[:], lhsT=Wup[:, c * 4096 + f * 128:c * 4096 + (f + 1) * 128],
                                                                  rhs=h2T[:, c * 512:(c + 1) * 512], start=(c == 0), stop=(c == 7)),
                         reads=[("Wup", c), "h2T"], writes=[("pU2", f % 2)])
                r_ = rr[f % 2]
                P.op("act", lambda e, pu=pu, r_=r_: e.activation(out=r_[:], in_=pu[:], func=AF.Relu),
                     reads=[("pU2", f % 2)], writes=[("rr", f % 2)])
                P.op("pool" if f % 2 else "dve", lambda e, f=f, r_=r_: e.tensor_tensor(out=uT[:, f * 512:(f + 1) * 512], in0=r_[:], in1=r_[:],
                                                                                     op=ALU.mult),
                     reads=[("rr", f % 2)], writes=[("uT", f)])
            for j in range(4):
                for hf in range(2):
                    i = (j * 2 + hf) % 2
                    for f in range(32):
                        P.op("pe", lambda e, f=f, j=j, hf=hf, i=i: e.matmul(
                            pDn[i][:], lhsT=uT[:, f * 512 + j * 128:f * 512 + (j + 1) * 128],
                            rhs=Wdn[:, f * 1024 + hf * 512:f * 1024 + (hf + 1) * 512], start=(f == 0), stop=(f == 31)),
                            reads=[("uT", f), ("Wdn", f // 4)], writes=[("pDn", i)])
                    o_ = ot[i]
                    P.op("dve", lambda e, i=i, o_=o_, j=j, hf=hf: e.tensor_tensor(
                        out=o_[:], in0=pDn[i][:], in1=x1g[:, j * 1024 + hf * 512:j * 1024 + (hf + 1) * 512], op=ALU.add),
                        reads=[("pDn", i), "x1g"], writes=[("ot", i)])
                    P.dma(out[gq * 512 + j * 128:gq * 512 + (j + 1) * 128, hf * 512:(hf + 1) * 512], o_[:], reads=[("ot", i)])
        P.emit()
    return nc


def _rope_tab(pos):
    inv = (10000.0 ** (-(np.arange(32, dtype=np.float32)) / np.float32(32))).astype(np.float32)
    ang = pos.astype(np.float32)[:, None] * inv[None, :]
    return np.cos(ang).astype(np.float32), np.sin(ang).astype(np.float32)


def make_core_inputs(inp, core, S):
    b, par = core // 2, core % 2
    NSL = S // 1024
    SO = NSL * 512
    NCMP = (S - 32) // 16 + 1
    NCT = (NCMP + 127) // 128
    NCP = NCT * 128
    f = lambda a: np.ascontiguousarray(np.asarray(a, dtype=np.float32))
    x = np.asarray(inp["x"], dtype=np.float32)
    own_pos = np.concatenate([np.arange((2 * s + par) * 512, (2 * s + par) * 512 + 512) for s in range(NSL)])
    d = {}
    d["xkv"] = f(x[b])
    d["xq"] = f(x[b][own_pos])
    d["w_in"] = f(inp["w_in"][0])
    d["w_pd"] = f(inp["w_proj_diff"][0])
    d["w_pn"] = f(inp["w_proj_nsa"][0])
    d["w_o"] = f(inp["w_out"][0])
    d["w_up"] = f(inp["w_mlp_up"][0])
    d["w_dn"] = f(inp["w_mlp_down"][0])
    d["c_w1k"] = f(inp["cmp_k_w1"][0])
    d["c_w1v"] = f(inp["cmp_v_w1"][0])
    d["c_w2k"] = f(inp["cmp_k_w2"][0])
    d["c_w2v"] = f(inp["cmp_v_w2"][0])
    p2 = lambda p: f(np.asarray(p, np.float32).reshape(16, 2, 64).transpose(1, 2, 0).reshape(128, 16))
    d["pos2k"] = p2(inp["cmp_pos_k"][0])
    d["pos2v"] = p2(inp["cmp_pos_v"][0])
    d["g_mix"] = f(inp["ln_mix_g"][0][None, :])
    d["g_mlp"] = f(inp["ln_mlp_g"][0][None, :])
    gk = np.asarray(inp["nsa_k_norm_g"][0], np.float32)
    d["g_k24"] = f(np.concatenate([np.tile(np.asarray(inp["diff_k_norm_g"][0], np.float32), 16),
                                   np.tile(gk[1], 4), np.tile(gk[2], 4)])[None, :])
    d["g_q32"] = f(np.concatenate([np.tile(np.asarray(inp["diff_q_norm_g"][0], np.float32), 16),
                                   np.tile(np.asarray(inp["nsa_q_norm_g"][0], np.float32), 16)])[None, :])
    d["g_kc"] = f(gk[0][None, :])
    d["g_sub"] = f(inp["diff_subln_g"][0][None, :])
    d["lam4"] = f(np.concatenate([np.asarray(inp[k][0], np.float32) for k in
                                  ("diff_lambda_q1", "diff_lambda_k1", "diff_lambda_q2", "diff_lambda_k2")])[None, :])
    d["rkv_c"], d["rkv_s"] = _rope_tab(np.arange(S))
    d["rq_c"], d["rq_s"] = _rope_tab(own_pos)
    cc = np.zeros(NCP, np.float32)
    cc[:NCMP] = np.arange(NCMP) * 16 + 15.5
    d["rc_c"], d["rc_s"] = _rope_tab(cc)
    d["ident"] = np.eye(128, dtype=np.float32).astype(NPBF)
    kk = np.arange(S)
    d["eind"] = (kk[None, :] // 64 == np.arange(64)[:, None]).astype(np.float32).astype(NPBF)
    n = np.arange(NCP)
    cs, ce = n * 16, n * 16 + 31
    ss_ = np.arange(64) * 64
    ov = ((cs[:, None] < ss_[None, :] + 64) & (ce[:, None] >= ss_[None, :]) & (n[:, None] < NCMP)).astype(np.float32)
    d["ovl"] = np.concatenate([ov, np.ones((NCP, 1), np.float32)], axis=1).astype(NPBF)
    k128 = np.arange(128)[:, None, None]
    q512 = np.arange(512)[None, None, :]
    kw = np.arange(8)[None, :, None] * 128 + k128
    d["m_dense"] = np.where((kw - par * 512) <= q512, 1.0, 0.0).astype(np.float32).astype(NPBF)
    kw = np.arange(12)[None, :, None] * 128 + k128 - 512
    dd = par * 512 + q512 - kw
    d["m_win"] = np.where((dd >= 0) & (dd < 512), 1.0, 0.0).astype(np.float32).astype(NPBF)
    mc = np.zeros((128, NSL * NCT, 512), np.float32)
    for s in range(NSL):
        for nt in range(NCT):
            nn = nt * 128 + np.arange(128)[:, None]
            qpos = (2 * s + par) * 512 + np.arange(512)[None, :]
            mc[:, s * NCT + nt, :] = np.where((nn < NCMP) & (nn * 16 + 31 <= qpos), 0.0, NEG)
    d["m_cmp"] = mc.astype(NPBF)
    fo = np.zeros((SO, 64), np.float32)
    cur = own_pos // 64
    r = np.arange(SO)
    fo[r[cur >= 1], (cur - 1)[cur >= 1]] = 1e4
    fo[r, cur] = 2e4
    fo[:, 0] = 3e4
    d["forced"] = fo
    return d


_NC_CACHE = {}


def kernel(**inputs):
    S = int(np.asarray(inputs["x"]).shape[1])
    B = int(np.asarray(inputs["x"]).shape[0])
    ncores = 2 * B
    if S not in _NC_CACHE:
        _NC_CACHE[S] = build(S)
    nc = _NC_CACHE[S]
    in_maps = [make_core_inputs(inputs, c, S) for c in range(ncores)]
    res = run_bass_kernel_spmd(nc, in_maps, core_ids=list(range(ncores)))
    out = np.zeros((B, S, 1024), np.float32)
    NSL = S // 1024
    for c in range(ncores):
        b, par = c // 2, c % 2
        o = np.asarray(res.results[c]["out"], dtype=np.float32)
        for s in range(NSL):
            ch = 2 * s + par
            out[b, ch * 512:(ch + 1) * 512] = o[s * 512:(s + 1) * 512]
    return out
```

```python
import contextlib
import numpy as np
import ml_dtypes
import concourse.bass as bass
import concourse.mybir as mybir
from concourse.bass_utils import run_bass_kernel_spmd

F32 = mybir.dt.float32
BF16 = mybir.dt.bfloat16
ALU = mybir.AluOpType
AF = mybir.ActivationFunctionType
AX = mybir.AxisListType
NPBF = ml_dtypes.bfloat16

NDMASEM = 4
EPS = 1e-6
NEG = -30000.0


class Phase:
    ENGS = ("pe", "act", "dve", "pool", "sp")

    def __init__(self, nc, name):
        self.nc = nc
        self.name = name
        self.ops = []

    def op(self, eng, fn, reads=(), writes=(), dma=False):
        self.ops.append((eng, fn, tuple(reads), tuple(writes), dma))

    def dma(self, out, in_, reads=(), writes=(), eng="sp"):
        self.op(eng, lambda e: e.dma_start(out=out, in_=in_), reads, writes, dma=True)

    def emit(self):
        nc = self.nc
        ops = self.ops
        cnt = {e: 0 for e in self.ENGS}
        dcnt = {e: 0 for e in self.ENGS}
        info = []
        last_w = {}
        readers = {}
        deps = []
        for i, (eng, fn, rd, wr, dma) in enumerate(ops):
            d = set()
            for k in rd:
                if k in last_w:
                    d.add(last_w[k])
            for k in wr:
                if k in last_w:
                    d.add(last_w[k])
                for r in readers.get(k, ()):
                    d.add(r)
            d.discard(i)
            deps.append(d)
            for k in rd:
                readers.setdefault(k, []).append(i)
            for k in wr:
                last_w[k] = i
                readers[k] = []
            if dma:
                info.append((eng, "d", dcnt[eng]))
                dcnt[eng] += 1
            else:
                info.append((eng, "c", cnt[eng]))
                cnt[eng] += 1
        with contextlib.ExitStack() as st:
            csem = {e: st.enter_context(nc.semaphore(f"{self.name}_c_{e}")) for e in self.ENGS}
            dsem = {e: [st.enter_context(nc.semaphore(f"{self.name}_d_{e}{j}")) for j in range(NDMASEM)]
                    for e in self.ENGS if dcnt[e] > 0}
            block = st.enter_context(nc.Block())
            per_eng = {e: [i for i, o in enumerate(ops) if o[0] == e] for e in self.ENGS}

            def make(eng_name):
                def body(eng):
                    waited = {}

                    def wait(key, sem, val):
                        if waited.get(key, 0) >= val:
                            return
                        waited[key] = val
                        eng.wait_ge(sem, val)

                    for i in per_eng[eng_name]:
                        _, fn, rd, wr, dma = ops[i]
                        for p in sorted(deps[i]):
                            pe_, pk, pidx = info[p]
                            if pk == "c":
                                if pe_ == "pe" and eng_name == "pe":
                                    continue
                                wait(("c", pe_), csem[pe_], pidx + 1)
                            else:
                                slot = pidx % NDMASEM
                                wait(("d", pe_, slot), dsem[pe_][slot], 16 * (pidx // NDMASEM + 1))
                        if dma:
                            didx = info[i][2]
                            slot = didx % NDMASEM
                            if didx >= NDMASEM:
                                wait(("d", eng_name, slot), dsem[eng_name][slot], 16 * (didx // NDMASEM))
                            fn(eng).then_inc(dsem[eng_name][slot], 16)
                        else:
                            fn(eng).then_inc(csem[eng_name], 1)
                    if eng_name in dsem:
                        n = dcnt[eng_name]
                        for slot in range(min(NDMASEM, n)):
                            total = (n - slot + NDMASEM - 1) // NDMASEM
                            wait(("d", eng_name, slot), dsem[eng_name][slot], 16 * total)
                return body

            if per_eng["pe"]:
                block.tensor(make("pe"))
            if per_eng["act"]:
                block.scalar(make("act"))
            if per_eng["dve"]:
                block.vector(make("dve"))
            if per_eng["pool"]:
                block.gpsimd(make("pool"))
            if per_eng["sp"]:
                block.sync(make("sp"))
        self.ops = []


KV_RANGES = [(1024, 2048), (4608, 4864), (5120, 5376), (2048, 3072), (4864, 5120), (5376, 5632),
             (4096, 4352), (4352, 4608)]
Q_RANGES = [(0, 1024), (3072, 4096), (5632, 5680), (5680, 7728)]
NKV = 3584
NQC = 4144


def build(S=4096, stage=99, dbg=False):
    NT = S // 128
    NCH = S // 512
    NSL = NCH // 2
    NOWN = NSL * 4
    SO = NSL * 512
    NCMP = (S - 32) // 16 + 1
    NCT = (NCMP + 127) // 128
    NCP = NCT * 128
    okind = "ExternalOutput" if dbg else "Internal"

    nc = bass.Bass("TRN2", target_bir_lowering=False)

    def din(name, shape, dt=F32):
        return nc.dram_tensor(name, list(shape), dt, kind="ExternalInput").ap()

    def dscr(name, shape, dt=BF16):
        return nc.dram_tensor(name, list(shape), dt, kind=okind).ap()

    xkv = din("xkv", [S, 1024])
    xq = din("xq", [SO, 1024])
    w_in = din("w_in", [1024, 7728])
    w_pd = din("w_pd", [1024, 1024])
    w_pn = din("w_pn", [1024, 1024])
    w_o = din("w_o", [1024, 1024])
    w_up = din("w_up", [1024, 4096])
    w_dn = din("w_dn", [4096, 1024])
    c_w1 = [din("c_w1k", [2048, 256]), din("c_w1v", [2048, 256])]
    c_w2 = [din("c_w2k", [256, 64]), din("c_w2v", [256, 64])]
    pos2 = [din("pos2k", [128, 16]), din("pos2v", [128, 16])]
    g_mix = din("g_mix", [1, 1024])
    g_mlp = din("g_mlp", [1, 1024])
    g_k24 = din("g_k24", [1, 1536])
    g_q32 = din("g_q32", [1, 2048])
    g_kc = din("g_kc", [1, 64])
    g_sub = din("g_sub", [1, 128])
    lam4 = din("lam4", [1, 256])
    rkv_c = din("rkv_c", [S, 32])
    rkv_s = din("rkv_s", [S, 32])
    rq_c = din("rq_c", [SO, 32])
    rq_s = din("rq_s", [SO, 32])
    rc_c = din("rc_c", [NCP, 32])
    rc_s = din("rc_s", [NCP, 32])
    identd = din("ident", [128, 128], BF16)
    eind = din("eind", [64, S], BF16)
    ovl = din("ovl", [NCP, 65], BF16)
    m_dense = din("m_dense", [128, 8, 512], BF16)
    m_win = din("m_win", [128, 12, 512], BF16)
    m_cmp = din("m_cmp", [128, NSL * NCT, 512], BF16)
    forced = din("forced", [SO, 64])
    out = nc.dram_tensor("out", [SO, 1024], F32, kind="ExternalOutput").ap()

    KdT = dscr("KdT", [8, 128, S])
    Vd = dscr("Vd", [S, 1024])
    KsT = dscr("KsT", [256, S])
    KwT = dscr("KwT", [256, S])
    kcT = dscr("kcT", [256, S])
    vcT = dscr("vcT", [256, S])
    Vs = dscr("Vs", [S, 256])
    Vw = dscr("Vw", [S, 256])
    QdT = dscr("QdT", [8, 128, SO])
    QnT = dscr("QnT", [1024, SO])
    BsT = dscr("BsT", [256, SO])
    mgS = dscr("mgS", [SO, 2048], F32)
    x1d = dscr("x1d", [SO, 1024], F32)

    with contextlib.ExitStack() as top:
        def sb(name, shape, dt, stack=top):
            return stack.enter_context(nc.sbuf_tensor(name, list(shape), dt))

        def ps(name, shape, dt, stack):
            return stack.enter_context(nc.psum_tensor(name, list(shape), dt))

        ident = sb("ident_sb", [128, 128], BF16)
        gates = sb("gates", [128, NOWN * 48], F32)
        KcT = sb("KcT", [128, 4 * NCP], BF16)
        Vca = sb("Vca", [128, NCT * 4 * 129], BF16)
        nlam = sb("nlam", [128, 1], F32)
        sgain = sb("sgain", [128, 128], F32)
        ss1 = sb("ss1", [128, 1], F32)
        rs1 = sb("rs1", [128, 1], F32)

        def bc(ap1, n):
            return ap1[0:1, :].broadcast_to([128, n])

        def rmsnorm_rows(P, xt, xkey, gt, hout, hkey, junk, sfx=""):
            P.op("pool", lambda e: e.memset(ss1[:], 0.0), writes=["ss1"])
            P.op("act", lambda e: e.activation(out=junk, in_=xt, func=AF.Square, accum_out=ss1[:]),
                 reads=[xkey, "ss1"], writes=["junk" + sfx, "ss1"])
            P.op("dve", lambda e: e.tensor_scalar(out=ss1[:], in0=ss1[:], scalar1=1.0 / 1024, scalar2=EPS,
                                                  op0=ALU.mult, op1=ALU.add), reads=["ss1"], writes=["ss1"])
            P.op("act", lambda e: e.activation(out=ss1[:], in_=ss1[:], func=AF.Ln), reads=["ss1"], writes=["ss1"])
            P.op("act", lambda e: e.activation(out=rs1[:], in_=ss1[:], func=AF.Exp, scale=-0.5),
                 reads=["ss1"], writes=["rs1"])
            P.op("dve", lambda e: e.scalar_tensor_tensor(out=hout, in0=xt, scalar=rs1[:], in1=gt,
                                                         op0=ALU.mult, op1=ALU.mult),
                 reads=[xkey, "rs1", "gt"], writes=[hkey])

        def normrope(P, src, srckey, U, gain, cos, sin, cskeys, outap, outkey, T, sfx):
            sq, ssu, rsu, xn, t1, t2, t3, t4 = T
            n = U * 64
            s3 = lambda ap: ap.rearrange("p (u d) -> p u d", d=64)
            P.op("act", lambda e: e.activation(out=sq[:, 0:n], in_=src, func=AF.Square),
                 reads=[srckey], writes=["sq" + sfx])
            P.op("dve", lambda e: e.tensor_reduce(out=ssu[:, 0:U], in_=s3(sq[:, 0:n]), axis=AX.X, op=ALU.add),
                 reads=["sq" + sfx], writes=["ssu" + sfx])
            P.op("dve", lambda e: e.tensor_scalar(out=ssu[:, 0:U], in0=ssu[:, 0:U], scalar1=1.0 / 64, scalar2=EPS,
                                                  op0=ALU.mult, op1=ALU.add), reads=["ssu" + sfx], writes=["ssu" + sfx])
            P.op("act", lambda e: e.activation(out=ssu[:, 0:U], in_=ssu[:, 0:U], func=AF.Ln),
                 reads=["ssu" + sfx], writes=["ssu" + sfx])
            P.op("act", lambda e: e.activation(out=rsu[:, 0:U], in_=ssu[:, 0:U], func=AF.Exp, scale=-0.5),
                 reads=["ssu" + sfx], writes=["rsu" + sfx])
            P.op("dve", lambda e: e.tensor_tensor(out=s3(xn[:, 0:n]), in0=s3(src),
                                                  in1=rsu[:, 0:U].unsqueeze(2).to_broadcast([128, U, 64]), op=ALU.mult),
                 reads=[srckey, "rsu" + sfx], writes=["xn" + sfx])
            P.op("pool", lambda e: e.tensor_tensor(out=xn[:, 0:n], in0=xn[:, 0:n], in1=gain, op=ALU.mult),
                 reads=["xn" + sfx, "gains"], writes=["xn" + sfx])
            x3 = s3(xn[:, 0:n])
            o3 = s3(outap)
            cb = cos.unsqueeze(1).to_broadcast([128, U, 32])
            sbb = sin.unsqueeze(1).to_broadcast([128, U, 32])
            h3 = lambda t: t[:, 0:U * 32].rearrange("p (u d) -> p u d", d=32)
            P.op("pool", lambda e: e.tensor_tensor(out=h3(t1), in0=x3[:, :, 0:32], in1=cb, op=ALU.mult),
                 reads=["xn" + sfx] + list(cskeys), writes=["t1" + sfx])
            P.op("pool", lambda e: e.tensor_tensor(out=h3(t2), in0=x3[:, :, 32:64], in1=sbb, op=ALU.mult),
                 reads=["xn" + sfx] + list(cskeys), writes=["t2" + sfx])
            P.op("pool", lambda e: e.tensor_tensor(out=o3[:, :, 0:32], in0=h3(t1), in1=h3(t2), op=ALU.subtract),
                 reads=["t1" + sfx, "t2" + sfx], writes=[outkey])
            P.op("dve", lambda e: e.tensor_tensor(out=h3(t3), in0=x3[:, :, 32:64], in1=cb, op=ALU.mult),
                 reads=["xn" + sfx] + list(cskeys), writes=["t3" + sfx])
            P.op("dve", lambda e: e.tensor_tensor(out=h3(t4), in0=x3[:, :, 0:32], in1=sbb, op=ALU.mult),
                 reads=["xn" + sfx] + list(cskeys), writes=["t4" + sfx])
            P.op("dve", lambda e: e.tensor_tensor(out=o3[:, :, 32:64], in0=h3(t3), in1=h3(t4), op=ALU.add),
                 reads=["t3" + sfx, "t4" + sfx], writes=[outkey])

        def load_w_cols(P, wdst, ranges, ncols):
            keys = []
            for c in range(8):
                off = 0
                kc_ = []
                for ri, (a, b) in enumerate(ranges):
                    k = ("wbuf", c, ri)
                    P.dma(wdst[:, c * ncols + off:c * ncols + off + (b - a)], w_in[c * 128:(c + 1) * 128, a:b],
                          writes=[k], eng="pool")
                    kc_.append(k)
                    off += b - a
                keys.append(kc_)
            return keys

        with contextlib.ExitStack() as sa:
            lt = sb("lt", [128, 256], F32, sa)
            pr = sb("pr", [128, 128], F32, sa)
            s2 = sb("s2", [128, 2], F32, sa)
            sgr = sb("sgr", [128, 128], F32, sa)
            P = Phase(nc, "pa")
            P.dma(ident[:], identd, writes=["ident"])
            P.dma(lt[:], bc(lam4, 256), writes=["lt"])
            P.dma(sgr[:], bc(g_sub, 128), writes=["sgr"])
            P.op("dve", lambda e: e.tensor_tensor(out=pr[:, 0:64], in0=lt[:, 0:64], in1=lt[:, 64:128], op=ALU.mult),
                 reads=["lt"], writes=["pr"])
            P.op("dve", lambda e: e.tensor_tensor(out=pr[:, 64:128], in0=lt[:, 128:192], in1=lt[:, 192:256], op=ALU.mult),
                 reads=["lt", "pr"], writes=["pr"])
            P.op("dve", lambda e: e.tensor_reduce(out=s2[:], in_=pr[:].rearrange("p (a d) -> p a d", d=64),
                                                  axis=AX.X, op=ALU.add), reads=["pr"], writes=["s2"])
            P.op("act", lambda e: e.activation(out=s2[:], in_=s2[:], func=AF.Exp), reads=["s2"], writes=["s2"])
            P.op("dve", lambda e: e.tensor_tensor(out=nlam[:], in0=s2[:, 1:2], in1=s2[:, 0:1], op=ALU.subtract),
                 reads=["s2"], writes=["nlam"])
            P.op("dve", lambda e: e.tensor_scalar(out=nlam[:], in0=nlam[:], scalar1=-0.2, scalar2=None, op0=ALU.add),
                 reads=["nlam"], writes=["nlam"])
            P.op("dve", lambda e: e.tensor_scalar(out=sgain[:], in0=sgr[:], scalar1=0.8, scalar2=None, op0=ALU.mult),
                 reads=["sgr"], writes=["sgain"])
            P.emit()

        with contextlib.ExitStack() as sbd:
            wbuf = sb("wbuf", [128, 8 * NQC], BF16, sbd)
            gmix = sb("gmix", [128, 1024], F32, sbd)
            gains = sb("gains", [128, 2048], F32, sbd)
            xts = [sb(f"xt{i}", [128, 1024], F32, sbd) for i in range(2)]
            junk = sb("junk", [128, 1024], F32, sbd)
            hb = sb("hb", [128, 1024], BF16, sbd)
            hT = [sb(f"hT{i}", [128, 1024], BF16, sbd) for i in range(2)]
            cst = [sb(f"cs{i}", [128, 64], F32, sbd) for i in range(2)]
            TT = []
            for i in range(2):
                TT.append((sb(f"sq{i}", [128, 512], F32, sbd), sb(f"ssu{i}", [128, 8], F32, sbd),
                           sb(f"rsu{i}", [128, 8], F32, sbd), sb(f"xn{i}", [128, 512], F32, sbd),
                           sb(f"t1{i}", [128, 256], F32, sbd), sb(f"t2{i}", [128, 256], F32, sbd),
                           sb(f"t3{i}", [128, 256], F32, sbd), sb(f"t4{i}", [128, 256], F32, sbd)))
            kb = sb("kb", [128, 2048], BF16, sbd)
            vb = sb("vb", [128, 2048], BF16, sbd)
            ktb = [sb(f"ktb{i}", [128, 16 * 128], BF16, sbd) for i in range(2)]
            mgt = [sb(f"mgt{i}", [128, 512], F32, sbd) for i in range(2)]
            pT = ps("pT", [128, 1024], BF16, sbd)
            pO = [ps(f"pO{i}", [128, 512], F32, sbd) for i in range(3)]
            pK = [ps(f"pK{i}", [128, 1024], BF16, sbd) for i in range(2)]

            def front_a(P, t, gt):
                xt = xts[t % 2]
                rmsnorm_rows(P, xt[:], ("xt", t % 2), gt[:], hb[:], "hb", junk[:])

            def front_b(P, t):
                for c in range(8):
                    P.op("pe", lambda e, c=c: e.transpose(out=pT[:, c * 128:(c + 1) * 128],
                                                          in_=hb[:, c * 128:(c + 1) * 128], identity=ident[:]),
                         reads=["hb", "ident"], writes=["pT"])
                P.op("dve", lambda e: e.tensor_copy(out=hT[t % 2][:], in_=pT[:]), reads=["pT"], writes=[("hT", t % 2)])

            def proj_group(P, t, j, ncols, col0, width, wkeys):
                po = pO[j % 3]
                for c in range(8):
                    P.op("pe", lambda e, c=c: e.matmul(po[:, 0:width], lhsT=hT[t % 2][:, c * 128:(c + 1) * 128],
                                                       rhs=wbuf[:, c * ncols + col0:c * ncols + col0 + width],
                                                       start=(c == 0), stop=(c == 7)),
                         reads=[("hT", t % 2)] + wkeys[c], writes=[("pO", j % 3)])
                return po

            P = Phase(nc, "pb")
            wk = load_w_cols(P, wbuf, KV_RANGES, NKV)
            P.dma(gmix[:], bc(g_mix, 1024), writes=["gt"])
            P.dma(gains[:, 0:1536], bc(g_k24, 1536), writes=["gains"])
            P.dma(xts[0][:], xkv[0:128, :], writes=[("xt", 0)])
            P.dma(xts[1][:], xkv[128:256, :], writes=[("xt", 1)])
            front_a(P, 0, gmix)
            front_b(P, 0)
            for t in range(NT):
                cs = cst[t % 2]
                P.dma(cs[:, 0:32], rkv_c[t * 128:(t + 1) * 128, :], writes=[("cs", t % 2, 0)])
                P.dma(cs[:, 32:64], rkv_s[t * 128:(t + 1) * 128, :], writes=[("cs", t % 2, 1)])
                if t + 1 < NT:
                    front_a(P, t + 1, gmix)
                for j in range(7):
                    po = proj_group(P, t, j, NKV, j * 512, 512, wk)
                    if j < 3:
                        normrope(P, po[:], ("pO", j % 3), 8, gains[:, j * 512:(j + 1) * 512], cs[:, 0:32], cs[:, 32:64],
                                 [("cs", t % 2, 0), ("cs", t % 2, 1)], kb[:, j * 512:(j + 1) * 512], ("kb", j), TT[j % 2], str(j % 2))
                    else:
                        P.op("act", lambda e, po=po, j=j: e.activation(out=vb[:, (j - 3) * 512:(j - 2) * 512], in_=po[:],
                                                                       func=AF.Identity),
                             reads=[("pO", j % 3)], writes=[("vb", j)])
                    if j == 2 and t + 1 < NT:
                        front_b(P, t + 1)
                        if t + 2 < NT:
                            P.dma(xts[t % 2][:], xkv[(t + 2) * 128:(t + 3) * 128, :], writes=[("xt", t % 2)])
                kt = ktb[t % 2]
                for blk in range(16):
                    src = kb[:, blk * 128:(blk + 1) * 128] if blk < 12 else vb[:, 1536 + (blk - 12) * 128:1536 + (blk - 11) * 128]
                    rk = [("kb", blk // 4)] if blk < 12 else [("vb", 6)]
                    P.op("pe", lambda e, src=src, blk=blk: e.transpose(out=pK[blk // 8][:, (blk % 8) * 128:(blk % 8 + 1) * 128],
                                                                        in_=src, identity=ident[:]),
                         reads=rk + ["ident"], writes=[("pK", blk // 8)])
                for hf in range(2):
                    P.op("dve", lambda e, hf=hf, kt=kt: e.tensor_copy(out=kt[:, hf * 1024:(hf + 1) * 1024], in_=pK[hf][:]),
                         reads=[("pK", hf)], writes=[("ktb", t % 2, hf)])
                tok = slice(t * 128, (t + 1) * 128)
                P.dma(KdT[:, :, tok].rearrange("h p k -> p h k"), kt[:, 0:1024].rearrange("p (h k) -> p h k", k=128),
                      reads=[("ktb", t % 2, 0)])
                for i, dst in enumerate((KsT, KwT, kcT, vcT)):
                    P.dma(dst[:, tok].rearrange("(i p) k -> p i k", p=128),
                          kt[:, 1024 + i * 256:1024 + (i + 1) * 256].rearrange("p (i k) -> p i k", k=128),
                          reads=[("ktb", t % 2, 1)])
                P.dma(Vd[tok, :], vb[:, 0:1024], reads=[("vb", 3), ("vb", 4)])
                P.dma(Vs[tok, :], vb[:, 1024:1280], reads=[("vb", 5)])
                P.dma(Vw[tok, :], vb[:, 1280:1536], reads=[("vb", 5)])
            P.emit()
            if stage <= 1:
                return nc

            P = Phase(nc, "pd")
            wk = load_w_cols(P, wbuf, Q_RANGES, NQC)
            P.dma(gains[:, 0:2048], bc(g_q32, 2048), writes=["gains"])
            P.dma(xts[0][:], xq[0:128, :], writes=[("xt", 0)])
            P.dma(xts[1][:], xq[128:256, :], writes=[("xt", 1)])
            front_a(P, 0, gmix)
            front_b(P, 0)
            for t in range(NOWN):
                cs = cst[t % 2]
                P.dma(cs[:, 0:32], rq_c[t * 128:(t + 1) * 128, :], writes=[("cs", t % 2, 0)])
                P.dma(cs[:, 32:64], rq_s[t * 128:(t + 1) * 128, :], writes=[("cs", t % 2, 1)])
                if t + 1 < NOWN:
                    front_a(P, t + 1, gmix)
                tok = slice(t * 128, (t + 1) * 128)
                for j in range(4):
                    po = proj_group(P, t, j, NQC, j * 512, 512, wk)
                    normrope(P, po[:], ("pO", j % 3), 8, gains[:, j * 512:(j + 1) * 512], cs[:, 0:32], cs[:, 32:64],
                             [("cs", t % 2, 0), ("cs", t % 2, 1)], kb[:, j * 512:(j + 1) * 512], ("kb", j), TT[j % 2], str(j % 2))
                    if j == 2 and t + 1 < NOWN:
                        front_b(P, t + 1)
                        if t + 2 < NOWN:
                            P.dma(xts[t % 2][:], xq[(t + 2) * 128:(t + 3) * 128, :], writes=[("xt", t % 2)])
                po = proj_group(P, t, 4, NQC, 2048, 48, wk)
                gsl = gates[:, t * 48:(t + 1) * 48]
                P.op("act", lambda e, po=po, gsl=gsl: e.activation(out=gsl, in_=po[:, 0:48], func=AF.Tanh, scale=0.5),
                     reads=[("pO", 1)], writes=["gates"])
                P.op("dve", lambda e, gsl=gsl: e.tensor_scalar(out=gsl, in0=gsl, scalar1=0.5, scalar2=0.5, op0=ALU.mult, op1=ALU.add),
                     reads=["gates"], writes=["gates"])
                for j in range(5, 9):
                    po = proj_group(P, t, j, NQC, 2096 + (j - 5) * 512, 512, wk)
                    mg_ = mgt[j % 2]
                    P.op("act", lambda e, po=po, mg_=mg_: e.activation(out=mg_[:], in_=po[:], func=AF.Tanh, scale=0.5),
                         reads=[("pO", j % 3)], writes=[("mgt", j % 2)])
                    P.op("pool" if j % 2 else "dve", lambda e, mg_=mg_: e.tensor_scalar(out=mg_[:], in0=mg_[:], scalar1=0.5, scalar2=0.5,
                                                                                       op0=ALU.mult, op1=ALU.add),
                         reads=[("mgt", j % 2)], writes=[("mgt", j % 2)])
                    P.dma(mgS[tok, (j - 5) * 512:(j - 4) * 512], mg_[:], reads=[("mgt", j % 2)])
                kt = ktb[t % 2]
                for blk in range(16):
                    P.op("pe", lambda e, blk=blk: e.transpose(out=pK[blk // 8][:, (blk % 8) * 128:(blk % 8 + 1) * 128],
                                                              in_=kb[:, blk * 128:(blk + 1) * 128], identity=ident[:]),
                         reads=[("kb", blk // 4), "ident"], writes=[("pK", blk // 8)])
                for hf in range(2):
                    P.op("dve", lambda e, hf=hf, kt=kt: e.tensor_copy(out=kt[:, hf * 1024:(hf + 1) * 1024], in_=pK[hf][:]),
                         reads=[("pK", hf)], writes=[("ktb", t % 2, hf)])
                P.dma(QdT[:, :, tok].rearrange("h p k -> p h k"), kt[:, 0:1024].rearrange("p (h k) -> p h k", k=128),
                      reads=[("ktb", t % 2, 0)])
                P.dma(QnT[:, tok].rearrange("(i p) k -> p i k", p=128), kt[:, 1024:2048].rearrange("p (i k) -> p i k", k=128),
                      reads=[("ktb", t % 2, 1)])
            P.emit()
        if stage <= 2:
            return nc

        ydiff = sb("ydiff", [128, NOWN * 1024], BF16)
        ynsa = sb("ynsa", [128, NOWN * 1024], BF16)
        with contextlib.ExitStack() as sc:
            W1s = [sb(f"W1s{i}", [128, 16 * 256], BF16, sc) for i in range(2)]
            W2s = [sb(f"W2s{i}", [128, 128], BF16, sc) for i in range(2)]
            p2b = [sb(f"p2b{i}", [128, 16], BF16, sc) for i in range(2)]
            kc2 = [sb(f"kc2_{i}", [128, S], BF16, sc) for i in range(2)]
            biasv = [sb(f"biasv{i}", [128, 2], F32, sc) for i in range(2)]
            h1T = sb("h1T", [128, 2 * NCP], BF16, sc)
            xg = sb("xg", [128, NCP], F32, sc)
            x2 = sb("x2", [128, NCP], F32, sc)
            th = sb("th", [128, NCP], F32, sc)
            kcn = sb("kcn", [128, 128], BF16, sc)
            gkc = sb("gkc", [128, 64], F32, sc)
            csc = sb("csc", [128, NCT * 64], F32, sc)
            TC = (sb("c_sq", [128, 64], F32, sc), sb("c_ssu", [128, 8], F32, sc), sb("c_rsu", [128, 8], F32, sc),
                  sb("c_xn", [128, 64], F32, sc), sb("c_t1", [128, 32], F32, sc), sb("c_t2", [128, 32], F32, sc),
                  sb("c_t3", [128, 32], F32, sc), sb("c_t4", [128, 32], F32, sc))
            pH = [ps(f"pH{i}", [128, 512], F32, sc) for i in range(2)]
            pB = ps("pB", [128, 512], F32, sc)
            pC = ps("pC", [128, 512], F32, sc)
            pKc = ps("pKc", [128, 1024], BF16, sc)
            P = Phase(nc, "pc")
            P.op("pool", lambda e: e.memset(h1T[:], 0.0), writes=["h1T"])
            P.op("pool", lambda e: e.memset(kcn[:], 0.0), writes=["kcn"])
            P.dma(gkc[:], bc(g_kc, 64), writes=["gains"])
            for nt in range(NCT):
                P.dma(csc[:, nt * 64:nt * 64 + 32], rc_c[nt * 128:(nt + 1) * 128, :], writes=[("csc", nt, 0)])
                P.dma(csc[:, nt * 64 + 32:nt * 64 + 64], rc_s[nt * 128:(nt + 1) * 128, :], writes=[("csc", nt, 1)])
                for g in range(4):
                    o0 = (nt * 4 + g) * 129
                    P.dma(Vca[:, o0 + 64:o0 + 129], ovl[nt * 128:(nt + 1) * 128, :], writes=[("vca_c", nt, g)])
            for kv in range(2):
                P.dma(W1s[kv][:].rearrange("p (u h) -> p u h", h=256), c_w1[kv].rearrange("(u p) h -> p u h", p=128),
                      writes=[("W1s", kv)], eng="pool")
                P.dma(W2s[kv][:].rearrange("p (a d) -> p a d", d=64), c_w2[kv].rearrange("(a p) d -> p a d", p=128),
                      writes=[("W2s", kv)], eng="pool")
                P.dma(p2b[kv][:], pos2[kv], writes=[("p2b", kv)], eng="pool")
                for half in range(2):
                    for u in range(16):
                        P.op("pe", lambda e, kv=kv, half=half, u=u: e.matmul(
                            pB[:, half:half + 1], lhsT=W1s[kv][:, u * 256 + half * 128:u * 256 + half * 128 + 128],
                            rhs=p2b[kv][:, u:u + 1], start=(u == 0), stop=(u == 15)),
                            reads=[("W1s", kv), ("p2b", kv)], writes=["pB"])
                P.op("dve", lambda e, kv=kv: e.tensor_copy(out=biasv[kv][:], in_=pB[:, 0:2]), reads=["pB"], writes=[("biasv", kv)])
                src = kcT if kv == 0 else vcT
                for g in range(4):
                    kb2 = kc2[g % 2]
                    kk = ("kc2", g % 2)
                    P.dma(kb2[0:64, :], src[g * 64:(g + 1) * 64, :], writes=[(kk, 0)])
                    P.dma(kb2[64:128, 0:S - 1], src[g * 64:(g + 1) * 64, 1:S], writes=[(kk, 1)])
                    for half in range(2):
                        for u in range(16):
                            rhs = bass.AP(kb2, 2 * u, [[S, 128], [16, NCMP]])
                            P.op("pe", lambda e, kv=kv, half=half, u=u, rhs=rhs: e.matmul(
                                pH[half][:, 0:NCMP], lhsT=W1s[kv][:, u * 256 + half * 128:u * 256 + half * 128 + 128],
                                rhs=rhs, start=(u == 0), stop=(u == 15)),
                                reads=[("W1s", kv), (kk, 0), (kk, 1)], writes=[("pH", half)])
                        hsl = h1T[:, half * NCP:half * NCP + NCMP]
                        P.op("dve", lambda e, kv=kv, half=half: e.tensor_scalar(
                            out=xg[:, 0:NCMP], in0=pH[half][:, 0:NCMP], scalar1=biasv[kv][:, half:half + 1], scalar2=None,
                            op0=ALU.add), reads=[("pH", half), ("biasv", kv)], writes=["xg"])
                        P.op("pool", lambda e: e.tensor_tensor(out=x2[:, 0:NCMP], in0=xg[:, 0:NCMP], in1=xg[:, 0:NCMP],
                                                               op=ALU.mult), reads=["xg"], writes=["x2"])
                        P.op("pool", lambda e: e.tensor_scalar(out=x2[:, 0:NCMP], in0=x2[:, 0:NCMP], scalar1=0.044715,
                                                               scalar2=1.0, op0=ALU.mult, op1=ALU.add),
                             reads=["x2"], writes=["x2"])
                        P.op("pool", lambda e: e.tensor_tensor(out=x2[:, 0:NCMP], in0=x2[:, 0:NCMP], in1=xg[:, 0:NCMP],
                                                               op=ALU.mult), reads=["x2", "xg"], writes=["x2"])
                        P.op("act", lambda e: e.activation(out=th[:, 0:NCMP], in_=x2[:, 0:NCMP], func=AF.Tanh,
                                                           scale=0.7978845608028654), reads=["x2"], writes=["th"])
                        P.op("dve", lambda e: e.tensor_scalar(out=th[:, 0:NCMP], in0=th[:, 0:NCMP], scalar1=0.5, scalar2=0.5,
                                                              op0=ALU.mult, op1=ALU.add), reads=["th"], writes=["th"])
                        P.op("dve", lambda e, hsl=hsl: e.tensor_tensor(out=hsl, in0=th[:, 0:NCMP], in1=xg[:, 0:NCMP],
                                                                        op=ALU.mult), reads=["th", "xg"], writes=["h1T"])
                    for nt in range(NCT):
                        for half in range(2):
                            P.op("pe", lambda e, kv=kv, nt=nt, half=half: e.matmul(
                                pC[:, 0:64], lhsT=h1T[:, half * NCP + nt * 128:half * NCP + nt * 128 + 128],
                                rhs=W2s[kv][:, half * 64:(half + 1) * 64], start=(half == 0), stop=(half == 1)),
                                reads=["h1T", ("W2s", kv)], writes=["pC"])
                        if kv == 0:
                            normrope(P, pC[:, 0:64], "pC", 1, gkc[:], csc[:, nt * 64:nt * 64 + 32],
                                     csc[:, nt * 64 + 32:nt * 64 + 64], [("csc", nt, 0), ("csc", nt, 1)],
                                     kcn[:, 0:64], "kcn", TC, "c")
                            P.op("pe", lambda e: e.transpose(out=pKc[:, 0:128], in_=kcn[:], identity=ident[:]),
                                 reads=["kcn", "ident"], writes=["pKc"])
                            P.op("dve", lambda e, g=g, nt=nt: e.tensor_copy(
                                out=KcT[0:64, g * NCP + nt * 128:g * NCP + nt * 128 + 128], in_=pKc[0:64, 0:128]),
                                reads=["pKc"], writes=["KcT"])
                        else:
                            o0 = (nt * 4 + g) * 129
                            P.op("act", lambda e, o0=o0: e.activation(out=Vca[:, o0:o0 + 64], in_=pC[:, 0:64], func=AF.Identity),
                                 reads=["pC"], writes=["Vca"])
            if dbg:
                dK = nc.dram_tensor("dbg_KcT", [64, 4 * NCP], BF16, kind="ExternalOutput").ap()
                dV = nc.dram_tensor("dbg_Vca", [128, NCT * 4 * 129], BF16, kind="ExternalOutput").ap()
                P.dma(dK, KcT[0:64, :], reads=["KcT"])
                P.dma(dV, Vca[:], reads=["Vca"] + [("vca_c", nt, g) for nt in range(NCT) for g in range(4)])
            P.emit()
        if stage <= 3:
            return nc

        with contextlib.ExitStack() as se:
            Qt = [sb(f"e_Qt{i}", [128, SO], BF16, se) for i in range(2)]
            mcmp = sb("e_mcmp", [128, NSL * NCT * 512], BF16, se)
            ET = [sb(f"e_ET{i}", [128, 512], BF16, se) for i in range(2)]
            imp = sb("e_imp", [128, NOWN * 64], F32, se)
            frc = sb("e_frc", [128, NOWN * 64], F32, se)
            rl = sb("e_rl", [128, 4], F32, se)
            impf = sb("e_impf", [128, 64], F32, se)
            tmpf = sb("e_tmpf", [128, 64], F32, se)
            m8 = sb("e_m8", [128, 8], F32, se)
            m8b = sb("e_m8b", [128, 8], F32, se)
            bt = sb("e_bt", [128, 128], BF16, se)
            bstage = sb("e_bstage", [128, SO], BF16, se)
            pS = [ps(f"e_pS{i}", [128, 512], F32, se) for i in range(2)]
            pU = [ps(f"e_pU{i}", [128, 512], F32, se) for i in range(2)]
            pBt = ps("e_pBt", [128, 1024], BF16, se)
            P = Phase(nc, "pe")
            P.dma(mcmp[:].rearrange("p (a q) -> p a q", q=512), m_cmp, writes=["mcmp"])
            P.dma(frc[:].rearrange("p (t j) -> p t j", j=64), forced.rearrange("(t p) j -> p t j", p=128), writes=["frc"])
            P.op("pool", lambda e: e.memset(bt[:], 0.0), writes=["bt"])
            it = 0
            for g in range(4):
                P.op("pool", lambda e: e.memset(imp[:], 0.0), writes=["imp"])
                for h in range(4):
                    hq = 4 * g + h
                    qt = Qt[hq % 2]
                    P.dma(qt[0:64, :], QnT[hq * 64:(hq + 1) * 64, :], writes=[("Qt", hq % 2)])
                    for s_ in range(NSL):
                        for nt in range(NCT):
                            i = it % 2
                            it += 1
                            P.op("pe", lambda e, i=i, g=g, nt=nt, s_=s_, qt=qt: e.matmul(
                                pS[i][:], lhsT=KcT[0:64, g * NCP + nt * 128:g * NCP + nt * 128 + 128],
                                rhs=qt[0:64, s_ * 512:(s_ + 1) * 512], start=True, stop=False),
                                reads=["KcT", ("Qt", hq % 2)], writes=[("pS", i)])
                            P.op("pe", lambda e, i=i, nt=nt, s_=s_: e.matmul(
                                pS[i][:], lhsT=ident[:], rhs=mcmp[:, (s_ * NCT + nt) * 512:(s_ * NCT + nt + 1) * 512],
                                start=False, stop=True), reads=["ident", "mcmp"], writes=[("pS", i)])
                            P.op("act", lambda e, i=i: e.activation(out=ET[i][:], in_=pS[i][:], func=AF.Exp, scale=0.125),
                                 reads=[("pS", i)], writes=[("ET", i)])
                            for qs in range(4):
                                o0 = (nt * 4 + g) * 129
                                P.op("pe", lambda e, i=i, qs=qs, o0=o0, nt=nt: e.matmul(
                                    pU[qs // 2][:, (qs % 2) * 129:(qs % 2) * 129 + 129], lhsT=ET[i][:, qs * 128:(qs + 1) * 128],
                                    rhs=Vca[:, o0:o0 + 129], start=(nt == 0 and qs % 2 == 0), stop=(nt == NCT - 1),
                                    skip_group_check=True),
                                    reads=[("ET", i), "Vca"], writes=[("pU", qs // 2)])
                        for qs in range(4):
                            tl = s_ * 4 + qs
                            u0 = (qs % 2) * 129
                            pu = pU[qs // 2]
                            pk = ("pU", qs // 2)
                            P.op("dve", lambda e, pu=pu, u0=u0, qs=qs: e.tensor_scalar(
                                out=rl[:, qs:qs + 1], in0=pu[:, u0 + 128:u0 + 129], scalar1=1e-30, scalar2=None, op0=ALU.max),
                                reads=[pk], writes=["rl"])
                            P.op("dve", lambda e, qs=qs: e.reciprocal(out=rl[:, qs:qs + 1], in_=rl[:, qs:qs + 1]),
                                 reads=["rl"], writes=["rl"])
                            ysl = ynsa[:, tl * 1024 + hq * 64:tl * 1024 + hq * 64 + 64]
                            gcol = gates[:, tl * 48 + hq * 3:tl * 48 + hq * 3 + 1]
                            P.op("dve", lambda e, pu=pu, u0=u0, qs=qs, ysl=ysl, gcol=gcol: e.tensor_scalar(
                                out=ysl, in0=pu[:, u0:u0 + 64], scalar1=rl[:, qs:qs + 1], scalar2=gcol,
                                op0=ALU.mult, op1=ALU.mult), reads=[pk, "rl", "gates"], writes=[("ynsa", tl)])
                            isl = imp[:, tl * 64:(tl + 1) * 64]
                            P.op("dve", lambda e, pu=pu, u0=u0, qs=qs, isl=isl: e.scalar_tensor_tensor(
                                out=isl, in0=pu[:, u0 + 64:u0 + 128], scalar=rl[:, qs:qs + 1], in1=isl,
                                op0=ALU.mult, op1=ALU.add), reads=[pk, "rl", "imp"], writes=["imp"])
                for tl in range(NOWN):
                    isl = imp[:, tl * 64:(tl + 1) * 64]
                    P.op("dve", lambda e, isl=isl, tl=tl: e.tensor_tensor(out=impf[:], in0=isl, in1=frc[:, tl * 64:(tl + 1) * 64],
                                                                          op=ALU.max), reads=["imp", "frc"], writes=["impf"])
                    P.op("dve", lambda e: e.max(out=m8[:], in_=impf[:]), reads=["impf"], writes=["m8"])
                    P.op("dve", lambda e: e.match_replace(out=tmpf[:], in_to_replace=m8[:], in_values=impf[:], imm_value=-1e9),
                         reads=["m8", "impf"], writes=["tmpf"])
                    P.op("dve", lambda e: e.max(out=m8b[:], in_=tmpf[:]), reads=["tmpf"], writes=["m8b"])
                    P.op("dve", lambda e: e.tensor_scalar(out=tmpf[:], in0=impf[:], scalar1=m8b[:, 7:8], scalar2=None,
                                                          op0=ALU.is_ge), reads=["impf", "m8b", "tmpf"], writes=["tmpf"])
                    P.op("dve", lambda e: e.tensor_scalar(out=bt[:, 64:128], in0=tmpf[:], scalar1=-1.0, scalar2=-NEG,
                                                          op0=ALU.add, op1=ALU.mult), reads=["tmpf"], writes=["bt"])
                    P.op("pe", lambda e: e.transpose(out=pBt[:, 0:128], in_=bt[:], identity=ident[:]),
                         reads=["bt", "ident"], writes=["pBt"])
                    P.op("dve", lambda e, tl=tl: e.tensor_copy(out=bstage[64:128, tl * 128:(tl + 1) * 128], in_=pBt[64:128, 0:128]),
                         reads=["pBt"], writes=["bstage"])
                P.dma(BsT[g * 64:(g + 1) * 64, :], bstage[64:128, :], reads=["bstage"])
            P.emit()
        if stage <= 4:
            return nc

        with contextlib.ExitStack() as sf:
            KDa = [sb(f"f_KDa{i}", [128, S], BF16, sf) for i in range(2)]
            KDb = [sb(f"f_KDb{i}", [128, S], BF16, sf) for i in range(2)]
            VD = [sb(f"f_VD{i}", [128, NT * 129], BF16, sf) for i in range(2)]
            QD = [sb(f"f_QD{i}", [128, SO], BF16, sf) for i in range(2)]
            KS = sb("f_KS", [128, S], BF16, sf)
            KW = sb("f_KW", [128, S], BF16, sf)
            VS = sb("f_VS", [128, NT * 65], BF16, sf)
            VW = sb("f_VW", [128, NT * 65], BF16, sf)
            QS = [sb(f"f_QS{i}", [128, SO], BF16, sf) for i in range(2)]
            mden = sb("f_mden", [128, 8 * 512], BF16, sf)
            mwin = sb("f_mwin", [128, 12 * 512], BF16, sf)
            PT = [sb(f"f_PT{i}", [128, 512], BF16, sf) for i in range(4)]
            oa4 = sb("f_oa4", [128, 512], F32, sf)
            ob4 = sb("f_ob4", [128, 512], F32, sf)
            r8 = sb("f_r8", [128, 12], F32, sf)
            ssd4 = sb("f_ssd4", [128, 4], F32, sf)
            rsd4 = sb("f_rsd4", [128, 4], F32, sf)
            jk = sb("f_jk", [128, 128], F32, sf)
            r4 = sb("f_r4", [128, 4], F32, sf)
            ssd = sb("f_ssd", [128, 1], F32, sf)
            rsd = sb("f_rsd", [128, 1], F32, sf)
            tn = sb("f_tn", [128, 64], F32, sf)
            pS = [ps(f"f_pS{i}", [128, 512], F32, sf) for i in range(4)]
            pO = [ps(f"f_pO{i}", [128, 512], F32, sf) for i in range(3)]
            P = Phase(nc, "pf")
            P.dma(mden[:].rearrange("p (a q) -> p a q", q=512), m_dense, writes=["mden"])
            P.dma(mwin[:].rearrange("p (a q) -> p a q", q=512), m_win, writes=["mwin"])
            P.dma(KS[64:128, :], eind, writes=["KS_e"])
            for i in range(2):
                P.op("pool", lambda e, i=i: e.memset(VD[i][:], 1.0), writes=[("VD", i)])
                P.op("pool", lambda e, i=i: e.memset(KDa[i][64:128, :], 0.0), writes=[("KDz", i)])
                P.op("pool", lambda e, i=i: e.memset(KDb[i][0:64, :], 0.0), writes=[("KDz", i)])
            P.op("pool", lambda e: e.memset(KW[64:128, :], 0.0), writes=["KWz"])
            P.op("pool", lambda e: e.memset(VS[:], 1.0), writes=["VS"])
            P.op("pool", lambda e: e.memset(VW[:], 1.0), writes=["VW"])

            def load_v(dst, dkey, src2d, width, stride):
                d3 = dst[:].rearrange("p (t c) -> p t c", c=stride)
                s3 = src2d.rearrange("(t p) c -> p t c", p=128)
                step = 8
                for a in range(0, NT, step):
                    b_ = min(NT, a + step)
                    P.dma(d3[:, a:b_, 0:width], s3[:, a:b_, :], writes=[dkey])

            def load_diff(h):
                i = h % 2
                P.dma(KDa[i][0:64, :], KdT[h][0:64, :], writes=[("KD", i, 0)])
                P.dma(KDb[i][64:128, :], KdT[h][64:128, :], writes=[("KD", i, 1)])
                load_v(VD[i], ("VD", i), Vd[:, h * 128:(h + 1) * 128], 128, 129)
                P.dma(QD[i][:], QdT[h], writes=[("QD", i)])

            load_diff(0)
            for h in range(8):
                if h + 1 < 8:
                    load_diff(h + 1)
                bi = h % 2
                for s_ in range(NSL):
                    nkt = 8 * (s_ + 1)

                    def qk(kt, s_=s_, nkt=nkt, bi=bi):
                        i2 = kt % 2
                        masked = kt >= nkt - 8
                        mi = kt - (nkt - 8)
                        for m in range(2):
                            bk = 2 * i2 + m
                            P.op("pe", lambda e, bk=bk, m=m, kt=kt: e.matmul(
                                pS[bk][:], lhsT=(KDa if m == 0 else KDb)[bi][:, kt * 128:(kt + 1) * 128],
                                rhs=QD[bi][:, s_ * 512:(s_ + 1) * 512], start=True, stop=True),
                                reads=[("KD", bi, m), ("KDz", bi), ("QD", bi)], writes=[("pS", bk)])
                            P.op("act", lambda e, bk=bk: e.activation(out=PT[bk][:], in_=pS[bk][:], func=AF.Exp, scale=0.125),
                                 reads=[("pS", bk)], writes=[("PT", bk)])
                            if masked:
                                P.op("dve", lambda e, bk=bk, mi=mi: e.tensor_tensor(
                                    out=PT[bk][:], in0=PT[bk][:], in1=mden[:, mi * 512:(mi + 1) * 512], op=ALU.mult),
                                    reads=[("PT", bk), "mden"], writes=[("PT", bk)])

                    def pv(kt, s_=s_, nkt=nkt, bi=bi):
                        i2 = kt % 2
                        for m in range(2):
                            bk = 2 * i2 + m
                            for qs in range(4):
                                a = m * 4 + qs
                                P.op("pe", lambda e, bk=bk, qs=qs, a=a, kt=kt: e.matmul(
                                    pO[a // 3][:, (a % 3) * 129:(a % 3) * 129 + 129], lhsT=PT[bk][:, qs * 128:(qs + 1) * 128],
                                    rhs=VD[bi][:, kt * 129:kt * 129 + 129], start=(kt == 0 and a % 3 == 0), stop=(kt == nkt - 1),
                                    skip_group_check=True),
                                    reads=[("PT", bk), ("VD", bi)], writes=[("pO", a // 3)])

                    qk(0)
                    for kt in range(nkt):
                        if kt + 1 < nkt:
                            qk(kt + 1)
                        pv(kt)
                    okeys = [("pO", 0), ("pO", 1), ("pO", 2)]
                    ov = lambda a: pO[a // 3][:, (a % 3) * 129:(a % 3) * 129 + 129]
                    for qs in range(4):
                        P.op("dve", lambda e, qs=qs: e.reciprocal(out=r8[:, qs:qs + 1], in_=ov(qs)[:, 128:129]),
                             reads=okeys, writes=["r8"])
                        P.op("dve", lambda e, qs=qs: e.reciprocal(out=r8[:, 4 + qs:5 + qs], in_=ov(4 + qs)[:, 128:129]),
                             reads=okeys + ["r8"], writes=["r8"])
                    P.op("dve", lambda e: e.tensor_scalar(out=r8[:, 8:12], in0=r8[:, 4:8], scalar1=nlam[:, 0:1], scalar2=None,
                                                          op0=ALU.mult), reads=["r8", "nlam"], writes=["r8"])
                    for qs in range(4):
                        P.op("dve", lambda e, qs=qs: e.tensor_scalar(out=oa4[:, qs * 128:(qs + 1) * 128], in0=ov(qs)[:, 0:128],
                                                                     scalar1=r8[:, qs:qs + 1], scalar2=None, op0=ALU.mult),
                             reads=okeys + ["r8"], writes=["oa4"])
                        P.op("dve", lambda e, qs=qs: e.scalar_tensor_tensor(
                            out=ob4[:, qs * 128:(qs + 1) * 128], in0=ov(4 + qs)[:, 0:128], scalar=r8[:, 8 + qs:9 + qs],
                            in1=oa4[:, qs * 128:(qs + 1) * 128], op0=ALU.mult, op1=ALU.add),
                            reads=okeys + ["r8", "oa4"], writes=["ob4"])
                    P.op("pool", lambda e: e.memset(ssd4[:], 0.0), writes=["ssd4"])
                    for qs in range(4):
                        P.op("act", lambda e, qs=qs: e.activation(out=jk[:], in_=ob4[:, qs * 128:(qs + 1) * 128], func=AF.Square,
                                                                  accum_out=ssd4[:, qs:qs + 1]),
                             reads=["ob4", "ssd4"], writes=["jk", "ssd4"])
                    P.op("dve", lambda e: e.tensor_scalar(out=ssd4[:], in0=ssd4[:], scalar1=1.0 / 128, scalar2=EPS,
                                                          op0=ALU.mult, op1=ALU.add), reads=["ssd4"], writes=["ssd4"])
                    P.op("act", lambda e: e.activation(out=ssd4[:], in_=ssd4[:], func=AF.Ln), reads=["ssd4"], writes=["ssd4"])
                    P.op("act", lambda e: e.activation(out=rsd4[:], in_=ssd4[:], func=AF.Exp, scale=-0.5),
                         reads=["ssd4"], writes=["rsd4"])
                    for qs in range(4):
                        tl = s_ * 4 + qs
                        ysl = ydiff[:, tl * 1024 + h * 128:tl * 1024 + (h + 1) * 128]
                        P.op("dve", lambda e, ysl=ysl, qs=qs: e.scalar_tensor_tensor(
                            out=ysl, in0=ob4[:, qs * 128:(qs + 1) * 128], scalar=rsd4[:, qs:qs + 1], in1=sgain[:],
                            op0=ALU.mult, op1=ALU.mult), reads=["ob4", "rsd4", "sgain"], writes=[("ydiff", tl)])

            cnt = [0]
            ucnt = [0]
            for g in range(4):
                P.dma(KS[0:64, :], KsT[g * 64:(g + 1) * 64, :], writes=["KS"])
                P.dma(KW[0:64, :], KwT[g * 64:(g + 1) * 64, :], writes=["KW"])
                load_v(VS, "VS", Vs[:, g * 64:(g + 1) * 64], 64, 65)
                load_v(VW, "VW", Vw[:, g * 64:(g + 1) * 64], 64, 65)
                for h in range(4):
                    hq = 4 * g + h
                    qi = hq % 2
                    P.dma(QS[qi][0:64, :], QnT[hq * 64:(hq + 1) * 64, :], writes=[("QS", qi, 0)])
                    P.dma(QS[qi][64:128, :], BsT[g * 64:(g + 1) * 64, :], writes=[("QS", qi, 1)])
                    for kind in (2, 1):
                        for s_ in range(NSL):
                            if kind == 1:
                                kts = list(range(0, 8 * (s_ + 1)))
                                mis = [kt - 8 * s_ if kt >= 8 * s_ else None for kt in kts]
                            else:
                                kts = list(range(max(0, 8 * s_ - 4), 8 * s_ + 8))
                                mis = [kt - (8 * s_ - 4) for kt in kts]
                            banks = []
                            for _ in kts:
                                banks.append(cnt[0] % 4)
                                cnt[0] += 1
                            ob_i = ucnt[0] % 3
                            ucnt[0] += 1

                            def qk(j, kind=kind, s_=s_, kts=kts, mis=mis, banks=banks, qi=qi):
                                kt, mi, bk = kts[j], mis[j], banks[j]
                                if kind == 1:
                                    lhsT = KS[:, kt * 128:(kt + 1) * 128]
                                    rhs = QS[qi][:, s_ * 512:(s_ + 1) * 512]
                                    rd = ["KS", "KS_e", ("QS", qi, 0), ("QS", qi, 1)]
                                    mt = mden
                                else:
                                    lhsT = KW[:, kt * 128:(kt + 1) * 128]
                                    rhs = QS[qi][:, s_ * 512:(s_ + 1) * 512]
                                    rd = ["KW", "KWz", ("QS", qi, 0), ("QS", qi, 1)]
                                    mt = mwin
                                P.op("pe", lambda e: e.matmul(pS[bk][:], lhsT=lhsT, rhs=rhs, start=True, stop=True),
                                     reads=rd, writes=[("pS", bk)])
                                P.op("act", lambda e: e.activation(out=PT[bk][:], in_=pS[bk][:], func=AF.Exp, scale=0.125),
                                     reads=[("pS", bk)], writes=[("PT", bk)])
                                if mi is not None:
                                    P.op("dve", lambda e: e.tensor_tensor(out=PT[bk][:], in0=PT[bk][:], in1=mt[:, mi * 512:(mi + 1) * 512],
                                                                          op=ALU.mult),
                                         reads=[("PT", bk), "mden", "mwin"], writes=[("PT", bk)])

                            def pv(j, kind=kind, kts=kts, banks=banks, ob_i=ob_i):
                                kt, bk = kts[j], banks[j]
                                vt, vk = (VS, "VS") if kind == 1 else (VW, "VW")
                                for qs in range(4):
                                    P.op("pe", lambda e, qs=qs: e.matmul(
                                        pO[ob_i][:, qs * 65:qs * 65 + 65], lhsT=PT[bk][:, qs * 128:(qs + 1) * 128],
                                        rhs=vt[:, kt * 65:kt * 65 + 65], start=(j == 0 and qs == 0), stop=(j == len(kts) - 1),
                                        skip_group_check=True),
                                        reads=[("PT", bk), vk], writes=[("pO", ob_i)])

                            qk(0)
                            if len(kts) > 1:
                                qk(1)
                            for j in range(len(kts)):
                                if j + 2 < len(kts):
                                    qk(j + 2)
                                pv(j)
                            for qs in range(4):
                                tl = s_ * 4 + qs
                                oq = pO[ob_i][:, qs * 65:qs * 65 + 65]
                                P.op("dve", lambda e, oq=oq, qs=qs: e.tensor_scalar(
                                    out=r4[:, qs:qs + 1], in0=oq[:, 64:65], scalar1=1e-30, scalar2=None, op0=ALU.max),
                                    reads=[("pO", ob_i)], writes=["r4"])
                                P.op("dve", lambda e, qs=qs: e.reciprocal(out=r4[:, qs:qs + 1], in_=r4[:, qs:qs + 1]),
                                     reads=["r4"], writes=["r4"])
                                gcol = gates[:, tl * 48 + hq * 3 + kind:tl * 48 + hq * 3 + kind + 1]
                                P.op("dve", lambda e, oq=oq, qs=qs, gcol=gcol: e.tensor_scalar(
                                    out=tn[:], in0=oq[:, 0:64], scalar1=r4[:, qs:qs + 1], scalar2=gcol,
                                    op0=ALU.mult, op1=ALU.mult), reads=[("pO", ob_i), "r4", "gates"], writes=["tn"])
                                ysl = ynsa[:, tl * 1024 + hq * 64:tl * 1024 + hq * 64 + 64]
                                P.op("pool", lambda e, ysl=ysl: e.tensor_tensor(out=ysl, in0=ysl, in1=tn[:], op=ALU.add),
                                     reads=["tn", ("ynsa", tl)], writes=[("ynsa", tl)])
            if dbg:
                dY = nc.dram_tensor("dbg_ydiff", [128, NOWN * 1024], BF16, kind="ExternalOutput").ap()
                dN = nc.dram_tensor("dbg_ynsa", [128, NOWN * 1024], BF16, kind="ExternalOutput").ap()
                P.dma(dY, ydiff[:], reads=[("ydiff", t) for t in range(NOWN)])
                P.dma(dN, ynsa[:], reads=[("ynsa", t) for t in range(NOWN)])
            P.emit()
        if stage <= 5:
            return nc

        with contextlib.ExitStack() as sg:
            Wpd = sb("g_Wpd", [128, 8 * 1024], BF16, sg)
            Wpn = sb("g_Wpn", [128, 8 * 1024], BF16, sg)
            Wo = sb("g_Wo", [128, 8 * 1024], BF16, sg)
            ydT = sb("g_ydT", [128, 1024], BF16, sg)
            ynT = sb("g_ynT", [128, 1024], BF16, sg)
            mgl = [sb(f"g_mgl{i}", [128, 2048], F32, sg) for i in range(2)]
            xql = [sb(f"g_xql{i}", [128, 1024], F32, sg) for i in range(2)]
            m1 = sb("g_m1", [128, 1024], F32, sg)
            m2 = sb("g_m2", [128, 1024], F32, sg)
            mixb = sb("g_mixb", [128, 1024], BF16, sg)
            mixT = sb("g_mixT", [128, 1024], BF16, sg)
            x1t = [sb(f"g_x1t{i}", [128, 1024], F32, sg) for i in range(2)]
            pTa = ps("g_pTa", [128, 1024], BF16, sg)
            pTb = ps("g_pTb", [128, 1024], BF16, sg)
            pD = [ps(f"g_pD{i}", [128, 512], F32, sg) for i in range(2)]
            pN = [ps(f"g_pN{i}", [128, 512], F32, sg) for i in range(2)]
            pW = [ps(f"g_pW{i}", [128, 512], F32, sg) for i in range(2)]
            P = Phase(nc, "pg")
            for (wd, wsrc, wk_) in ((Wpd, w_pd, "Wpd"), (Wpn, w_pn, "Wpn"), (Wo, w_o, "Wo")):
                for hc in range(2):
                    P.dma(wd[:, hc * 4096:(hc + 1) * 4096].rearrange("p (c n) -> p c n", n=1024),
                          wsrc[hc * 512:(hc + 1) * 512, :].rearrange("(c p) n -> p c n", p=128), writes=[(wk_, hc)], eng="pool")
            for t in range(NOWN):
                tok = slice(t * 128, (t + 1) * 128)
                mg_ = mgl[t % 2]
                xq_ = xql[t % 2]
                P.dma(mg_[:], mgS[tok, :], writes=[("mgl", t % 2)])
                P.dma(xq_[:], xq[tok, :], writes=[("xql", t % 2)])
                for c in range(8):
                    P.op("pe", lambda e, c=c, t=t: e.transpose(out=pTa[:, c * 128:(c + 1) * 128],
                                                              in_=ydiff[:, t * 1024 + c * 128:t * 1024 + (c + 1) * 128], identity=ident[:]),
                         reads=["ident"], writes=["pTa"])
                P.op("dve", lambda e: e.tensor_copy(out=ydT[:], in_=pTa[:]), reads=["pTa"], writes=["ydT"])
                for c in range(8):
                    P.op("pe", lambda e, c=c, t=t: e.transpose(out=pTb[:, c * 128:(c + 1) * 128],
                                                              in_=ynsa[:, t * 1024 + c * 128:t * 1024 + (c + 1) * 128], identity=ident[:]),
                         reads=["ident"], writes=["pTb"])
                P.op("dve", lambda e: e.tensor_copy(out=ynT[:], in_=pTb[:]), reads=["pTb"], writes=["ynT"])
                for hf in range(2):
                    for c in range(8):
                        P.op("pe", lambda e, c=c, hf=hf: e.matmul(pD[hf][:], lhsT=ydT[:, c * 128:(c + 1) * 128],
                                                                  rhs=Wpd[:, c * 1024 + hf * 512:c * 1024 + (hf + 1) * 512],
                                                                  start=(c == 0), stop=(c == 7)),
                             reads=["ydT", ("Wpd", c // 4)], writes=[("pD", hf)])
                    for c in range(8):
                        P.op("pe", lambda e, c=c, hf=hf: e.matmul(pN[hf][:], lhsT=ynT[:, c * 128:(c + 1) * 128],
                                                                  rhs=Wpn[:, c * 1024 + hf * 512:c * 1024 + (hf + 1) * 512],
                                                                  start=(c == 0), stop=(c == 7)),
                             reads=["ynT", ("Wpn", c // 4)], writes=[("pN", hf)])
                    P.op("dve", lambda e, hf=hf, mg_=mg_: e.tensor_tensor(out=m1[:, hf * 512:(hf + 1) * 512], in0=pD[hf][:],
                                                                          in1=mg_[:, hf * 512:(hf + 1) * 512], op=ALU.mult),
                         reads=[("pD", hf), ("mgl", t % 2)], writes=[("m1", hf)])
                    P.op("dve", lambda e, hf=hf, mg_=mg_: e.tensor_tensor(out=m2[:, hf * 512:(hf + 1) * 512], in0=pN[hf][:],
                                                                          in1=mg_[:, 1024 + hf * 512:1024 + (hf + 1) * 512], op=ALU.mult),
                         reads=[("pN", hf), ("mgl", t % 2)], writes=[("m2", hf)])
                    P.op("pool", lambda e, hf=hf: e.tensor_tensor(out=mixb[:, hf * 512:(hf + 1) * 512], in0=m1[:, hf * 512:(hf + 1) * 512],
                                                                  in1=m2[:, hf * 512:(hf + 1) * 512], op=ALU.add),
                         reads=[("m1", hf), ("m2", hf)], writes=[("mixb", hf)])
                for c in range(8):
                    P.op("pe", lambda e, c=c: e.transpose(out=pTa[:, c * 128:(c + 1) * 128], in_=mixb[:, c * 128:(c + 1) * 128],
                                                          identity=ident[:]),
                         reads=[("mixb", c // 4), "ident"], writes=["pTa"])
                P.op("dve", lambda e: e.tensor_copy(out=mixT[:], in_=pTa[:]), reads=["pTa"], writes=["mixT"])
                x1_ = x1t[t % 2]
                for hf in range(2):
                    for c in range(8):
                        P.op("pe", lambda e, c=c, hf=hf: e.matmul(pW[hf][:], lhsT=mixT[:, c * 128:(c + 1) * 128],
                                                                  rhs=Wo[:, c * 1024 + hf * 512:c * 1024 + (hf + 1) * 512],
                                                                  start=(c == 0), stop=(c == 7)),
                             reads=["mixT", ("Wo", c // 4)], writes=[("pW", hf)])
                    P.op("dve", lambda e, hf=hf, x1_=x1_, xq_=xq_: e.tensor_tensor(out=x1_[:, hf * 512:(hf + 1) * 512], in0=pW[hf][:],
                                                                                  in1=xq_[:, hf * 512:(hf + 1) * 512], op=ALU.add),
                         reads=[("pW", hf), ("xql", t % 2)], writes=[("x1t", t % 2, hf)])
                P.dma(x1d[tok, :], x1_[:], reads=[("x1t", t % 2, 0), ("x1t", t % 2, 1)])
            P.emit()
        if stage <= 6:
            return nc
    with contextlib.ExitStack() as top2:
        def sb2(name, shape, dt):
            return top2.enter_context(nc.sbuf_tensor(name, list(shape), dt))

        ident2 = sb2("ident2", [128, 128], BF16)
        Wup = sb2("h_Wup", [128, 8 * 4096], BF16)
        Wdn = sb2("h_Wdn", [128, 32 * 1024], BF16)
        gml = sb2("h_gml", [128, 1024], F32)
        x1g = sb2("h_x1g", [128, 4 * 1024], F32)
        junk2 = sb2("h_junk", [128, 1024], F32)
        h2 = sb2("h_h2", [128, 1024], BF16)
        h2T = sb2("h_h2T", [128, 8 * 512], BF16)
        uT = sb2("h_uT", [128, 32 * 512], BF16)
        rr = [sb2(f"h_rr{i}", [128, 512], F32) for i in range(2)]
        ot = [sb2(f"h_ot{i}", [128, 512], F32) for i in range(2)]
        ss2 = sb2("h_ss", [128, 1], F32)
        rs2 = sb2("h_rs", [128, 1], F32)
        pT2 = top2.enter_context(nc.psum_tensor("h_pT", [128, 1024], BF16))
        pU2 = [top2.enter_context(nc.psum_tensor(f"h_pU{i}", [128, 512], F32)) for i in range(2)]
        pDn = [top2.enter_context(nc.psum_tensor(f"h_pDn{i}", [128, 512], F32)) for i in range(2)]
        P = Phase(nc, "ph")
        P.dma(ident2[:], identd, writes=["ident"])
        P.dma(gml[:], g_mlp[0:1, :].broadcast_to([128, 1024]), writes=["gt"])
        for c in range(8):
            P.dma(Wup[:, c * 4096:(c + 1) * 4096], w_up[c * 128:(c + 1) * 128, :], writes=[("Wup", c)], eng="pool")
        for f4 in range(8):
            P.dma(Wdn[:, f4 * 4096:(f4 + 1) * 4096].rearrange("p (a n) -> p a n", n=1024),
                  w_dn[f4 * 512:(f4 + 1) * 512, :].rearrange("(a p) n -> p a n", p=128), writes=[("Wdn", f4)], eng="pool")
        for gq in range(NSL):
            rows = slice(gq * 512, (gq + 1) * 512)
            P.dma(x1g[:].rearrange("p (t n) -> p t n", n=1024), x1d[rows, :].rearrange("(t p) n -> p t n", p=128), writes=["x1g"])
            for j in range(4):
                xt_ = x1g[:, j * 1024:(j + 1) * 1024]
                P.op("pool", lambda e: e.memset(ss2[:], 0.0), writes=["ss2"])
                P.op("act", lambda e, xt_=xt_: e.activation(out=junk2[:], in_=xt_, func=AF.Square, accum_out=ss2[:]),
                     reads=["x1g", "ss2"], writes=["junk2", "ss2"])
                P.op("dve", lambda e: e.tensor_scalar(out=ss2[:], in0=ss2[:], scalar1=1.0 / 1024, scalar2=EPS,
                                                      op0=ALU.mult, op1=ALU.add), reads=["ss2"], writes=["ss2"])
                P.op("act", lambda e: e.activation(out=ss2[:], in_=ss2[:], func=AF.Ln), reads=["ss2"], writes=["ss2"])
                P.op("act", lambda e: e.activation(out=rs2[:], in_=ss2[:], func=AF.Exp, scale=-0.5), reads=["ss2"], writes=["rs2"])
                P.op("dve", lambda e, xt_=xt_: e.scalar_tensor_tensor(out=h2[:], in0=xt_, scalar=rs2[:], in1=gml[:],
                                                                     op0=ALU.mult, op1=ALU.mult),
                     reads=["x1g", "rs2", "gt"], writes=["h2"])
                for c in range(8):
                    P.op("pe", lambda e, c=c: e.transpose(out=pT2[:, c * 128:(c + 1) * 128], in_=h2[:, c * 128:(c + 1) * 128],
                                                          identity=ident2[:]), reads=["h2", "ident"], writes=["pT2"])
                P.op("dve", lambda e, j=j: e.tensor_copy(
                    out=h2T[:].rearrange("p (c q) -> p c q", q=512)[:, :, j * 128:(j + 1) * 128],
                    in_=pT2[:].rearrange("p (c q) -> p c q", q=128)), reads=["pT2"], writes=["h2T"])
            for f in range(32):
                pu = pU2[f % 2]
                for c in range(8):
                    P.op("pe", lambda e, c=c, f=f, pu=pu: e.matmul(pu[:], lhsT=Wup[:, c * 4096 + f * 128:c * 4096 + (f + 1) * 128],
                                                                  rhs=h2T[:, c * 512:(c + 1) * 512], start=(c == 0), stop=(c == 7)),
                         reads=[("Wup", c), "h2T"], writes=[("pU2", f % 2)])
                r_ = rr[f % 2]
                P.op("act", lambda e, pu=pu, r_=r_: e.activation(out=r_[:], in_=pu[:], func=AF.Relu),
                     reads=[("pU2", f % 2)], writes=[("rr", f % 2)])
                P.op("pool" if f % 2 else "dve", lambda e, f=f, r_=r_: e.tensor_tensor(out=uT[:, f * 512:(f + 1) * 512], in0=r_[:], in1=r_[:],
                                                                                     op=ALU.mult),
                     reads=[("rr", f % 2)], writes=[("uT", f)])
            for j in range(4):
                for hf in range(2):
                    i = (j * 2 + hf) % 2
                    for f in range(32):
                        P.op("pe", lambda e, f=f, j=j, hf=hf, i=i: e.matmul(
                            pDn[i][:], lhsT=uT[:, f * 512 + j * 128:f * 512 + (j + 1) * 128],
                            rhs=Wdn[:, f * 1024 + hf * 512:f * 1024 + (hf + 1) * 512], start=(f == 0), stop=(f == 31)),
                            reads=[("uT", f), ("Wdn", f // 4)], writes=[("pDn", i)])
                    o_ = ot[i]
                    P.op("dve", lambda e, i=i, o_=o_, j=j, hf=hf: e.tensor_tensor(
                        out=o_[:], in0=pDn[i][:], in1=x1g[:, j * 1024 + hf * 512:j * 1024 + (hf + 1) * 512], op=ALU.add),
                        reads=[("pDn", i), "x1g"], writes=[("ot", i)])
                    P.dma(out[gq * 512 + j * 128:gq * 512 + (j + 1) * 128, hf * 512:(hf + 1) * 512], o_[:], reads=[("ot", i)])
        P.emit()
    return nc


def _rope_tab(pos):
    inv = (10000.0 ** (-(np.arange(32, dtype=np.float32)) / np.float32(32))).astype(np.float32)
    ang = pos.astype(np.float32)[:, None] * inv[None, :]
    return np.cos(ang).astype(np.float32), np.sin(ang).astype(np.float32)


def make_core_inputs(inp, core, S):
    b, par = core // 2, core % 2
    NSL = S // 1024
    SO = NSL * 512
    NCMP = (S - 32) // 16 + 1
    NCT = (NCMP + 127) // 128
    NCP = NCT * 128
    f = lambda a: np.ascontiguousarray(np.asarray(a, dtype=np.float32))
    x = np.asarray(inp["x"], dtype=np.float32)
    own_pos = np.concatenate([np.arange((2 * s + par) * 512, (2 * s + par) * 512 + 512) for s in range(NSL)])
    d = {}
    d["xkv"] = f(x[b])
    d["xq"] = f(x[b][own_pos])
    d["w_in"] = f(inp["w_in"][0])
    d["w_pd"] = f(inp["w_proj_diff"][0])
    d["w_pn"] = f(inp["w_proj_nsa"][0])
    d["w_o"] = f(inp["w_out"][0])
    d["w_up"] = f(inp["w_mlp_up"][0])
    d["w_dn"] = f(inp["w_mlp_down"][0])
    d["c_w1k"] = f(inp["cmp_k_w1"][0])
    d["c_w1v"] = f(inp["cmp_v_w1"][0])
    d["c_w2k"] = f(inp["cmp_k_w2"][0])
    d["c_w2v"] = f(inp["cmp_v_w2"][0])
    p2 = lambda p: f(np.asarray(p, np.float32).reshape(16, 2, 64).transpose(1, 2, 0).reshape(128, 16))
    d["pos2k"] = p2(inp["cmp_pos_k"][0])
    d["pos2v"] = p2(inp["cmp_pos_v"][0])
    d["g_mix"] = f(inp["ln_mix_g"][0][None, :])
    d["g_mlp"] = f(inp["ln_mlp_g"][0][None, :])
    gk = np.asarray(inp["nsa_k_norm_g"][0], np.float32)
    d["g_k24"] = f(np.concatenate([np.tile(np.asarray(inp["diff_k_norm_g"][0], np.float32), 16),
                                   np.tile(gk[1], 4), np.tile(gk[2], 4)])[None, :])
    d["g_q32"] = f(np.concatenate([np.tile(np.asarray(inp["diff_q_norm_g"][0], np.float32), 16),
                                   np.tile(np.asarray(inp["nsa_q_norm_g"][0], np.float32), 16)])[None, :])
    d["g_kc"] = f(gk[0][None, :])
    d["g_sub"] = f(inp["diff_subln_g"][0][None, :])
    d["lam4"] = f(np.concatenate([np.asarray(inp[k][0], np.float32) for k in
                                  ("diff_lambda_q1", "diff_lambda_k1", "diff_lambda_q2", "diff_lambda_k2")])[None, :])
    d["rkv_c"], d["rkv_s"] = _rope_tab(np.arange(S))
    d["rq_c"], d["rq_s"] = _rope_tab(own_pos)
    cc = np.zeros(NCP, np.float32)
    cc[:NCMP] = np.arange(NCMP) * 16 + 15.5
    d["rc_c"], d["rc_s"] = _rope_tab(cc)
    d["ident"] = np.eye(128, dtype=np.float32).astype(NPBF)
    kk = np.arange(S)
    d["eind"] = (kk[None, :] // 64 == np.arange(64)[:, None]).astype(np.float32).astype(NPBF)
    n = np.arange(NCP)
    cs, ce = n * 16, n * 16 + 31
    ss_ = np.arange(64) * 64
    ov = ((cs[:, None] < ss_[None, :] + 64) & (ce[:, None] >= ss_[None, :]) & (n[:, None] < NCMP)).astype(np.float32)
    d["ovl"] = np.concatenate([ov, np.ones((NCP, 1), np.float32)], axis=1).astype(NPBF)
    k128 = np.arange(128)[:, None, None]
    q512 = np.arange(512)[None, None, :]
    kw = np.arange(8)[None, :, None] * 128 + k128
    d["m_dense"] = np.where((kw - par * 512) <= q512, 1.0, 0.0).astype(np.float32).astype(NPBF)
    kw = np.arange(12)[None, :, None] * 128 + k128 - 512
    dd = par * 512 + q512 - kw
    d["m_win"] = np.where((dd >= 0) & (dd < 512), 1.0, 0.0).astype(np.float32).astype(NPBF)
    mc = np.zeros((128, NSL * NCT, 512), np.float32)
    for s in range(NSL):
        for nt in range(NCT):
            nn = nt * 128 + np.arange(128)[:, None]
            qpos = (2 * s + par) * 512 + np.arange(512)[None, :]
            mc[:, s * NCT + nt, :] = np.where((nn < NCMP) & (nn * 16 + 31 <= qpos), 0.0, NEG)
    d["m_cmp"] = mc.astype(NPBF)
    fo = np.zeros((SO, 64), np.float32)
    cur = own_pos // 64
    r = np.arange(SO)
    fo[r[cur >= 1], (cur - 1)[cur >= 1]] = 1e4
    fo[r, cur] = 2e4
    fo[:, 0] = 3e4
    d["forced"] = fo
    return d


_NC_CACHE = {}


def kernel(**inputs):
    S = int(np.asarray(inputs["x"]).shape[1])
    B = int(np.asarray(inputs["x"]).shape[0])
    ncores = 2 * B
    if S not in _NC_CACHE:
        _NC_CACHE[S] = build(S)
    nc = _NC_CACHE[S]
    in_maps = [make_core_inputs(inputs, c, S) for c in range(ncores)]
    res = run_bass_kernel_spmd(nc, in_maps, core_ids=list(range(ncores)))
    out = np.zeros((B, S, 1024), np.float32)
    NSL = S // 1024
    for c in range(ncores):
        b, par = c // 2, c % 2
        o = np.asarray(res.results[c]["out"], dtype=np.float32)
        for s in range(NSL):
            ch = 2 * s + par
            out[b, ch * 512:(ch + 1) * 512] = o[s * 512:(s + 1) * 512]
    return out
```

```python
import contextlib
import numpy as np
import ml_dtypes
import concourse.bass as bass
import concourse.mybir as mybir
from concourse.bass_utils import run_bass_kernel_spmd

F32 = mybir.dt.float32
BF16 = mybir.dt.bfloat16
ALU = mybir.AluOpType
AF = mybir.ActivationFunctionType
AX = mybir.AxisListType
NPBF = ml_dtypes.bfloat16

NDMASEM = 4
EPS = 1e-6
NEG = -30000.0


class Phase:
    ENGS = ("pe", "act", "dve", "pool", "sp")

    def __init__(self, nc, name):
        self.nc = nc
        self.name = name
        self.ops = []

    def op(self, eng, fn, reads=(), writes=(), dma=False):
        self.ops.append((eng, fn, tuple(reads), tuple(writes), dma))

    def dma(self, out, in_, reads=(), writes=(), eng="sp"):
        self.op(eng, lambda e: e.dma_start(out=out, in_=in_), reads, writes, dma=True)

    def emit(self):
        nc = self.nc
        ops = self.ops
        cnt = {e: 0 for e in self.ENGS}
        dcnt = {e: 0 for e in self.ENGS}
        info = []
        last_w = {}
        readers = {}
        deps = []
        for i, (eng, fn, rd, wr, dma) in enumerate(ops):
            d = set()
            for k in rd:
                if k in last_w:
                    d.add(last_w[k])
            for k in wr:
                if k in last_w:
                    d.add(last_w[k])
                for r in readers.get(k, ()):
                    d.add(r)
            d.discard(i)
            deps.append(d)
            for k in rd:
                readers.setdefault(k, []).append(i)
            for k in wr:
                last_w[k] = i
                readers[k] = []
            if dma:
                info.append((eng, "d", dcnt[eng]))
                dcnt[eng] += 1
            else:
                info.append((eng, "c", cnt[eng]))
                cnt[eng] += 1
        with contextlib.ExitStack() as st:
            csem = {e: st.enter_context(nc.semaphore(f"{self.name}_c_{e}")) for e in self.ENGS}
            dsem = {e: [st.enter_context(nc.semaphore(f"{self.name}_d_{e}{j}")) for j in range(NDMASEM)]
                    for e in self.ENGS if dcnt[e] > 0}
            block = st.enter_context(nc.Block())
            per_eng = {e: [i for i, o in enumerate(ops) if o[0] == e] for e in self.ENGS}

            def make(eng_name):
                def body(eng):
                    waited = {}

                    def wait(key, sem, val):
                        if waited.get(key, 0) >= val:
                            return
                        waited[key] = val
                        eng.wait_ge(sem, val)

                    for i in per_eng[eng_name]:
                        _, fn, rd, wr, dma = ops[i]
                        for p in sorted(deps[i]):
                            pe_, pk, pidx = info[p]
                            if pk == "c":
                                if pe_ == "pe" and eng_name == "pe":
                                    continue
                                wait(("c", pe_), csem[pe_], pidx + 1)
                            else:
                                slot = pidx % NDMASEM
                                wait(("d", pe_, slot), dsem[pe_][slot], 16 * (pidx // NDMASEM + 1))
                        if dma:
                            didx = info[i][2]
                            slot = didx % NDMASEM
                            if didx >= NDMASEM:
                                wait(("d", eng_name, slot), dsem[eng_name][slot], 16 * (didx // NDMASEM))
                            fn(eng).then_inc(dsem[eng_name][slot], 16)
                        else:
                            fn(eng).then_inc(csem[eng_name], 1)
                    if eng_name in dsem:
                        n = dcnt[eng_name]
                        for slot in range(min(NDMASEM, n)):
                            total = (n - slot + NDMASEM - 1) // NDMASEM
                            wait(("d", eng_name, slot), dsem[eng_name][slot], 16 * total)
                return body

            if per_eng["pe"]:
                block.tensor(make("pe"))
            if per_eng["act"]:
                block.scalar(make("act"))
            if per_eng["dve"]:
                block.vector(make("dve"))
            if per_eng["pool"]:
                block.gpsimd(make("pool"))
            if per_eng["sp"]:
                block.sync(make("sp"))
        self.ops = []


KV_RANGES = [(1024, 2048), (4608, 4864), (5120, 5376), (2048, 3072), (4864, 5120), (5376, 5632),
             (4096, 4352), (4352, 4608)]
Q_RANGES = [(0, 1024), (3072, 4096), (5632, 5680), (5680, 7728)]
NKV = 3584
NQC = 4144


def build(S=4096, stage=99, dbg=False):
    NT = S // 128
    NCH = S // 512
    NSL = NCH // 2
    NOWN = NSL * 4
    SO = NSL * 512
    NCMP = (S - 32) // 16 + 1
    NCT = (NCMP + 127) // 128
    NCP = NCT * 128
    okind = "ExternalOutput" if dbg else "Internal"

    nc = bass.Bass("TRN2", target_bir_lowering=False)

    def din(name, shape, dt=F32):
        return nc.dram_tensor(name, list(shape), dt, kind="ExternalInput").ap()

    def dscr(name, shape, dt=BF16):
        return nc.dram_tensor(name, list(shape), dt, kind=okind).ap()

    xkv = din("xkv", [S, 1024])
    xq = din("xq", [SO, 1024])
    w_in = din("w_in", [1024, 7728])
    w_pd = din("w_pd", [1024, 1024])
    w_pn = din("w_pn", [1024, 1024])
    w_o = din("w_o", [1024, 1024])
    w_up = din("w_up", [1024, 4096])
    w_dn = din("w_dn", [4096, 1024])
    c_w1 = [din("c_w1k", [2048, 256]), din("c_w1v", [2048, 256])]
    c_w2 = [din("c_w2k", [256, 64]), din("c_w2v", [256, 64])]
    pos2 = [din("pos2k", [128, 16]), din("pos2v", [128, 16])]
    g_mix = din("g_mix", [1, 1024])
    g_mlp = din("g_mlp", [1, 1024])
    g_k24 = din("g_k24", [1, 1536])
    g_q32 = din("g_q32", [1, 2048])
    g_kc = din("g_kc", [1, 64])
    g_sub = din("g_sub", [1, 128])
    lam4 = din("lam4", [1, 256])
    rkv_c = din("rkv_c", [S, 32])
    rkv_s = din("rkv_s", [S, 32])
    rq_c = din("rq_c", [SO, 32])
    rq_s = din("rq_s", [SO, 32])
    rc_c = din("rc_c", [NCP, 32])
    rc_s = din("rc_s", [NCP, 32])
    identd = din("ident", [128, 128], BF16)
    eind = din("eind", [64, S], BF16)
    ovl = din("ovl", [NCP, 65], BF16)
    m_dense = din("m_dense", [128, 8, 512], BF16)
    m_win = din("m_win", [128, 12, 512], BF16)
    m_cmp = din("m_cmp", [128, NSL * NCT, 512], BF16)
    forced = din("forced", [SO, 64])
    out = nc.dram_tensor("out", [SO, 1024], F32, kind="ExternalOutput").ap()

    KdT = dscr("KdT", [8, 128, S])
    Vd = dscr("Vd", [S, 1024])
    KsT = dscr("KsT", [256, S])
    KwT = dscr("KwT", [256, S])
    kcT = dscr("kcT", [256, S])
    vcT = dscr("vcT", [256, S])
    Vs = dscr("Vs", [S, 256])
    Vw = dscr("Vw", [S, 256])
    QdT = dscr("QdT", [8, 128, SO])
    QnT = dscr("QnT", [1024, SO])
    BsT = dscr("BsT", [256, SO])
    mgS = dscr("mgS", [SO, 2048], F32)
    x1d = dscr("x1d", [SO, 1024], F32)

    with contextlib.ExitStack() as top:
        def sb(name, shape, dt, stack=top):
            return stack.enter_context(nc.sbuf_tensor(name, list(shape), dt))

        def ps(name, shape, dt, stack):
            return stack.enter_context(nc.psum_tensor(name, list(shape), dt))

        ident = sb("ident_sb", [128, 128], BF16)
        gates = sb("gates", [128, NOWN * 48], F32)
        KcT = sb("KcT", [128, 4 * NCP], BF16)
        Vca = sb("Vca", [128, NCT * 4 * 129], BF16)
        nlam = sb("nlam", [128, 1], F32)
        sgain = sb("sgain", [128, 128], F32)
        ss1 = sb("ss1", [128, 1], F32)
        rs1 = sb("rs1", [128, 1], F32)

        def bc(ap1, n):
            return ap1[0:1, :].broadcast_to([128, n])

        def rmsnorm_rows(P, xt, xkey, gt, hout, hkey, junk, sfx=""):
            P.op("pool", lambda e: e.memset(ss1[:], 0.0), writes=["ss1"])
            P.op("act", lambda e: e.activation(out=junk, in_=xt, func=AF.Square, accum_out=ss1[:]),
                 reads=[xkey, "ss1"], writes=["junk" + sfx, "ss1"])
            P.op("dve", lambda e: e.tensor_scalar(out=ss1[:], in0=ss1[:], scalar1=1.0 / 1024, scalar2=EPS,
                                                  op0=ALU.mult, op1=ALU.add), reads=["ss1"], writes=["ss1"])
            P.op("act", lambda e: e.activation(out=ss1[:], in_=ss1[:], func=AF.Ln), reads=["ss1"], writes=["ss1"])
            P.op("act", lambda e: e.activation(out=rs1[:], in_=ss1[:], func=AF.Exp, scale=-0.5),
                 reads=["ss1"], writes=["rs1"])
            P.op("dve", lambda e: e.scalar_tensor_tensor(out=hout, in0=xt, scalar=rs1[:], in1=gt,
                                                         op0=ALU.mult, op1=ALU.mult),
                 reads=[xkey, "rs1", "gt"], writes=[hkey])

        def normrope(P, src, srckey, U, gain, cos, sin, cskeys, outap, outkey, T, sfx):
            sq, ssu, rsu, xn, t1, t2, t3, t4 = T[:8]
            n = U * 64
            s3 = lambda ap: ap.rearrange("p (u d) -> p u d", d=64)
            if len(T) > 8:
                xraw = T[8]
                P.op("act", lambda e, src0=src: e.activation(out=xraw[:, 0:n], in_=src0, func=AF.Identity),
                     reads=[srckey], writes=["xraw" + sfx])
                src = xraw[:, 0:n]
                srckey = "xraw" + sfx
            P.op("act", lambda e: e.activation(out=sq[:, 0:n], in_=src, func=AF.Square),
                 reads=[srckey], writes=["sq" + sfx])
            P.op("dve", lambda e: e.tensor_reduce(out=ssu[:, 0:U], in_=s3(sq[:, 0:n]), axis=AX.X, op=ALU.add),
                 reads=["sq" + sfx], writes=["ssu" + sfx])
            P.op("dve", lambda e: e.tensor_scalar(out=ssu[:, 0:U], in0=ssu[:, 0:U], scalar1=1.0 / 64, scalar2=EPS,
                                                  op0=ALU.mult, op1=ALU.add), reads=["ssu" + sfx], writes=["ssu" + sfx])
            P.op("act", lambda e: e.activation(out=ssu[:, 0:U], in_=ssu[:, 0:U], func=AF.Ln),
                 reads=["ssu" + sfx], writes=["ssu" + sfx])
            P.op("act", lambda e: e.activation(out=rsu[:, 0:U], in_=ssu[:, 0:U], func=AF.Exp, scale=-0.5),
                 reads=["ssu" + sfx], writes=["rsu" + sfx])
            P.op("dve", lambda e: e.tensor_tensor(out=s3(xn[:, 0:n]), in0=s3(src),
                                                  in1=rsu[:, 0:U].unsqueeze(2).to_broadcast([128, U, 64]), op=ALU.mult),
                 reads=[srckey, "rsu" + sfx], writes=["xn" + sfx])
            P.op("pool", lambda e: e.tensor_tensor(out=xn[:, 0:n], in0=xn[:, 0:n], in1=gain, op=ALU.mult),
                 reads=["xn" + sfx, "gains"], writes=["xn" + sfx])
            x3 = s3(xn[:, 0:n])
            o3 = s3(outap)
            cb = cos.unsqueeze(1).to_broadcast([128, U, 32])
            sbb = sin.unsqueeze(1).to_broadcast([128, U, 32])
            h3 = lambda t: t[:, 0:U * 32].rearrange("p (u d) -> p u d", d=32)
            P.op("pool", lambda e: e.tensor_tensor(out=h3(t1), in0=x3[:, :, 0:32], in1=cb, op=ALU.mult),
                 reads=["xn" + sfx] + list(cskeys), writes=["t1" + sfx])
            P.op("pool", lambda e: e.tensor_tensor(out=h3(t2), in0=x3[:, :, 32:64], in1=sbb, op=ALU.mult),
                 reads=["xn" + sfx] + list(cskeys), writes=["t2" + sfx])
            P.op("pool", lambda e: e.tensor_tensor(out=o3[:, :, 0:32], in0=h3(t1), in1=h3(t2), op=ALU.subtract),
                 reads=["t1" + sfx, "t2" + sfx], writes=[outkey])
            P.op("dve", lambda e: e.tensor_tensor(out=h3(t3), in0=x3[:, :, 32:64], in1=cb, op=ALU.mult),
                 reads=["xn" + sfx] + list(cskeys), writes=["t3" + sfx])
            P.op("dve", lambda e: e.tensor_tensor(out=h3(t4), in0=x3[:, :, 0:32], in1=sbb, op=ALU.mult),
                 reads=["xn" + sfx] + list(cskeys), writes=["t4" + sfx])
            P.op("dve", lambda e: e.tensor_tensor(out=o3[:, :, 32:64], in0=h3(t3), in1=h3(t4), op=ALU.add),
                 reads=["t3" + sfx, "t4" + sfx], writes=[outkey])

        def load_w_cols(P, wdst, ranges, ncols):
            keys = []
            for c in range(8):
                off = 0
                kc_ = []
                for ri, (a, b) in enumerate(ranges):
                    k = ("wbuf", c, ri)
                    P.dma(wdst[:, c * ncols + off:c * ncols + off + (b - a)], w_in[c * 128:(c + 1) * 128, a:b],
                          writes=[k], eng="pool")
                    kc_.append(k)
                    off += b - a
                keys.append(kc_)
            return keys

        with contextlib.ExitStack() as sa:
            lt = sb("lt", [128, 256], F32, sa)
            pr = sb("pr", [128, 128], F32, sa)
            s2 = sb("s2", [128, 2], F32, sa)
            sgr = sb("sgr", [128, 128], F32, sa)
            P = Phase(nc, "pa")
            P.dma(ident[:], identd, writes=["ident"])
            P.dma(lt[:], bc(lam4, 256), writes=["lt"])
            P.dma(sgr[:], bc(g_sub, 128), writes=["sgr"])
            P.op("dve", lambda e: e.tensor_tensor(out=pr[:, 0:64], in0=lt[:, 0:64], in1=lt[:, 64:128], op=ALU.mult),
                 reads=["lt"], writes=["pr"])
            P.op("dve", lambda e: e.tensor_tensor(out=pr[:, 64:128], in0=lt[:, 128:192], in1=lt[:, 192:256], op=ALU.mult),
                 reads=["lt", "pr"], writes=["pr"])
            P.op("dve", lambda e: e.tensor_reduce(out=s2[:], in_=pr[:].rearrange("p (a d) -> p a d", d=64),
                                                  axis=AX.X, op=ALU.add), reads=["pr"], writes=["s2"])
            P.op("act", lambda e: e.activation(out=s2[:], in_=s2[:], func=AF.Exp), reads=["s2"], writes=["s2"])
            P.op("dve", lambda e: e.tensor_tensor(out=nlam[:], in0=s2[:, 1:2], in1=s2[:, 0:1], op=ALU.subtract),
                 reads=["s2"], writes=["nlam"])
            P.op("dve", lambda e: e.tensor_scalar(out=nlam[:], in0=nlam[:], scalar1=-0.2, scalar2=None, op0=ALU.add),
                 reads=["nlam"], writes=["nlam"])
            P.op("dve", lambda e: e.tensor_scalar(out=sgain[:], in0=sgr[:], scalar1=0.8, scalar2=None, op0=ALU.mult),
                 reads=["sgr"], writes=["sgain"])
            P.emit()

        with contextlib.ExitStack() as sbd:
            wbuf = sb("wbuf", [128, 8 * NQC], BF16, sbd)
            gmix = sb("gmix", [128, 1024], F32, sbd)
            gains = sb("gains", [128, 2048], F32, sbd)
            xts = [sb(f"xt{i}", [128, 1024], F32, sbd) for i in range(2)]
            junk = sb("junk", [128, 1024], F32, sbd)
            hb = sb("hb", [128, 1024], BF16, sbd)
            hT = [sb(f"hT{i}", [128, 1024], BF16, sbd) for i in range(2)]
            cst = [sb(f"cs{i}", [128, 64], F32, sbd) for i in range(2)]
            TT = []
            for i in range(2):
                TT.append((sb(f"sq{i}", [128, 512], F32, sbd), sb(f"ssu{i}", [128, 8], F32, sbd),
                           sb(f"rsu{i}", [128, 8], F32, sbd), sb(f"xn{i}", [128, 512], F32, sbd),
                           sb(f"t1{i}", [128, 256], F32, sbd), sb(f"t2{i}", [128, 256], F32, sbd),
                           sb(f"t3{i}", [128, 256], F32, sbd), sb(f"t4{i}", [128, 256], F32, sbd),
                           sb(f"xraw{i}", [128, 512], F32, sbd)))
            kb = sb("kb", [128, 2048], BF16, sbd)
            vb = sb("vb", [128, 2048], BF16, sbd)
            ktb = [sb(f"ktb{i}", [128, 16 * 128], BF16, sbd) for i in range(2)]
            mgt = [sb(f"mgt{i}", [128, 512], F32, sbd) for i in range(2)]
            pT = ps("pT", [128, 1024], BF16, sbd)
            pO = [ps(f"pO{i}", [128, 512], F32, sbd) for i in range(3)]
            pK = [ps(f"pK{i}", [128, 1024], BF16, sbd) for i in range(2)]

            def front_a(P, t, gt):
                xt = xts[t % 2]
                rmsnorm_rows(P, xt[:], ("xt", t % 2), gt[:], hb[:], "hb", junk[:])

            def front_b(P, t):
                for c in range(8):
                    P.op("pe", lambda e, c=c: e.transpose(out=pT[:, c * 128:(c + 1) * 128],
                                                          in_=hb[:, c * 128:(c + 1) * 128], identity=ident[:]),
                         reads=["hb", "ident"], writes=["pT"])
                P.op("dve", lambda e: e.tensor_copy(out=hT[t % 2][:], in_=pT[:]), reads=["pT"], writes=[("hT", t % 2)])

            def proj_group(P, t, j, ncols, col0, width, wkeys):
                po = pO[j % 3]
                for c in range(8):
                    P.op("pe", lambda e, c=c: e.matmul(po[:, 0:width], lhsT=hT[t % 2][:, c * 128:(c + 1) * 128],
                                                       rhs=wbuf[:, c * ncols + col0:c * ncols + col0 + width],
                                                       start=(c == 0), stop=(c == 7)),
                         reads=[("hT", t % 2)] + wkeys[c], writes=[("pO", j % 3)])
                return po

            P = Phase(nc, "pb")
            wk = load_w_cols(P, wbuf, KV_RANGES, NKV)
            P.dma(gmix[:], bc(g_mix, 1024), writes=["gt"])
            P.dma(gains[:, 0:1536], bc(g_k24, 1536), writes=["gains"])
            P.dma(xts[0][:], xkv[0:128, :], writes=[("xt", 0)])
            P.dma(xts[1][:], xkv[128:256, :], writes=[("xt", 1)])
            front_a(P, 0, gmix)
            front_b(P, 0)
            for t in range(NT):
                cs = cst[t % 2]
                P.dma(cs[:, 0:32], rkv_c[t * 128:(t + 1) * 128, :], writes=[("cs", t % 2, 0)])
                P.dma(cs[:, 32:64], rkv_s[t * 128:(t + 1) * 128, :], writes=[("cs", t % 2, 1)])
                if t + 1 < NT:
                    front_a(P, t + 1, gmix)
                for j in range(7):
                    po = proj_group(P, t, j, NKV, j * 512, 512, wk)
                    if j < 3:
                        normrope(P, po[:], ("pO", j % 3), 8, gains[:, j * 512:(j + 1) * 512], cs[:, 0:32], cs[:, 32:64],
                                 [("cs", t % 2, 0), ("cs", t % 2, 1)], kb[:, j * 512:(j + 1) * 512], ("kb", j), TT[j % 2], str(j % 2))
                    else:
                        P.op("act", lambda e, po=po, j=j: e.activation(out=vb[:, (j - 3) * 512:(j - 2) * 512], in_=po[:],
                                                                       func=AF.Identity),
                             reads=[("pO", j % 3)], writes=[("vb", j)])
                    if j == 2 and t + 1 < NT:
                        front_b(P, t + 1)
                        if t + 2 < NT:
                            P.dma(xts[t % 2][:], xkv[(t + 2) * 128:(t + 3) * 128, :], writes=[("xt", t % 2)])
                kt = ktb[t % 2]
                for blk in range(16):
                    src = kb[:, blk * 128:(blk + 1) * 128] if blk < 12 else vb[:, 1536 + (blk - 12) * 128:1536 + (blk - 11) * 128]
                    rk = [("kb", blk // 4)] if blk < 12 else [("vb", 6)]
                    P.op("pe", lambda e, src=src, blk=blk: e.transpose(out=pK[blk // 8][:, (blk % 8) * 128:(blk % 8 + 1) * 128],
                                                                        in_=src, identity=ident[:]),
                         reads=rk + ["ident"], writes=[("pK", blk // 8)])
                for hf in range(2):
                    P.op("dve", lambda e, hf=hf, kt=kt: e.tensor_copy(out=kt[:, hf * 1024:(hf + 1) * 1024], in_=pK[hf][:]),
                         reads=[("pK", hf)], writes=[("ktb", t % 2, hf)])
                tok = slice(t * 128, (t + 1) * 128)
                P.dma(KdT[:, :, tok].rearrange("h p k -> p h k"), kt[:, 0:1024].rearrange("p (h k) -> p h k", k=128),
                      reads=[("ktb", t % 2, 0)])
                for i, dst in enumerate((KsT, KwT, kcT, vcT)):
                    P.dma(dst[:, tok].rearrange("(i p) k -> p i k", p=128),
                          kt[:, 1024 + i * 256:1024 + (i + 1) * 256].rearrange("p (i k) -> p i k", k=128),
                          reads=[("ktb", t % 2, 1)])
                P.dma(Vd[tok, :], vb[:, 0:1024], reads=[("vb", 3), ("vb", 4)])
                P.dma(Vs[tok, :], vb[:, 1024:1280], reads=[("vb", 5)])
                P.dma(Vw[tok, :], vb[:, 1280:1536], reads=[("vb", 5)])
            P.emit()
            if stage <= 1:
                return nc

            P = Phase(nc, "pd")
            wk = load_w_cols(P, wbuf, Q_RANGES, NQC)
            P.dma(gains[:, 0:2048], bc(g_q32, 2048), writes=["gains"])
            P.dma(xts[0][:], xq[0:128, :], writes=[("xt", 0)])
            P.dma(xts[1][:], xq[128:256, :], writes=[("xt", 1)])
            front_a(P, 0, gmix)
            front_b(P, 0)
            for t in range(NOWN):
                cs = cst[t % 2]
                P.dma(cs[:, 0:32], rq_c[t * 128:(t + 1) * 128, :], writes=[("cs", t % 2, 0)])
                P.dma(cs[:, 32:64], rq_s[t * 128:(t + 1) * 128, :], writes=[("cs", t % 2, 1)])
                if t + 1 < NOWN:
                    front_a(P, t + 1, gmix)
                tok = slice(t * 128, (t + 1) * 128)
                for j in range(4):
                    po = proj_group(P, t, j, NQC, j * 512, 512, wk)
                    normrope(P, po[:], ("pO", j % 3), 8, gains[:, j * 512:(j + 1) * 512], cs[:, 0:32], cs[:, 32:64],
                             [("cs", t % 2, 0), ("cs", t % 2, 1)], kb[:, j * 512:(j + 1) * 512], ("kb", j), TT[j % 2], str(j % 2))
                    if j == 2 and t + 1 < NOWN:
                        front_b(P, t + 1)
                        if t + 2 < NOWN:
                            P.dma(xts[t % 2][:], xq[(t + 2) * 128:(t + 3) * 128, :], writes=[("xt", t % 2)])
                po = proj_group(P, t, 4, NQC, 2048, 48, wk)
                gsl = gates[:, t * 48:(t + 1) * 48]
                P.op("act", lambda e, po=po, gsl=gsl: e.activation(out=gsl, in_=po[:, 0:48], func=AF.Tanh, scale=0.5),
                     reads=[("pO", 1)], writes=["gates"])
                P.op("dve", lambda e, gsl=gsl: e.tensor_scalar(out=gsl, in0=gsl, scalar1=0.5, scalar2=0.5, op0=ALU.mult, op1=ALU.add),
                     reads=["gates"], writes=["gates"])
                for j in range(5, 9):
                    po = proj_group(P, t, j, NQC, 2096 + (j - 5) * 512, 512, wk)
                    mg_ = mgt[j % 2]
                    P.op("act", lambda e, po=po, mg_=mg_: e.activation(out=mg_[:], in_=po[:], func=AF.Tanh, scale=0.5),
                         reads=[("pO", j % 3)], writes=[("mgt", j % 2)])
                    P.op("pool" if j % 2 else "dve", lambda e, mg_=mg_: e.tensor_scalar(out=mg_[:], in0=mg_[:], scalar1=0.5, scalar2=0.5,
                                                                                       op0=ALU.mult, op1=ALU.add),
                         reads=[("mgt", j % 2)], writes=[("mgt", j % 2)])
                    P.dma(mgS[tok, (j - 5) * 512:(j - 4) * 512], mg_[:], reads=[("mgt", j % 2)])
                kt = ktb[t % 2]
                for blk in range(16):
                    P.op("pe", lambda e, blk=blk: e.transpose(out=pK[blk // 8][:, (blk % 8) * 128:(blk % 8 + 1) * 128],
                                                              in_=kb[:, blk * 128:(blk + 1) * 128], identity=ident[:]),
                         reads=[("kb", blk // 4), "ident"], writes=[("pK", blk // 8)])
                for hf in range(2):
                    P.op("dve", lambda e, hf=hf, kt=kt: e.tensor_copy(out=kt[:, hf * 1024:(hf + 1) * 1024], in_=pK[hf][:]),
                         reads=[("pK", hf)], writes=[("ktb", t % 2, hf)])
                P.dma(QdT[:, :, tok].rearrange("h p k -> p h k"), kt[:, 0:1024].rearrange("p (h k) -> p h k", k=128),
                      reads=[("ktb", t % 2, 0)])
                P.dma(QnT[:, tok].rearrange("(i p) k -> p i k", p=128), kt[:, 1024:2048].rearrange("p (i k) -> p i k", k=128),
                      reads=[("ktb", t % 2, 1)])
            P.emit()
        if stage <= 2:
            return nc

        ydiff = sb("ydiff", [128, NOWN * 1024], BF16)
        ynsa = sb("ynsa", [128, NOWN * 1024], BF16)
        with contextlib.ExitStack() as sc:
            W1s = [sb(f"W1s{i}", [128, 16 * 256], BF16, sc) for i in range(2)]
            W2s = [sb(f"W2s{i}", [128, 128], BF16, sc) for i in range(2)]
            p2b = [sb(f"p2b{i}", [128, 16], BF16, sc) for i in range(2)]
            kc2 = [sb(f"kc2_{i}", [128, S], BF16, sc) for i in range(2)]
            biasv = [sb(f"biasv{i}", [128, 2], F32, sc) for i in range(2)]
            h1T = sb("h1T", [128, 2 * NCP], BF16, sc)
            xg = sb("xg", [128, NCP], F32, sc)
            x2 = sb("x2", [128, NCP], F32, sc)
            th = sb("th", [128, NCP], F32, sc)
            kcn = sb("kcn", [128, 128], BF16, sc)
            gkc = sb("gkc", [128, 64], F32, sc)
            csc = sb("csc", [128, NCT * 64], F32, sc)
            TC = (sb("c_sq", [128, 64], F32, sc), sb("c_ssu", [128, 8], F32, sc), sb("c_rsu", [128, 8], F32, sc),
                  sb("c_xn", [128, 64], F32, sc), sb("c_t1", [128, 32], F32, sc), sb("c_t2", [128, 32], F32, sc),
                  sb("c_t3", [128, 32], F32, sc), sb("c_t4", [128, 32], F32, sc))
            pH = [ps(f"pH{i}", [128, 512], F32, sc) for i in range(2)]
            pB = ps("pB", [128, 512], F32, sc)
            pC = ps("pC", [128, 512], F32, sc)
            pKc = ps("pKc", [128, 1024], BF16, sc)
            P = Phase(nc, "pc")
            P.op("pool", lambda e: e.memset(h1T[:], 0.0), writes=["h1T"])
            P.op("pool", lambda e: e.memset(kcn[:], 0.0), writes=["kcn"])
            P.dma(gkc[:], bc(g_kc, 64), writes=["gains"])
            for nt in range(NCT):
                P.dma(csc[:, nt * 64:nt * 64 + 32], rc_c[nt * 128:(nt + 1) * 128, :], writes=[("csc", nt, 0)])
                P.dma(csc[:, nt * 64 + 32:nt * 64 + 64], rc_s[nt * 128:(nt + 1) * 128, :], writes=[("csc", nt, 1)])
                for g in range(4):
                    o0 = (nt * 4 + g) * 129
                    P.dma(Vca[:, o0 + 64:o0 + 129], ovl[nt * 128:(nt + 1) * 128, :], writes=[("vca_c", nt, g)])
            for kv in range(2):
                P.dma(W1s[kv][:].rearrange("p (u h) -> p u h", h=256), c_w1[kv].rearrange("(u p) h -> p u h", p=128),
                      writes=[("W1s", kv)], eng="pool")
                P.dma(W2s[kv][:].rearrange("p (a d) -> p a d", d=64), c_w2[kv].rearrange("(a p) d -> p a d", p=128),
                      writes=[("W2s", kv)], eng="pool")
                P.dma(p2b[kv][:], pos2[kv], writes=[("p2b", kv)], eng="pool")
                for half in range(2):
                    for u in range(16):
                        P.op("pe", lambda e, kv=kv, half=half, u=u: e.matmul(
                            pB[:, half:half + 1], lhsT=W1s[kv][:, u * 256 + half * 128:u * 256 + half * 128 + 128],
                            rhs=p2b[kv][:, u:u + 1], start=(u == 0), stop=(u == 15)),
                            reads=[("W1s", kv), ("p2b", kv)], writes=["pB"])
                P.op("dve", lambda e, kv=kv: e.tensor_copy(out=biasv[kv][:], in_=pB[:, 0:2]), reads=["pB"], writes=[("biasv", kv)])
                src = kcT if kv == 0 else vcT
                for g in range(4):
                    kb2 = kc2[g % 2]
                    kk = ("kc2", g % 2)
                    P.dma(kb2[0:64, :], src[g * 64:(g + 1) * 64, :], writes=[(kk, 0)])
                    P.dma(kb2[64:128, 0:S - 1], src[g * 64:(g + 1) * 64, 1:S], writes=[(kk, 1)])
                    for half in range(2):
                        for u in range(16):
                            rhs = bass.AP(kb2, 2 * u, [[S, 128], [16, NCMP]])
                            P.op("pe", lambda e, kv=kv, half=half, u=u, rhs=rhs: e.matmul(
                                pH[half][:, 0:NCMP], lhsT=W1s[kv][:, u * 256 + half * 128:u * 256 + half * 128 + 128],
                                rhs=rhs, start=(u == 0), stop=(u == 15)),
                                reads=[("W1s", kv), (kk, 0), (kk, 1)], writes=[("pH", half)])
                        hsl = h1T[:, half * NCP:half * NCP + NCMP]
                        P.op("dve", lambda e, kv=kv, half=half: e.tensor_scalar(
                            out=xg[:, 0:NCMP], in0=pH[half][:, 0:NCMP], scalar1=biasv[kv][:, half:half + 1], scalar2=None,
                            op0=ALU.add), reads=[("pH", half), ("biasv", kv)], writes=["xg"])
                        P.op("pool", lambda e: e.tensor_tensor(out=x2[:, 0:NCMP], in0=xg[:, 0:NCMP], in1=xg[:, 0:NCMP],
                                                               op=ALU.mult), reads=["xg"], writes=["x2"])
                        P.op("pool", lambda e: e.tensor_scalar(out=x2[:, 0:NCMP], in0=x2[:, 0:NCMP], scalar1=0.044715,
                                                               scalar2=1.0, op0=ALU.mult, op1=ALU.add),
                             reads=["x2"], writes=["x2"])
                        P.op("pool", lambda e: e.tensor_tensor(out=x2[:, 0:NCMP], in0=x2[:, 0:NCMP], in1=xg[:, 0:NCMP],
                                                               op=ALU.mult), reads=["x2", "xg"], writes=["x2"])
                        P.op("act", lambda e: e.activation(out=th[:, 0:NCMP], in_=x2[:, 0:NCMP], func=AF.Tanh,
                                                           scale=0.7978845608028654), reads=["x2"], writes=["th"])
                        P.op("dve", lambda e: e.tensor_scalar(out=th[:, 0:NCMP], in0=th[:, 0:NCMP], scalar1=0.5, scalar2=0.5,
                                                              op0=ALU.mult, op1=ALU.add), reads=["th"], writes=["th"])
                        P.op("dve", lambda e, hsl=hsl: e.tensor_tensor(out=hsl, in0=th[:, 0:NCMP], in1=xg[:, 0:NCMP],
                                                                        op=ALU.mult), reads=["th", "xg"], writes=["h1T"])
                    for nt in range(NCT):
                        for half in range(2):
                            P.op("pe", lambda e, kv=kv, nt=nt, half=half: e.matmul(
                                pC[:, 0:64], lhsT=h1T[:, half * NCP + nt * 128:half * NCP + nt * 128 + 128],
                                rhs=W2s[kv][:, half * 64:(half + 1) * 64], start=(half == 0), stop=(half == 1)),
                                reads=["h1T", ("W2s", kv)], writes=["pC"])
                        if kv == 0:
                            normrope(P, pC[:, 0:64], "pC", 1, gkc[:], csc[:, nt * 64:nt * 64 + 32],
                                     csc[:, nt * 64 + 32:nt * 64 + 64], [("csc", nt, 0), ("csc", nt, 1)],
                                     kcn[:, 0:64], "kcn", TC, "c")
                            P.op("pe", lambda e: e.transpose(out=pKc[:, 0:128], in_=kcn[:], identity=ident[:]),
                                 reads=["kcn", "ident"], writes=["pKc"])
                            P.op("dve", lambda e, g=g, nt=nt: e.tensor_copy(
                                out=KcT[0:64, g * NCP + nt * 128:g * NCP + nt * 128 + 128], in_=pKc[0:64, 0:128]),
                                reads=["pKc"], writes=["KcT"])
                        else:
                            o0 = (nt * 4 + g) * 129
                            P.op("act", lambda e, o0=o0: e.activation(out=Vca[:, o0:o0 + 64], in_=pC[:, 0:64], func=AF.Identity),
                                 reads=["pC"], writes=["Vca"])
            if dbg:
                dK = nc.dram_tensor("dbg_KcT", [64, 4 * NCP], BF16, kind="ExternalOutput").ap()
                dV = nc.dram_tensor("dbg_Vca", [128, NCT * 4 * 129], BF16, kind="ExternalOutput").ap()
                P.dma(dK, KcT[0:64, :], reads=["KcT"])
                P.dma(dV, Vca[:], reads=["Vca"] + [("vca_c", nt, g) for nt in range(NCT) for g in range(4)])
            P.emit()
        if stage <= 3:
            return nc

        with contextlib.ExitStack() as se:
            Qt = [sb(f"e_Qt{i}", [128, SO], BF16, se) for i in range(2)]
            mcmp = sb("e_mcmp", [128, NSL * NCT * 512], BF16, se)
            ET = [sb(f"e_ET{i}", [128, 512], BF16, se) for i in range(2)]
            imp = sb("e_imp", [128, NOWN * 64], F32, se)
            frc = sb("e_frc", [128, NOWN * 64], F32, se)
            rl = sb("e_rl", [128, 4], F32, se)
            impf = sb("e_impf", [128, 64], F32, se)
            tmpf = sb("e_tmpf", [128, 64], F32, se)
            m8 = sb("e_m8", [128, 8], F32, se)
            m8b = sb("e_m8b", [128, 8], F32, se)
            bt = sb("e_bt", [128, 128], BF16, se)
            bstage = sb("e_bstage", [128, SO], BF16, se)
            pS = [ps(f"e_pS{i}", [128, 512], F32, se) for i in range(2)]
            pU = [ps(f"e_pU{i}", [128, 512], F32, se) for i in range(2)]
            pBt = ps("e_pBt", [128, 1024], BF16, se)
            P = Phase(nc, "pe")
            P.dma(mcmp[:].rearrange("p (a q) -> p a q", q=512), m_cmp, writes=["mcmp"])
            P.dma(frc[:].rearrange("p (t j) -> p t j", j=64), forced.rearrange("(t p) j -> p t j", p=128), writes=["frc"])
            P.op("pool", lambda e: e.memset(bt[:], 0.0), writes=["bt"])
            it = 0
            for g in range(4):
                P.op("pool", lambda e: e.memset(imp[:], 0.0), writes=["imp"])
                for h in range(4):
                    hq = 4 * g + h
                    qt = Qt[hq % 2]
                    P.dma(qt[0:64, :], QnT[hq * 64:(hq + 1) * 64, :], writes=[("Qt", hq % 2)])
                    for s_ in range(NSL):
                        for nt in range(NCT):
                            i = nt % 2
                            P.op("pe", lambda e, i=i, g=g, nt=nt, s_=s_, qt=qt: e.matmul(
                                pS[i][:], lhsT=KcT[0:64, g * NCP + nt * 128:g * NCP + nt * 128 + 128],
                                rhs=qt[0:64, s_ * 512:(s_ + 1) * 512], start=True, stop=False),
                                reads=["KcT", ("Qt", hq % 2)], writes=[("pS", i)])
                            P.op("pe", lambda e, i=i, nt=nt, s_=s_: e.matmul(
                                pS[i][:], lhsT=ident[:], rhs=mcmp[:, (s_ * NCT + nt) * 512:(s_ * NCT + nt + 1) * 512],
                                start=False, stop=True), reads=["ident", "mcmp"], writes=[("pS", i)])
                        for nt in range(NCT):
                            i = nt % 2
                            P.op("act", lambda e, i=i: e.activation(out=ET[i][:], in_=pS[i][:], func=AF.Exp, scale=0.125),
                                 reads=[("pS", i)], writes=[("ET", i)])
                            for qs in range(4):
                                o0 = (nt * 4 + g) * 129
                                P.op("pe", lambda e, i=i, qs=qs, o0=o0, nt=nt: e.matmul(
                                    pU[qs // 2][:, (qs % 2) * 129:(qs % 2) * 129 + 129], lhsT=ET[i][:, qs * 128:(qs + 1) * 128],
                                    rhs=Vca[:, o0:o0 + 129], start=(nt == 0 and qs % 2 == 0), stop=(nt == NCT - 1),
                                    skip_group_check=True),
                                    reads=[("ET", i), "Vca"], writes=[("pU", qs // 2)])
                        for qs in range(4):
                            tl = s_ * 4 + qs
                            u0 = (qs % 2) * 129
                            pu = pU[qs // 2]
                            pk = ("pU", qs // 2)
                            P.op("dve", lambda e, pu=pu, u0=u0, qs=qs: e.tensor_scalar(
                                out=rl[:, qs:qs + 1], in0=pu[:, u0 + 128:u0 + 129], scalar1=1e-30, scalar2=None, op0=ALU.max),
                                reads=[pk], writes=["rl"])
                            P.op("dve", lambda e, qs=qs: e.reciprocal(out=rl[:, qs:qs + 1], in_=rl[:, qs:qs + 1]),
                                 reads=["rl"], writes=["rl"])
                            ysl = ynsa[:, tl * 1024 + hq * 64:tl * 1024 + hq * 64 + 64]
                            gcol = gates[:, tl * 48 + hq * 3:tl * 48 + hq * 3 + 1]
                            P.op("dve", lambda e, pu=pu, u0=u0, qs=qs, ysl=ysl, gcol=gcol: e.tensor_scalar(
                                out=ysl, in0=pu[:, u0:u0 + 64], scalar1=rl[:, qs:qs + 1], scalar2=gcol,
                                op0=ALU.mult, op1=ALU.mult), reads=[pk, "rl", "gates"], writes=[("ynsa", tl)])
                            isl = imp[:, tl * 64:(tl + 1) * 64]
                            P.op("dve", lambda e, pu=pu, u0=u0, qs=qs, isl=isl: e.scalar_tensor_tensor(
                                out=isl, in0=pu[:, u0 + 64:u0 + 128], scalar=rl[:, qs:qs + 1], in1=isl,
                                op0=ALU.mult, op1=ALU.add), reads=[pk, "rl", "imp"], writes=["imp"])
                for tl in range(NOWN):
                    isl = imp[:, tl * 64:(tl + 1) * 64]
                    P.op("dve", lambda e, isl=isl, tl=tl: e.tensor_tensor(out=impf[:], in0=isl, in1=frc[:, tl * 64:(tl + 1) * 64],
                                                                          op=ALU.max), reads=["imp", "frc"], writes=["impf"])
                    P.op("dve", lambda e: e.max(out=m8[:], in_=impf[:]), reads=["impf"], writes=["m8"])
                    P.op("dve", lambda e: e.match_replace(out=tmpf[:], in_to_replace=m8[:], in_values=impf[:], imm_value=-1e9),
                         reads=["m8", "impf"], writes=["tmpf"])
                    P.op("dve", lambda e: e.max(out=m8b[:], in_=tmpf[:]), reads=["tmpf"], writes=["m8b"])
                    P.op("dve", lambda e: e.tensor_scalar(out=tmpf[:], in0=impf[:], scalar1=m8b[:, 7:8], scalar2=None,
                                                          op0=ALU.is_ge), reads=["impf", "m8b", "tmpf"], writes=["tmpf"])
                    P.op("dve", lambda e: e.tensor_scalar(out=bt[:, 64:128], in0=tmpf[:], scalar1=-1.0, scalar2=-NEG,
                                                          op0=ALU.add, op1=ALU.mult), reads=["tmpf"], writes=["bt"])
                    P.op("pe", lambda e: e.transpose(out=pBt[:, 0:128], in_=bt[:], identity=ident[:]),
                         reads=["bt", "ident"], writes=["pBt"])
                    P.op("dve", lambda e, tl=tl: e.tensor_copy(out=bstage[64:128, tl * 128:(tl + 1) * 128], in_=pBt[64:128, 0:128]),
                         reads=["pBt"], writes=["bstage"])
                P.dma(BsT[g * 64:(g + 1) * 64, :], bstage[64:128, :], reads=["bstage"])
            P.emit()
        if stage <= 4:
            return nc

        with contextlib.ExitStack() as sf:
            KDa = [sb(f"f_KDa{i}", [128, S], BF16, sf) for i in range(2)]
            KDb = [sb(f"f_KDb{i}", [128, S], BF16, sf) for i in range(2)]
            VD = [sb(f"f_VD{i}", [128, NT * 129], BF16, sf) for i in range(2)]
            QD = [sb(f"f_QD{i}", [128, SO], BF16, sf) for i in range(2)]
            KS = sb("f_KS", [128, S], BF16, sf)
            KW = sb("f_KW", [128, S], BF16, sf)
            VS = sb("f_VS", [128, NT * 65], BF16, sf)
            VW = sb("f_VW", [128, NT * 65], BF16, sf)
            QS = [sb(f"f_QS{i}", [128, SO], BF16, sf) for i in range(2)]
            mden = sb("f_mden", [128, 8 * 512], BF16, sf)
            mwin = sb("f_mwin", [128, 12 * 512], BF16, sf)
            PT = [sb(f"f_PT{i}", [128, 512], BF16, sf) for i in range(4)]
            oa4 = sb("f_oa4", [128, 512], F32, sf)
            ob4 = sb("f_ob4", [128, 512], F32, sf)
            r8 = sb("f_r8", [128, 12], F32, sf)
            ssd4 = sb("f_ssd4", [128, 4], F32, sf)
            rsd4 = sb("f_rsd4", [128, 4], F32, sf)
            jk = sb("f_jk", [128, 128], F32, sf)
            r4 = sb("f_r4", [128, 4], F32, sf)
            ssd = sb("f_ssd", [128, 1], F32, sf)
            rsd = sb("f_rsd", [128, 1], F32, sf)
            tn = sb("f_tn", [128, 64], F32, sf)
            pS = [ps(f"f_pS{i}", [128, 512], F32, sf) for i in range(4)]
            pO = [ps(f"f_pO{i}", [128, 512], F32, sf) for i in range(3)]
            P = Phase(nc, "pf")
            P.dma(mden[:].rearrange("p (a q) -> p a q", q=512), m_dense, writes=["mden"])
            P.dma(mwin[:].rearrange("p (a q) -> p a q", q=512), m_win, writes=["mwin"])
            P.dma(KS[64:128, :], eind, writes=["KS_e"])
            for i in range(2):
                P.op("pool", lambda e, i=i: e.memset(VD[i][:], 1.0), writes=[("VD", i)])
                P.op("pool", lambda e, i=i: e.memset(KDa[i][64:128, :], 0.0), writes=[("KDz", i)])
                P.op("pool", lambda e, i=i: e.memset(KDb[i][0:64, :], 0.0), writes=[("KDz", i)])
            P.op("pool", lambda e: e.memset(KW[64:128, :], 0.0), writes=["KWz"])
            P.op("pool", lambda e: e.memset(VS[:], 1.0), writes=["VS"])
            P.op("pool", lambda e: e.memset(VW[:], 1.0), writes=["VW"])

            def load_v(dst, dkey, src2d, width, stride):
                d3 = dst[:].rearrange("p (t c) -> p t c", c=stride)
                s3 = src2d.rearrange("(t p) c -> p t c", p=128)
                step = 8
                for a in range(0, NT, step):
                    b_ = min(NT, a + step)
                    P.dma(d3[:, a:b_, 0:width], s3[:, a:b_, :], writes=[dkey])

            def load_diff(h):
                i = h % 2
                P.dma(KDa[i][0:64, :], KdT[h][0:64, :], writes=[("KD", i, 0)])
                P.dma(KDb[i][64:128, :], KdT[h][64:128, :], writes=[("KD", i, 1)])
                load_v(VD[i], ("VD", i), Vd[:, h * 128:(h + 1) * 128], 128, 129)
                P.dma(QD[i][:], QdT[h], writes=[("QD", i)])

            load_diff(0)
            for h in range(8):
                if h + 1 < 8:
                    load_diff(h + 1)
                bi = h % 2
                for s_ in range(NSL):
                    nkt = 8 * (s_ + 1)

                    def qk(kt, s_=s_, nkt=nkt, bi=bi):
                        i2 = kt % 2
                        masked = kt >= nkt - 8
                        mi = kt - (nkt - 8)
                        for m in range(2):
                            bk = 2 * i2 + m
                            P.op("pe", lambda e, bk=bk, m=m, kt=kt: e.matmul(
                                pS[bk][:], lhsT=(KDa if m == 0 else KDb)[bi][:, kt * 128:(kt + 1) * 128],
                                rhs=QD[bi][:, s_ * 512:(s_ + 1) * 512], start=True, stop=True),
                                reads=[("KD", bi, m), ("KDz", bi), ("QD", bi)], writes=[("pS", bk)])
                            P.op("act", lambda e, bk=bk: e.activation(out=PT[bk][:], in_=pS[bk][:], func=AF.Exp, scale=0.125),
                                 reads=[("pS", bk)], writes=[("PT", bk)])
                            if masked:
                                P.op("dve", lambda e, bk=bk, mi=mi: e.tensor_tensor(
                                    out=PT[bk][:], in0=PT[bk][:], in1=mden[:, mi * 512:(mi + 1) * 512], op=ALU.mult),
                                    reads=[("PT", bk), "mden"], writes=[("PT", bk)])

                    def pv(kt, s_=s_, nkt=nkt, bi=bi):
                        i2 = kt % 2
                        for m in range(2):
                            bk = 2 * i2 + m
                            for qs in range(4):
                                a = m * 4 + qs
                                P.op("pe", lambda e, bk=bk, qs=qs, a=a, kt=kt: e.matmul(
                                    pO[a // 3][:, (a % 3) * 129:(a % 3) * 129 + 129], lhsT=PT[bk][:, qs * 128:(qs + 1) * 128],
                                    rhs=VD[bi][:, kt * 129:kt * 129 + 129], start=(kt == 0 and a % 3 == 0), stop=(kt == nkt - 1),
                                    skip_group_check=True),
                                    reads=[("PT", bk), ("VD", bi)], writes=[("pO", a // 3)])

                    qk(0)
                    for kt in range(nkt):
                        if kt + 1 < nkt:
                            qk(kt + 1)
                        pv(kt)
                    okeys = [("pO", 0), ("pO", 1), ("pO", 2)]
                    ov = lambda a: pO[a // 3][:, (a % 3) * 129:(a % 3) * 129 + 129]
                    for qs in range(4):
                        P.op("dve", lambda e, qs=qs: e.reciprocal(out=r8[:, qs:qs + 1], in_=ov(qs)[:, 128:129]),
                             reads=okeys, writes=["r8"])
                        P.op("dve", lambda e, qs=qs: e.reciprocal(out=r8[:, 4 + qs:5 + qs], in_=ov(4 + qs)[:, 128:129]),
                             reads=okeys + ["r8"], writes=["r8"])
                    P.op("dve", lambda e: e.tensor_scalar(out=r8[:, 8:12], in0=r8[:, 4:8], scalar1=nlam[:, 0:1], scalar2=None,
                                                          op0=ALU.mult), reads=["r8", "nlam"], writes=["r8"])
                    for qs in range(4):
                        P.op("dve", lambda e, qs=qs: e.tensor_scalar(out=oa4[:, qs * 128:(qs + 1) * 128], in0=ov(qs)[:, 0:128],
                                                                     scalar1=r8[:, qs:qs + 1], scalar2=None, op0=ALU.mult),
                             reads=okeys + ["r8"], writes=["oa4"])
                        P.op("dve", lambda e, qs=qs: e.scalar_tensor_tensor(
                            out=ob4[:, qs * 128:(qs + 1) * 128], in0=ov(4 + qs)[:, 0:128], scalar=r8[:, 8 + qs:9 + qs],
                            in1=oa4[:, qs * 128:(qs + 1) * 128], op0=ALU.mult, op1=ALU.add),
                            reads=okeys + ["r8", "oa4"], writes=["ob4"])
                    P.op("pool", lambda e: e.memset(ssd4[:], 0.0), writes=["ssd4"])
                    for qs in range(4):
                        P.op("act", lambda e, qs=qs: e.activation(out=jk[:], in_=ob4[:, qs * 128:(qs + 1) * 128], func=AF.Square,
                                                                  accum_out=ssd4[:, qs:qs + 1]),
                             reads=["ob4", "ssd4"], writes=["jk", "ssd4"])
                    P.op("dve", lambda e: e.tensor_scalar(out=ssd4[:], in0=ssd4[:], scalar1=1.0 / 128, scalar2=EPS,
                                                          op0=ALU.mult, op1=ALU.add), reads=["ssd4"], writes=["ssd4"])
                    P.op("act", lambda e: e.activation(out=ssd4[:], in_=ssd4[:], func=AF.Ln), reads=["ssd4"], writes=["ssd4"])
                    P.op("act", lambda e: e.activation(out=rsd4[:], in_=ssd4[:], func=AF.Exp, scale=-0.5),
                         reads=["ssd4"], writes=["rsd4"])
                    for qs in range(4):
                        tl = s_ * 4 + qs
                        ysl = ydiff[:, tl * 1024 + h * 128:tl * 1024 + (h + 1) * 128]
                        P.op("dve", lambda e, ysl=ysl, qs=qs: e.scalar_tensor_tensor(
                            out=ysl, in0=ob4[:, qs * 128:(qs + 1) * 128], scalar=rsd4[:, qs:qs + 1], in1=sgain[:],
                            op0=ALU.mult, op1=ALU.mult), reads=["ob4", "rsd4", "sgain"], writes=[("ydiff", tl)])

            cnt = [0]
            ucnt = [0]
            for g in range(4):
                P.dma(KS[0:64, :], KsT[g * 64:(g + 1) * 64, :], writes=["KS"])
                P.dma(KW[0:64, :], KwT[g * 64:(g + 1) * 64, :], writes=["KW"])
                load_v(VS, "VS", Vs[:, g * 64:(g + 1) * 64], 64, 65)
                load_v(VW, "VW", Vw[:, g * 64:(g + 1) * 64], 64, 65)
                for h in range(4):
                    hq = 4 * g + h
                    qi = hq % 2
                    P.dma(QS[qi][0:64, :], QnT[hq * 64:(hq + 1) * 64, :], writes=[("QS", qi, 0)])
                    P.dma(QS[qi][64:128, :], BsT[g * 64:(g + 1) * 64, :], writes=[("QS", qi, 1)])
                    for kind in (2, 1):
                        for s_ in range(NSL):
                            if kind == 1:
                                kts = list(range(0, 8 * (s_ + 1)))
                                mis = [kt - 8 * s_ if kt >= 8 * s_ else None for kt in kts]
                            else:
                                kts = list(range(max(0, 8 * s_ - 4), 8 * s_ + 8))
                                mis = [kt - (8 * s_ - 4) for kt in kts]
                            banks = []
                            for _ in kts:
                                banks.append(cnt[0] % 4)
                                cnt[0] += 1
                            ob_i = ucnt[0] % 3
                            ucnt[0] += 1

                            def qk(j, kind=kind, s_=s_, kts=kts, mis=mis, banks=banks, qi=qi):
                                kt, mi, bk = kts[j], mis[j], banks[j]
                                if kind == 1:
                                    lhsT = KS[:, kt * 128:(kt + 1) * 128]
                                    rhs = QS[qi][:, s_ * 512:(s_ + 1) * 512]
                                    rd = ["KS", "KS_e", ("QS", qi, 0), ("QS", qi, 1)]
                                    mt = mden
                                else:
                                    lhsT = KW[:, kt * 128:(kt + 1) * 128]
                                    rhs = QS[qi][:, s_ * 512:(s_ + 1) * 512]
                                    rd = ["KW", "KWz", ("QS", qi, 0), ("QS", qi, 1)]
                                    mt = mwin
                                P.op("pe", lambda e: e.matmul(pS[bk][:], lhsT=lhsT, rhs=rhs, start=True, stop=True),
                                     reads=rd, writes=[("pS", bk)])
                                P.op("act", lambda e: e.activation(out=PT[bk][:], in_=pS[bk][:], func=AF.Exp, scale=0.125),
                                     reads=[("pS", bk)], writes=[("PT", bk)])
                                if mi is not None:
                                    P.op("dve", lambda e: e.tensor_tensor(out=PT[bk][:], in0=PT[bk][:], in1=mt[:, mi * 512:(mi + 1) * 512],
                                                                          op=ALU.mult),
                                         reads=[("PT", bk), "mden", "mwin"], writes=[("PT", bk)])

                            def pv(j, kind=kind, kts=kts, banks=banks, ob_i=ob_i):
                                kt, bk = kts[j], banks[j]
                                vt, vk = (VS, "VS") if kind == 1 else (VW, "VW")
                                for qs in range(4):
                                    P.op("pe", lambda e, qs=qs: e.matmul(
                                        pO[ob_i][:, qs * 65:qs * 65 + 65], lhsT=PT[bk][:, qs * 128:(qs + 1) * 128],
                                        rhs=vt[:, kt * 65:kt * 65 + 65], start=(j == 0 and qs == 0), stop=(j == len(kts) - 1),
                                        skip_group_check=True),
                                        reads=[("PT", bk), vk], writes=[("pO", ob_i)])

                            qk(0)
                            if len(kts) > 1:
                                qk(1)
                            for j in range(len(kts)):
                                if j + 2 < len(kts):
                                    qk(j + 2)
                                pv(j)
                            for qs in range(4):
                                tl = s_ * 4 + qs
                                oq = pO[ob_i][:, qs * 65:qs * 65 + 65]
                                P.op("dve", lambda e, oq=oq, qs=qs: e.tensor_scalar(
                                    out=r4[:, qs:qs + 1], in0=oq[:, 64:65], scalar1=1e-30, scalar2=None, op0=ALU.max),
                                    reads=[("pO", ob_i)], writes=["r4"])
                                P.op("dve", lambda e, qs=qs: e.reciprocal(out=r4[:, qs:qs + 1], in_=r4[:, qs:qs + 1]),
                                     reads=["r4"], writes=["r4"])
                                gcol = gates[:, tl * 48 + hq * 3 + kind:tl * 48 + hq * 3 + kind + 1]
                                P.op("dve", lambda e, oq=oq, qs=qs, gcol=gcol: e.tensor_scalar(
                                    out=tn[:], in0=oq[:, 0:64], scalar1=r4[:, qs:qs + 1], scalar2=gcol,
                                    op0=ALU.mult, op1=ALU.mult), reads=[("pO", ob_i), "r4", "gates"], writes=["tn"])
                                ysl = ynsa[:, tl * 1024 + hq * 64:tl * 1024 + hq * 64 + 64]
                                P.op("pool", lambda e, ysl=ysl: e.tensor_tensor(out=ysl, in0=ysl, in1=tn[:], op=ALU.add),
                                     reads=["tn", ("ynsa", tl)], writes=[("ynsa", tl)])
            if dbg:
                dY = nc.dram_tensor("dbg_ydiff", [128, NOWN * 1024], BF16, kind="ExternalOutput").ap()
                dN = nc.dram_tensor("dbg_ynsa", [128, NOWN * 1024], BF16, kind="ExternalOutput").ap()
                P.dma(dY, ydiff[:], reads=[("ydiff", t) for t in range(NOWN)])
                P.dma(dN, ynsa[:], reads=[("ynsa", t) for t in range(NOWN)])
            P.emit()
        if stage <= 5:
            return nc

        with contextlib.ExitStack() as sg:
            Wpd = sb("g_Wpd", [128, 8 * 1024], BF16, sg)
            Wpn = sb("g_Wpn", [128, 8 * 1024], BF16, sg)
            Wo = sb("g_Wo", [128, 8 * 1024], BF16, sg)
            ydT = sb("g_ydT", [128, 1024], BF16, sg)
            ynT = sb("g_ynT", [128, 1024], BF16, sg)
            mgl = [sb(f"g_mgl{i}", [128, 2048], F32, sg) for i in range(2)]
            xql = [sb(f"g_xql{i}", [128, 1024], F32, sg) for i in range(2)]
            m1 = sb("g_m1", [128, 1024], F32, sg)
            m2 = sb("g_m2", [128, 1024], F32, sg)
            mixb = sb("g_mixb", [128, 1024], BF16, sg)
            mixT = sb("g_mixT", [128, 1024], BF16, sg)
            x1t = [sb(f"g_x1t{i}", [128, 1024], F32, sg) for i in range(2)]
            pTa = ps("g_pTa", [128, 1024], BF16, sg)
            pTb = ps("g_pTb", [128, 1024], BF16, sg)
            pD = [ps(f"g_pD{i}", [128, 512], F32, sg) for i in range(2)]
            pN = [ps(f"g_pN{i}", [128, 512], F32, sg) for i in range(2)]
            pW = [ps(f"g_pW{i}", [128, 512], F32, sg) for i in range(2)]
            P = Phase(nc, "pg")
            for (wd, wsrc, wk_) in ((Wpd, w_pd, "Wpd"), (Wpn, w_pn, "Wpn"), (Wo, w_o, "Wo")):
                for hc in range(2):
                    P.dma(wd[:, hc * 4096:(hc + 1) * 4096].rearrange("p (c n) -> p c n", n=1024),
                          wsrc[hc * 512:(hc + 1) * 512, :].rearrange("(c p) n -> p c n", p=128), writes=[(wk_, hc)], eng="pool")
            for t in range(NOWN):
                tok = slice(t * 128, (t + 1) * 128)
                mg_ = mgl[t % 2]
                xq_ = xql[t % 2]
                P.dma(mg_[:], mgS[tok, :], writes=[("mgl", t % 2)])
                P.dma(xq_[:], xq[tok, :], writes=[("xql", t % 2)])
                for c in range(8):
                    P.op("pe", lambda e, c=c, t=t: e.transpose(out=pTa[:, c * 128:(c + 1) * 128],
                                                              in_=ydiff[:, t * 1024 + c * 128:t * 1024 + (c + 1) * 128], identity=ident[:]),
                         reads=["ident"], writes=["pTa"])
                P.op("dve", lambda e: e.tensor_copy(out=ydT[:], in_=pTa[:]), reads=["pTa"], writes=["ydT"])
                for c in range(8):
                    P.op("pe", lambda e, c=c, t=t: e.transpose(out=pTb[:, c * 128:(c + 1) * 128],
                                                              in_=ynsa[:, t * 1024 + c * 128:t * 1024 + (c + 1) * 128], identity=ident[:]),
                         reads=["ident"], writes=["pTb"])
                P.op("dve", lambda e: e.tensor_copy(out=ynT[:], in_=pTb[:]), reads=["pTb"], writes=["ynT"])
                for hf in range(2):
                    for c in range(8):
                        P.op("pe", lambda e, c=c, hf=hf: e.matmul(pD[hf][:], lhsT=ydT[:, c * 128:(c + 1) * 128],
                                                                  rhs=Wpd[:, c * 1024 + hf * 512:c * 1024 + (hf + 1) * 512],
                                                                  start=(c == 0), stop=(c == 7)),
                             reads=["ydT", ("Wpd", c // 4)], writes=[("pD", hf)])
                    for c in range(8):
                        P.op("pe", lambda e, c=c, hf=hf: e.matmul(pN[hf][:], lhsT=ynT[:, c * 128:(c + 1) * 128],
                                                                  rhs=Wpn[:, c * 1024 + hf * 512:c * 1024 + (hf + 1) * 512],
                                                                  start=(c == 0), stop=(c == 7)),
                             reads=["ynT", ("Wpn", c // 4)], writes=[("pN", hf)])
                    P.op("dve", lambda e, hf=hf, mg_=mg_: e.tensor_tensor(out=m1[:, hf * 512:(hf + 1) * 512], in0=pD[hf][:],
                                                                          in1=mg_[:, hf * 512:(hf + 1) * 512], op=ALU.mult),
                         reads=[("pD", hf), ("mgl", t % 2)], writes=[("m1", hf)])
                    P.op("dve", lambda e, hf=hf, mg_=mg_: e.tensor_tensor(out=m2[:, hf * 512:(hf + 1) * 512], in0=pN[hf][:],
                                                                          in1=mg_[:, 1024 + hf * 512:1024 + (hf + 1) * 512], op=ALU.mult),
                         reads=[("pN", hf), ("mgl", t % 2)], writes=[("m2", hf)])
                    P.op("pool", lambda e, hf=hf: e.tensor_tensor(out=mixb[:, hf * 512:(hf + 1) * 512], in0=m1[:, hf * 512:(hf + 1) * 512],
                                                                  in1=m2[:, hf * 512:(hf + 1) * 512], op=ALU.add),
                         reads=[("m1", hf), ("m2", hf)], writes=[("mixb", hf)])
                for c in range(8):
                    P.op("pe", lambda e, c=c: e.transpose(out=pTa[:, c * 128:(c + 1) * 128], in_=mixb[:, c * 128:(c + 1) * 128],
                                                          identity=ident[:]),
                         reads=[("mixb", c // 4), "ident"], writes=["pTa"])
                P.op("dve", lambda e: e.tensor_copy(out=mixT[:], in_=pTa[:]), reads=["pTa"], writes=["mixT"])
                x1_ = x1t[t % 2]
                for hf in range(2):
                    for c in range(8):
                        P.op("pe", lambda e, c=c, hf=hf: e.matmul(pW[hf][:], lhsT=mixT[:, c * 128:(c + 1) * 128],
                                                                  rhs=Wo[:, c * 1024 + hf * 512:c * 1024 + (hf + 1) * 512],
                                                                  start=(c == 0), stop=(c == 7)),
                             reads=["mixT", ("Wo", c // 4)], writes=[("pW", hf)])
                    P.op("dve", lambda e, hf=hf, x1_=x1_, xq_=xq_: e.tensor_tensor(out=x1_[:, hf * 512:(hf + 1) * 512], in0=pW[hf][:],
                                                                                  in1=xq_[:, hf * 512:(hf + 1) * 512], op=ALU.add),
                         reads=[("pW", hf), ("xql", t % 2)], writes=[("x1t", t % 2, hf)])
                P.dma(x1d[tok, :], x1_[:], reads=[("x1t", t % 2, 0), ("x1t", t % 2, 1)])
            P.emit()
        if stage <= 6:
            return nc
    with contextlib.ExitStack() as top2:
        def sb2(name, shape, dt):
            return top2.enter_context(nc.sbuf_tensor(name, list(shape), dt))

        ident2 = sb2("ident2", [128, 128], BF16)
        Wup = sb2("h_Wup", [128, 8 * 4096], BF16)
        Wdn = sb2("h_Wdn", [128, 32 * 1024], BF16)
        gml = sb2("h_gml", [128, 1024], F32)
        x1g = sb2("h_x1g", [128, 4 * 1024], F32)
        junk2 = sb2("h_junk", [128, 1024], F32)
        h2 = sb2("h_h2", [128, 1024], BF16)
        h2T = sb2("h_h2T", [128, 8 * 512], BF16)
        uT = sb2("h_uT", [128, 32 * 512], BF16)
        rr = [sb2(f"h_rr{i}", [128, 512], F32) for i in range(2)]
        ot = [sb2(f"h_ot{i}", [128, 512], F32) for i in range(2)]
        ss2 = sb2("h_ss", [128, 1], F32)
        rs2 = sb2("h_rs", [128, 1], F32)
        pT2 = top2.enter_context(nc.psum_tensor("h_pT", [128, 1024], BF16))
        pU2 = [top2.enter_context(nc.psum_tensor(f"h_pU{i}", [128, 512], F32)) for i in range(2)]
        pDn = [top2.enter_context(nc.psum_tensor(f"h_pDn{i}", [128, 512], F32)) for i in range(2)]
        P = Phase(nc, "ph")
        P.dma(ident2[:], identd, writes=["ident"])
        P.dma(gml[:], g_mlp[0:1, :].broadcast_to([128, 1024]), writes=["gt"])
        for c in range(8):
            P.dma(Wup[:, c * 4096:(c + 1) * 4096], w_up[c * 128:(c + 1) * 128, :], writes=[("Wup", c)], eng="pool")
        for f4 in range(8):
            P.dma(Wdn[:, f4 * 4096:(f4 + 1) * 4096].rearrange("p (a n) -> p a n", n=1024),
                  w_dn[f4 * 512:(f4 + 1) * 512, :].rearrange("(a p) n -> p a n", p=128), writes=[("Wdn", f4)], eng="pool")
        for gq in range(NSL):
            rows = slice(gq * 512, (gq + 1) * 512)
            P.dma(x1g[:].rearrange("p (t n) -> p t n", n=1024), x1d[rows, :].rearrange("(t p) n -> p t n", p=128), writes=["x1g"])
            for j in range(4):
                xt_ = x1g[:, j * 1024:(j + 1) * 1024]
                P.op("pool", lambda e: e.memset(ss2[:], 0.0), writes=["ss2"])
                P.op("act", lambda e, xt_=xt_: e.activation(out=junk2[:], in_=xt_, func=AF.Square, accum_out=ss2[:]),
                     reads=["x1g", "ss2"], writes=["junk2", "ss2"])
                P.op("dve", lambda e: e.tensor_scalar(out=ss2[:], in0=ss2[:], scalar1=1.0 / 1024, scalar2=EPS,
                                                      op0=ALU.mult, op1=ALU.add), reads=["ss2"], writes=["ss2"])
                P.op("act", lambda e: e.activation(out=ss2[:], in_=ss2[:], func=AF.Ln), reads=["ss2"], writes=["ss2"])
                P.op("act", lambda e: e.activation(out=rs2[:], in_=ss2[:], func=AF.Exp, scale=-0.5), reads=["ss2"], writes=["rs2"])
                P.op("dve", lambda e, xt_=xt_: e.scalar_tensor_tensor(out=h2[:], in0=xt_, scalar=rs2[:], in1=gml[:],
                                                                     op0=ALU.mult, op1=ALU.mult),
                     reads=["x1g", "rs2", "gt"], writes=["h2"])
                for c in range(8):
                    P.op("pe", lambda e, c=c: e.transpose(out=pT2[:, c * 128:(c + 1) * 128], in_=h2[:, c * 128:(c + 1) * 128],
                                                          identity=ident2[:]), reads=["h2", "ident"], writes=["pT2"])
                P.op("dve", lambda e, j=j: e.tensor_copy(
                    out=h2T[:].rearrange("p (c q) -> p c q", q=512)[:, :, j * 128:(j + 1) * 128],
                    in_=pT2[:].rearrange("p (c q) -> p c q", q=128)), reads=["pT2"], writes=["h2T"])
            for f in range(32):
                pu = pU2[f % 2]
                for c in range(8):
                    P.op("pe", lambda e, c=c, f=f, pu=pu: e.matmul(pu[:], lhsT=Wup[:, c * 4096 + f * 128:c * 4096 + (f + 1) * 128],
                                                                  rhs=h2T[:, c * 512:(c + 1) * 512], start=(c == 0), stop=(c == 7)),
                         reads=[("Wup", c), "h2T"], writes=[("pU2", f % 2)])
                r_ = rr[f % 2]
                P.op("act", lambda e, pu=pu, r_=r_: e.activation(out=r_[:], in_=pu[:], func=AF.Relu),
                     reads=[("pU2", f % 2)], writes=[("rr", f % 2)])
                P.op("pool" if f % 2 else "dve", lambda e, f=f, r_=r_: e.tensor_tensor(out=uT[:, f * 512:(f + 1) * 512], in0=r_[:], in1=r_[:],
                                                                                     op=ALU.mult),
                     reads=[("rr", f % 2)], writes=[("uT", f)])
            for j in range(4):
                for hf in range(2):
                    i = (j * 2 + hf) % 2
                    for f in range(32):
                        P.op("pe", lambda e, f=f, j=j, hf=hf, i=i: e.matmul(
                            pDn[i][:], lhsT=uT[:, f * 512 + j * 128:f * 512 + (j + 1) * 128],
                            rhs=Wdn[:, f * 1024 + hf * 512:f * 1024 + (hf + 1) * 512], start=(f == 0), stop=(f == 31)),
                            reads=[("uT", f), ("Wdn", f // 4)], writes=[("pDn", i)])
                    o_ = ot[i]
                    P.op("dve", lambda e, i=i, o_=o_, j=j, hf=hf: e.tensor_tensor(
                        out=o_[:], in0=pDn[i][:], in1=x1g[:, j * 1024 + hf * 512:j * 1024 + (hf + 1) * 512], op=ALU.add),
                        reads=[("pDn", i), "x1g"], writes=[("ot", i)])
                    P.dma(out[gq * 512 + j * 128:gq * 512 + (j + 1) * 128, hf * 512:(hf + 1) * 512], o_[:], reads=[("ot", i)])
        P.emit()
    return nc


def _rope_tab(pos):
    inv = (10000.0 ** (-(np.arange(32, dtype=np.float32)) / np.float32(32))).astype(np.float32)
    ang = pos.astype(np.float32)[:, None] * inv[None, :]
    return np.cos(ang).astype(np.float32), np.sin(ang).astype(np.float32)


def make_core_inputs(inp, core, S):
    b, par = core // 2, core % 2
    NSL = S // 1024
    SO = NSL * 512
    NCMP = (S - 32) // 16 + 1
    NCT = (NCMP + 127) // 128
    NCP = NCT * 128
    f = lambda a: np.ascontiguousarray(np.asarray(a, dtype=np.float32))
    x = np.asarray(inp["x"], dtype=np.float32)
    own_pos = np.concatenate([np.arange((2 * s + par) * 512, (2 * s + par) * 512 + 512) for s in range(NSL)])
    d = {}
    d["xkv"] = f(x[b])
    d["xq"] = f(x[b][own_pos])
    d["w_in"] = f(inp["w_in"][0])
    d["w_pd"] = f(inp["w_proj_diff"][0])
    d["w_pn"] = f(inp["w_proj_nsa"][0])
    d["w_o"] = f(inp["w_out"][0])
    d["w_up"] = f(inp["w_mlp_up"][0])
    d["w_dn"] = f(inp["w_mlp_down"][0])
    d["c_w1k"] = f(inp["cmp_k_w1"][0])
    d["c_w1v"] = f(inp["cmp_v_w1"][0])
    d["c_w2k"] = f(inp["cmp_k_w2"][0])
    d["c_w2v"] = f(inp["cmp_v_w2"][0])
    p2 = lambda p: f(np.asarray(p, np.float32).reshape(16, 2, 64).transpose(1, 2, 0).reshape(128, 16))
    d["pos2k"] = p2(inp["cmp_pos_k"][0])
    d["pos2v"] = p2(inp["cmp_pos_v"][0])
    d["g_mix"] = f(inp["ln_mix_g"][0][None, :])
    d["g_mlp"] = f(inp["ln_mlp_g"][0][None, :])
    gk = np.asarray(inp["nsa_k_norm_g"][0], np.float32)
    d["g_k24"] = f(np.concatenate([np.tile(np.asarray(inp["diff_k_norm_g"][0], np.float32), 16),
                                   np.tile(gk[1], 4), np.tile(gk[2], 4)])[None, :])
    d["g_q32"] = f(np.concatenate([np.tile(np.asarray(inp["diff_q_norm_g"][0], np.float32), 16),
                                   np.tile(np.asarray(inp["nsa_q_norm_g"][0], np.float32), 16)])[None, :])
    d["g_kc"] = f(gk[0][None, :])
    d["g_sub"] = f(inp["diff_subln_g"][0][None, :])
    d["lam4"] = f(np.concatenate([np.asarray(inp[k][0], np.float32) for k in
                                  ("diff_lambda_q1", "diff_lambda_k1", "diff_lambda_q2", "diff_lambda_k2")])[None, :])
    d["rkv_c"], d["rkv_s"] = _rope_tab(np.arange(S))
    d["rq_c"], d["rq_s"] = _rope_tab(own_pos)
    cc = np.zeros(NCP, np.float32)
    cc[:NCMP] = np.arange(NCMP) * 16 + 15.5
    d["rc_c"], d["rc_s"] = _rope_tab(cc)
    d["ident"] = np.eye(128, dtype=np.float32).astype(NPBF)
    kk = np.arange(S)
    d["eind"] = (kk[None, :] // 64 == np.arange(64)[:, None]).astype(np.float32).astype(NPBF)
    n = np.arange(NCP)
    cs, ce = n * 16, n * 16 + 31
    ss_ = np.arange(64) * 64
    ov = ((cs[:, None] < ss_[None, :] + 64) & (ce[:, None] >= ss_[None, :]) & (n[:, None] < NCMP)).astype(np.float32)
    d["ovl"] = np.concatenate([ov, np.ones((NCP, 1), np.float32)], axis=1).astype(NPBF)
    k128 = np.arange(128)[:, None, None]
    q512 = np.arange(512)[None, None, :]
    kw = np.arange(8)[None, :, None] * 128 + k128
    d["m_dense"] = np.where((kw - par * 512) <= q512, 1.0, 0.0).astype(np.float32).astype(NPBF)
    kw = np.arange(12)[None, :, None] * 128 + k128 - 512
    dd = par * 512 + q512 - kw
    d["m_win"] = np.where((dd >= 0) & (dd < 512), 1.0, 0.0).astype(np.float32).astype(NPBF)
    mc = np.zeros((128, NSL * NCT, 512), np.float32)
    for s in range(NSL):
        for nt in range(NCT):
            nn = nt * 128 + np.arange(128)[:, None]
            qpos = (2 * s + par) * 512 + np.arange(512)[None, :]
            mc[:, s * NCT + nt, :] = np.where((nn < NCMP) & (nn * 16 + 31 <= qpos), 0.0, NEG)
    d["m_cmp"] = mc.astype(NPBF)
    fo = np.zeros((SO, 64), np.float32)
    cur = own_pos // 64
    r = np.arange(SO)
    fo[r[cur >= 1], (cur - 1)[cur >= 1]] = 1e4
    fo[r, cur] = 2e4
    fo[:, 0] = 3e4
    d["forced"] = fo
    return d


_NC_CACHE = {}


def kernel(**inputs):
    S = int(np.asarray(inputs["x"]).shape[1])
    B = int(np.asarray(inputs["x"]).shape[0])
    ncores = 2 * B
    if S not in _NC_CACHE:
        _NC_CACHE[S] = build(S)
    nc = _NC_CACHE[S]
    in_maps = [make_core_inputs(inputs, c, S) for c in range(ncores)]
    res = run_bass_kernel_spmd(nc, in_maps, core_ids=list(range(ncores)))
    out = np.zeros((B, S, 1024), np.float32)
    NSL = S // 1024
    for c in range(ncores):
        b, par = c // 2, c % 2
        o = np.asarray(res.results[c]["out"], dtype=np.float32)
        for s in range(NSL):
            ch = 2 * s + par
            out[b, ch * 512:(ch + 1) * 512] = o[s * 512:(s + 1) * 512]
    return out
```
